# Optimizing a Trainium2 kernel written in Bass

```python
import jax, jax.numpy as jnp
from jax import lax
import numpy as np

D_MODEL = 2048
BATCH = 8
SEQ = 2048
DEPTH = 1

N_MEM = 256
NORM_EPS = 1e-6
NEG_INF = -1e30
A_HEADS = 12
A_KV_HEADS = 4
A_HEAD_DIM = 64
A_WIDTH = A_HEADS * A_HEAD_DIM
A_KV_WIDTH = A_KV_HEADS * A_HEAD_DIM
WINDOW = 128
BLOCK = 128
ROPE_THETA = 10000.0
R_HEADS = 12
R_HEAD_DIM = 64
R_WIDTH = R_HEADS * R_HEAD_DIM
DECAY_LORA = 64
ICLR_LORA = 64
N_DIR = 2
GN_EPS = 64e-5
R_SHIFT_WIDTH = 3 * R_WIDTH + N_DIR * DECAY_LORA + N_DIR * ICLR_LORA
X_HEADS = 4
X_HEAD_DIM = 128
X_WIDTH = X_HEADS * X_HEAD_DIM
N_BRANCH = 3
IN_WIDTH = (2 * A_WIDTH + 2 * A_KV_WIDTH + R_SHIFT_WIDTH + R_WIDTH
            + 2 * X_WIDTH + N_BRANCH * D_MODEL)

kernel_name = "hybrid_swa_rwkv7_memxattn_gated_encoder"


def _split(t, sizes):
    idx = np.cumsum(sizes)[:-1].tolist()
    return jnp.split(t, idx, axis=-1)


def rms_norm(t, g, eps=NORM_EPS):
    tf = t.astype(jnp.float32)
    y = tf * lax.rsqrt(jnp.mean(tf * tf, axis=-1, keepdims=True) + eps)
    return (y * g.astype(jnp.float32)).astype(t.dtype)


def rope(t, positions):
    half = t.shape[-1] // 2
    inv = ROPE_THETA ** (-jnp.arange(half, dtype=jnp.float32) / half)
    ang = positions.astype(jnp.float32)[:, None] * inv[None, :]
    cos = jnp.cos(ang)[None, :, None, :]
    sin = jnp.sin(ang)[None, :, None, :]
    t1 = t[..., :half].astype(jnp.float32)
    t2 = t[..., half:].astype(jnp.float32)
    return jnp.concatenate([t1 * cos - t2 * sin, t2 * cos + t1 * sin], axis=-1).astype(t.dtype)


def windowed_gqa(q, k, v, sink):
    B, S, Hq, D = q.shape
    nb = S // BLOCK
    G = Hq // A_KV_HEADS
    qb = q.reshape(B, nb, BLOCK, A_KV_HEADS, G, D)
    pad = ((0, 0), (BLOCK, BLOCK), (0, 0), (0, 0))
    kp = jnp.pad(k, pad).reshape(B, nb + 2, BLOCK, A_KV_HEADS, D)
    vp = jnp.pad(v, pad).reshape(B, nb + 2, BLOCK, A_KV_HEADS, D)
    kb = jnp.concatenate([kp[:, :-2], kp[:, 1:-1], kp[:, 2:]], axis=2)
    vb = jnp.concatenate([vp[:, :-2], vp[:, 1:-1], vp[:, 2:]], axis=2)
    s = jnp.einsum('bnqhgd,bnkhd->bnhgqk', qb, kb).astype(jnp.float32) * (D ** -0.5)
    blk = jnp.arange(nb)[:, None] * BLOCK
    qpos = blk + jnp.arange(BLOCK)[None, :]
    kpos = blk - BLOCK + jnp.arange(3 * BLOCK)[None, :]
    rel = kpos[:, None, :] - qpos[:, :, None]
    valid = (jnp.abs(rel) <= WINDOW) & (kpos[:, None, :] >= 0) & (kpos[:, None, :] < S)
    s = jnp.where(valid[None, :, None, None, :, :], s, NEG_INF)
    sink_l = sink.astype(jnp.float32).reshape(A_KV_HEADS, G)[None, None, :, :, None, None]
    sink_l = jnp.broadcast_to(sink_l, s.shape[:-1] + (1,))
    p = jax.nn.softmax(jnp.concatenate([s, sink_l], axis=-1), axis=-1)[..., :-1]
    o = jnp.einsum('bnhgqk,bnkhd->bnqhgd', p.astype(v.dtype), vb)
    return o.reshape(B, S, Hq * D)


def rwkv7_scan(r, w, k, v, kk, a):
    B, T, H, N = r.shape
    decay = jnp.exp(-jnp.exp(w))
    b = kk * a
    xs = tuple(t.swapaxes(0, 1) for t in (r, decay, k, v, kk, b))

    def step(S, inp):
        r_t, d_t, k_t, v_t, kk_t, b_t = inp
        sa = jnp.einsum('bhij,bhj->bhi', S, -kk_t)
        S = S * d_t[:, :, None, :] + sa[..., None] * b_t[:, :, None, :] + v_t[..., None] * k_t[:, :, None, :]
        return S, jnp.einsum('bhij,bhj->bhi', S, r_t)

    S0 = jnp.zeros((B, H, N, N), jnp.float32)
    _, y = lax.scan(step, S0, xs)
    return y.swapaxes(0, 1)


def rwkv7_branch(p_shift, mu, k_k, k_a, r_k, w0, w2, a0, a2, ln_w, ln_b):
    B, T, _ = p_shift.shape
    p = p_shift.astype(jnp.float32)
    prev = jnp.pad(p, ((0, 0), (1, 0), (0, 0)))[:, :-1]
    nxt = jnp.pad(p, ((0, 0), (0, 1), (0, 0)))[:, 1:]
    p = p + mu.astype(jnp.float32) * (0.5 * (prev + nxt) - p)
    r, k, v, wf, wb, af, ab = _split(p, (R_WIDTH, R_WIDTH, R_WIDTH, DECAY_LORA, DECAY_LORA,
                                         ICLR_LORA, ICLR_LORA))
    hs = (B, T, R_HEADS, R_HEAD_DIM)
    r4 = r.reshape(hs)
    v4 = v.reshape(hs)
    kk = (k * k_k.astype(jnp.float32)).reshape(hs)
    kk = kk / jnp.maximum(jnp.linalg.norm(kk, axis=-1, keepdims=True), 1e-12)
    w_in = (wf, wb)
    a_in = (af, ab)
    y_sum = jnp.zeros(hs, jnp.float32)
    bonus = jnp.zeros(hs, jnp.float32)
    r_k4 = r_k.astype(jnp.float32)[None, None]
    for d in range(N_DIR):
        wd = -jax.nn.softplus(-(w0[d].astype(jnp.float32)
                                + jnp.tanh(w_in[d]) @ w2[d].astype(jnp.float32))) - 0.5
        ad = jax.nn.sigmoid(a0[d].astype(jnp.float32) + a_in[d] @ a2[d].astype(jnp.float32))
        kd = k * (1.0 + (ad - 1.0) * k_a.astype(jnp.float32))
        wd4, ad4, kd4 = wd.reshape(hs), ad.reshape(hs), kd.reshape(hs)
        if d == 0:
            y_sum = y_sum + rwkv7_scan(r4, wd4, kd4, v4, kk, ad4)
        else:
            flip = lambda t: jnp.flip(t, axis=1)
            y_sum = y_sum + flip(rwkv7_scan(flip(r4), flip(wd4), flip(kd4), flip(v4), flip(kk), flip(ad4)))
        bonus = bonus + jnp.sum(r4 * kd4 * r_k4, axis=-1, keepdims=True) * v4
    mean = jnp.mean(y_sum, axis=-1, keepdims=True)
    var = jnp.mean(jnp.square(y_sum - mean), axis=-1, keepdims=True)
    y = ((y_sum - mean) * lax.rsqrt(var + GN_EPS)).reshape(B, T, R_WIDTH)
    y = y * ln_w.astype(jnp.float32) + ln_b.astype(jnp.float32)
    return y + bonus.reshape(B, T, R_WIDTH)


def memory_xattn(q, km, vm):
    s = jnp.einsum('bshd,bmhd->bhsm', q, km).astype(jnp.float32) * (X_HEAD_DIM ** -0.5)
    p = jax.nn.softmax(s, axis=-1)
    o = jnp.einsum('bhsm,bmhd->bshd', p.astype(vm.dtype), vm)
    return o.reshape(q.shape[0], q.shape[1], X_WIDTH)


def hybrid_layer(x, mem, norm_g, mem_norm_g, w_in, gate_b, attn_q_norm_g, attn_k_norm_g,
                 attn_sink, attn_w_o, rwkv_mu, rwkv_k_k, rwkv_k_a, rwkv_r_k, rwkv_w0, rwkv_w2,
                 rwkv_a0, rwkv_a2, rwkv_ln_w, rwkv_ln_b, rwkv_w_o, x_w_kv, x_q_norm_g,
                 x_k_norm_g, x_w_o, w_out):
    B, S, _ = x.shape
    positions = jnp.arange(S)
    h = rms_norm(x, norm_g)
    proj = h @ w_in
    aq, ak, av, ag, rs, rg, xq, xg, mg = _split(
        proj, (A_WIDTH, A_KV_WIDTH, A_KV_WIDTH, A_WIDTH, R_SHIFT_WIDTH, R_WIDTH,
               X_WIDTH, X_WIDTH, N_BRANCH * D_MODEL))

    q = rope(rms_norm(aq.reshape(B, S, A_HEADS, A_HEAD_DIM), attn_q_norm_g), positions)
    k = rope(rms_norm(ak.reshape(B, S, A_KV_HEADS, A_HEAD_DIM), attn_k_norm_g), positions)
    va = av.reshape(B, S, A_KV_HEADS, A_HEAD_DIM)
    y_a = windowed_gqa(q, k, va, attn_sink) * jax.nn.silu(ag)

    y_r = rwkv7_branch(rs, rwkv_mu, rwkv_k_k, rwkv_k_a, rwkv_r_k, rwkv_w0, rwkv_w2,
                       rwkv_a0, rwkv_a2, rwkv_ln_w, rwkv_ln_b)
    y_r = y_r.astype(x.dtype) * jax.nn.silu(rg)

    mkv = rms_norm(mem, mem_norm_g) @ x_w_kv
    km, vm = _split(mkv, (X_WIDTH, X_WIDTH))
    km = rms_norm(km.reshape(B, -1, X_HEADS, X_HEAD_DIM), x_k_norm_g)
    vm = vm.reshape(B, -1, X_HEADS, X_HEAD_DIM)
    qx = rms_norm(xq.reshape(B, S, X_HEADS, X_HEAD_DIM), x_q_norm_g)
    y_x = memory_xattn(qx, km, vm) * jax.nn.silu(xg)

    gates = jax.nn.sigmoid((mg + gate_b).astype(jnp.float32)).reshape(B, S, N_BRANCH, D_MODEL)
    gates = gates.astype(x.dtype)
    merged = (gates[:, :, 0] * (y_a @ attn_w_o)
              + gates[:, :, 1] * (y_r @ rwkv_w_o)
              + gates[:, :, 2] * (y_x @ x_w_o))
    return x + merged @ w_out


def setup_inputs(seed: int = 0) -> dict:
    key = jax.random.key(seed)
    ks = jax.random.split(key, 32)
    n = lambda i, shape, s=1.0: jax.random.normal(ks[i], shape, jnp.float32) * s
    L = DEPTH
    return {
        "x": n(0, (BATCH, SEQ, D_MODEL)),
        "mem": n(1, (BATCH, N_MEM, D_MODEL)),
        "norm_g": 1.0 + n(2, (L, D_MODEL), 0.02),
        "mem_norm_g": 1.0 + n(3, (L, D_MODEL), 0.02),
        "w_in": n(4, (L, D_MODEL, IN_WIDTH), D_MODEL ** -0.5),
        "gate_b": n(5, (L, N_BRANCH * D_MODEL), 0.1),
        "attn_q_norm_g": 1.0 + n(6, (L, A_HEAD_DIM), 0.02),
        "attn_k_norm_g": 1.0 + n(7, (L, A_HEAD_DIM), 0.02),
        "attn_sink": n(8, (L, A_HEADS), 0.5),
        "attn_w_o": n(9, (L, A_WIDTH, D_MODEL), A_WIDTH ** -0.5),
        "rwkv_mu": jax.random.uniform(ks[10], (L, R_SHIFT_WIDTH), jnp.float32, 0.0, 1.0),
        "rwkv_k_k": 0.85 + n(11, (L, R_WIDTH), 0.05),
        "rwkv_k_a": 1.0 + n(12, (L, R_WIDTH), 0.05),
        "rwkv_r_k": n(13, (L, R_HEADS, R_HEAD_DIM), 0.1),
        "rwkv_w0": jax.random.uniform(ks[14], (L, N_DIR, R_WIDTH), jnp.float32, -5.0, -1.0),
        "rwkv_w2": n(15, (L, N_DIR, DECAY_LORA, R_WIDTH), 0.1),
        "rwkv_a0": n(16, (L, N_DIR, R_WIDTH), 0.1),
        "rwkv_a2": n(17, (L, N_DIR, ICLR_LORA, R_WIDTH), 0.5 * ICLR_LORA ** -0.5),
        "rwkv_ln_w": 1.0 + n(18, (L, R_WIDTH), 0.02),
        "rwkv_ln_b": n(19, (L, R_WIDTH), 0.02),
        "rwkv_w_o": n(20, (L, R_WIDTH, D_MODEL), R_WIDTH ** -0.5),
        "x_w_kv": n(21, (L, D_MODEL, 2 * X_WIDTH), D_MODEL ** -0.5),
        "x_q_norm_g": 1.0 + n(22, (L, X_HEAD_DIM), 0.02),
        "x_k_norm_g": 1.0 + n(23, (L, X_HEAD_DIM), 0.02),
        "x_w_o": n(24, (L, X_WIDTH, D_MODEL), X_WIDTH ** -0.5),
        "w_out": n(25, (L, D_MODEL, D_MODEL), D_MODEL ** -0.5),
    }


def reference(x, mem, norm_g, mem_norm_g, w_in, gate_b, attn_q_norm_g, attn_k_norm_g,
              attn_sink, attn_w_o, rwkv_mu, rwkv_k_k, rwkv_k_a, rwkv_r_k, rwkv_w0, rwkv_w2,
              rwkv_a0, rwkv_a2, rwkv_ln_w, rwkv_ln_b, rwkv_w_o, x_w_kv, x_q_norm_g,
              x_k_norm_g, x_w_o, w_out):
    h = x
    for l in range(DEPTH):
        h = hybrid_layer(h, mem, norm_g[l], mem_norm_g[l], w_in[l], gate_b[l],
                         attn_q_norm_g[l], attn_k_norm_g[l], attn_sink[l], attn_w_o[l],
                         rwkv_mu[l], rwkv_k_k[l], rwkv_k_a[l], rwkv_r_k[l], rwkv_w0[l],
                         rwkv_w2[l], rwkv_a0[l], rwkv_a2[l], rwkv_ln_w[l], rwkv_ln_b[l],
                         rwkv_w_o[l], x_w_kv[l], x_q_norm_g[l], x_k_norm_g[l], x_w_o[l],
                         w_out[l])
    return h
```

```python
import math
import numpy as np
import ml_dtypes
import concourse.bass as bass
import concourse.mybir as mybir
from concourse.bass_utils import run_bass_kernel_spmd

F32 = mybir.dt.float32
BF16 = mybir.dt.bfloat16
AF = mybir.ActivationFunctionType
ALU = mybir.AluOpType
AX = mybir.AxisListType

S = 2048
D = 2048
NMEM = 256
INW = 12544
NQKV = 1280
NPF = INW - NQKV
EPS = 1e-6
GN_EPS = 64e-5
C1 = -0.5 * math.exp(-0.5)

AG0 = 0
R0 = 2048 - NQKV
K0 = R0 + 768
V0 = K0 + 768
LW0 = V0 + 768
LA0 = LW0 + 128
RG0 = 4608 - NQKV
XQ0 = 5376 - NQKV
XG0 = 5888 - NQKV
MG0 = 6400 - NQKV

NG, MG_, GB, MU, KK, KA, RK, LW, LB, W0, A0, XQG, XKG = 0, 16, 32, 80, 100, 106, 112, 118, 124, 130, 142, 154, 155
OMM, HMU, OMK, HW0, HA0, XG2 = 156, 176, 196, 202, 214, 226
NPRM = 228


class Buf:
    __slots__ = ("name", "w", "r", "pending")

    def __init__(self, name=""):
        self.name = name
        self.w = None
        self.r = {}
        self.pending = False


class Sched:
    ENG = ("pe", "act", "dve", "pool", "sp")

    def __init__(self, nc, n_dma_sems=40):
        self.nc = nc
        self.eng = {"pe": nc.tensor, "act": nc.scalar, "dve": nc.vector, "pool": nc.gpsimd, "sp": nc.sync}
        self.sem = {}
        self.cnt = {e: 0 for e in self.ENG}
        self.known = {e: {} for e in self.ENG}
        self._cms = []
        for e in self.ENG:
            cm = nc.semaphore("s_" + e)
            self.sem[e] = cm.__enter__()
            self._cms.append(cm)
        self.dsem = []
        for i in range(n_dma_sems):
            cm = nc.semaphore("d%d" % i)
            self.dsem.append([cm.__enter__(), 0])
            self._cms.append(cm)
        self.dnext = 0
        self.nwait = 0
        self.log = {e: [] for e in self.ENG}

    def close(self):
        for cm in reversed(self._cms):
            cm.__exit__(None, None, None)

    def _wait(self, e, key, semh, val):
        k = self.known[e]
        if k.get(key, 0) >= val:
            return
        self.eng[e].wait_ge(semh, val)
        self.nwait += 1
        self.log[e].append(("w", key, val))
        k[key] = val

    def wait_tok(self, e, tok):
        if tok is None:
            return
        kind, a, v = tok
        if kind == "eng":
            self._wait(e, a, self.sem[a], v)
        else:
            self._wait(e, "d%d" % a, self.dsem[a][0], v)

    def deps(self, e, reads, writes):
        for b in reads:
            self.wait_tok(e, b.w)
        for b in writes:
            self.wait_tok(e, b.w)
            for tok in b.r.values():
                self.wait_tok(e, tok)

    def op(self, e, fn, reads=(), writes=(), inc=True, setw=None):
        self.deps(e, reads, writes)
        ins = fn()
        if inc:
            self.cnt[e] += 1
            ins.then_inc(self.sem[e], 1)
            self.log[e].append(("i", e, 1))
            tok = ("eng", e, self.cnt[e])
        else:
            tok = ("eng", e, self.cnt[e] + 1)
        for b in reads:
            b.r[("eng", e)] = tok
            b.pending = False
        for b in (writes if setw is None else setw):
            b.w = tok
            b.r = {}
            b.pending = True
        for b in writes:
            b.pending = True
        return ins

    def dma(self, e, out, in_, reads=(), writes=()):
        idx = self.dnext
        self.dnext = (self.dnext + 1) % len(self.dsem)
        semh, val = self.dsem[idx]
        if val > 0:
            self._wait(e, "d%d" % idx, semh, val)
        self.deps(e, reads, writes)
        ins = self.eng[e].dma_start(out=out, in_=in_)
        val += 16
        ins.then_inc(semh, 16)
        self.log[e].append(("i", "d%d" % idx, 16))
        self.dsem[idx][1] = val
        tok = ("dma", idx, val)
        for b in reads:
            b.r[("dma", idx)] = tok
        for b in writes:
            b.w = tok
            b.r = {}
        return tok

    def check_deadlock(self):
        sem = {}
        pos = {e: 0 for e in self.ENG}
        prog = True
        while prog:
            prog = False
            for e in self.ENG:
                lg = self.log[e]
                while pos[e] < len(lg):
                    kind, key, val = lg[pos[e]]
                    if kind == "w":
                        if sem.get(key, 0) < val:
                            break
                    else:
                        sem[key] = sem.get(key, 0) + val
                    pos[e] += 1
                    prog = True
        stuck = {e: (pos[e], len(self.log[e]), self.log[e][pos[e]] if pos[e] < len(self.log[e]) else None) for e in self.ENG}
        ok = all(pos[e] == len(self.log[e]) for e in self.ENG)
        return ok, stuck, sem

    def barrier(self):
        for e in self.ENG:
            for e2 in self.ENG:
                if e2 != e and self.cnt[e2] > 0:
                    self._wait(e, e2, self.sem[e2], self.cnt[e2])
            for idx, (semh, val) in enumerate(self.dsem):
                if val > 0:
                    self._wait(e, "d%d" % idx, semh, val)


class Arena:
    def __init__(self, big, n):
        self.big = big
        self.n = n
        self.off = 0

    def mark(self):
        return self.off

    def release(self, m):
        self.off = m

    def _raw(self, nf32):
        a = self.off
        self.off += nf32
        assert self.off <= self.n, "SBUF arena overflow %d > %d" % (self.off, self.n)
        return self.big[:, a:a + nf32]

    @staticmethod
    def _shape(ap, dims):
        if len(dims) == 1:
            return ap
        if len(dims) == 2:
            return ap.rearrange("p (a b) -> p a b", b=dims[1])
        if len(dims) == 3:
            return ap.rearrange("p (a b c) -> p a b c", b=dims[1], c=dims[2])
        raise ValueError

    def f32(self, *dims):
        n = int(np.prod(dims))
        return self._shape(self._raw(n), dims)

    def bf16(self, *dims):
        n = int(np.prod(dims))
        assert n % 2 == 0
        return self._shape(self._raw(n // 2).bitcast(BF16), dims)


def host_consts():
    c = {}
    c["identf"] = np.eye(128, dtype=np.float32)
    bo = np.zeros((128, 128), np.float32)
    bo[:64, :64] = 1.0
    bo[64:, 64:] = 1.0
    c["bones"] = bo
    c["bo64"] = bo / 64.0
    half = 32
    inv = (10000.0 ** (-np.arange(half, dtype=np.float64) / half))
    ang = np.arange(S, dtype=np.float64)[:, None] * inv[None, :]
    c["cosr"] = np.ascontiguousarray(np.cos(ang).reshape(16, 128, 32).transpose(1, 0, 2)).astype(np.float32)
    c["sinr"] = np.ascontiguousarray(np.sin(ang).reshape(16, 128, 32).transpose(1, 0, 2)).astype(np.float32)
    j = np.arange(128)[:, None]
    i = np.arange(128)[None, :]
    c["maskLR"] = np.stack([(j >= i), (j <= i)], axis=1).astype(np.float32)
    t = np.arange(64)
    st_f = (t[:, None] < t[None, :]).astype(np.float32)
    in_f = (t[:, None] <= t[None, :]).astype(np.float32)
    def bd(m):
        z = np.zeros((128, 128), np.float32)
        z[:64, :64] = m
        z[64:, 64:] = m
        return z
    rwm = np.zeros((128, 2, 3, 128), np.float32)
    rwm[:, 0, 0] = bd(st_f); rwm[:, 0, 1] = bd(in_f); rwm[:, 0, 2] = bd(st_f.T)
    rwm[:, 1, 0] = bd(st_f.T); rwm[:, 1, 1] = bd(in_f.T); rwm[:, 1, 2] = bd(st_f)
    c["rwm"] = rwm
    seg = np.ones((128, 512), np.float32)
    seg[:, ::64] = 0.0
    c["segm"] = seg
    return c


def build_nc(stop_after="all", debug=()):
    nc = bass.Bass("TRN2", target_bir_lowering=False)

    def din(name, shape):
        return nc.dram_tensor(name, list(shape), F32, kind="ExternalInput").ap()

    x = din("x", [S, D]); mem = din("mem", [NMEM, D]); w_in = din("w_in", [D, INW])
    attn_w_o = din("attn_w_o", [768, D]); rwkv_w_o = din("rwkv_w_o", [768, D]); x_w_o = din("x_w_o", [512, D])
    x_w_kv = din("x_w_kv", [D, 1024]); w_out = din("w_out", [D, D])
    w2cat = din("w2cat", [128, 768]); a2cat = din("a2cat", [128, 768])
    prm_d = din("prm", [128, NPRM]); gqk_d = din("gqk", [128, 1024]); sink_d = din("sinkb", [128, 12])
    identf_d = din("identf", [128, 128]); bones_d = din("bones", [128, 128]); bo64_d = din("bo64", [128, 128])
    cos_d = din("cosr", [128, 16, 32]); sin_d = din("sinr", [128, 16, 32]); maskLR_d = din("maskLR", [128, 2, 128])
    rwm_d = din("rwm", [128, 2, 3, 128]); segm_d = din("segm", [128, 512])
    out = nc.dram_tensor("out", [S, D], F32, kind="ExternalOutput").ap()

    def dscr(name, shape, dt):
        kind = "ExternalOutput" if name in debug else "Internal"
        return nc.dram_tensor(name, list(shape), dt, kind=kind).ap()

    proj_f = dscr("proj_f", [NPF, S], F32)
    qkv_t = dscr("qkv_t", [S, NQKV], F32)
    ybuf = dscr("ybuf", [2048, S], BF16)
    dbg = dscr("dbg", [128, 4096], F32) if "dbg" in debug else None

    NBIG = 52600
    big_cm = nc.sbuf_tensor("big", [128, NBIG], F32)
    big = big_cm.__enter__()
    ps_cms = [nc.psum_tensor("ps%d" % i, [128, 512], F32) for i in range(8)]
    ps = [cm.__enter__() for cm in ps_cms]
    Bps = [Buf("ps%d" % i) for i in range(8)]
    psb = [p[:, :].bitcast(BF16) for p in ps]
    Sc = Sched(nc)
    AR = Arena(big, NBIG)
    V, A_, P_, G_, T_ = nc.vector, nc.scalar, nc.gpsimd, nc.sync, nc.tensor
    bank_rr = [0]
    dumped = set()

    def dump(name, sb_ap, shape, dt, reads):
        if name in debug and name not in dumped:
            dumped.add(name)
            t = nc.dram_tensor(name, list(shape), dt, kind="ExternalOutput").ap()
            Sc.dma("sp", t, sb_ap, reads=reads)

    def nbank(lo=0, hi=8):
        for _ in range(hi - lo):
            b = lo + bank_rr[0] % (hi - lo)
            bank_rr[0] += 1
            if not Bps[b].pending:
                return b
        raise RuntimeError("all PSUM banks in [%d,%d) hold unconsumed data" % (lo, hi))

    def mm_group(out_ap, items, obuf):
        n = len(items)
        for i, (l, r, rd) in enumerate(items):
            first, last = i == 0, i == n - 1
            Sc.op("pe", lambda: T_.matmul(out_ap, lhsT=l, rhs=r, start=first, stop=last), reads=rd,
                  writes=[obuf] if first else [], inc=last, setw=[obuf] if last else [])

    def mm_multi(items, obuf):
        n = len(items)
        for i, (o, l, r, rd) in enumerate(items):
            first, last = i == 0, i == n - 1
            Sc.op("pe", lambda: T_.matmul(o, lhsT=l, rhs=r, start=True, stop=True), reads=rd,
                  writes=[obuf] if first else [], inc=last, setw=[obuf] if last else [])

    def rsqrt_act(out_ap, in_ap, scale, eps, reads, writes, tmp_ap=None):
        t = out_ap if tmp_ap is None else tmp_ap
        Sc.op("act", lambda: A_.activation(out=t, in_=in_ap, func=AF.Ln, bias=eps, scale=scale), reads=reads, writes=writes)
        Sc.op("act", lambda: A_.activation(out=out_ap, in_=t, func=AF.Exp, scale=-0.5), reads=writes, writes=writes)

    identf = AR.f32(128); identb = AR.bf16(128); prm = AR.f32(NPRM)
    kmT = AR.bf16(4, 256); vm = AR.bf16(2, 512)
    Bid, Bprm, Bkm, Bvm = Buf("id"), Buf("prm"), Buf("kmT"), Buf("vm")
    Sc.dma("sp", identf, identf_d[:, :], writes=[Bid])
    Sc.dma("sp", prm[:, 0:OMM], prm_d[:, 0:OMM], writes=[Bprm])
    Sc.op("dve", lambda: V.tensor_copy(out=identb, in_=identf), reads=[Bid], writes=[Bid])
    Sc.op("dve", lambda: V.tensor_scalar(out=prm[:, OMM:OMM + 20], in0=prm[:, MU:MU + 20], scalar1=-1.0, scalar2=1.0, op0=ALU.mult, op1=ALU.add), reads=[Bprm], writes=[Bprm])
    Sc.op("dve", lambda: V.tensor_scalar(out=prm[:, HMU:HMU + 20], in0=prm[:, MU:MU + 20], scalar1=0.5, scalar2=None, op0=ALU.mult), reads=[Bprm], writes=[Bprm])
    Sc.op("dve", lambda: V.tensor_scalar(out=prm[:, OMK:OMK + 6], in0=prm[:, KA:KA + 6], scalar1=-1.0, scalar2=1.0, op0=ALU.mult, op1=ALU.add), reads=[Bprm], writes=[Bprm])
    Sc.op("dve", lambda: V.tensor_scalar(out=prm[:, HW0:HW0 + 24], in0=prm[:, W0:W0 + 24], scalar1=0.5, scalar2=None, op0=ALU.mult), reads=[Bprm], writes=[Bprm])
    Sc.op("dve", lambda: V.tensor_tensor(out=prm[:, XG2:XG2 + 1], in0=prm[:, XQG:XQG + 1], in1=prm[:, XKG:XKG + 1], op=ALU.mult), reads=[Bprm], writes=[Bprm])
    m_persist = AR.mark()

    hT = AR.bf16(16, S)
    BhT = [Buf("hT%d" % g) for g in range(4)]
    memT = AR.bf16(16, NMEM); BmemT = Buf("memT")
    m_ph0 = AR.mark()
    xbuf = [AR.f32(4, D) for _ in range(2)]
    Bx = [[Buf() for _ in range(4)] for _ in range(2)]
    junk = AR.f32(D); Bjunk = Buf("junk")
    evac_rr = [0]

    def build_T(src, nblk_total, gcol, dstT, dst_bufs):
        ngrp = (nblk_total + 3) // 4
        for g in range(ngrp):
            nb = min(4, nblk_total - g * 4)
            xb, bx = xbuf[g % 2], Bx[g % 2]
            ssq = AR.f32(4); rt = AR.f32(4); Bss = Buf("ss")
            Sc.op("dve", lambda: V.memset(ssq, 0.0), writes=[Bss])
            for i in range(nb):
                r0 = (g * 4 + i) * 128
                Sc.dma("sp", xb[:, i, :], src[r0:r0 + 128, :], writes=[bx[i]])
            for i in range(nb):
                Sc.op("act", lambda: A_.activation(out=junk, in_=xb[:, i, :], func=AF.Square, accum_out=ssq[:, i:i + 1]),
                      reads=[bx[i]], writes=[Bjunk, Bss])
            rsqrt_act(rt[:, 0:nb], ssq[:, 0:nb], 1.0 / D, EPS, [Bss], [Bss])
            for i in range(nb):
                Sc.op("dve", lambda: V.tensor_scalar(out=xb[:, i, :], in0=xb[:, i, :], scalar1=rt[:, i:i + 1], scalar2=None, op0=ALU.mult),
                      reads=[bx[i], Bss], writes=[bx[i]])
            for c in range(16):
                b = nbank()
                for i in range(nb):
                    Sc.op("pe", lambda: T_.transpose(out=ps[b][:, i * 128:(i + 1) * 128], in_=xb[:, i, c * 128:(c + 1) * 128], identity=identf),
                          reads=[bx[i], Bid], writes=[Bps[b]] if i == 0 else [], inc=(i == nb - 1), setw=[Bps[b]] if i == nb - 1 else [])
                dst = dstT[:, c, g * 512:g * 512 + nb * 128]
                gc = prm[:, gcol + c:gcol + c + 1]
                if evac_rr[0] % 2 == 0:
                    Sc.op("act", lambda: A_.activation(out=dst, in_=ps[b][:, 0:nb * 128], func=AF.Copy, scale=gc),
                          reads=[Bps[b], Bprm], writes=[dst_bufs[g]])
                else:
                    Sc.op("dve", lambda: V.tensor_scalar(out=dst, in0=ps[b][:, 0:nb * 128], scalar1=gc, scalar2=None, op0=ALU.mult),
                          reads=[Bps[b], Bprm], writes=[dst_bufs[g]])
                evac_rr[0] += 1

    build_T(mem, 2, MG_, memT, [BmemT])
    build_T(x, 16, NG, hT, BhT)

    Sc.barrier()
    AR.release(m_ph0)
    memT2 = memT
    wkv = AR.bf16(16, 1024); Bwkv = [Buf("wkv%d" % q) for q in range(4)]
    kmn = AR.f32(512); Bkmn = Buf("kmn")
    ssk = AR.f32(4); rk_ = AR.f32(4); Bssk = Buf("ssk")
    junk2 = AR.f32(128); Bjunk2 = Buf("junk2")
    wkv_src = x_w_kv.rearrange("(k p) n -> p k n", p=128)
    for q in range(4):
        Sc.dma("pool", wkv[:, q * 4:(q + 1) * 4, :], wkv_src[:, q * 4:(q + 1) * 4, :], writes=[Bwkv[q]])
    for mb in range(2):
        for half in range(2):
            b = nbank()
            mm_group(ps[b][:, :], [(memT2[:, k, mb * 128:(mb + 1) * 128], wkv[:, k, half * 512:(half + 1) * 512], [BmemT, Bwkv[k // 4]]) for k in range(16)], Bps[b])
            if half == 0:
                Sc.op("dve", lambda: V.memset(ssk, 0.0), writes=[Bssk])
                for h in range(4):
                    Sc.op("act", lambda: A_.activation(out=junk2, in_=ps[b][:, h * 128:(h + 1) * 128], func=AF.Square, accum_out=ssk[:, h:h + 1]),
                          reads=[Bps[b]], writes=[Bjunk2, Bssk])
                rsqrt_act(rk_, ssk, 1.0 / 128, EPS, [Bssk], [Bssk])
                for h in range(4):
                    Sc.op("dve", lambda: V.tensor_scalar(out=kmn[:, h * 128:(h + 1) * 128], in0=ps[b][:, h * 128:(h + 1) * 128], scalar1=rk_[:, h:h + 1], scalar2=None, op0=ALU.mult),
                          reads=[Bps[b], Bssk], writes=[Bkmn])
                b2 = nbank()
                for h in range(4):
                    Sc.op("pe", lambda: T_.transpose(out=ps[b2][:, h * 128:(h + 1) * 128], in_=kmn[:, h * 128:(h + 1) * 128], identity=identf),
                          reads=[Bkmn, Bid], writes=[Bps[b2]] if h == 0 else [], inc=(h == 3), setw=[Bps[b2]] if h == 3 else [])
                Sc.op("dve", lambda: V.tensor_scalar(out=kmT[:, :, mb * 128:(mb + 1) * 128], in0=ps[b2][:, :].rearrange("p (h m) -> p h m", m=128),
                                                     scalar1=prm[:, XG2:XG2 + 1], scalar2=None, op0=ALU.mult),
                      reads=[Bps[b2], Bprm], writes=[Bkm])
            else:
                Sc.op("act", lambda: A_.activation(out=vm[:, mb, :], in_=ps[b][:, :], func=AF.Copy), reads=[Bps[b]], writes=[Bvm])
    dump("d_kmT", kmT, [128, 4, 256], BF16, [Bkm]); dump("d_vm", vm, [128, 2, 512], BF16, [Bvm])
    dump("d_hT", hT, [128, 16, S], BF16, BhT)
    Sc.barrier()
    if stop_after == "hT":
        return finish_debug(nc, Sc, locals())

    AR.release(m_ph0)
    NWB = 3
    wt = [AR.bf16(16, 512) for _ in range(NWB)]
    Bwt = [[Buf("wt%d_%d" % (i, q)) for q in range(4)] for i in range(NWB)]
    stf = [AR.f32(S) for _ in range(2)]; Bstf = [Buf("stf%d" % i) for i in range(2)]
    stt = [AR.f32(512) for _ in range(3)]; Bstt = [Buf("stt%d" % i) for i in range(3)]
    Bproj = [Buf("pf%d" % i) for i in range(NPF // 128)]
    Bqkv = Buf("qkv")
    w_src = w_in.rearrange("(k p) n -> p k n", p=128)
    NT = (INW + 511) // 512

    def load_w(t):
        c0 = t * 512
        ncol = min(512, INW - c0)
        for q in range(4):
            Sc.dma("pool", wt[t % NWB][:, q * 4:(q + 1) * 4, 0:ncol], w_src[:, q * 4:(q + 1) * 4, c0:c0 + ncol], writes=[Bwt[t % NWB][q]])

    def act_for(feat):
        if (1280 <= feat < 2048) or (4608 <= feat < 5376) or (5888 <= feat < 6400):
            return "silu"
        if feat >= 6400:
            return "sig"
        return "copy"

    stf_rr = [0]; stt_rr = [0]; ev_rr = [0]
    import os as _os2
    TESTCOPY = bool(_os2.environ.get("TESTCOPY"))
    load_w(0); load_w(1)

    def rr(gens, weights=None):
        act_ = [[g, (weights[i] if weights else 1)] for i, g in enumerate(gens)]
        while act_:
            for ent in list(act_):
                for _ in range(ent[1]):
                    try:
                        next(ent[0])
                        yield
                    except StopIteration:
                        act_.remove(ent)
                        break

    def run(gen):
        for _ in gen:
            pass

    def proj_gen(t_lo, t_hi, bk):
      for t in range(t_lo, t_hi):
          if t + 2 < NT:
              load_w(t + 2)
          c0 = t * 512
          ncol = min(512, INW - c0)
          w = wt[t % NWB]; bw = Bwt[t % NWB]
          ntok = max(0, min(ncol, NQKV - c0))
          if ntok > 0:
              for tb in range(16):
                  b = nbank(*bk)
                  mm_group(ps[b][:, 0:ntok], [(hT[:, k, tb * 128:(tb + 1) * 128], w[:, k, 0:ntok], [BhT[tb // 4], bw[k // 4]]) for k in range(16)], Bps[b])
                  si = stt_rr[0] % 3; stt_rr[0] += 1
                  if ev_rr[0] % 2 == 0:
                      Sc.op("act", lambda: A_.activation(out=stt[si][:, 0:ntok], in_=ps[b][:, 0:ntok], func=AF.Copy), reads=[Bps[b]], writes=[Bstt[si]])
                  else:
                      Sc.op("dve", lambda: V.tensor_copy(out=stt[si][:, 0:ntok], in_=ps[b][:, 0:ntok]), reads=[Bps[b]], writes=[Bstt[si]])
                  ev_rr[0] += 1
                  Sc.dma("sp", qkv_t[tb * 128:(tb + 1) * 128, c0:c0 + ntok], stt[si][:, 0:ntok], reads=[Bstt[si]])
                  yield
          for sub in range(ntok // 128, ncol // 128):
              feat = c0 + sub * 128
              fi = (feat - NQKV) // 128
              kind = act_for(feat)
              si = stf_rr[0] % 2; stf_rr[0] += 1
              for tc in range(4):
                  b = nbank(*bk)
                  mm_group(ps[b][:, :], [(w[:, k, sub * 128:(sub + 1) * 128], hT[:, k, tc * 512:(tc + 1) * 512], [bw[k // 4], BhT[tc]]) for k in range(16)], Bps[b])
                  dst = stf[si][:, tc * 512:(tc + 1) * 512]
                  if kind == "silu":
                      Sc.op("act", lambda: A_.activation(out=dst, in_=ps[b][:, :], func=AF.Silu), reads=[Bps[b]], writes=[Bstf[si]])
                  elif kind == "sig":
                      gcol = GB + (feat - 6400) // 128
                      Sc.op("act", lambda: A_.activation(out=dst, in_=ps[b][:, :], func=(AF.Tanh if TESTCOPY else AF.Sigmoid), bias=prm[:, gcol:gcol + 1], scale=1.0),
                            reads=[Bps[b], Bprm], writes=[Bstf[si]])
                  else:
                      if ev_rr[0] % 2 == 0:
                          Sc.op("act", lambda: A_.activation(out=dst, in_=ps[b][:, :], func=AF.Copy), reads=[Bps[b]], writes=[Bstf[si]])
                      else:
                          Sc.op("dve", lambda: V.tensor_copy(out=dst, in_=ps[b][:, :]), reads=[Bps[b]], writes=[Bstf[si]])
                      ev_rr[0] += 1
                  yield
              Sc.dma("sp", proj_f[fi * 128:(fi + 1) * 128, :], stf[si], reads=[Bstf[si]], writes=[Bproj[fi]])

    TSPLIT = 13
    run(proj_gen(0, TSPLIT, (0, 8)))
    onesf = AR.f32(128); onesb = AR.bf16(128); Bones = Buf("ones")
    Sc.op("pool", lambda: P_.memset(onesf, 1.0), writes=[Bones])
    Sc.op("pool", lambda: P_.tensor_copy(out=onesb, in_=onesf), reads=[Bones], writes=[Bones])
    qTc = [AR.f32(S) for _ in range(2)]; gtc = [AR.f32(S)] * 2
    BqTc = [Buf(), Buf()]; Bgtc = [Buf()] * 2
    sqc = AR.f32(512); sc2 = AR.f32(512); qn_c = AR.bf16(512); pTc = [AR.bf16(512) for _ in range(2)]; rden = AR.f32(512); yo = AR.f32(512)
    yxs = AR.bf16(S)
    Bsqc, Bsc2, Bqnc, BpTc, Brden, Byo, Byxs = Buf(), Buf(), Buf(), [Buf(), Buf()], Buf(), Buf(), Buf()

    c_rr = [0]

    def cbank():
        c_rr[0] += 1
        return 6 + c_rr[0] % 2

    def xattn_gen():
        for h in range(4):
            Sc.dma("sp", qTc[h % 2], proj_f[XQ0 + h * 128:XQ0 + (h + 1) * 128, :], reads=[Bproj[XQ0 // 128 + h]], writes=[BqTc[h % 2]])
            Sc.dma("sp", gtc[h % 2], proj_f[XG0 + h * 128:XG0 + (h + 1) * 128, :], reads=[Bproj[XG0 // 128 + h]], writes=[Bgtc[h % 2]])
            q_ = qTc[h % 2]; bq_ = BqTc[h % 2]
            for tc in range(4):
                sl = slice(tc * 512, (tc + 1) * 512)
                Sc.op("act", lambda: A_.activation(out=sqc, in_=q_[:, sl], func=AF.Square), reads=[bq_], writes=[Bsqc]); yield
                b = cbank()
                mm_group(ps[b][:, :], [(onesf, sqc, [Bones, Bsqc])], Bps[b]); yield
                rsqrt_act(sc2, ps[b][:, :], 1.0, 128.0 * EPS, [Bps[b]], [Bsc2]); yield
                Sc.op("dve", lambda: V.tensor_tensor(out=qn_c, in0=q_[:, sl], in1=sc2, op=ALU.mult), reads=[bq_, Bsc2], writes=[Bqnc]); yield
                for mb in range(2):
                    b = cbank()
                    mm_group(ps[b][:, :], [(kmT[:, h, mb * 128:(mb + 1) * 128], qn_c, [Bkm, Bqnc])], Bps[b]); yield
                    Sc.op("act", lambda: A_.activation(out=pTc[mb], in_=ps[b][:, :], func=AF.Exp), reads=[Bps[b]], writes=[BpTc[mb]]); yield
                bo_ = cbank()
                mm_group(ps[bo_][:, :], [(vm[:, mb, h * 128:(h + 1) * 128], pTc[mb], [Bvm, BpTc[mb]]) for mb in range(2)], Bps[bo_]); yield
                bd_ = cbank()
                mm_group(ps[bd_][:, :], [(onesb, pTc[mb], [Bones, BpTc[mb]]) for mb in range(2)], Bps[bd_]); yield
                Sc.op("act", lambda: A_.activation(out=rden, in_=ps[bd_][:, :], func=AF.Ln), reads=[Bps[bd_]], writes=[Brden]); yield
                Sc.op("act", lambda: A_.activation(out=rden, in_=rden, func=AF.Exp, scale=-1.0), reads=[Brden], writes=[Brden]); yield
                Sc.op("dve", lambda: V.tensor_tensor(out=yo, in0=ps[bo_][:, :], in1=rden, op=ALU.mult), reads=[Bps[bo_], Brden], writes=[Byo]); yield
                Sc.op("pool", lambda: P_.tensor_tensor(out=yxs[:, sl], in0=yo, in1=gtc[h % 2][:, sl], op=ALU.mult), reads=[Byo, Bgtc[h % 2]], writes=[Byxs]); yield
            Sc.dma("sp", ybuf[1536 + h * 128:1536 + (h + 1) * 128, :], yxs, reads=[Byxs]); yield


    run(rr([proj_gen(TSPLIT, NT, (0, 6)), xattn_gen()]))
    Sc.barrier()
    if stop_after == "proj":
        return finish_debug(nc, Sc, locals())

    AR.release(m_persist)
    bones = AR.f32(128); bo64 = AR.f32(128); rwm = AR.bf16(2, 3, 128); segm = AR.f32(512)
    w2b = AR.bf16(768); a2b = AR.bf16(768)
    Bc = Buf("rwconst")
    Sc.dma("sp", bones, bones_d[:, :], writes=[Bc])
    tb1 = Buf(); tb2 = Buf(); tb3 = Buf(); tb4 = Buf(); tb5 = Buf()
    Sc.dma("sp", bo64, bo64_d[:, :], writes=[tb1])
    Sc.dma("sp", segm, segm_d[:, :], writes=[tb2])
    Sc.dma("pool", rwm, rwm_d[:, :, :, :], writes=[tb3])
    Sc.dma("pool", w2b, w2cat[:, :], writes=[tb4])
    Sc.dma("pool", a2b, a2cat[:, :], writes=[tb5])
    lw_t = AR.bf16(S); la_s = AR.bf16(S); Blw = Buf("lw"); Bla = Buf("la")
    r_ = AR.bf16(S); k_ = AR.bf16(S); v_ = AR.bf16(S); kk_ = AR.bf16(S); bon = AR.f32(S); ysum = AR.f32(S)
    Br, Bk, Bv, Bkk, Bbon, Bys = Buf("r"), Buf("k"), Buf("v"), Buf("kk"), Buf("bon"), Buf("ysum")
    Vtok = AR.bf16(32, 128); BVtok = [Buf("vtok%d" % q) for q in range(4)]
    vbd = AR.bf16(8, 128); Bvbd = Buf("vbd")
    rkones = AR.f32(128); Brk = Buf("rkones")
    NTMP = 7168
    tmp_raw = AR._raw(NTMP)
    WD = []
    for d in range(2):
        W = {}
        for nm in ("ARt", "BKtok", "Gb", "Gk"):
            W[nm] = [AR.bf16(8, 2, 128) for _ in range(2)]
        W["Tt"] = [AR.bf16(8, 128) for _ in range(2)]
        W["Pc"] = [AR.f32(8) for _ in range(2)]
        W["BKt"] = AR.bf16(8, 2, 128)
        W["Xs"] = AR.bf16(128); W["Ubf"] = AR.bf16(128); W["Ybd"] = AR.f32(8, 128)
        W["St"] = AR.f32(128); W["Stbf"] = AR.bf16(128); W["tmpS"] = AR.f32(128); W["tot"] = AR.f32(8)
        for nm in ("BBKt", "BXs", "BUbf", "BSt", "BStbf", "BtmpS", "Btot", "BYbd"):
            W[nm] = Buf(nm)
        for nm in ("BARt", "BPc"):
            W[nm] = [Buf(nm + "0"), Buf(nm + "1")]
        for nm in ("BBKtok", "BGb", "BGk", "BTt"):
            W[nm] = [[Buf(), Buf()], [Buf(), Buf()]]
        W["BLab"] = [Buf(), Buf()]
        W["BAn"] = [[Buf(), Buf()], [Buf(), Buf()]]; W["BBn"] = [[Buf(), Buf()], [Buf(), Buf()]]
        TA = Arena(tmp_raw[:, d * (NTMP // 2):(d + 1) * (NTMP // 2)], NTMP // 2)
        for nm in ("a", "ld", "cum", "E1", "E2", "E3", "u"):
            W[nm] = TA.f32(512); W["B" + nm] = Buf(nm)
        asb = lambda ap: ap.bitcast(BF16).rearrange("p (a b) -> p a b", b=128)
        W["An"] = [asb(W["E1"]), asb(W["E2"])]; W["Bn"] = [asb(W["E3"]), asb(W["u"])]; W["Lab"] = asb(W["ld"])
        W["alias_An"] = [W["BE1"], W["BE2"]]; W["alias_Bn"] = [W["BE3"], W["Bu"]]
        WD.append(W)
    for d in range(2):
        W = WD[d]
        for jp in range(2):
            Sc.op("pool", lambda: P_.memset(W["ARt"][jp], 0.0), writes=[W["BARt"][jp]])
        Sc.op("pool", lambda: P_.memset(W["BKt"], 0.0), writes=[W["BBKt"]])
    Sc.op("pool", lambda: P_.memset(vbd, 0.0), writes=[Bvbd])

    rawc = [AR.f32(514) for _ in range(2)]; Brawc = [Buf(), Buf()]
    nbc = AR.f32(512); sqc_ = AR.f32(512); Bnbc, Bsqc_ = Buf(), Buf()
    sqk = nbc; rnk = sqc_; Bsqk, Brnk = Bnbc, Bsqc_
    gatec = AR.f32(512); dtmp = AR.f32(512); sq2 = AR.f32(512); rstd = AR.f32(512); yn = AR.f32(512); ystc = [AR.bf16(512) for _ in range(2)]
    Bgatec, Bdt, Bs2, Brs, Byn, Bystc = Buf(), Buf(), Buf(), Buf(), Buf(), [Buf(), Buf()]
    rc_rr = [0]

    def shift_gen(dst, bdst, row0, mi, func=AF.Copy):
        for tc in range(4):
            sl = slice(tc * 512, (tc + 1) * 512)
            lo = max(0, tc * 512 - 1); hi = min(S, tc * 512 + 513)
            off = lo - (tc * 512 - 1)
            ri = rc_rr[0] % 2; rc_rr[0] += 1
            rc = rawc[ri]; brc = Brawc[ri]
            if tc == 0:
                Sc.op("pool", lambda: P_.memset(rc[:, 0:1], 0.0), writes=[brc])
            if tc == 3:
                Sc.op("pool", lambda: P_.memset(rc[:, 513:514], 0.0), writes=[brc])
            Sc.dma("sp", rc[:, off:off + (hi - lo)], proj_f[row0:row0 + 128, lo:hi], writes=[brc]); yield
            Sc.op("pool", lambda: P_.tensor_tensor(out=nbc, in0=rc[:, 0:512], in1=rc[:, 2:514], op=ALU.add), reads=[brc], writes=[Bnbc]); yield
            Sc.op("act", lambda: A_.activation(out=sqc_, in_=rc[:, 1:513], func=AF.Copy, scale=prm[:, OMM + mi:OMM + mi + 1]), reads=[brc, Bprm], writes=[Bsqc_]); yield
            if func == AF.Copy:
                Sc.op("dve", lambda: V.scalar_tensor_tensor(out=dst[:, sl], in0=nbc, scalar=prm[:, HMU + mi:HMU + mi + 1], in1=sqc_, op0=ALU.mult, op1=ALU.add),
                      reads=[Bnbc, Bprm, Bsqc_], writes=[bdst]); yield
            else:
                Sc.op("dve", lambda: V.scalar_tensor_tensor(out=sqc_, in0=nbc, scalar=prm[:, HMU + mi:HMU + mi + 1], in1=sqc_, op0=ALU.mult, op1=ALU.add),
                      reads=[Bnbc, Bprm, Bsqc_], writes=[Bsqc_]); yield
                Sc.op("act", lambda: A_.activation(out=dst[:, sl], in_=sqc_, func=func), reads=[Bsqc_], writes=[bdst]); yield

    def v3(ap, n=64):
        return ap.rearrange("p (c t) -> p c t", t=n)

    def rr(gens):
        act_ = list(gens)
        while act_:
            for g in list(act_):
                try:
                    next(g)
                    yield
                except StopIteration:
                    act_.remove(g)

    def run(gen):
        for _ in gen:
            pass

    def unit_prep(p, d, sc, jp):
        W = WD[d]
        sl = slice(sc * 512, (sc + 1) * 512)
        dh = slice(d * 64, (d + 1) * 64)
        pc = slice(p * 128, (p + 1) * 128)
        a, ld, cum, E1, E2, E3, u = W["a"], W["ld"], W["cum"], W["E1"], W["E2"], W["E3"], W["u"]
        Ba, Bld, Bcum, BE1, BE2, BE3, Bu = W["Ba"], W["Bld"], W["Bcum"], W["BE1"], W["BE2"], W["BE3"], W["Bu"]
        ARt, BKtok, Gb, Gk, Tt, Pc = W["ARt"][jp], W["BKtok"][jp], W["Gb"][jp], W["Gk"][jp], W["Tt"][jp], W["Pc"][jp]
        BARt, BBKtok, BGb, BGk, BTt, BPc = W["BARt"][jp], W["BBKtok"][jp], W["BGb"][jp], W["BGk"][jp], W["BTt"][jp], W["BPc"][jp]
        BKt, Lab = W["BKt"], W["Lab"]
        b = nbank(2, 8)
        mm_group(ps[b][:, :], [(a2b[dh, pc], la_s[dh, sl], [tb5, Bla])], Bps[b]); yield
        hc = HA0 + d * 6 + p
        Sc.op("act", lambda: A_.activation(out=a, in_=ps[b][:, :], func=AF.Tanh, bias=prm[:, hc:hc + 1], scale=0.5), reads=[Bps[b], Bprm], writes=[Ba]); yield
        Sc.op("dve", lambda: V.tensor_scalar(out=a, in0=a, scalar1=0.5, scalar2=0.5, op0=ALU.mult, op1=ALU.add), reads=[Ba], writes=[Ba]); yield
        b = nbank(2, 8)
        mm_group(ps[b][:, :], [(w2b[dh, pc], lw_t[dh, sl], [tb4, Blw])], Bps[b]); yield
        hc2 = HW0 + d * 6 + p
        Sc.op("act", lambda: A_.activation(out=ld, in_=ps[b][:, :], func=AF.Tanh, bias=prm[:, hc2:hc2 + 1], scale=0.5), reads=[Bps[b], Bprm], writes=[Bld] + W["BLab"]); yield
        Sc.op("dve", lambda: V.tensor_scalar(out=ld, in0=ld, scalar1=C1, scalar2=C1, op0=ALU.mult, op1=ALU.add), reads=[Bld], writes=[Bld]); yield
        Sc.op("dve", lambda: V.tensor_tensor_scan(out=cum, data0=segm, data1=ld, initial=0.0, op0=ALU.mult, op1=ALU.add), reads=[tb2, Bld], writes=[Bcum]); yield
        if d == 1:
            Sc.op("dve", lambda: V.tensor_copy(out=W["tot"], in_=v3(cum)[:, :, 63]), reads=[Bcum], writes=[W["Btot"]]); yield
            Sc.op("dve", lambda: V.tensor_tensor(out=cum, in0=ld, in1=cum, op=ALU.subtract), reads=[Bld, Bcum], writes=[Bcum]); yield
            Sc.op("dve", lambda: V.tensor_tensor(out=v3(cum), in0=v3(cum), in1=W["tot"].unsqueeze(2).to_broadcast([128, 8, 64]), op=ALU.add),
                  reads=[Bcum, W["Btot"]], writes=[Bcum]); yield
        Sc.op("dve", lambda: V.tensor_tensor(out=ld, in0=cum, in1=ld, op=ALU.subtract), reads=[Bcum, Bld], writes=[Bld]); yield
        Sc.op("act", lambda: A_.activation(out=E3, in_=ld, func=AF.Exp), reads=[Bld], writes=[BE3] + W["BBn"][0]); yield
        Sc.op("act", lambda: A_.activation(out=E1, in_=cum, func=AF.Exp), reads=[Bcum], writes=[BE1] + W["BAn"][0]); yield
        Sc.op("act", lambda: A_.activation(out=E2, in_=cum, func=AF.Exp, scale=-1.0), reads=[Bcum], writes=[BE2] + W["BAn"][1]); yield
        pcol = 63 if d == 0 else 0
        Sc.op("dve", lambda: V.tensor_copy(out=Pc, in_=v3(E1)[:, :, pcol]), reads=[BE1], writes=[BPc]); yield
        Sc.op("dve", lambda: V.tensor_scalar(out=u, in0=a, scalar1=prm[:, KA + p:KA + p + 1], scalar2=prm[:, OMK + p:OMK + p + 1], op0=ALU.mult, op1=ALU.add),
              reads=[Ba, Bprm], writes=[Bu] + W["BBn"][1]); yield
        Sc.op("dve", lambda: V.tensor_tensor(out=u, in0=k_[:, sl], in1=u, op=ALU.mult), reads=[Bk, Bu], writes=[Bu]); yield
        Sc.op("pool", lambda: P_.tensor_tensor(out=a, in0=kk_[:, sl], in1=a, op=ALU.mult), reads=[Bkk, Ba], writes=[Ba]); yield
        for half in range(2):
            hs = slice(half * 64, (half + 1) * 64)
            bc = slice(half * 64, (half + 1) * 64)
            Sc.op("dve", lambda: V.scalar_tensor_tensor(out=ARt[hs, :, 0, bc], in0=v3(kk_[hs, sl]), scalar=-1.0, in1=v3(E3[hs, :]), op0=ALU.mult, op1=ALU.mult),
                  reads=[Bkk, BE3], writes=[BARt]); yield
            Sc.op("pool", lambda: P_.tensor_tensor(out=ARt[hs, :, 1, bc], in0=v3(r_[hs, sl]), in1=v3(E1[hs, :]), op=ALU.mult), reads=[Br, BE1], writes=[BARt]); yield
            Sc.op("dve", lambda: V.tensor_tensor(out=BKt[hs, :, 1, bc], in0=v3(u[hs, :]), in1=v3(E2[hs, :]), op=ALU.mult), reads=[Bu, BE2], writes=[W["BBKt"]]); yield
            Sc.op("dve", lambda: V.tensor_tensor(out=BKt[hs, :, 0, bc], in0=v3(a[hs, :]), in1=v3(E2[hs, :]), op=ALU.mult), reads=[Ba, BE2], writes=[W["BBKt"]]); yield
        Sc.op("pool", lambda: P_.tensor_tensor(out=cum, in0=r_[:, sl], in1=u, op=ALU.mult), reads=[Br, Bu, Bcum], writes=[Bcum]); yield
        b = nbank(2, 8)
        mm_group(ps[b][:, :], [(rkones, cum, [Brk, Bcum])], Bps[b]); yield
        Sc.op("dve", lambda: V.tensor_tensor(out=cum, in0=ps[b][:, :], in1=v_[:, sl], op=ALU.mult), reads=[Bps[b], Bv, Bcum], writes=[Bcum]); yield
        Sc.op("pool", lambda: P_.tensor_tensor(out=bon[:, sl], in0=bon[:, sl], in1=cum, op=ALU.add), reads=[Bbon, Bcum], writes=[Bbon]); yield
        for hb in range(2):
            b = nbank(2, 8)
            for ci in range(4):
                for s2 in range(2):
                    j = ci * 2 + s2
                    Sc.op("pe", lambda: T_.transpose(out=psb[b][:, j * 128:(j + 1) * 128], in_=BKt[:, hb * 4 + ci, s2, :], identity=identb),
                          reads=[W["BBKt"], Bid], writes=[Bps[b]] if j == 0 else [], inc=(j == 7), setw=[Bps[b]] if j == 7 else [])
            yield
            dstv = BKtok[:, hb * 4:(hb + 1) * 4, :, :].rearrange("p a b c -> p (a b c)")
            Sc.op("act", lambda: A_.activation(out=dstv, in_=psb[b], func=AF.Copy), reads=[Bps[b]], writes=[BBKtok[hb]]); yield
        M2 = rwm[:, d, 0:2, :].rearrange("p a b -> p (a b)")
        ML = rwm[:, d, 2, :]
        for hb in range(2):
            bL = nbank(2, 8)
            mm_multi([(ps[bL][:, ci * 128:(ci + 1) * 128], ARt[:, hb * 4 + ci, 0, :], BKt[:, hb * 4 + ci, 0, :], [W["BBKt"], BARt]) for ci in range(4)], Bps[bL])
            yield
            Sc.op("dve", lambda: V.tensor_tensor(out=Lab[:, hb * 4:(hb + 1) * 4, :], in0=ps[bL][:, :].rearrange("p (a b) -> p a b", b=128),
                                                 in1=ML.unsqueeze(1).to_broadcast([128, 4, 128]), op=ALU.mult), reads=[Bps[bL], tb3], writes=[W["BLab"][hb], Bld]); yield
            bB = [nbank(2, 8), nbank(2, 8)]
            for i in range(2):
                mm_multi([(ps[bB[i]][:, q2 * 256:(q2 + 1) * 256], BKt[:, hb * 4 + i * 2 + q2, 0, :], ARt[:, hb * 4 + i * 2 + q2, :, :], [W["BBKt"], BARt]) for q2 in range(2)], Bps[bB[i]])
            yield
            for i in range(2):
                c0 = hb * 4 + i * 2
                Sc.op("dve", lambda: V.tensor_tensor(out=Gb[:, c0:c0 + 2, :, :].rearrange("p a b c -> p a (b c)"), in0=ps[bB[i]][:, :].rearrange("p (a b) -> p a b", b=256),
                                                     in1=M2.unsqueeze(1).to_broadcast([128, 2, 256]), op=ALU.mult), reads=[Bps[bB[i]], tb3], writes=[BGb[hb]]); yield
        for hb in range(2):
            cs = slice(hb * 4, (hb + 1) * 4)
            Sc.op("dve", lambda: V.tensor_tensor(out=Tt[:, cs, :], in0=Gb[:, cs, 0, :], in1=identb.unsqueeze(1).to_broadcast([128, 4, 128]), op=ALU.add),
                  reads=[BGb[hb], Bid], writes=[BTt[hb]]); yield
        for lvl in range(0, 6):
            for hb in range(2):
                cs = slice(hb * 4, (hb + 1) * 4)
                if lvl == 0:
                    Aget = lambda c: Gb[:, c, 0, :]
                    Bget = lambda c: Lab[:, c, :]
                    BAsrc, BBsrc = BGb[hb], W["BLab"][hb]
                else:
                    Aprev, Bprev = W["An"][lvl % 2], W["Bn"][lvl % 2]
                    Aget = lambda c: Aprev[:, c, :]
                    Bget = lambda c: Bprev[:, c, :]
                    BAsrc, BBsrc = W["BAn"][lvl % 2][hb], W["BBn"][lvl % 2][hb]
                Anew, Bnew = W["An"][(lvl + 1) % 2], W["Bn"][(lvl + 1) % 2]
                BAnew, BBnew = W["BAn"][(lvl + 1) % 2][hb], W["BBn"][(lvl + 1) % 2][hb]
                if lvl >= 1:
                    bT = nbank(2, 8)
                    mm_multi([(ps[bT][:, ci * 128:(ci + 1) * 128], Bget(hb * 4 + ci), Tt[:, hb * 4 + ci, :], [BBsrc, BTt[hb]]) for ci in range(4)], Bps[bT])
                    yield
                if lvl < 5:
                    if lvl < 4:
                        bA = nbank(2, 8)
                        mm_multi([(ps[bA][:, ci * 128:(ci + 1) * 128], Bget(hb * 4 + ci), Aget(hb * 4 + ci), [BAsrc, BBsrc]) for ci in range(4)], Bps[bA])
                        yield
                    bBm = nbank(2, 8)
                    mm_multi([(ps[bBm][:, ci * 128:(ci + 1) * 128], Aget(hb * 4 + ci), Bget(hb * 4 + ci), [BAsrc, BBsrc]) for ci in range(4)], Bps[bBm])
                    yield
                if lvl >= 1:
                    Sc.op("dve", lambda: V.tensor_tensor(out=Tt[:, cs, :], in0=ps[bT][:, :].rearrange("p (a b) -> p a b", b=128), in1=Tt[:, cs, :], op=ALU.add),
                          reads=[Bps[bT], BTt[hb]], writes=[BTt[hb]]); yield
                if lvl < 5:
                    if lvl < 4:
                        Sc.op("act", lambda: A_.activation(out=Anew[:, cs, :], in_=ps[bA][:, :].rearrange("p (a b) -> p a b", b=128), func=AF.Copy),
                              reads=[Bps[bA]], writes=[BAnew, W["alias_An"][(lvl + 1) % 2]]); yield
                    Sc.op("act", lambda: A_.activation(out=Bnew[:, cs, :], in_=ps[bBm][:, :].rearrange("p (a b) -> p a b", b=128), func=AF.Copy),
                          reads=[Bps[bBm]], writes=[BBnew, W["alias_Bn"][(lvl + 1) % 2]]); yield
        for hb in range(2):
            bK = [nbank(2, 8), nbank(2, 8)]
            for i in range(2):
                mm_multi([(ps[bK[i]][:, q2 * 256:(q2 + 1) * 256], BKt[:, hb * 4 + i * 2 + q2, 1, :], ARt[:, hb * 4 + i * 2 + q2, :, :], [W["BBKt"], BARt]) for q2 in range(2)], Bps[bK[i]])
            yield
            for i in range(2):
                c0 = hb * 4 + i * 2
                Sc.op("dve", lambda: V.tensor_tensor(out=Gk[:, c0:c0 + 2, :, :].rearrange("p a b c -> p a (b c)"), in0=ps[bK[i]][:, :].rearrange("p (a b) -> p a b", b=256),
                                                     in1=M2.unsqueeze(1).to_broadcast([128, 2, 256]), op=ALU.mult), reads=[Bps[bK[i]], tb3], writes=[BGk[hb]]); yield

    def scan_step(p, d, sc, ci, jp):
        W = WD[d]
        hb = ci // 4
        cg = sc * 8 + ci
        sb = d
        bkb = Bps[sb]
        ARt, BKtok, Gb, Gk, Tt, Pc = W["ARt"][jp], W["BKtok"][jp], W["Gb"][jp], W["Gk"][jp], W["Tt"][jp], W["Pc"][jp]
        BARt, BBKtok, BGb, BGk, BTt, BPc = W["BARt"][jp], W["BBKtok"][jp], W["BGb"][jp], W["BGk"][jp], W["BTt"][jp], W["BPc"][jp]
        St, Stbf, Xs, Ubf, tmpS = W["St"], W["Stbf"], W["Xs"], W["Ubf"], W["tmpS"]
        vt = Vtok[:, cg, :]
        bvt = BVtok[cg // 8]
        pcc = Pc[:, ci:ci + 1]
        Sc.op("act", lambda: A_.activation(out=tmpS, in_=St, func=AF.Copy, scale=pcc), reads=[W["BSt"], BPc], writes=[W["BtmpS"]]); yield
        mm_group(ps[sb][:, 0:128], [(ARt[:, ci, 0, :], Stbf, [BARt, W["BStbf"]]), (Gk[:, ci, 0, :], vt, [BGk[hb], bvt])], bkb); yield
        Sc.op("act", lambda: A_.activation(out=Xs, in_=ps[sb][:, 0:128], func=AF.Copy), writes=[bkb, W["BXs"]]); yield
        mm_group(ps[sb][:, 128:256], [(Tt[:, ci, :], Xs, [BTt[hb], W["BXs"]])], bkb); yield
        Sc.op("dve", lambda: V.tensor_copy(out=Ubf, in_=ps[sb][:, 128:256]), writes=[bkb, W["BUbf"]]); yield
        mm_group(ps[sb][:, 384:512], [(BKtok[:, ci, 0, :], Ubf, [BBKtok[hb], W["BUbf"]]), (BKtok[:, ci, 1, :], vt, [BBKtok[hb], bvt])], bkb); yield
        mm_group(ps[sb][:, 256:384], [(Stbf, ARt[:, ci, 1, :], [BARt, W["BStbf"]]), (Ubf, Gb[:, ci, 1, :], [W["BUbf"], BGb[hb]]),
                                      (vt, Gk[:, ci, 1, :], [bvt, BGk[hb]])], bkb); yield
        Sc.op("dve", lambda: V.scalar_tensor_tensor(out=St, in0=ps[sb][:, 384:512], scalar=pcc, in1=tmpS, op0=ALU.mult, op1=ALU.add),
              reads=[BPc, W["BtmpS"]], writes=[bkb, W["BSt"]]); yield
        Sc.op("act", lambda: A_.activation(out=Stbf, in_=St, func=AF.Copy), reads=[W["BSt"]], writes=[W["BStbf"]]); yield
        Sc.op("act", lambda: A_.activation(out=W["Ybd"][:, ci, :], in_=ps[sb][:, 256:384], func=AF.Copy), writes=[bkb, W["BYbd"]]); yield

    def chain(p, d, sc, jp):
        order = range(8) if d == 0 else range(7, -1, -1)
        for ci in order:
            yield from scan_step(p, d, sc, ci, jp)
        W = WD[d]
        sl = slice(sc * 512, (sc + 1) * 512)
        for half in range(2):
            hs = slice(half * 64, (half + 1) * 64)
            Sc.op("pool", lambda: P_.tensor_tensor(out=v3(ysum[hs, sl]), in0=v3(ysum[hs, sl]), in1=W["Ybd"][hs, :, half * 64:(half + 1) * 64], op=ALU.add),
                  reads=[Bys, W["BYbd"]], writes=[Bys]); yield

    run(shift_gen(lw_t, Blw, LW0, 18, func=AF.Tanh))
    run(shift_gen(la_s, Bla, LA0, 19))

    PAIRS = list(range(6))
    if stop_after.startswith("rwkvp"):
        PAIRS = [int(ch) for ch in stop_after[5:]]

    def prologue(p):
        yield from shift_gen(r_, Br, R0 + p * 128, p)
        yield from shift_gen(k_, Bk, K0 + p * 128, 6 + p)
        yield from shift_gen(v_, Bv, V0 + p * 128, 12 + p)
        kcol = prm[:, KK + p:KK + p + 1]
        for tc in range(4):
            sl = slice(tc * 512, (tc + 1) * 512)
            Sc.op("act", lambda: A_.activation(out=sqk, in_=k_[:, sl], func=AF.Square, scale=kcol), reads=[Bk, Bprm], writes=[Bsqk]); yield
            b = nbank(2, 8)
            mm_group(ps[b][:, :], [(bones, sqk, [Bc, Bsqk])], Bps[b]); yield
            rsqrt_act(rnk, ps[b][:, :], 1.0, 1e-24, [Bps[b]], [Brnk]); yield
            Sc.op("dve", lambda: V.scalar_tensor_tensor(out=kk_[:, sl], in0=k_[:, sl], scalar=kcol, in1=rnk, op0=ALU.mult, op1=ALU.mult),
                  reads=[Bk, Bprm, Brnk], writes=[Bkk]); yield
        Sc.op("dve", lambda: V.tensor_scalar(out=rkones, in0=bones, scalar1=prm[:, RK + p:RK + p + 1], scalar2=None, op0=ALU.mult), reads=[Bc, Bprm], writes=[Brk]); yield

    def vtok_build(p):
        for q in range(4):
            for half in range(2):
                hs = slice(half * 64, (half + 1) * 64)
                Sc.op("pool", lambda: P_.tensor_copy(out=vbd[hs, :, half * 64:(half + 1) * 64], in_=v3(v_[hs, q * 512:(q + 1) * 512])), reads=[Bv], writes=[Bvbd]); yield
            b = nbank(2, 8)
            for j in range(8):
                Sc.op("pe", lambda: T_.transpose(out=psb[b][:, j * 128:(j + 1) * 128], in_=vbd[:, j, :], identity=identb),
                      reads=[Bvbd, Bid], writes=[Bps[b]] if j == 0 else [], inc=(j == 7), setw=[Bps[b]] if j == 7 else [])
            yield
            Sc.op("dve", lambda: V.tensor_copy(out=Vtok[:, q * 8:(q + 1) * 8, :].rearrange("p a b -> p (a b)"), in_=psb[b]), reads=[Bps[b]], writes=[BVtok[q]]); yield

    def resets(p):
        Sc.op("pool", lambda: P_.memset(bon, 0.0), writes=[Bbon])
        Sc.op("pool", lambda: P_.memset(ysum, 0.0), writes=[Bys])
        for d in range(2):
            W = WD[d]
            Sc.op("pool", lambda: P_.memset(W["St"], 0.0), writes=[W["BSt"]])
            Sc.op("pool", lambda: P_.memset(W["Stbf"], 0.0), writes=[W["BStbf"]])

    def epilogue(p):
        dump("d_ysum%d" % p, ysum, [128, S], F32, [Bys])
        for tc in range(4):
            sl = slice(tc * 512, (tc + 1) * 512)
            yc = ystc[tc % 2]; byc = Bystc[tc % 2]
            Sc.dma("sp", gatec, proj_f[RG0 + p * 128:RG0 + (p + 1) * 128, sl], writes=[Bgatec])
            b = nbank(2, 8)
            mm_group(ps[b][:, :], [(bo64, ysum[:, sl], [tb1, Bys])], Bps[b]); yield
            Sc.op("dve", lambda: V.tensor_tensor(out=dtmp, in0=ysum[:, sl], in1=ps[b][:, :], op=ALU.subtract), reads=[Bys, Bps[b]], writes=[Bdt]); yield
            Sc.op("act", lambda: A_.activation(out=sq2, in_=dtmp, func=AF.Square), reads=[Bdt], writes=[Bs2]); yield
            b2 = nbank(2, 8)
            mm_group(ps[b2][:, :], [(bo64, sq2, [tb1, Bs2])], Bps[b2]); yield
            rsqrt_act(rstd, ps[b2][:, :], 1.0, GN_EPS, [Bps[b2]], [Brs]); yield
            Sc.op("dve", lambda: V.tensor_tensor(out=yn, in0=dtmp, in1=rstd, op=ALU.mult), reads=[Bdt, Brs], writes=[Byn]); yield
            Sc.op("dve", lambda: V.tensor_scalar(out=yn, in0=yn, scalar1=prm[:, LW + p:LW + p + 1], scalar2=prm[:, LB + p:LB + p + 1], op0=ALU.mult, op1=ALU.add),
                  reads=[Byn, Bprm], writes=[Byn]); yield
            Sc.op("pool", lambda: P_.tensor_tensor(out=yn, in0=yn, in1=bon[:, sl], op=ALU.add), reads=[Byn, Bbon], writes=[Byn]); yield
            Sc.op("dve", lambda: V.tensor_tensor(out=yc, in0=yn, in1=gatec, op=ALU.mult), reads=[Byn, Bgatec], writes=[byc]); yield
            Sc.dma("sp", ybuf[768 + p * 128:768 + (p + 1) * 128, sl], yc, reads=[byc]); yield

    def preps(p, j):
        return [unit_prep(p, 0, j, j % 2), unit_prep(p, 1, 3 - j, j % 2)]

    Sc.barrier()
    run(prologue(PAIRS[0]))
    run(vtok_build(PAIRS[0]))
    resets(PAIRS[0])
    run(rr(preps(PAIRS[0], 0)))
    for i, p in enumerate(PAIRS):
        nxt = PAIRS[i + 1] if i + 1 < len(PAIRS) else None
        for j in range(4):
            gens = [chain(p, 0, j, j % 2), chain(p, 1, 3 - j, j % 2)]
            if j < 3:
                gens += preps(p, j + 1)
            elif nxt is not None:
                gens.append(prologue(nxt))
            run(rr(gens))
        gens = [epilogue(p)]
        if nxt is not None:
            gens.append(vtok_build(nxt))
        run(rr(gens))
        if nxt is not None:
            resets(nxt)
            run(rr(preps(nxt, 0)))
    Sc.barrier()
    if stop_after.startswith("rwkv"):
        return finish_debug(nc, Sc, locals())

    AR.release(m_persist)
    cosr = AR.f32(16, 32); sinr = AR.f32(16, 32); gqk = AR.f32(16, 64); esink = AR.f32(12); mLR = AR.bf16(2, 128)
    Bcs, Bsn, Bgq, Bes, Bml = Buf(), Buf(), Buf(), Buf(), Buf()
    Sc.dma("sp", cosr, cos_d[:, :, :], writes=[Bcs]); Sc.dma("sp", sinr, sin_d[:, :, :], writes=[Bsn])
    Sc.dma("sp", gqk, gqk_d.rearrange("p (a b) -> p a b", b=64), writes=[Bgq]); Sc.dma("sp", esink, sink_d[:, :], writes=[Bes])
    Sc.dma("pool", mLR, maskLR_d[:, :, :], writes=[Bml])
    Sc.op("act", lambda: A_.activation(out=esink, in_=esink, func=AF.Exp), reads=[Bes], writes=[Bes])
    qTa = AR.bf16(12, S); kTa = AR.bf16(4, S); vext = AR.bf16(16, 4, 128); ya = AR.bf16(6, S)
    BqTa = [Buf() for _ in range(16)]; BkTa = [Buf() for _ in range(16)]; Bvx = [Buf() for _ in range(16)]; Bya = [Buf() for _ in range(16)]
    Sc.op("pool", lambda: P_.memset(vext, 1.0), writes=Bvx)
    qkv = [AR.f32(NQKV) for _ in range(2)]; Bqkv2 = [Buf(), Buf()]
    gta = [AR.f32(6, 128) for _ in range(3)]; Bgta = [Buf(), Buf(), Buf()]
    sqa = AR.f32(1024); ssa = AR.f32(16); rsa = AR.f32(16); qna = AR.f32(16, 64); qra = AR.bf16(16, 64)
    rt_ = [AR.f32(16, 32) for _ in range(4)]
    Bsqa, Bssa, Bqna, Bqra = Buf(), Buf(), Buf(), Buf()
    Brt = [Buf() for _ in range(4)]
    pTa = [AR.bf16(384) for _ in range(3)]; BpTa = [Buf() for _ in range(3)]
    rda = AR.f32(4, 128); Brda = Buf(); yodd = AR.f32(2, 128); Byodd = Buf(); yev = AR.f32(2, 128); Byev = Buf()
    Bo = [Buf("o5"), Buf("o6"), Buf("o7")]
    pta_rr = [0]
    ag_src = proj_f[AG0:AG0 + 768, :].rearrange("(c p) t -> p c t", p=128)

    def prep_blk(tb):
        qk_ = qkv[tb % 2]; bqk = Bqkv2[tb % 2]
        Sc.dma("sp", qk_, qkv_t[tb * 128:(tb + 1) * 128, :], writes=[bqk])
        Sc.dma("sp", gta[tb % 3], ag_src[:, :, tb * 128:(tb + 1) * 128], writes=[Bgta[tb % 3]])
        Sc.op("act", lambda: A_.activation(out=sqa, in_=qk_[:, 0:1024], func=AF.Square), reads=[bqk], writes=[Bsqa]); yield
        Sc.op("dve", lambda: V.tensor_reduce(out=ssa, in_=sqa.rearrange("p (a b) -> p a b", b=64), axis=AX.X, op=ALU.add), reads=[Bsqa], writes=[Bssa]); yield
        rsqrt_act(rsa, ssa, 1.0 / 64, EPS, [Bssa], [Bssa]); yield
        Sc.op("dve", lambda: V.tensor_tensor(out=qna, in0=qk_[:, 0:1024].rearrange("p (a b) -> p a b", b=64), in1=rsa.unsqueeze(2).to_broadcast([128, 16, 64]), op=ALU.mult),
              reads=[bqk, Bssa], writes=[Bqna]); yield
        Sc.op("pool", lambda: P_.tensor_tensor(out=qna, in0=qna, in1=gqk, op=ALU.mult), reads=[Bqna, Bgq], writes=[Bqna]); yield
        t1 = qna[:, :, 0:32]; t2 = qna[:, :, 32:64]
        cb = cosr[:, tb, :].unsqueeze(1).to_broadcast([128, 16, 32]); sb_ = sinr[:, tb, :].unsqueeze(1).to_broadcast([128, 16, 32])
        Sc.op("dve", lambda: V.tensor_tensor(out=rt_[0], in0=t1, in1=cb, op=ALU.mult), reads=[Bqna, Bcs], writes=[Brt[0]]); yield
        Sc.op("pool", lambda: P_.tensor_tensor(out=rt_[1], in0=t2, in1=sb_, op=ALU.mult), reads=[Bqna, Bsn], writes=[Brt[1]]); yield
        Sc.op("dve", lambda: V.tensor_tensor(out=qra[:, :, 0:32], in0=rt_[0], in1=rt_[1], op=ALU.subtract), reads=[Brt[0], Brt[1]], writes=[Bqra]); yield
        Sc.op("pool", lambda: P_.tensor_tensor(out=rt_[2], in0=t2, in1=cb, op=ALU.mult), reads=[Bqna, Bcs], writes=[Brt[2]]); yield
        Sc.op("dve", lambda: V.tensor_tensor(out=rt_[3], in0=t1, in1=sb_, op=ALU.mult), reads=[Bqna, Bsn], writes=[Brt[3]]); yield
        Sc.op("dve", lambda: V.tensor_tensor(out=qra[:, :, 32:64], in0=rt_[2], in1=rt_[3], op=ALU.add), reads=[Brt[2], Brt[3]], writes=[Bqra]); yield
        Sc.op("act", lambda: A_.activation(out=vext[:, tb, :, 0:64], in_=qk_[:, 1024:1280].rearrange("p (a b) -> p a b", b=64), func=AF.Copy), reads=[bqk], writes=[Bvx[tb]]); yield
        b = 0
        qflat = qra.rearrange("p a b -> p (a b)")
        for j in range(8):
            Sc.op("pe", lambda: T_.transpose(out=psb[b][:, j * 128:(j + 1) * 128], in_=qflat[:, j * 128:(j + 1) * 128], identity=identb),
                  reads=[Bqra, Bid], writes=[Bps[b]] if j == 0 else [], inc=(j == 7), setw=[Bps[b]] if j == 7 else [])
        yield
        psv = psb[b].rearrange("p (a b) -> p a b", b=128)
        tsl = slice(tb * 128, (tb + 1) * 128)
        qv = qTa.rearrange("p (h two) t -> p h two t", two=2)
        kv = kTa.rearrange("p (h two) t -> p h two t", two=2)
        Sc.op("dve", lambda: V.tensor_copy(out=qv[0:64, :, 0, tsl], in_=psv[0:64, 0:6, :]), reads=[Bps[b]], writes=[BqTa[tb]]); yield
        Sc.op("act", lambda: A_.activation(out=qv[0:64, :, 1, tsl], in_=psv[64:128, 0:6, :], func=AF.Copy), reads=[Bps[b]], writes=[BqTa[tb]]); yield
        Sc.op("dve", lambda: V.tensor_copy(out=kv[0:64, :, 0, tsl], in_=psv[0:64, 6:8, :]), reads=[Bps[b]], writes=[BkTa[tb]]); yield
        Sc.op("act", lambda: A_.activation(out=kv[0:64, :, 1, tsl], in_=psv[64:128, 6:8, :], func=AF.Copy), reads=[Bps[b]], writes=[BkTa[tb]]); yield

    def attend(n):
        qsl = slice(n * 128, (n + 1) * 128)
        gt_ = gta[n % 3]; bgt = Bgta[n % 3]
        seq = []
        for g in range(4):
            kbs = [kb for kb in (n - 1, n, n + 1) if 0 <= kb < 16]
            for kb in kbs:
                seq.append((g, kb, kb == kbs[0], kb == kbs[-1]))
        order = []
        for (g, kb, fst, lst) in seq:
            for hh in range(3):
                order.append((g, kb, hh, fst, lst))
        firsts = {}; lasts = {}
        for idx, (g, kb, hh, fst, lst) in enumerate(order):
            ob = (3 * g + hh) // 4
            firsts.setdefault(ob, idx); lasts[ob] = idx
        idx = 0
        for (g, kb, fst, lst) in seq:
            b = sbanks[pta_rr[0] % len(sbanks)]
            mm_group(ps[b][:, 0:384], [(kTa[0:64, g, kb * 128:(kb + 1) * 128], qTa[0:64, 3 * g:3 * g + 3, qsl], [BkTa[kb], BqTa[n]])], Bps[b])
            pi = pta_rr[0] % 3; pta_rr[0] += 1
            yield
            Sc.op("act", lambda: A_.activation(out=pTa[pi], in_=ps[b][:, 0:384], func=AF.Exp, scale=0.125), reads=[Bps[b]], writes=[BpTa[pi]]); yield
            if kb != n:
                mk = mLR[:, 0 if kb < n else 1, :].unsqueeze(1).to_broadcast([128, 3, 128])
                Sc.op("pool", lambda: P_.tensor_tensor(out=pTa[pi].rearrange("p (a b) -> p a b", b=128), in0=pTa[pi].rearrange("p (a b) -> p a b", b=128), in1=mk, op=ALU.mult),
                      reads=[BpTa[pi], Bml], writes=[BpTa[pi]]); yield
            for hh in range(3):
                head = 3 * g + hh
                ob = head // 4
                col = (head % 4) * 128
                isf = firsts[ob] == idx; isl = lasts[ob] == idx
                st_flag = fst and (hh == 0 or head % 4 == 0)
                Sc.op("pe", lambda: T_.matmul(ps[obanks[ob]][:, col:col + 128], lhsT=vext[:, kb, g, :], rhs=pTa[pi][:, hh * 128:(hh + 1) * 128], start=st_flag, stop=lst),
                      reads=[Bvx[kb], BpTa[pi]], writes=[Bo[ob]] if isf else [], inc=True, setw=[Bo[ob]] if isl else [])
                idx += 1
            yield
        for ob in range(3):
            pv = ps[obanks[ob]][:, :].rearrange("p (a b) -> p a b", b=128)
            Sc.op("dve", lambda: V.tensor_tensor(out=rda[0:64, :, :], in0=pv[64:128, :, :], in1=esink[64:128, ob * 4:(ob + 1) * 4].unsqueeze(2).to_broadcast([64, 4, 128]), op=ALU.add),
                  reads=[Bo[ob], Bes], writes=[Brda]); yield
            Sc.op("act", lambda: A_.activation(out=rda[0:64, :, :], in_=rda[0:64, :, :], func=AF.Ln), reads=[Brda], writes=[Brda]); yield
            Sc.op("act", lambda: A_.activation(out=rda[0:64, :, :], in_=rda[0:64, :, :], func=AF.Exp, scale=-1.0), reads=[Brda], writes=[Brda]); yield
            pv2 = pv.rearrange("p (h two) t -> p h two t", two=2)
            rd2 = rda.rearrange("p (h two) t -> p h two t", two=2)
            c0 = ob * 2
            Sc.op("dve", lambda: V.tensor_tensor(out=yev[0:64, :, :], in0=pv2[0:64, :, 0, :], in1=rd2[0:64, :, 0, :], op=ALU.mult), reads=[Bo[ob], Brda], writes=[Byev]); yield
            Sc.op("dve", lambda: V.tensor_tensor(out=yodd[64:128, :, :], in0=pv2[0:64, :, 1, :], in1=rd2[0:64, :, 1, :], op=ALU.mult), reads=[Bo[ob], Brda], writes=[Byodd]); yield
            Sc.op("pool", lambda: P_.tensor_tensor(out=ya[0:64, c0:c0 + 2, qsl], in0=yev[0:64, :, :], in1=gt_[0:64, c0:c0 + 2, :], op=ALU.mult), reads=[Byev, bgt], writes=[Bya[n]]); yield
            Sc.op("pool", lambda: P_.tensor_tensor(out=ya[64:128, c0:c0 + 2, qsl], in0=yodd[64:128, :, :], in1=gt_[64:128, c0:c0 + 2, :], op=ALU.mult), reads=[Byodd, bgt], writes=[Bya[n]]); yield

    sbanks = [1, 2]
    obanks = [3, 4, 5]

    def attn_driver():
        for tb in range(16):
            gens = [prep_blk(tb)]
            if tb >= 2:
                gens.append(attend(tb - 2))
            yield from rr(gens)
        yield from attend(14)
        yield from attend(15)
        for c in range(6):
            Sc.dma("sp", ybuf[c * 128:(c + 1) * 128, :], ya[:, c, :], reads=Bya); yield

    run(attn_driver())
    Sc.barrier()
    if stop_after in ("attn", "xattn"):
        return finish_debug(nc, Sc, locals())

    AR.release(m_persist)
    mT = AR.bf16(16, S); BmT = [Buf() for _ in range(16)]
    m_p3 = AR.mark()
    yall = AR.bf16(16, S); Byall = [Buf() for _ in range(16)]
    wo = [AR.bf16(16, 256) for _ in range(2)]; Bwo = [[Buf(), Buf(), Buf()] for _ in range(2)]
    gt3 = [AR.f32(S) for _ in range(2)]; Bgt3 = [Buf(), Buf()]
    macc = AR.f32(S); Bmacc = [Buf() for _ in range(4)]
    ptmp = [AR.f32(512) for _ in range(2)]; Bptmp = [Buf(), Buf()]
    for k in range(16):
        Sc.dma("sp", yall[:, k, :], ybuf[k * 128:(k + 1) * 128, :], writes=[Byall[k]])
    wsrcs = [(attn_w_o.rearrange("(k p) n -> p k n", p=128), 0, 6), (rwkv_w_o.rearrange("(k p) n -> p k n", p=128), 6, 6), (x_w_o.rearrange("(k p) n -> p k n", p=128), 12, 4)]
    kranges = [range(0, 6), range(6, 12), range(12, 16)]

    def load_wo(fg):
        for bi, (src, k0, nk) in enumerate(wsrcs):
            Sc.dma("pool", wo[fg % 2][:, k0:k0 + nk, :], src[:, :, fg * 256:(fg + 1) * 256], writes=[Bwo[fg % 2][bi]])

    g3_rr = [0]; pt_rr = [0]
    load_wo(0)
    for fg in range(8):
        if fg + 1 < 8:
            load_wo(fg + 1)
        for fi in range(2):
            f = fg * 2 + fi
            for bi in range(3):
                gi = g3_rr[0] % 2; g3_rr[0] += 1
                r0 = MG0 + bi * 2048 + f * 128
                Sc.dma("sp", gt3[gi], proj_f[r0:r0 + 128, :], writes=[Bgt3[gi]])
                for tc in range(4):
                    sl = slice(tc * 512, (tc + 1) * 512)
                    bk = nbank()
                    mm_group(ps[bk][:, :], [(wo[fg % 2][:, kc, fi * 128:(fi + 1) * 128], yall[:, kc, sl], [Bwo[fg % 2][bi], Byall[kc]]) for kc in kranges[bi]], Bps[bk])
                    if bi == 0:
                        Sc.op("dve", lambda: V.tensor_tensor(out=macc[:, sl], in0=ps[bk][:, :], in1=gt3[gi][:, sl], op=ALU.mult), reads=[Bps[bk], Bgt3[gi]], writes=[Bmacc[tc]])
                    else:
                        pi = pt_rr[0] % 2; pt_rr[0] += 1
                        Sc.op("dve", lambda: V.tensor_tensor(out=ptmp[pi], in0=ps[bk][:, :], in1=gt3[gi][:, sl], op=ALU.mult), reads=[Bps[bk], Bgt3[gi]], writes=[Bptmp[pi]])
                        if bi == 1:
                            Sc.op("pool", lambda: P_.tensor_tensor(out=macc[:, sl], in0=macc[:, sl], in1=ptmp[pi], op=ALU.add), reads=[Bmacc[tc], Bptmp[pi]], writes=[Bmacc[tc]])
                        else:
                            Sc.op("pool", lambda: P_.tensor_tensor(out=mT[:, f, sl], in0=macc[:, sl], in1=ptmp[pi], op=ALU.add), reads=[Bmacc[tc], Bptmp[pi]], writes=[BmT[f]])
    dump("d_mT", mT, [128, 16, S], BF16, BmT)
    Sc.barrier()
    if stop_after == "merge":
        return finish_debug(nc, Sc, locals())
    AR.release(m_p3)
    wout = [AR.bf16(16, 512) for _ in range(2)]; Bwout = [[Buf() for _ in range(4)] for _ in range(2)]
    xres = [AR.f32(512) for _ in range(3)]; Bxres = [Buf() for _ in range(3)]
    ost = [AR.f32(512) for _ in range(3)]; Bost = [Buf() for _ in range(3)]
    wo_src = w_out.rearrange("(k p) n -> p k n", p=128)

    def load_wout(ng):
        for q in range(4):
            Sc.dma("pool", wout[ng % 2][:, q * 4:(q + 1) * 4, :], wo_src[:, q * 4:(q + 1) * 4, ng * 512:(ng + 1) * 512], writes=[Bwout[ng % 2][q]])

    load_wout(0)
    xr_rr = [0]
    final_toks = []
    for ng in range(4):
        if ng + 1 < 4:
            load_wout(ng + 1)
        for tb in range(16):
            xi = xr_rr[0] % 3; xr_rr[0] += 1
            Sc.dma("sp", xres[xi], x[tb * 128:(tb + 1) * 128, ng * 512:(ng + 1) * 512], writes=[Bxres[xi]])
            bk = nbank()
            mm_group(ps[bk][:, :], [(mT[:, f, tb * 128:(tb + 1) * 128], wout[ng % 2][:, f, :], [BmT[f], Bwout[ng % 2][f // 4]]) for f in range(16)], Bps[bk])
            Sc.op("dve", lambda: V.tensor_tensor(out=ost[xi], in0=ps[bk][:, :], in1=xres[xi], op=ALU.add), reads=[Bps[bk], Bxres[xi]], writes=[Bost[xi]])
            final_toks.append(Sc.dma("sp", out[tb * 128:(tb + 1) * 128, ng * 512:(ng + 1) * 512], ost[xi], reads=[Bost[xi]]))
    return finish_debug(nc, Sc, locals())


def finish_debug(nc, Sc, env):
    Sc.barrier()
    ok, stuck, _ = Sc.check_deadlock()
    if not ok:
        raise RuntimeError("logical deadlock in emitted program: %r" % (stuck,))
    Sc.close()
    for cm in reversed(env["ps_cms"]):
        cm.__exit__(None, None, None)
    env["big_cm"].__exit__(None, None, None)
    return nc


def make_in_maps(inputs):
    c = host_consts()
    f = lambda k: np.asarray(inputs[k], dtype=np.float32)
    sq = lambda k: f(k)[0]
    prm = np.zeros((128, NPRM), np.float32)

    def put(col, vec):
        m = vec.size // 128
        prm[:, col:col + m] = vec.reshape(m, 128).T

    put(NG, sq("norm_g")); put(MG_, sq("mem_norm_g")); put(GB, sq("gate_b")); put(MU, sq("rwkv_mu"))
    put(KK, sq("rwkv_k_k")); put(KA, sq("rwkv_k_a")); put(RK, sq("rwkv_r_k").reshape(-1)); put(LW, sq("rwkv_ln_w"))
    put(LB, sq("rwkv_ln_b")); put(W0, sq("rwkv_w0").reshape(-1)); put(A0, sq("rwkv_a0").reshape(-1))
    put(XQG, sq("x_q_norm_g")); put(XKG, sq("x_k_norm_g"))
    gqk = np.concatenate([np.tile(sq("attn_q_norm_g"), 12), np.tile(sq("attn_k_norm_g"), 4)])
    shared = {
        "w_in": sq("w_in"), "attn_w_o": sq("attn_w_o"), "rwkv_w_o": sq("rwkv_w_o"), "x_w_o": sq("x_w_o"),
        "x_w_kv": sq("x_w_kv"), "w_out": sq("w_out"),
        "w2cat": np.ascontiguousarray(sq("rwkv_w2").reshape(128, 768)), "a2cat": np.ascontiguousarray(sq("rwkv_a2").reshape(128, 768)),
        "prm": prm, "gqk": np.ascontiguousarray(np.broadcast_to(gqk[None, :], (128, 1024))),
        "sinkb": np.ascontiguousarray(np.broadcast_to(sq("attn_sink")[None, :], (128, 12))),
    }
    shared.update(c)
    xs = f("x"); ms = f("mem")
    return [dict(shared, x=np.ascontiguousarray(xs[b]), mem=np.ascontiguousarray(ms[b])) for b in range(xs.shape[0])]


_NC_CACHE = {}


def kernel(**inputs):
    in_maps = make_in_maps(inputs)
    if "nc" not in _NC_CACHE:
        _NC_CACHE["nc"] = build_nc()
    nc = _NC_CACHE["nc"]
    res = run_bass_kernel_spmd(nc, in_maps, core_ids=list(range(len(in_maps))))
    return np.stack([np.asarray(r["out"], dtype=np.float32) for r in res.results], axis=0)
```

```python
import math
import numpy as np
import ml_dtypes
import concourse.bass as bass
import concourse.mybir as mybir
from concourse.bass_utils import run_bass_kernel_spmd

F32 = mybir.dt.float32
BF16 = mybir.dt.bfloat16
AF = mybir.ActivationFunctionType
ALU = mybir.AluOpType
AX = mybir.AxisListType

S = 2048
D = 2048
NMEM = 256
INW = 12544
NQKV = 1280
NPF = INW - NQKV
EPS = 1e-6
GN_EPS = 64e-5
C1 = -0.5 * math.exp(-0.5)

AG0 = 0
R0 = 2048 - NQKV
K0 = R0 + 768
V0 = K0 + 768
LW0 = V0 + 768
LA0 = LW0 + 128
RG0 = 4608 - NQKV
XQ0 = 5376 - NQKV
XG0 = 5888 - NQKV
MG0 = 6400 - NQKV

NG, MG_, GB, MU, KK, KA, RK, LW, LB, W0, A0, XQG, XKG = 0, 16, 32, 80, 100, 106, 112, 118, 124, 130, 142, 154, 155
OMM, HMU, OMK, HW0, HA0, XG2 = 156, 176, 196, 202, 214, 226
NPRM = 228


class Buf:
    __slots__ = ("name", "w", "r", "pending")

    def __init__(self, name=""):
        self.name = name
        self.w = None
        self.r = {}
        self.pending = False


class Sched:
    ENG = ("pe", "act", "dve", "pool", "sp")

    def __init__(self, nc, n_dma_sems=40):
        self.nc = nc
        self.eng = {"pe": nc.tensor, "act": nc.scalar, "dve": nc.vector, "pool": nc.gpsimd, "sp": nc.sync}
        self.sem = {}
        self.cnt = {e: 0 for e in self.ENG}
        self.known = {e: {} for e in self.ENG}
        self._cms = []
        for e in self.ENG:
            cm = nc.semaphore("s_" + e)
            self.sem[e] = cm.__enter__()
            self._cms.append(cm)
        self.dsem = []
        for i in range(n_dma_sems):
            cm = nc.semaphore("d%d" % i)
            self.dsem.append([cm.__enter__(), 0])
            self._cms.append(cm)
        self.dnext = 0
        self.nwait = 0
        self.log = {e: [] for e in self.ENG}

    def close(self):
        for cm in reversed(self._cms):
            cm.__exit__(None, None, None)

    def _wait(self, e, key, semh, val):
        k = self.known[e]
        if k.get(key, 0) >= val:
            return
        self.eng[e].wait_ge(semh, val)
        self.nwait += 1
        self.log[e].append(("w", key, val))
        k[key] = val

    def wait_tok(self, e, tok):
        if tok is None:
            return
        kind, a, v = tok
        if kind == "eng":
            self._wait(e, a, self.sem[a], v)
        else:
            self._wait(e, "d%d" % a, self.dsem[a][0], v)

    def deps(self, e, reads, writes):
        for b in reads:
            self.wait_tok(e, b.w)
        for b in writes:
            self.wait_tok(e, b.w)
            for tok in b.r.values():
                self.wait_tok(e, tok)

    def op(self, e, fn, reads=(), writes=(), inc=True, setw=None):
        self.deps(e, reads, writes)
        ins = fn()
        if inc:
            self.cnt[e] += 1
            ins.then_inc(self.sem[e], 1)
            self.log[e].append(("i", e, 1))
            tok = ("eng", e, self.cnt[e])
        else:
            tok = ("eng", e, self.cnt[e] + 1)
        for b in reads:
            b.r[("eng", e)] = tok
            b.pending = False
        for b in (writes if setw is None else setw):
            b.w = tok
            b.r = {}
            b.pending = True
        for b in writes:
            b.pending = True
        return ins

    def dma(self, e, out, in_, reads=(), writes=()):
        idx = self.dnext
        self.dnext = (self.dnext + 1) % len(self.dsem)
        semh, val = self.dsem[idx]
        if val > 0:
            self._wait(e, "d%d" % idx, semh, val)
        self.deps(e, reads, writes)
        ins = self.eng[e].dma_start(out=out, in_=in_)
        val += 16
        ins.then_inc(semh, 16)
        self.log[e].append(("i", "d%d" % idx, 16))
        self.dsem[idx][1] = val
        tok = ("dma", idx, val)
        for b in reads:
            b.r[("dma", idx)] = tok
        for b in writes:
            b.w = tok
            b.r = {}
        return tok

    def check_deadlock(self):
        sem = {}
        pos = {e: 0 for e in self.ENG}
        prog = True
        while prog:
            prog = False
            for e in self.ENG:
                lg = self.log[e]
                while pos[e] < len(lg):
                    kind, key, val = lg[pos[e]]
                    if kind == "w":
                        if sem.get(key, 0) < val:
                            break
                    else:
                        sem[key] = sem.get(key, 0) + val
                    pos[e] += 1
                    prog = True
        stuck = {e: (pos[e], len(self.log[e]), self.log[e][pos[e]] if pos[e] < len(self.log[e]) else None) for e in self.ENG}
        ok = all(pos[e] == len(self.log[e]) for e in self.ENG)
        return ok, stuck, sem

    def barrier(self):
        for e in self.ENG:
            for e2 in self.ENG:
                if e2 != e and self.cnt[e2] > 0:
                    self._wait(e, e2, self.sem[e2], self.cnt[e2])
            for idx, (semh, val) in enumerate(self.dsem):
                if val > 0:
                    self._wait(e, "d%d" % idx, semh, val)


class Arena:
    def __init__(self, big, n):
        self.big = big
        self.n = n
        self.off = 0

    def mark(self):
        return self.off

    def release(self, m):
        self.off = m

    def _raw(self, nf32):
        a = self.off
        self.off += nf32
        assert self.off <= self.n, "SBUF arena overflow %d > %d" % (self.off, self.n)
        return self.big[:, a:a + nf32]

    @staticmethod
    def _shape(ap, dims):
        if len(dims) == 1:
            return ap
        if len(dims) == 2:
            return ap.rearrange("p (a b) -> p a b", b=dims[1])
        if len(dims) == 3:
            return ap.rearrange("p (a b c) -> p a b c", b=dims[1], c=dims[2])
        raise ValueError

    def f32(self, *dims):
        n = int(np.prod(dims))
        return self._shape(self._raw(n), dims)

    def bf16(self, *dims):
        n = int(np.prod(dims))
        assert n % 2 == 0
        return self._shape(self._raw(n // 2).bitcast(BF16), dims)


def host_consts():
    c = {}
    c["identf"] = np.eye(128, dtype=np.float32)
    bo = np.zeros((128, 128), np.float32)
    bo[:64, :64] = 1.0
    bo[64:, 64:] = 1.0
    c["bones"] = bo
    c["bo64"] = bo / 64.0
    half = 32
    inv = (10000.0 ** (-np.arange(half, dtype=np.float64) / half))
    ang = np.arange(S, dtype=np.float64)[:, None] * inv[None, :]
    c["cosr"] = np.ascontiguousarray(np.cos(ang).reshape(16, 128, 32).transpose(1, 0, 2)).astype(np.float32)
    c["sinr"] = np.ascontiguousarray(np.sin(ang).reshape(16, 128, 32).transpose(1, 0, 2)).astype(np.float32)
    j = np.arange(128)[:, None]
    i = np.arange(128)[None, :]
    c["maskLR"] = np.stack([(j >= i), (j <= i)], axis=1).astype(np.float32)
    t = np.arange(64)
    st_f = (t[:, None] < t[None, :]).astype(np.float32)
    in_f = (t[:, None] <= t[None, :]).astype(np.float32)
    def bd(m):
        z = np.zeros((128, 128), np.float32)
        z[:64, :64] = m
        z[64:, 64:] = m
        return z
    rwm = np.zeros((128, 2, 3, 128), np.float32)
    rwm[:, 0, 0] = bd(st_f); rwm[:, 0, 1] = bd(in_f); rwm[:, 0, 2] = bd(st_f.T)
    rwm[:, 1, 0] = bd(st_f.T); rwm[:, 1, 1] = bd(in_f.T); rwm[:, 1, 2] = bd(st_f)
    c["rwm"] = rwm
    seg = np.ones((128, 512), np.float32)
    seg[:, ::64] = 0.0
    c["segm"] = seg
    return c


def build_nc(stop_after="all", debug=()):
    nc = bass.Bass("TRN2", target_bir_lowering=False)

    def din(name, shape):
        return nc.dram_tensor(name, list(shape), F32, kind="ExternalInput").ap()

    x = din("x", [S, D]); mem = din("mem", [NMEM, D]); w_in = din("w_in", [D, INW])
    attn_w_o = din("attn_w_o", [768, D]); rwkv_w_o = din("rwkv_w_o", [768, D]); x_w_o = din("x_w_o", [512, D])
    x_w_kv = din("x_w_kv", [D, 1024]); w_out = din("w_out", [D, D])
    w2cat = din("w2cat", [128, 768]); a2cat = din("a2cat", [128, 768])
    prm_d = din("prm", [128, NPRM]); gqk_d = din("gqk", [128, 1024]); sink_d = din("sinkb", [128, 12])
    identf_d = din("identf", [128, 128]); bones_d = din("bones", [128, 128]); bo64_d = din("bo64", [128, 128])
    cos_d = din("cosr", [128, 16, 32]); sin_d = din("sinr", [128, 16, 32]); maskLR_d = din("maskLR", [128, 2, 128])
    rwm_d = din("rwm", [128, 2, 3, 128]); segm_d = din("segm", [128, 512])
    out = nc.dram_tensor("out", [S, D], F32, kind="ExternalOutput").ap()

    def dscr(name, shape, dt):
        kind = "ExternalOutput" if name in debug else "Internal"
        return nc.dram_tensor(name, list(shape), dt, kind=kind).ap()

    proj_f = dscr("proj_f", [NPF, S], F32)
    qkv_t = dscr("qkv_t", [S, NQKV], F32)
    ybuf = dscr("ybuf", [2048, S], BF16)
    gates_b = dscr("gates_b", [6144, S], BF16)
    dbg = dscr("dbg", [128, 4096], F32) if "dbg" in debug else None

    NBIG = 52600
    big_cm = nc.sbuf_tensor("big", [128, NBIG], F32)
    big = big_cm.__enter__()
    ps_cms = [nc.psum_tensor("ps%d" % i, [128, 512], F32) for i in range(8)]
    ps = [cm.__enter__() for cm in ps_cms]
    Bps = [Buf("ps%d" % i) for i in range(8)]
    psb = [p[:, :].bitcast(BF16) for p in ps]
    Sc = Sched(nc)
    AR = Arena(big, NBIG)
    V, A_, P_, G_, T_ = nc.vector, nc.scalar, nc.gpsimd, nc.sync, nc.tensor
    bank_rr = [0]
    dumped = set()

    def dump(name, sb_ap, shape, dt, reads):
        if name in debug and name not in dumped:
            dumped.add(name)
            t = nc.dram_tensor(name, list(shape), dt, kind="ExternalOutput").ap()
            Sc.dma("sp", t, sb_ap, reads=reads)

    def nbank(lo=0, hi=8):
        for _ in range(hi - lo):
            b = lo + bank_rr[0] % (hi - lo)
            bank_rr[0] += 1
            if not Bps[b].pending:
                return b
        raise RuntimeError("all PSUM banks in [%d,%d) hold unconsumed data" % (lo, hi))

    def mm_group(out_ap, items, obuf):
        n = len(items)
        for i, (l, r, rd) in enumerate(items):
            first, last = i == 0, i == n - 1
            Sc.op("pe", lambda: T_.matmul(out_ap, lhsT=l, rhs=r, start=first, stop=last), reads=rd,
                  writes=[obuf] if first else [], inc=last, setw=[obuf] if last else [])

    def mm_multi(items, obuf):
        n = len(items)
        for i, (o, l, r, rd) in enumerate(items):
            first, last = i == 0, i == n - 1
            Sc.op("pe", lambda: T_.matmul(o, lhsT=l, rhs=r, start=True, stop=True), reads=rd,
                  writes=[obuf] if first else [], inc=last, setw=[obuf] if last else [])

    def rsqrt_act(out_ap, in_ap, scale, eps, reads, writes, tmp_ap=None):
        t = out_ap if tmp_ap is None else tmp_ap
        Sc.op("act", lambda: A_.activation(out=t, in_=in_ap, func=AF.Ln, bias=eps, scale=scale), reads=reads, writes=writes)
        Sc.op("act", lambda: A_.activation(out=out_ap, in_=t, func=AF.Exp, scale=-0.5), reads=writes, writes=writes)

    identf = AR.f32(128); identb = AR.bf16(128); prm = AR.f32(NPRM)
    kmT = AR.bf16(4, 256); vm = AR.bf16(2, 512)
    Bid, Bprm, Bkm, Bvm = Buf("id"), Buf("prm"), Buf("kmT"), Buf("vm")
    Sc.dma("sp", identf, identf_d[:, :], writes=[Bid])
    Sc.dma("sp", prm[:, 0:OMM], prm_d[:, 0:OMM], writes=[Bprm])
    Sc.op("dve", lambda: V.tensor_copy(out=identb, in_=identf), reads=[Bid], writes=[Bid])
    Sc.op("dve", lambda: V.tensor_scalar(out=prm[:, OMM:OMM + 20], in0=prm[:, MU:MU + 20], scalar1=-1.0, scalar2=1.0, op0=ALU.mult, op1=ALU.add), reads=[Bprm], writes=[Bprm])
    Sc.op("dve", lambda: V.tensor_scalar(out=prm[:, HMU:HMU + 20], in0=prm[:, MU:MU + 20], scalar1=0.5, scalar2=None, op0=ALU.mult), reads=[Bprm], writes=[Bprm])
    Sc.op("dve", lambda: V.tensor_scalar(out=prm[:, OMK:OMK + 6], in0=prm[:, KA:KA + 6], scalar1=-1.0, scalar2=1.0, op0=ALU.mult, op1=ALU.add), reads=[Bprm], writes=[Bprm])
    Sc.op("dve", lambda: V.tensor_scalar(out=prm[:, HW0:HW0 + 24], in0=prm[:, W0:W0 + 24], scalar1=0.5, scalar2=None, op0=ALU.mult), reads=[Bprm], writes=[Bprm])
    Sc.op("dve", lambda: V.tensor_tensor(out=prm[:, XG2:XG2 + 1], in0=prm[:, XQG:XQG + 1], in1=prm[:, XKG:XKG + 1], op=ALU.mult), reads=[Bprm], writes=[Bprm])
    m_persist = AR.mark()

    hT = AR.bf16(16, S)
    BhT = [Buf("hT%d" % g) for g in range(4)]
    memT = AR.bf16(16, NMEM); BmemT = Buf("memT")
    m_ph0 = AR.mark()
    xbuf = [AR.f32(4, D) for _ in range(2)]
    Bx = [[Buf() for _ in range(4)] for _ in range(2)]
    junk = AR.f32(D); Bjunk = Buf("junk")
    evac_rr = [0]

    def build_T(src, nblk_total, gcol, dstT, dst_bufs):
        ngrp = (nblk_total + 3) // 4
        for g in range(ngrp):
            nb = min(4, nblk_total - g * 4)
            xb, bx = xbuf[g % 2], Bx[g % 2]
            ssq = AR.f32(4); rt = AR.f32(4); Bss = Buf("ss")
            Sc.op("dve", lambda: V.memset(ssq, 0.0), writes=[Bss])
            for i in range(nb):
                r0 = (g * 4 + i) * 128
                Sc.dma("sp", xb[:, i, :], src[r0:r0 + 128, :], writes=[bx[i]])
            for i in range(nb):
                Sc.op("act", lambda: A_.activation(out=junk, in_=xb[:, i, :], func=AF.Square, accum_out=ssq[:, i:i + 1]),
                      reads=[bx[i]], writes=[Bjunk, Bss])
            rsqrt_act(rt[:, 0:nb], ssq[:, 0:nb], 1.0 / D, EPS, [Bss], [Bss])
            for i in range(nb):
                Sc.op("dve", lambda: V.tensor_scalar(out=xb[:, i, :], in0=xb[:, i, :], scalar1=rt[:, i:i + 1], scalar2=None, op0=ALU.mult),
                      reads=[bx[i], Bss], writes=[bx[i]])
            for c in range(16):
                b = nbank()
                for i in range(nb):
                    Sc.op("pe", lambda: T_.transpose(out=ps[b][:, i * 128:(i + 1) * 128], in_=xb[:, i, c * 128:(c + 1) * 128], identity=identf),
                          reads=[bx[i], Bid], writes=[Bps[b]] if i == 0 else [], inc=(i == nb - 1), setw=[Bps[b]] if i == nb - 1 else [])
                dst = dstT[:, c, g * 512:g * 512 + nb * 128]
                gc = prm[:, gcol + c:gcol + c + 1]
                if evac_rr[0] % 2 == 0:
                    Sc.op("act", lambda: A_.activation(out=dst, in_=ps[b][:, 0:nb * 128], func=AF.Copy, scale=gc),
                          reads=[Bps[b], Bprm], writes=[dst_bufs[g]])
                else:
                    Sc.op("dve", lambda: V.tensor_scalar(out=dst, in0=ps[b][:, 0:nb * 128], scalar1=gc, scalar2=None, op0=ALU.mult),
                          reads=[Bps[b], Bprm], writes=[dst_bufs[g]])
                evac_rr[0] += 1

    build_T(mem, 2, MG_, memT, [BmemT])
    build_T(x, 16, NG, hT, BhT)

    Sc.barrier()
    AR.release(m_ph0)
    memT2 = memT
    wkv = AR.bf16(16, 1024); Bwkv = [Buf("wkv%d" % q) for q in range(4)]
    kmn = AR.f32(512); Bkmn = Buf("kmn")
    ssk = AR.f32(4); rk_ = AR.f32(4); Bssk = Buf("ssk")
    junk2 = AR.f32(128); Bjunk2 = Buf("junk2")
    wkv_src = x_w_kv.rearrange("(k p) n -> p k n", p=128)
    for q in range(4):
        Sc.dma("pool", wkv[:, q * 4:(q + 1) * 4, :], wkv_src[:, q * 4:(q + 1) * 4, :], writes=[Bwkv[q]])
    for mb in range(2):
        for half in range(2):
            b = nbank()
            mm_group(ps[b][:, :], [(memT2[:, k, mb * 128:(mb + 1) * 128], wkv[:, k, half * 512:(half + 1) * 512], [BmemT, Bwkv[k // 4]]) for k in range(16)], Bps[b])
            if half == 0:
                Sc.op("dve", lambda: V.memset(ssk, 0.0), writes=[Bssk])
                for h in range(4):
                    Sc.op("act", lambda: A_.activation(out=junk2, in_=ps[b][:, h * 128:(h + 1) * 128], func=AF.Square, accum_out=ssk[:, h:h + 1]),
                          reads=[Bps[b]], writes=[Bjunk2, Bssk])
                rsqrt_act(rk_, ssk, 1.0 / 128, EPS, [Bssk], [Bssk])
                for h in range(4):
                    Sc.op("dve", lambda: V.tensor_scalar(out=kmn[:, h * 128:(h + 1) * 128], in0=ps[b][:, h * 128:(h + 1) * 128], scalar1=rk_[:, h:h + 1], scalar2=None, op0=ALU.mult),
                          reads=[Bps[b], Bssk], writes=[Bkmn])
                b2 = nbank()
                for h in range(4):
                    Sc.op("pe", lambda: T_.transpose(out=ps[b2][:, h * 128:(h + 1) * 128], in_=kmn[:, h * 128:(h + 1) * 128], identity=identf),
                          reads=[Bkmn, Bid], writes=[Bps[b2]] if h == 0 else [], inc=(h == 3), setw=[Bps[b2]] if h == 3 else [])
                Sc.op("dve", lambda: V.tensor_scalar(out=kmT[:, :, mb * 128:(mb + 1) * 128], in0=ps[b2][:, :].rearrange("p (h m) -> p h m", m=128),
                                                     scalar1=prm[:, XG2:XG2 + 1], scalar2=None, op0=ALU.mult),
                      reads=[Bps[b2], Bprm], writes=[Bkm])
            else:
                Sc.op("act", lambda: A_.activation(out=vm[:, mb, :], in_=ps[b][:, :], func=AF.Copy), reads=[Bps[b]], writes=[Bvm])
    dump("d_kmT", kmT, [128, 4, 256], BF16, [Bkm]); dump("d_vm", vm, [128, 2, 512], BF16, [Bvm])
    dump("d_hT", hT, [128, 16, S], BF16, BhT)
    Sc.barrier()
    if stop_after == "hT":
        return finish_debug(nc, Sc, locals())

    AR.release(m_ph0)
    NWB = 3
    wt = [AR.bf16(16, 512) for _ in range(NWB)]
    Bwt = [[Buf("wt%d_%d" % (i, q)) for q in range(4)] for i in range(NWB)]
    stf = [AR.f32(S) for _ in range(2)]; Bstf = [Buf("stf%d" % i) for i in range(2)]
    stt = [AR.f32(512) for _ in range(3)]; Bstt = [Buf("stt%d" % i) for i in range(3)]
    Bproj = [Buf("pf%d" % i) for i in range(NPF // 128)]
    Bqkv = Buf("qkv")
    w_src = w_in.rearrange("(k p) n -> p k n", p=128)
    NT = (INW + 511) // 512

    def load_w(t):
        c0 = t * 512
        ncol = min(512, INW - c0)
        for q in range(4):
            Sc.dma("pool", wt[t % NWB][:, q * 4:(q + 1) * 4, 0:ncol], w_src[:, q * 4:(q + 1) * 4, c0:c0 + ncol], writes=[Bwt[t % NWB][q]])

    def act_for(feat):
        if (1280 <= feat < 2048) or (4608 <= feat < 5376) or (5888 <= feat < 6400):
            return "silu"
        if feat >= 6400:
            return "sig"
        return "copy"

    stf_rr = [0]; stt_rr = [0]; ev_rr = [0]
    import os as _os2
    TESTCOPY = bool(_os2.environ.get("TESTCOPY"))
    load_w(0); load_w(1)

    def rr(gens, weights=None):
        act_ = [[g, (weights[i] if weights else 1)] for i, g in enumerate(gens)]
        while act_:
            for ent in list(act_):
                for _ in range(ent[1]):
                    try:
                        next(ent[0])
                        yield
                    except StopIteration:
                        act_.remove(ent)
                        break

    def run(gen):
        for _ in gen:
            pass

    def proj_gen(t_lo, t_hi, bk):
      for t in range(t_lo, t_hi):
          if t + 2 < NT:
              load_w(t + 2)
          c0 = t * 512
          ncol = min(512, INW - c0)
          w = wt[t % NWB]; bw = Bwt[t % NWB]
          ntok = max(0, min(ncol, NQKV - c0))
          if ntok > 0:
              for tb in range(16):
                  b = nbank(*bk)
                  mm_group(ps[b][:, 0:ntok], [(hT[:, k, tb * 128:(tb + 1) * 128], w[:, k, 0:ntok], [BhT[tb // 4], bw[k // 4]]) for k in range(16)], Bps[b])
                  si = stt_rr[0] % 3; stt_rr[0] += 1
                  if ev_rr[0] % 2 == 0:
                      Sc.op("act", lambda: A_.activation(out=stt[si][:, 0:ntok], in_=ps[b][:, 0:ntok], func=AF.Copy), reads=[Bps[b]], writes=[Bstt[si]])
                  else:
                      Sc.op("dve", lambda: V.tensor_copy(out=stt[si][:, 0:ntok], in_=ps[b][:, 0:ntok]), reads=[Bps[b]], writes=[Bstt[si]])
                  ev_rr[0] += 1
                  Sc.dma("sp", qkv_t[tb * 128:(tb + 1) * 128, c0:c0 + ntok], stt[si][:, 0:ntok], reads=[Bstt[si]])
                  yield
          for sub in range(ntok // 128, ncol // 128):
              feat = c0 + sub * 128
              fi = (feat - NQKV) // 128
              kind = act_for(feat)
              si = stf_rr[0] % 2; stf_rr[0] += 1
              for tc in range(4):
                  b = nbank(*bk)
                  mm_group(ps[b][:, :], [(w[:, k, sub * 128:(sub + 1) * 128], hT[:, k, tc * 512:(tc + 1) * 512], [bw[k // 4], BhT[tc]]) for k in range(16)], Bps[b])
                  dst = stf[si][:, tc * 512:(tc + 1) * 512]
                  if kind == "sig":
                      dst = stf[si].bitcast(BF16)[:, tc * 512:(tc + 1) * 512]
                  if kind == "silu":
                      Sc.op("act", lambda: A_.activation(out=dst, in_=ps[b][:, :], func=AF.Silu), reads=[Bps[b]], writes=[Bstf[si]])
                  elif kind == "sig":
                      gcol = GB + (feat - 6400) // 128
                      Sc.op("act", lambda: A_.activation(out=dst, in_=ps[b][:, :], func=(AF.Tanh if TESTCOPY else AF.Sigmoid), bias=prm[:, gcol:gcol + 1], scale=1.0),
                            reads=[Bps[b], Bprm], writes=[Bstf[si]])
                  else:
                      if ev_rr[0] % 2 == 0:
                          Sc.op("act", lambda: A_.activation(out=dst, in_=ps[b][:, :], func=AF.Copy), reads=[Bps[b]], writes=[Bstf[si]])
                      else:
                          Sc.op("dve", lambda: V.tensor_copy(out=dst, in_=ps[b][:, :]), reads=[Bps[b]], writes=[Bstf[si]])
                      ev_rr[0] += 1
                  yield
              if kind == "sig":
                  g0 = feat - 6400
                  Sc.dma("sp", gates_b[g0:g0 + 128, :], stf[si].bitcast(BF16)[:, 0:S], reads=[Bstf[si]], writes=[Bproj[fi]])
              else:
                  Sc.dma("sp", proj_f[fi * 128:(fi + 1) * 128, :], stf[si], reads=[Bstf[si]], writes=[Bproj[fi]])

    TSPLIT = 13
    run(proj_gen(0, TSPLIT, (0, 8)))
    onesf = AR.f32(128); onesb = AR.bf16(128); Bones = Buf("ones")
    Sc.op("pool", lambda: P_.memset(onesf, 1.0), writes=[Bones])
    Sc.op("pool", lambda: P_.tensor_copy(out=onesb, in_=onesf), reads=[Bones], writes=[Bones])
    qTc = [AR.f32(S) for _ in range(2)]; gtc = [AR.f32(S)] * 2
    BqTc = [Buf(), Buf()]; Bgtc = [Buf()] * 2
    sqc = AR.f32(512); sc2 = AR.f32(512); qn_c = AR.bf16(512); pTc = [AR.bf16(512) for _ in range(2)]; rden = AR.f32(512); yo = AR.f32(512)
    yxs = AR.bf16(S)
    Bsqc, Bsc2, Bqnc, BpTc, Brden, Byo, Byxs = Buf(), Buf(), Buf(), [Buf(), Buf()], Buf(), Buf(), Buf()

    c_rr = [0]

    def cbank():
        c_rr[0] += 1
        return 6 + c_rr[0] % 2

    def xattn_gen():
        for h in range(4):
            Sc.dma("sp", qTc[h % 2], proj_f[XQ0 + h * 128:XQ0 + (h + 1) * 128, :], reads=[Bproj[XQ0 // 128 + h]], writes=[BqTc[h % 2]])
            Sc.dma("sp", gtc[h % 2], proj_f[XG0 + h * 128:XG0 + (h + 1) * 128, :], reads=[Bproj[XG0 // 128 + h]], writes=[Bgtc[h % 2]])
            q_ = qTc[h % 2]; bq_ = BqTc[h % 2]
            for tc in range(4):
                sl = slice(tc * 512, (tc + 1) * 512)
                Sc.op("act", lambda: A_.activation(out=sqc, in_=q_[:, sl], func=AF.Square), reads=[bq_], writes=[Bsqc]); yield
                b = cbank()
                mm_group(ps[b][:, :], [(onesf, sqc, [Bones, Bsqc])], Bps[b]); yield
                rsqrt_act(sc2, ps[b][:, :], 1.0, 128.0 * EPS, [Bps[b]], [Bsc2]); yield
                Sc.op("dve", lambda: V.tensor_tensor(out=qn_c, in0=q_[:, sl], in1=sc2, op=ALU.mult), reads=[bq_, Bsc2], writes=[Bqnc]); yield
                for mb in range(2):
                    b = cbank()
                    mm_group(ps[b][:, :], [(kmT[:, h, mb * 128:(mb + 1) * 128], qn_c, [Bkm, Bqnc])], Bps[b]); yield
                    Sc.op("act", lambda: A_.activation(out=pTc[mb], in_=ps[b][:, :], func=AF.Exp), reads=[Bps[b]], writes=[BpTc[mb]]); yield
                bo_ = cbank()
                mm_group(ps[bo_][:, :], [(vm[:, mb, h * 128:(h + 1) * 128], pTc[mb], [Bvm, BpTc[mb]]) for mb in range(2)], Bps[bo_]); yield
                bd_ = cbank()
                mm_group(ps[bd_][:, :], [(onesb, pTc[mb], [Bones, BpTc[mb]]) for mb in range(2)], Bps[bd_]); yield
                Sc.op("act", lambda: A_.activation(out=rden, in_=ps[bd_][:, :], func=AF.Ln), reads=[Bps[bd_]], writes=[Brden]); yield
                Sc.op("act", lambda: A_.activation(out=rden, in_=rden, func=AF.Exp, scale=-1.0), reads=[Brden], writes=[Brden]); yield
                Sc.op("dve", lambda: V.tensor_tensor(out=yo, in0=ps[bo_][:, :], in1=rden, op=ALU.mult), reads=[Bps[bo_], Brden], writes=[Byo]); yield
                Sc.op("pool", lambda: P_.tensor_tensor(out=yxs[:, sl], in0=yo, in1=gtc[h % 2][:, sl], op=ALU.mult), reads=[Byo, Bgtc[h % 2]], writes=[Byxs]); yield
            Sc.dma("sp", ybuf[1536 + h * 128:1536 + (h + 1) * 128, :], yxs, reads=[Byxs]); yield


    run(rr([proj_gen(TSPLIT, NT, (0, 6)), xattn_gen()]))
    Sc.barrier()
    if stop_after == "proj":
        return finish_debug(nc, Sc, locals())

    AR.release(m_persist)
    bones = AR.f32(128); bo64 = AR.f32(128); rwm = AR.bf16(2, 3, 128); segm = AR.f32(512)
    w2b = AR.bf16(768); a2b = AR.bf16(768)
    Bc = Buf("rwconst")
    Sc.dma("sp", bones, bones_d[:, :], writes=[Bc])
    tb1 = Buf(); tb2 = Buf(); tb3 = Buf(); tb4 = Buf(); tb5 = Buf()
    Sc.dma("sp", bo64, bo64_d[:, :], writes=[tb1])
    Sc.dma("sp", segm, segm_d[:, :], writes=[tb2])
    Sc.dma("pool", rwm, rwm_d[:, :, :, :], writes=[tb3])
    Sc.dma("pool", w2b, w2cat[:, :], writes=[tb4])
    Sc.dma("pool", a2b, a2cat[:, :], writes=[tb5])
    lw_t = AR.bf16(S); la_s = AR.bf16(S); Blw = Buf("lw"); Bla = Buf("la")
    r_ = AR.bf16(S); k_ = AR.bf16(S); v_ = AR.bf16(S); kk_ = AR.bf16(S); bon = AR.f32(S); ysum = AR.f32(S)
    Br, Bk, Bv, Bkk, Bbon, Bys = Buf("r"), Buf("k"), Buf("v"), Buf("kk"), Buf("bon"), Buf("ysum")
    Vtok = AR.bf16(32, 128); BVtok = [Buf("vtok%d" % q) for q in range(4)]
    vbd = AR.bf16(8, 128); Bvbd = Buf("vbd")
    rkones = AR.f32(128); Brk = Buf("rkones")
    NTMP = 7168
    tmp_raw = AR._raw(NTMP)
    WD = []
    for d in range(2):
        W = {}
        for nm in ("ARt", "BKtok", "Gb", "Gk"):
            W[nm] = [AR.bf16(8, 2, 128) for _ in range(2)]
        W["Tt"] = [AR.bf16(8, 128) for _ in range(2)]
        W["Pc"] = [AR.f32(8) for _ in range(2)]
        W["BKt"] = AR.bf16(8, 2, 128)
        W["Xs"] = AR.bf16(128); W["Ubf"] = AR.bf16(128); W["Ybd"] = AR.f32(8, 128)
        W["St"] = AR.f32(128); W["Stbf"] = AR.bf16(128); W["tmpS"] = AR.f32(128); W["tot"] = AR.f32(8)
        for nm in ("BBKt", "BXs", "BUbf", "BSt", "BStbf", "BtmpS", "Btot", "BYbd"):
            W[nm] = Buf(nm)
        for nm in ("BARt", "BPc"):
            W[nm] = [Buf(nm + "0"), Buf(nm + "1")]
        for nm in ("BBKtok", "BGb", "BGk", "BTt"):
            W[nm] = [[Buf(), Buf()], [Buf(), Buf()]]
        W["BLab"] = [Buf(), Buf()]
        W["BAn"] = [[Buf(), Buf()], [Buf(), Buf()]]; W["BBn"] = [[Buf(), Buf()], [Buf(), Buf()]]
        TA = Arena(tmp_raw[:, d * (NTMP // 2):(d + 1) * (NTMP // 2)], NTMP // 2)
        for nm in ("a", "ld", "cum", "E1", "E2", "E3", "u"):
            W[nm] = TA.f32(512); W["B" + nm] = Buf(nm)
        asb = lambda ap: ap.bitcast(BF16).rearrange("p (a b) -> p a b", b=128)
        W["An"] = [asb(W["E1"]), asb(W["E2"])]; W["Bn"] = [asb(W["E3"]), asb(W["u"])]; W["Lab"] = asb(W["ld"])
        W["alias_An"] = [W["BE1"], W["BE2"]]; W["alias_Bn"] = [W["BE3"], W["Bu"]]
        WD.append(W)
    for d in range(2):
        W = WD[d]
        for jp in range(2):
            Sc.op("pool", lambda: P_.memset(W["ARt"][jp], 0.0), writes=[W["BARt"][jp]])
        Sc.op("pool", lambda: P_.memset(W["BKt"], 0.0), writes=[W["BBKt"]])
    Sc.op("pool", lambda: P_.memset(vbd, 0.0), writes=[Bvbd])

    rawc = [AR.f32(514) for _ in range(2)]; Brawc = [Buf(), Buf()]
    nbc = AR.f32(512); sqc_ = AR.f32(512); Bnbc, Bsqc_ = Buf(), Buf()
    sqk = nbc; rnk = sqc_; Bsqk, Brnk = Bnbc, Bsqc_
    gatec = AR.f32(512); dtmp = AR.f32(512); sq2 = AR.f32(512); rstd = AR.f32(512); yn = AR.f32(512); ystc = [AR.bf16(512) for _ in range(2)]
    Bgatec, Bdt, Bs2, Brs, Byn, Bystc = Buf(), Buf(), Buf(), Buf(), Buf(), [Buf(), Buf()]
    rc_rr = [0]

    def shift_gen(dst, bdst, row0, mi, func=AF.Copy):
        for tc in range(4):
            sl = slice(tc * 512, (tc + 1) * 512)
            lo = max(0, tc * 512 - 1); hi = min(S, tc * 512 + 513)
            off = lo - (tc * 512 - 1)
            ri = rc_rr[0] % 2; rc_rr[0] += 1
            rc = rawc[ri]; brc = Brawc[ri]
            if tc == 0:
                Sc.op("pool", lambda: P_.memset(rc[:, 0:1], 0.0), writes=[brc])
            if tc == 3:
                Sc.op("pool", lambda: P_.memset(rc[:, 513:514], 0.0), writes=[brc])
            Sc.dma("sp", rc[:, off:off + (hi - lo)], proj_f[row0:row0 + 128, lo:hi], writes=[brc]); yield
            Sc.op("pool", lambda: P_.tensor_tensor(out=nbc, in0=rc[:, 0:512], in1=rc[:, 2:514], op=ALU.add), reads=[brc], writes=[Bnbc]); yield
            Sc.op("act", lambda: A_.activation(out=sqc_, in_=rc[:, 1:513], func=AF.Copy, scale=prm[:, OMM + mi:OMM + mi + 1]), reads=[brc, Bprm], writes=[Bsqc_]); yield
            if func == AF.Copy:
                Sc.op("dve", lambda: V.scalar_tensor_tensor(out=dst[:, sl], in0=nbc, scalar=prm[:, HMU + mi:HMU + mi + 1], in1=sqc_, op0=ALU.mult, op1=ALU.add),
                      reads=[Bnbc, Bprm, Bsqc_], writes=[bdst]); yield
            else:
                Sc.op("dve", lambda: V.scalar_tensor_tensor(out=sqc_, in0=nbc, scalar=prm[:, HMU + mi:HMU + mi + 1], in1=sqc_, op0=ALU.mult, op1=ALU.add),
                      reads=[Bnbc, Bprm, Bsqc_], writes=[Bsqc_]); yield
                Sc.op("act", lambda: A_.activation(out=dst[:, sl], in_=sqc_, func=func), reads=[Bsqc_], writes=[bdst]); yield

    def v3(ap, n=64):
        return ap.rearrange("p (c t) -> p c t", t=n)

    def rr(gens):
        act_ = list(gens)
        while act_:
            for g in list(act_):
                try:
                    next(g)
                    yield
                except StopIteration:
                    act_.remove(g)

    def run(gen):
        for _ in gen:
            pass

    def unit_prep(p, d, sc, jp):
        W = WD[d]
        sl = slice(sc * 512, (sc + 1) * 512)
        dh = slice(d * 64, (d + 1) * 64)
        pc = slice(p * 128, (p + 1) * 128)
        a, ld, cum, E1, E2, E3, u = W["a"], W["ld"], W["cum"], W["E1"], W["E2"], W["E3"], W["u"]
        Ba, Bld, Bcum, BE1, BE2, BE3, Bu = W["Ba"], W["Bld"], W["Bcum"], W["BE1"], W["BE2"], W["BE3"], W["Bu"]
        ARt, BKtok, Gb, Gk, Tt, Pc = W["ARt"][jp], W["BKtok"][jp], W["Gb"][jp], W["Gk"][jp], W["Tt"][jp], W["Pc"][jp]
        BARt, BBKtok, BGb, BGk, BTt, BPc = W["BARt"][jp], W["BBKtok"][jp], W["BGb"][jp], W["BGk"][jp], W["BTt"][jp], W["BPc"][jp]
        BKt, Lab = W["BKt"], W["Lab"]
        b = nbank(2, 8)
        mm_group(ps[b][:, :], [(a2b[dh, pc], la_s[dh, sl], [tb5, Bla])], Bps[b]); yield
        hc = HA0 + d * 6 + p
        Sc.op("act", lambda: A_.activation(out=a, in_=ps[b][:, :], func=AF.Tanh, bias=prm[:, hc:hc + 1], scale=0.5), reads=[Bps[b], Bprm], writes=[Ba]); yield
        Sc.op("dve", lambda: V.tensor_scalar(out=a, in0=a, scalar1=0.5, scalar2=0.5, op0=ALU.mult, op1=ALU.add), reads=[Ba], writes=[Ba]); yield
        b = nbank(2, 8)
        mm_group(ps[b][:, :], [(w2b[dh, pc], lw_t[dh, sl], [tb4, Blw])], Bps[b]); yield
        hc2 = HW0 + d * 6 + p
        Sc.op("act", lambda: A_.activation(out=ld, in_=ps[b][:, :], func=AF.Tanh, bias=prm[:, hc2:hc2 + 1], scale=0.5), reads=[Bps[b], Bprm], writes=[Bld] + W["BLab"]); yield
        Sc.op("dve", lambda: V.tensor_scalar(out=ld, in0=ld, scalar1=C1, scalar2=C1, op0=ALU.mult, op1=ALU.add), reads=[Bld], writes=[Bld]); yield
        Sc.op("dve", lambda: V.tensor_tensor_scan(out=cum, data0=segm, data1=ld, initial=0.0, op0=ALU.mult, op1=ALU.add), reads=[tb2, Bld], writes=[Bcum]); yield
        if d == 1:
            Sc.op("dve", lambda: V.tensor_copy(out=W["tot"], in_=v3(cum)[:, :, 63]), reads=[Bcum], writes=[W["Btot"]]); yield
            Sc.op("dve", lambda: V.tensor_tensor(out=cum, in0=ld, in1=cum, op=ALU.subtract), reads=[Bld, Bcum], writes=[Bcum]); yield
            Sc.op("dve", lambda: V.tensor_tensor(out=v3(cum), in0=v3(cum), in1=W["tot"].unsqueeze(2).to_broadcast([128, 8, 64]), op=ALU.add),
                  reads=[Bcum, W["Btot"]], writes=[Bcum]); yield
        Sc.op("dve", lambda: V.tensor_tensor(out=ld, in0=cum, in1=ld, op=ALU.subtract), reads=[Bcum, Bld], writes=[Bld]); yield
        Sc.op("act", lambda: A_.activation(out=E3, in_=ld, func=AF.Exp), reads=[Bld], writes=[BE3] + W["BBn"][0]); yield
        Sc.op("act", lambda: A_.activation(out=E1, in_=cum, func=AF.Exp), reads=[Bcum], writes=[BE1] + W["BAn"][0]); yield
        Sc.op("act", lambda: A_.activation(out=E2, in_=cum, func=AF.Exp, scale=-1.0), reads=[Bcum], writes=[BE2] + W["BAn"][1]); yield
        pcol = 63 if d == 0 else 0
        Sc.op("dve", lambda: V.tensor_copy(out=Pc, in_=v3(E1)[:, :, pcol]), reads=[BE1], writes=[BPc]); yield
        Sc.op("dve", lambda: V.tensor_scalar(out=u, in0=a, scalar1=prm[:, KA + p:KA + p + 1], scalar2=prm[:, OMK + p:OMK + p + 1], op0=ALU.mult, op1=ALU.add),
              reads=[Ba, Bprm], writes=[Bu] + W["BBn"][1]); yield
        Sc.op("dve", lambda: V.tensor_tensor(out=u, in0=k_[:, sl], in1=u, op=ALU.mult), reads=[Bk, Bu], writes=[Bu]); yield
        Sc.op("pool", lambda: P_.tensor_tensor(out=a, in0=kk_[:, sl], in1=a, op=ALU.mult), reads=[Bkk, Ba], writes=[Ba]); yield
        for half in range(2):
            hs = slice(half * 64, (half + 1) * 64)
            bc = slice(half * 64, (half + 1) * 64)
            Sc.op("dve", lambda: V.scalar_tensor_tensor(out=ARt[hs, :, 0, bc], in0=v3(kk_[hs, sl]), scalar=-1.0, in1=v3(E3[hs, :]), op0=ALU.mult, op1=ALU.mult),
                  reads=[Bkk, BE3], writes=[BARt]); yield
            Sc.op("pool", lambda: P_.tensor_tensor(out=ARt[hs, :, 1, bc], in0=v3(r_[hs, sl]), in1=v3(E1[hs, :]), op=ALU.mult), reads=[Br, BE1], writes=[BARt]); yield
            Sc.op("dve", lambda: V.tensor_tensor(out=BKt[hs, :, 1, bc], in0=v3(u[hs, :]), in1=v3(E2[hs, :]), op=ALU.mult), reads=[Bu, BE2], writes=[W["BBKt"]]); yield
            Sc.op("dve", lambda: V.tensor_tensor(out=BKt[hs, :, 0, bc], in0=v3(a[hs, :]), in1=v3(E2[hs, :]), op=ALU.mult), reads=[Ba, BE2], writes=[W["BBKt"]]); yield
        Sc.op("pool", lambda: P_.tensor_tensor(out=cum, in0=r_[:, sl], in1=u, op=ALU.mult), reads=[Br, Bu, Bcum], writes=[Bcum]); yield
        b = nbank(2, 8)
        mm_group(ps[b][:, :], [(rkones, cum, [Brk, Bcum])], Bps[b]); yield
        Sc.op("dve", lambda: V.tensor_tensor(out=cum, in0=ps[b][:, :], in1=v_[:, sl], op=ALU.mult), reads=[Bps[b], Bv, Bcum], writes=[Bcum]); yield
        Sc.op("pool", lambda: P_.tensor_tensor(out=bon[:, sl], in0=bon[:, sl], in1=cum, op=ALU.add), reads=[Bbon, Bcum], writes=[Bbon]); yield
        for hb in range(2):
            b = nbank(2, 8)
            for ci in range(4):
                for s2 in range(2):
                    j = ci * 2 + s2
                    Sc.op("pe", lambda: T_.transpose(out=psb[b][:, j * 128:(j + 1) * 128], in_=BKt[:, hb * 4 + ci, s2, :], identity=identb),
                          reads=[W["BBKt"], Bid], writes=[Bps[b]] if j == 0 else [], inc=(j == 7), setw=[Bps[b]] if j == 7 else [])
            yield
            dstv = BKtok[:, hb * 4:(hb + 1) * 4, :, :].rearrange("p a b c -> p (a b c)")
            Sc.op("act", lambda: A_.activation(out=dstv, in_=psb[b], func=AF.Copy), reads=[Bps[b]], writes=[BBKtok[hb]]); yield
        M2 = rwm[:, d, 0:2, :].rearrange("p a b -> p (a b)")
        ML = rwm[:, d, 2, :]
        for hb in range(2):
            bL = nbank(2, 8)
            mm_multi([(ps[bL][:, ci * 128:(ci + 1) * 128], ARt[:, hb * 4 + ci, 0, :], BKt[:, hb * 4 + ci, 0, :], [W["BBKt"], BARt]) for ci in range(4)], Bps[bL])
            yield
            Sc.op("dve", lambda: V.tensor_tensor(out=Lab[:, hb * 4:(hb + 1) * 4, :], in0=ps[bL][:, :].rearrange("p (a b) -> p a b", b=128),
                                                 in1=ML.unsqueeze(1).to_broadcast([128, 4, 128]), op=ALU.mult), reads=[Bps[bL], tb3], writes=[W["BLab"][hb], Bld]); yield
            bB = [nbank(2, 8), nbank(2, 8)]
            for i in range(2):
                mm_multi([(ps[bB[i]][:, q2 * 256:(q2 + 1) * 256], BKt[:, hb * 4 + i * 2 + q2, 0, :], ARt[:, hb * 4 + i * 2 + q2, :, :], [W["BBKt"], BARt]) for q2 in range(2)], Bps[bB[i]])
            yield
            for i in range(2):
                c0 = hb * 4 + i * 2
                Sc.op("dve", lambda: V.tensor_tensor(out=Gb[:, c0:c0 + 2, :, :].rearrange("p a b c -> p a (b c)"), in0=ps[bB[i]][:, :].rearrange("p (a b) -> p a b", b=256),
                                                     in1=M2.unsqueeze(1).to_broadcast([128, 2, 256]), op=ALU.mult), reads=[Bps[bB[i]], tb3], writes=[BGb[hb]]); yield
        for hb in range(2):
            cs = slice(hb * 4, (hb + 1) * 4)
            Sc.op("dve", lambda: V.tensor_tensor(out=Tt[:, cs, :], in0=Gb[:, cs, 0, :], in1=identb.unsqueeze(1).to_broadcast([128, 4, 128]), op=ALU.add),
                  reads=[BGb[hb], Bid], writes=[BTt[hb]]); yield
        for lvl in range(0, 6):
            for hb in range(2):
                cs = slice(hb * 4, (hb + 1) * 4)
                if lvl == 0:
                    Aget = lambda c: Gb[:, c, 0, :]
                    Bget = lambda c: Lab[:, c, :]
                    BAsrc, BBsrc = BGb[hb], W["BLab"][hb]
                else:
                    Aprev, Bprev = W["An"][lvl % 2], W["Bn"][lvl % 2]
                    Aget = lambda c: Aprev[:, c, :]
                    Bget = lambda c: Bprev[:, c, :]
                    BAsrc, BBsrc = W["BAn"][lvl % 2][hb], W["BBn"][lvl % 2][hb]
                Anew, Bnew = W["An"][(lvl + 1) % 2], W["Bn"][(lvl + 1) % 2]
                BAnew, BBnew = W["BAn"][(lvl + 1) % 2][hb], W["BBn"][(lvl + 1) % 2][hb]
                if lvl >= 1:
                    bT = nbank(2, 8)
                    mm_multi([(ps[bT][:, ci * 128:(ci + 1) * 128], Bget(hb * 4 + ci), Tt[:, hb * 4 + ci, :], [BBsrc, BTt[hb]]) for ci in range(4)], Bps[bT])
                    yield
                if lvl < 5:
                    if lvl < 4:
                        bA = nbank(2, 8)
                        mm_multi([(ps[bA][:, ci * 128:(ci + 1) * 128], Bget(hb * 4 + ci), Aget(hb * 4 + ci), [BAsrc, BBsrc]) for ci in range(4)], Bps[bA])
                        yield
                    bBm = nbank(2, 8)
                    mm_multi([(ps[bBm][:, ci * 128:(ci + 1) * 128], Aget(hb * 4 + ci), Bget(hb * 4 + ci), [BAsrc, BBsrc]) for ci in range(4)], Bps[bBm])
                    yield
                if lvl >= 1:
                    Sc.op("dve", lambda: V.tensor_tensor(out=Tt[:, cs, :], in0=ps[bT][:, :].rearrange("p (a b) -> p a b", b=128), in1=Tt[:, cs, :], op=ALU.add),
                          reads=[Bps[bT], BTt[hb]], writes=[BTt[hb]]); yield
                if lvl < 5:
                    if lvl < 4:
                        Sc.op("act", lambda: A_.activation(out=Anew[:, cs, :], in_=ps[bA][:, :].rearrange("p (a b) -> p a b", b=128), func=AF.Copy),
                              reads=[Bps[bA]], writes=[BAnew, W["alias_An"][(lvl + 1) % 2]]); yield
                    Sc.op("act", lambda: A_.activation(out=Bnew[:, cs, :], in_=ps[bBm][:, :].rearrange("p (a b) -> p a b", b=128), func=AF.Copy),
                          reads=[Bps[bBm]], writes=[BBnew, W["alias_Bn"][(lvl + 1) % 2]]); yield
        for hb in range(2):
            bK = [nbank(2, 8), nbank(2, 8)]
            for i in range(2):
                mm_multi([(ps[bK[i]][:, q2 * 256:(q2 + 1) * 256], BKt[:, hb * 4 + i * 2 + q2, 1, :], ARt[:, hb * 4 + i * 2 + q2, :, :], [W["BBKt"], BARt]) for q2 in range(2)], Bps[bK[i]])
            yield
            for i in range(2):
                c0 = hb * 4 + i * 2
                Sc.op("dve", lambda: V.tensor_tensor(out=Gk[:, c0:c0 + 2, :, :].rearrange("p a b c -> p a (b c)"), in0=ps[bK[i]][:, :].rearrange("p (a b) -> p a b", b=256),
                                                     in1=M2.unsqueeze(1).to_broadcast([128, 2, 256]), op=ALU.mult), reads=[Bps[bK[i]], tb3], writes=[BGk[hb]]); yield

    def scan_step(p, d, sc, ci, jp):
        W = WD[d]
        hb = ci // 4
        cg = sc * 8 + ci
        sb = d
        bkb = Bps[sb]
        ARt, BKtok, Gb, Gk, Tt, Pc = W["ARt"][jp], W["BKtok"][jp], W["Gb"][jp], W["Gk"][jp], W["Tt"][jp], W["Pc"][jp]
        BARt, BBKtok, BGb, BGk, BTt, BPc = W["BARt"][jp], W["BBKtok"][jp], W["BGb"][jp], W["BGk"][jp], W["BTt"][jp], W["BPc"][jp]
        St, Stbf, Xs, Ubf, tmpS = W["St"], W["Stbf"], W["Xs"], W["Ubf"], W["tmpS"]
        vt = Vtok[:, cg, :]
        bvt = BVtok[cg // 8]
        pcc = Pc[:, ci:ci + 1]
        Sc.op("act", lambda: A_.activation(out=tmpS, in_=St, func=AF.Copy, scale=pcc), reads=[W["BSt"], BPc], writes=[W["BtmpS"]]); yield
        mm_group(ps[sb][:, 0:128], [(ARt[:, ci, 0, :], Stbf, [BARt, W["BStbf"]]), (Gk[:, ci, 0, :], vt, [BGk[hb], bvt])], bkb); yield
        Sc.op("act", lambda: A_.activation(out=Xs, in_=ps[sb][:, 0:128], func=AF.Copy), writes=[bkb, W["BXs"]]); yield
        mm_group(ps[sb][:, 128:256], [(Tt[:, ci, :], Xs, [BTt[hb], W["BXs"]])], bkb); yield
        Sc.op("dve", lambda: V.tensor_copy(out=Ubf, in_=ps[sb][:, 128:256]), writes=[bkb, W["BUbf"]]); yield
        mm_group(ps[sb][:, 384:512], [(BKtok[:, ci, 0, :], Ubf, [BBKtok[hb], W["BUbf"]]), (BKtok[:, ci, 1, :], vt, [BBKtok[hb], bvt])], bkb); yield
        mm_group(ps[sb][:, 256:384], [(Stbf, ARt[:, ci, 1, :], [BARt, W["BStbf"]]), (Ubf, Gb[:, ci, 1, :], [W["BUbf"], BGb[hb]]),
                                      (vt, Gk[:, ci, 1, :], [bvt, BGk[hb]])], bkb); yield
        Sc.op("dve", lambda: V.scalar_tensor_tensor(out=St, in0=ps[sb][:, 384:512], scalar=pcc, in1=tmpS, op0=ALU.mult, op1=ALU.add),
              reads=[BPc, W["BtmpS"]], writes=[bkb, W["BSt"]]); yield
        Sc.op("act", lambda: A_.activation(out=Stbf, in_=St, func=AF.Copy), reads=[W["BSt"]], writes=[W["BStbf"]]); yield
        Sc.op("act", lambda: A_.activation(out=W["Ybd"][:, ci, :], in_=ps[sb][:, 256:384], func=AF.Copy), writes=[bkb, W["BYbd"]]); yield

    def chain(p, d, sc, jp):
        order = range(8) if d == 0 else range(7, -1, -1)
        for ci in order:
            yield from scan_step(p, d, sc, ci, jp)
        W = WD[d]
        sl = slice(sc * 512, (sc + 1) * 512)
        for half in range(2):
            hs = slice(half * 64, (half + 1) * 64)
            Sc.op("pool", lambda: P_.tensor_tensor(out=v3(ysum[hs, sl]), in0=v3(ysum[hs, sl]), in1=W["Ybd"][hs, :, half * 64:(half + 1) * 64], op=ALU.add),
                  reads=[Bys, W["BYbd"]], writes=[Bys]); yield

    run(shift_gen(lw_t, Blw, LW0, 18, func=AF.Tanh))
    run(shift_gen(la_s, Bla, LA0, 19))

    PAIRS = list(range(6))
    if stop_after.startswith("rwkvp"):
        PAIRS = [int(ch) for ch in stop_after[5:]]

    def prologue(p):
        yield from shift_gen(r_, Br, R0 + p * 128, p)
        yield from shift_gen(k_, Bk, K0 + p * 128, 6 + p)
        yield from shift_gen(v_, Bv, V0 + p * 128, 12 + p)
        kcol = prm[:, KK + p:KK + p + 1]
        for tc in range(4):
            sl = slice(tc * 512, (tc + 1) * 512)
            Sc.op("act", lambda: A_.activation(out=sqk, in_=k_[:, sl], func=AF.Square, scale=kcol), reads=[Bk, Bprm], writes=[Bsqk]); yield
            b = nbank(2, 8)
            mm_group(ps[b][:, :], [(bones, sqk, [Bc, Bsqk])], Bps[b]); yield
            rsqrt_act(rnk, ps[b][:, :], 1.0, 1e-24, [Bps[b]], [Brnk]); yield
            Sc.op("dve", lambda: V.scalar_tensor_tensor(out=kk_[:, sl], in0=k_[:, sl], scalar=kcol, in1=rnk, op0=ALU.mult, op1=ALU.mult),
                  reads=[Bk, Bprm, Brnk], writes=[Bkk]); yield
        Sc.op("dve", lambda: V.tensor_scalar(out=rkones, in0=bones, scalar1=prm[:, RK + p:RK + p + 1], scalar2=None, op0=ALU.mult), reads=[Bc, Bprm], writes=[Brk]); yield

    def vtok_build(p):
        for q in range(4):
            for half in range(2):
                hs = slice(half * 64, (half + 1) * 64)
                Sc.op("pool", lambda: P_.tensor_copy(out=vbd[hs, :, half * 64:(half + 1) * 64], in_=v3(v_[hs, q * 512:(q + 1) * 512])), reads=[Bv], writes=[Bvbd]); yield
            b = nbank(2, 8)
            for j in range(8):
                Sc.op("pe", lambda: T_.transpose(out=psb[b][:, j * 128:(j + 1) * 128], in_=vbd[:, j, :], identity=identb),
                      reads=[Bvbd, Bid], writes=[Bps[b]] if j == 0 else [], inc=(j == 7), setw=[Bps[b]] if j == 7 else [])
            yield
            Sc.op("dve", lambda: V.tensor_copy(out=Vtok[:, q * 8:(q + 1) * 8, :].rearrange("p a b -> p (a b)"), in_=psb[b]), reads=[Bps[b]], writes=[BVtok[q]]); yield

    def resets(p):
        Sc.op("pool", lambda: P_.memset(bon, 0.0), writes=[Bbon])
        Sc.op("pool", lambda: P_.memset(ysum, 0.0), writes=[Bys])
        for d in range(2):
            W = WD[d]
            Sc.op("pool", lambda: P_.memset(W["St"], 0.0), writes=[W["BSt"]])
            Sc.op("pool", lambda: P_.memset(W["Stbf"], 0.0), writes=[W["BStbf"]])

    def epilogue(p):
        dump("d_ysum%d" % p, ysum, [128, S], F32, [Bys])
        for tc in range(4):
            sl = slice(tc * 512, (tc + 1) * 512)
            yc = ystc[tc % 2]; byc = Bystc[tc % 2]
            Sc.dma("sp", gatec, proj_f[RG0 + p * 128:RG0 + (p + 1) * 128, sl], writes=[Bgatec])
            b = nbank(2, 8)
            mm_group(ps[b][:, :], [(bo64, ysum[:, sl], [tb1, Bys])], Bps[b]); yield
            Sc.op("dve", lambda: V.tensor_tensor(out=dtmp, in0=ysum[:, sl], in1=ps[b][:, :], op=ALU.subtract), reads=[Bys, Bps[b]], writes=[Bdt]); yield
            Sc.op("act", lambda: A_.activation(out=sq2, in_=dtmp, func=AF.Square), reads=[Bdt], writes=[Bs2]); yield
            b2 = nbank(2, 8)
            mm_group(ps[b2][:, :], [(bo64, sq2, [tb1, Bs2])], Bps[b2]); yield
            rsqrt_act(rstd, ps[b2][:, :], 1.0, GN_EPS, [Bps[b2]], [Brs]); yield
            Sc.op("dve", lambda: V.tensor_tensor(out=yn, in0=dtmp, in1=rstd, op=ALU.mult), reads=[Bdt, Brs], writes=[Byn]); yield
            Sc.op("dve", lambda: V.tensor_scalar(out=yn, in0=yn, scalar1=prm[:, LW + p:LW + p + 1], scalar2=prm[:, LB + p:LB + p + 1], op0=ALU.mult, op1=ALU.add),
                  reads=[Byn, Bprm], writes=[Byn]); yield
            Sc.op("pool", lambda: P_.tensor_tensor(out=yn, in0=yn, in1=bon[:, sl], op=ALU.add), reads=[Byn, Bbon], writes=[Byn]); yield
            Sc.op("dve", lambda: V.tensor_tensor(out=yc, in0=yn, in1=gatec, op=ALU.mult), reads=[Byn, Bgatec], writes=[byc]); yield
            Sc.dma("sp", ybuf[768 + p * 128:768 + (p + 1) * 128, sl], yc, reads=[byc]); yield

    def preps(p, j):
        return [unit_prep(p, 0, j, j % 2), unit_prep(p, 1, 3 - j, j % 2)]

    Sc.barrier()
    run(prologue(PAIRS[0]))
    run(vtok_build(PAIRS[0]))
    resets(PAIRS[0])
    run(rr(preps(PAIRS[0], 0)))
    for i, p in enumerate(PAIRS):
        nxt = PAIRS[i + 1] if i + 1 < len(PAIRS) else None
        for j in range(4):
            gens = [chain(p, 0, j, j % 2), chain(p, 1, 3 - j, j % 2)]
            if j < 3:
                gens += preps(p, j + 1)
            elif nxt is not None:
                gens.append(prologue(nxt))
            run(rr(gens))
        gens = [epilogue(p)]
        if nxt is not None:
            gens.append(vtok_build(nxt))
        run(rr(gens))
        if nxt is not None:
            resets(nxt)
            run(rr(preps(nxt, 0)))
    Sc.barrier()
    if stop_after.startswith("rwkv"):
        return finish_debug(nc, Sc, locals())

    AR.release(m_persist)
    cosr = AR.f32(16, 32); sinr = AR.f32(16, 32); gqk = AR.f32(16, 64); esink = AR.f32(12); mLR = AR.bf16(2, 128)
    Bcs, Bsn, Bgq, Bes, Bml = Buf(), Buf(), Buf(), Buf(), Buf()
    Sc.dma("sp", cosr, cos_d[:, :, :], writes=[Bcs]); Sc.dma("sp", sinr, sin_d[:, :, :], writes=[Bsn])
    Sc.dma("sp", gqk, gqk_d.rearrange("p (a b) -> p a b", b=64), writes=[Bgq]); Sc.dma("sp", esink, sink_d[:, :], writes=[Bes])
    Sc.dma("pool", mLR, maskLR_d[:, :, :], writes=[Bml])
    Sc.op("act", lambda: A_.activation(out=esink, in_=esink, func=AF.Exp), reads=[Bes], writes=[Bes])
    qTa = AR.bf16(12, S); kTa = AR.bf16(4, S); vext = AR.bf16(16, 4, 128); ya = AR.bf16(6, S)
    BqTa = [Buf() for _ in range(16)]; BkTa = [Buf() for _ in range(16)]; Bvx = [Buf() for _ in range(16)]; Bya = [Buf() for _ in range(16)]
    Sc.op("pool", lambda: P_.memset(vext, 1.0), writes=Bvx)
    qkv = [AR.f32(NQKV) for _ in range(2)]; Bqkv2 = [Buf(), Buf()]
    gta = [AR.f32(6, 128) for _ in range(3)]; Bgta = [Buf(), Buf(), Buf()]
    sqa = AR.f32(1024); ssa = AR.f32(16); rsa = AR.f32(16); qna = AR.f32(16, 64); qra = AR.bf16(16, 64)
    rt_ = [AR.f32(16, 32) for _ in range(4)]
    Bsqa, Bssa, Bqna, Bqra = Buf(), Buf(), Buf(), Buf()
    Brt = [Buf() for _ in range(4)]
    pTa = [AR.bf16(384) for _ in range(3)]; BpTa = [Buf() for _ in range(3)]
    rda = AR.f32(4, 128); Brda = Buf(); yodd = AR.f32(2, 128); Byodd = Buf(); yev = AR.f32(2, 128); Byev = Buf()
    Bo = [Buf("o5"), Buf("o6"), Buf("o7")]
    pta_rr = [0]
    ag_src = proj_f[AG0:AG0 + 768, :].rearrange("(c p) t -> p c t", p=128)

    def prep_blk(tb):
        qk_ = qkv[tb % 2]; bqk = Bqkv2[tb % 2]
        Sc.dma("sp", qk_, qkv_t[tb * 128:(tb + 1) * 128, :], writes=[bqk])
        Sc.dma("sp", gta[tb % 3], ag_src[:, :, tb * 128:(tb + 1) * 128], writes=[Bgta[tb % 3]])
        Sc.op("act", lambda: A_.activation(out=sqa, in_=qk_[:, 0:1024], func=AF.Square), reads=[bqk], writes=[Bsqa]); yield
        Sc.op("dve", lambda: V.tensor_reduce(out=ssa, in_=sqa.rearrange("p (a b) -> p a b", b=64), axis=AX.X, op=ALU.add), reads=[Bsqa], writes=[Bssa]); yield
        rsqrt_act(rsa, ssa, 1.0 / 64, EPS, [Bssa], [Bssa]); yield
        Sc.op("dve", lambda: V.tensor_tensor(out=qna, in0=qk_[:, 0:1024].rearrange("p (a b) -> p a b", b=64), in1=rsa.unsqueeze(2).to_broadcast([128, 16, 64]), op=ALU.mult),
              reads=[bqk, Bssa], writes=[Bqna]); yield
        Sc.op("pool", lambda: P_.tensor_tensor(out=qna, in0=qna, in1=gqk, op=ALU.mult), reads=[Bqna, Bgq], writes=[Bqna]); yield
        t1 = qna[:, :, 0:32]; t2 = qna[:, :, 32:64]
        cb = cosr[:, tb, :].unsqueeze(1).to_broadcast([128, 16, 32]); sb_ = sinr[:, tb, :].unsqueeze(1).to_broadcast([128, 16, 32])
        Sc.op("dve", lambda: V.tensor_tensor(out=rt_[0], in0=t1, in1=cb, op=ALU.mult), reads=[Bqna, Bcs], writes=[Brt[0]]); yield
        Sc.op("pool", lambda: P_.tensor_tensor(out=rt_[1], in0=t2, in1=sb_, op=ALU.mult), reads=[Bqna, Bsn], writes=[Brt[1]]); yield
        Sc.op("dve", lambda: V.tensor_tensor(out=qra[:, :, 0:32], in0=rt_[0], in1=rt_[1], op=ALU.subtract), reads=[Brt[0], Brt[1]], writes=[Bqra]); yield
        Sc.op("pool", lambda: P_.tensor_tensor(out=rt_[2], in0=t2, in1=cb, op=ALU.mult), reads=[Bqna, Bcs], writes=[Brt[2]]); yield
        Sc.op("dve", lambda: V.tensor_tensor(out=rt_[3], in0=t1, in1=sb_, op=ALU.mult), reads=[Bqna, Bsn], writes=[Brt[3]]); yield
        Sc.op("dve", lambda: V.tensor_tensor(out=qra[:, :, 32:64], in0=rt_[2], in1=rt_[3], op=ALU.add), reads=[Brt[2], Brt[3]], writes=[Bqra]); yield
        Sc.op("act", lambda: A_.activation(out=vext[:, tb, :, 0:64], in_=qk_[:, 1024:1280].rearrange("p (a b) -> p a b", b=64), func=AF.Copy), reads=[bqk], writes=[Bvx[tb]]); yield
        b = 0
        qflat = qra.rearrange("p a b -> p (a b)")
        for j in range(8):
            Sc.op("pe", lambda: T_.transpose(out=psb[b][:, j * 128:(j + 1) * 128], in_=qflat[:, j * 128:(j + 1) * 128], identity=identb),
                  reads=[Bqra, Bid], writes=[Bps[b]] if j == 0 else [], inc=(j == 7), setw=[Bps[b]] if j == 7 else [])
        yield
        psv = psb[b].rearrange("p (a b) -> p a b", b=128)
        tsl = slice(tb * 128, (tb + 1) * 128)
        qv = qTa.rearrange("p (h two) t -> p h two t", two=2)
        kv = kTa.rearrange("p (h two) t -> p h two t", two=2)
        Sc.op("dve", lambda: V.tensor_copy(out=qv[0:64, :, 0, tsl], in_=psv[0:64, 0:6, :]), reads=[Bps[b]], writes=[BqTa[tb]]); yield
        Sc.op("act", lambda: A_.activation(out=qv[0:64, :, 1, tsl], in_=psv[64:128, 0:6, :], func=AF.Copy), reads=[Bps[b]], writes=[BqTa[tb]]); yield
        Sc.op("dve", lambda: V.tensor_copy(out=kv[0:64, :, 0, tsl], in_=psv[0:64, 6:8, :]), reads=[Bps[b]], writes=[BkTa[tb]]); yield
        Sc.op("act", lambda: A_.activation(out=kv[0:64, :, 1, tsl], in_=psv[64:128, 6:8, :], func=AF.Copy), reads=[Bps[b]], writes=[BkTa[tb]]); yield

    def attend(n):
        qsl = slice(n * 128, (n + 1) * 128)
        gt_ = gta[n % 3]; bgt = Bgta[n % 3]
        seq = []
        for g in range(4):
            kbs = [kb for kb in (n - 1, n, n + 1) if 0 <= kb < 16]
            for kb in kbs:
                seq.append((g, kb, kb == kbs[0], kb == kbs[-1]))
        order = []
        for (g, kb, fst, lst) in seq:
            for hh in range(3):
                order.append((g, kb, hh, fst, lst))
        firsts = {}; lasts = {}
        for idx, (g, kb, hh, fst, lst) in enumerate(order):
            ob = (3 * g + hh) // 4
            firsts.setdefault(ob, idx); lasts[ob] = idx
        idx = 0
        for (g, kb, fst, lst) in seq:
            b = sbanks[pta_rr[0] % len(sbanks)]
            mm_group(ps[b][:, 0:384], [(kTa[0:64, g, kb * 128:(kb + 1) * 128], qTa[0:64, 3 * g:3 * g + 3, qsl], [BkTa[kb], BqTa[n]])], Bps[b])
            pi = pta_rr[0] % 3; pta_rr[0] += 1
            yield
            Sc.op("act", lambda: A_.activation(out=pTa[pi], in_=ps[b][:, 0:384], func=AF.Exp, scale=0.125), reads=[Bps[b]], writes=[BpTa[pi]]); yield
            if kb != n:
                mk = mLR[:, 0 if kb < n else 1, :].unsqueeze(1).to_broadcast([128, 3, 128])
                Sc.op("pool", lambda: P_.tensor_tensor(out=pTa[pi].rearrange("p (a b) -> p a b", b=128), in0=pTa[pi].rearrange("p (a b) -> p a b", b=128), in1=mk, op=ALU.mult),
                      reads=[BpTa[pi], Bml], writes=[BpTa[pi]]); yield
            for hh in range(3):
                head = 3 * g + hh
                ob = head // 4
                col = (head % 4) * 128
                isf = firsts[ob] == idx; isl = lasts[ob] == idx
                st_flag = fst and (hh == 0 or head % 4 == 0)
                Sc.op("pe", lambda: T_.matmul(ps[obanks[ob]][:, col:col + 128], lhsT=vext[:, kb, g, :], rhs=pTa[pi][:, hh * 128:(hh + 1) * 128], start=st_flag, stop=lst),
                      reads=[Bvx[kb], BpTa[pi]], writes=[Bo[ob]] if isf else [], inc=True, setw=[Bo[ob]] if isl else [])
                idx += 1
            yield
        for ob in range(3):
            pv = ps[obanks[ob]][:, :].rearrange("p (a b) -> p a b", b=128)
            Sc.op("dve", lambda: V.tensor_tensor(out=rda[0:64, :, :], in0=pv[64:128, :, :], in1=esink[64:128, ob * 4:(ob + 1) * 4].unsqueeze(2).to_broadcast([64, 4, 128]), op=ALU.add),
                  reads=[Bo[ob], Bes], writes=[Brda]); yield
            Sc.op("act", lambda: A_.activation(out=rda[0:64, :, :], in_=rda[0:64, :, :], func=AF.Ln), reads=[Brda], writes=[Brda]); yield
            Sc.op("act", lambda: A_.activation(out=rda[0:64, :, :], in_=rda[0:64, :, :], func=AF.Exp, scale=-1.0), reads=[Brda], writes=[Brda]); yield
            pv2 = pv.rearrange("p (h two) t -> p h two t", two=2)
            rd2 = rda.rearrange("p (h two) t -> p h two t", two=2)
            c0 = ob * 2
            Sc.op("dve", lambda: V.tensor_tensor(out=yev[0:64, :, :], in0=pv2[0:64, :, 0, :], in1=rd2[0:64, :, 0, :], op=ALU.mult), reads=[Bo[ob], Brda], writes=[Byev]); yield
            Sc.op("dve", lambda: V.tensor_tensor(out=yodd[64:128, :, :], in0=pv2[0:64, :, 1, :], in1=rd2[0:64, :, 1, :], op=ALU.mult), reads=[Bo[ob], Brda], writes=[Byodd]); yield
            Sc.op("pool", lambda: P_.tensor_tensor(out=ya[0:64, c0:c0 + 2, qsl], in0=yev[0:64, :, :], in1=gt_[0:64, c0:c0 + 2, :], op=ALU.mult), reads=[Byev, bgt], writes=[Bya[n]]); yield
            Sc.op("pool", lambda: P_.tensor_tensor(out=ya[64:128, c0:c0 + 2, qsl], in0=yodd[64:128, :, :], in1=gt_[64:128, c0:c0 + 2, :], op=ALU.mult), reads=[Byodd, bgt], writes=[Bya[n]]); yield

    sbanks = [1, 2]
    obanks = [3, 4, 5]

    def attn_driver():
        for tb in range(16):
            gens = [prep_blk(tb)]
            if tb >= 2:
                gens.append(attend(tb - 2))
            yield from rr(gens)
        yield from attend(14)
        yield from attend(15)
        for c in range(6):
            Sc.dma("sp", ybuf[c * 128:(c + 1) * 128, :], ya[:, c, :], reads=Bya); yield

    run(attn_driver())
    Sc.barrier()
    if stop_after in ("attn", "xattn"):
        return finish_debug(nc, Sc, locals())

    AR.release(m_persist)
    mT = AR.bf16(16, S); BmT = [Buf() for _ in range(16)]
    m_p3 = AR.mark()
    yall = AR.bf16(16, S); Byall = [Buf() for _ in range(16)]
    wo = [AR.bf16(16, 256) for _ in range(2)]; Bwo = [[Buf(), Buf(), Buf()] for _ in range(2)]
    gt3 = [AR.bf16(S) for _ in range(2)]; Bgt3 = [Buf(), Buf()]
    macc = AR.f32(S); Bmacc = [Buf() for _ in range(4)]
    ptmp = [AR.f32(512) for _ in range(2)]; Bptmp = [Buf(), Buf()]
    for k in range(16):
        Sc.dma("sp", yall[:, k, :], ybuf[k * 128:(k + 1) * 128, :], writes=[Byall[k]])
    wsrcs = [(attn_w_o.rearrange("(k p) n -> p k n", p=128), 0, 6), (rwkv_w_o.rearrange("(k p) n -> p k n", p=128), 6, 6), (x_w_o.rearrange("(k p) n -> p k n", p=128), 12, 4)]
    kranges = [range(0, 6), range(6, 12), range(12, 16)]

    def load_wo(fg):
        for bi, (src, k0, nk) in enumerate(wsrcs):
            Sc.dma("pool", wo[fg % 2][:, k0:k0 + nk, :], src[:, :, fg * 256:(fg + 1) * 256], writes=[Bwo[fg % 2][bi]])

    g3_rr = [0]; pt_rr = [0]
    load_wo(0)
    for fg in range(8):
        if fg + 1 < 8:
            load_wo(fg + 1)
        for fi in range(2):
            f = fg * 2 + fi
            for bi in range(3):
                gi = g3_rr[0] % 2; g3_rr[0] += 1
                r0 = bi * 2048 + f * 128
                Sc.dma("sp", gt3[gi], gates_b[r0:r0 + 128, :], writes=[Bgt3[gi]])
                for tc in range(4):
                    sl = slice(tc * 512, (tc + 1) * 512)
                    bk = nbank()
                    mm_group(ps[bk][:, :], [(wo[fg % 2][:, kc, fi * 128:(fi + 1) * 128], yall[:, kc, sl], [Bwo[fg % 2][bi], Byall[kc]]) for kc in kranges[bi]], Bps[bk])
                    if bi == 0:
                        Sc.op("dve", lambda: V.tensor_tensor(out=macc[:, sl], in0=ps[bk][:, :], in1=gt3[gi][:, sl], op=ALU.mult), reads=[Bps[bk], Bgt3[gi]], writes=[Bmacc[tc]])
                    else:
                        pi = pt_rr[0] % 2; pt_rr[0] += 1
                        Sc.op("dve", lambda: V.tensor_tensor(out=ptmp[pi], in0=ps[bk][:, :], in1=gt3[gi][:, sl], op=ALU.mult), reads=[Bps[bk], Bgt3[gi]], writes=[Bptmp[pi]])
                        if bi == 1:
                            Sc.op("pool", lambda: P_.tensor_tensor(out=macc[:, sl], in0=macc[:, sl], in1=ptmp[pi], op=ALU.add), reads=[Bmacc[tc], Bptmp[pi]], writes=[Bmacc[tc]])
                        else:
                            Sc.op("pool", lambda: P_.tensor_tensor(out=mT[:, f, sl], in0=macc[:, sl], in1=ptmp[pi], op=ALU.add), reads=[Bmacc[tc], Bptmp[pi]], writes=[BmT[f]])
    dump("d_mT", mT, [128, 16, S], BF16, BmT)
    Sc.barrier()
    if stop_after == "merge":
        return finish_debug(nc, Sc, locals())
    AR.release(m_p3)
    wout = [AR.bf16(16, 512) for _ in range(2)]; Bwout = [[Buf() for _ in range(4)] for _ in range(2)]
    xres = [AR.f32(512) for _ in range(3)]; Bxres = [Buf() for _ in range(3)]
    ost = [AR.f32(512) for _ in range(3)]; Bost = [Buf() for _ in range(3)]
    wo_src = w_out.rearrange("(k p) n -> p k n", p=128)

    def load_wout(ng):
        for q in range(4):
            Sc.dma("pool", wout[ng % 2][:, q * 4:(q + 1) * 4, :], wo_src[:, q * 4:(q + 1) * 4, ng * 512:(ng + 1) * 512], writes=[Bwout[ng % 2][q]])

    load_wout(0)
    xr_rr = [0]
    final_toks = []
    for ng in range(4):
        if ng + 1 < 4:
            load_wout(ng + 1)
        for tb in range(16):
            xi = xr_rr[0] % 3; xr_rr[0] += 1
            Sc.dma("sp", xres[xi], x[tb * 128:(tb + 1) * 128, ng * 512:(ng + 1) * 512], writes=[Bxres[xi]])
            bk = nbank()
            mm_group(ps[bk][:, :], [(mT[:, f, tb * 128:(tb + 1) * 128], wout[ng % 2][:, f, :], [BmT[f], Bwout[ng % 2][f // 4]]) for f in range(16)], Bps[bk])
            Sc.op("dve", lambda: V.tensor_tensor(out=ost[xi], in0=ps[bk][:, :], in1=xres[xi], op=ALU.add), reads=[Bps[bk], Bxres[xi]], writes=[Bost[xi]])
            final_toks.append(Sc.dma("sp", out[tb * 128:(tb + 1) * 128, ng * 512:(ng + 1) * 512], ost[xi], reads=[Bost[xi]]))
    return finish_debug(nc, Sc, locals())


def finish_debug(nc, Sc, env):
    Sc.barrier()
    ok, stuck, _ = Sc.check_deadlock()
    if not ok:
        raise RuntimeError("logical deadlock in emitted program: %r" % (stuck,))
    Sc.close()
    for cm in reversed(env["ps_cms"]):
        cm.__exit__(None, None, None)
    env["big_cm"].__exit__(None, None, None)
    return nc


def make_in_maps(inputs):
    c = host_consts()
    f = lambda k: np.asarray(inputs[k], dtype=np.float32)
    sq = lambda k: f(k)[0]
    prm = np.zeros((128, NPRM), np.float32)

    def put(col, vec):
        m = vec.size // 128
        prm[:, col:col + m] = vec.reshape(m, 128).T

    put(NG, sq("norm_g")); put(MG_, sq("mem_norm_g")); put(GB, sq("gate_b")); put(MU, sq("rwkv_mu"))
    put(KK, sq("rwkv_k_k")); put(KA, sq("rwkv_k_a")); put(RK, sq("rwkv_r_k").reshape(-1)); put(LW, sq("rwkv_ln_w"))
    put(LB, sq("rwkv_ln_b")); put(W0, sq("rwkv_w0").reshape(-1)); put(A0, sq("rwkv_a0").reshape(-1))
    put(XQG, sq("x_q_norm_g")); put(XKG, sq("x_k_norm_g"))
    gqk = np.concatenate([np.tile(sq("attn_q_norm_g"), 12), np.tile(sq("attn_k_norm_g"), 4)])
    shared = {
        "w_in": sq("w_in"), "attn_w_o": sq("attn_w_o"), "rwkv_w_o": sq("rwkv_w_o"), "x_w_o": sq("x_w_o"),
        "x_w_kv": sq("x_w_kv"), "w_out": sq("w_out"),
        "w2cat": np.ascontiguousarray(sq("rwkv_w2").reshape(128, 768)), "a2cat": np.ascontiguousarray(sq("rwkv_a2").reshape(128, 768)),
        "prm": prm, "gqk": np.ascontiguousarray(np.broadcast_to(gqk[None, :], (128, 1024))),
        "sinkb": np.ascontiguousarray(np.broadcast_to(sq("attn_sink")[None, :], (128, 12))),
    }
    shared.update(c)
    xs = f("x"); ms = f("mem")
    return [dict(shared, x=np.ascontiguousarray(xs[b]), mem=np.ascontiguousarray(ms[b])) for b in range(xs.shape[0])]


_NC_CACHE = {}


def kernel(**inputs):
    in_maps = make_in_maps(inputs)
    if "nc" not in _NC_CACHE:
        _NC_CACHE["nc"] = build_nc()
    nc = _NC_CACHE["nc"]
    res = run_bass_kernel_spmd(nc, in_maps, core_ids=list(range(len(in_maps))))
    return np.stack([np.asarray(r["out"], dtype=np.float32) for r in res.results], axis=0)
```

```python
import math
import numpy as np
import ml_dtypes
import concourse.bass as bass
import concourse.mybir as mybir
from concourse.bass_utils import run_bass_kernel_spmd

F32 = mybir.dt.float32
BF16 = mybir.dt.bfloat16
AF = mybir.ActivationFunctionType
ALU = mybir.AluOpType
AX = mybir.AxisListType

S = 2048
D = 2048
NMEM = 256
INW = 12544
NQKV = 1280
NPF = INW - NQKV
EPS = 1e-6
GN_EPS = 64e-5
C1 = -0.5 * math.exp(-0.5)

AG0 = 0
R0 = 2048 - NQKV
K0 = R0 + 768
V0 = K0 + 768
LW0 = V0 + 768
LA0 = LW0 + 128
RG0 = 4608 - NQKV
XQ0 = 5376 - NQKV
XG0 = 5888 - NQKV
MG0 = 6400 - NQKV

NG, MG_, GB, MU, KK, KA, RK, LW, LB, W0, A0, XQG, XKG = 0, 16, 32, 80, 100, 106, 112, 118, 124, 130, 142, 154, 155
OMM, HMU, OMK, HW0, HA0, XG2 = 156, 176, 196, 202, 214, 226
NPRM = 228


class Buf:
    __slots__ = ("name", "w", "r", "pending")

    def __init__(self, name=""):
        self.name = name
        self.w = None
        self.r = {}
        self.pending = False


class Sched:
    ENG = ("pe", "act", "dve", "pool", "sp")

    def __init__(self, nc, n_dma_sems=40):
        self.nc = nc
        self.eng = {"pe": nc.tensor, "act": nc.scalar, "dve": nc.vector, "pool": nc.gpsimd, "sp": nc.sync}
        self.sem = {}
        self.cnt = {e: 0 for e in self.ENG}
        self.known = {e: {} for e in self.ENG}
        self._cms = []
        for e in self.ENG:
            cm = nc.semaphore("s_" + e)
            self.sem[e] = cm.__enter__()
            self._cms.append(cm)
        self.dsem = []
        for i in range(n_dma_sems):
            cm = nc.semaphore("d%d" % i)
            self.dsem.append([cm.__enter__(), 0])
            self._cms.append(cm)
        self.dnext = 0
        self.nwait = 0
        self.log = {e: [] for e in self.ENG}

    def close(self):
        for cm in reversed(self._cms):
            cm.__exit__(None, None, None)

    def _wait(self, e, key, semh, val):
        k = self.known[e]
        if k.get(key, 0) >= val:
            return
        self.eng[e].wait_ge(semh, val)
        self.nwait += 1
        self.log[e].append(("w", key, val))
        k[key] = val

    def wait_tok(self, e, tok):
        if tok is None:
            return
        kind, a, v = tok
        if kind == "eng":
            self._wait(e, a, self.sem[a], v)
        else:
            self._wait(e, "d%d" % a, self.dsem[a][0], v)

    def deps(self, e, reads, writes):
        for b in reads:
            self.wait_tok(e, b.w)
        for b in writes:
            self.wait_tok(e, b.w)
            for tok in b.r.values():
                self.wait_tok(e, tok)

    def op(self, e, fn, reads=(), writes=(), inc=True, setw=None):
        self.deps(e, reads, writes)
        ins = fn()
        if inc:
            self.cnt[e] += 1
            ins.then_inc(self.sem[e], 1)
            self.log[e].append(("i", e, 1))
            tok = ("eng", e, self.cnt[e])
        else:
            tok = ("eng", e, self.cnt[e] + 1)
        for b in reads:
            b.r[("eng", e)] = tok
            b.pending = False
        for b in (writes if setw is None else setw):
            b.w = tok
            b.r = {}
            b.pending = True
        for b in writes:
            b.pending = True
        return ins

    def dma(self, e, out, in_, reads=(), writes=()):
        idx = self.dnext
        self.dnext = (self.dnext + 1) % len(self.dsem)
        semh, val = self.dsem[idx]
        if val > 0:
            self._wait(e, "d%d" % idx, semh, val)
        self.deps(e, reads, writes)
        ins = self.eng[e].dma_start(out=out, in_=in_)
        val += 16
        ins.then_inc(semh, 16)
        self.log[e].append(("i", "d%d" % idx, 16))
        self.dsem[idx][1] = val
        tok = ("dma", idx, val)
        for b in reads:
            b.r[("dma", idx)] = tok
        for b in writes:
            b.w = tok
            b.r = {}
        return tok

    def check_deadlock(self):
        sem = {}
        pos = {e: 0 for e in self.ENG}
        prog = True
        while prog:
            prog = False
            for e in self.ENG:
                lg = self.log[e]
                while pos[e] < len(lg):
                    kind, key, val = lg[pos[e]]
                    if kind == "w":
                        if sem.get(key, 0) < val:
                            break
                    else:
                        sem[key] = sem.get(key, 0) + val
                    pos[e] += 1
                    prog = True
        stuck = {e: (pos[e], len(self.log[e]), self.log[e][pos[e]] if pos[e] < len(self.log[e]) else None) for e in self.ENG}
        ok = all(pos[e] == len(self.log[e]) for e in self.ENG)
        return ok, stuck, sem

    def barrier(self):
        for e in self.ENG:
            for e2 in self.ENG:
                if e2 != e and self.cnt[e2] > 0:
                    self._wait(e, e2, self.sem[e2], self.cnt[e2])
            for idx, (semh, val) in enumerate(self.dsem):
                if val > 0:
                    self._wait(e, "d%d" % idx, semh, val)


class Arena:
    def __init__(self, big, n):
        self.big = big
        self.n = n
        self.off = 0

    def mark(self):
        return self.off

    def release(self, m):
        self.off = m

    def _raw(self, nf32):
        a = self.off
        self.off += nf32
        assert self.off <= self.n, "SBUF arena overflow %d > %d" % (self.off, self.n)
        return self.big[:, a:a + nf32]

    @staticmethod
    def _shape(ap, dims):
        if len(dims) == 1:
            return ap
        if len(dims) == 2:
            return ap.rearrange("p (a b) -> p a b", b=dims[1])
        if len(dims) == 3:
            return ap.rearrange("p (a b c) -> p a b c", b=dims[1], c=dims[2])
        raise ValueError

    def f32(self, *dims):
        n = int(np.prod(dims))
        return self._shape(self._raw(n), dims)

    def bf16(self, *dims):
        n = int(np.prod(dims))
        assert n % 2 == 0
        return self._shape(self._raw(n // 2).bitcast(BF16), dims)


def host_consts():
    c = {}
    c["identf"] = np.eye(128, dtype=np.float32)
    bo = np.zeros((128, 128), np.float32)
    bo[:64, :64] = 1.0
    bo[64:, 64:] = 1.0
    c["bones"] = bo
    c["bo64"] = bo / 64.0
    half = 32
    inv = (10000.0 ** (-np.arange(half, dtype=np.float64) / half))
    ang = np.arange(S, dtype=np.float64)[:, None] * inv[None, :]
    c["cosr"] = np.ascontiguousarray(np.cos(ang).reshape(16, 128, 32).transpose(1, 0, 2)).astype(np.float32)
    c["sinr"] = np.ascontiguousarray(np.sin(ang).reshape(16, 128, 32).transpose(1, 0, 2)).astype(np.float32)
    j = np.arange(128)[:, None]
    i = np.arange(128)[None, :]
    c["maskLR"] = np.stack([(j >= i), (j <= i)], axis=1).astype(np.float32)
    t = np.arange(64)
    st_f = (t[:, None] < t[None, :]).astype(np.float32)
    in_f = (t[:, None] <= t[None, :]).astype(np.float32)
    def bd(m):
        z = np.zeros((128, 128), np.float32)
        z[:64, :64] = m
        z[64:, 64:] = m
        return z
    rwm = np.zeros((128, 2, 3, 128), np.float32)
    rwm[:, 0, 0] = bd(st_f); rwm[:, 0, 1] = bd(in_f); rwm[:, 0, 2] = bd(st_f.T)
    rwm[:, 1, 0] = bd(st_f.T); rwm[:, 1, 1] = bd(in_f.T); rwm[:, 1, 2] = bd(st_f)
    c["rwm"] = rwm
    seg = np.ones((128, 512), np.float32)
    seg[:, ::64] = 0.0
    c["segm"] = seg
    return c


def build_nc(stop_after="all", debug=()):
    nc = bass.Bass("TRN2", target_bir_lowering=False)

    def din(name, shape):
        return nc.dram_tensor(name, list(shape), F32, kind="ExternalInput").ap()

    x = din("x", [S, D]); mem = din("mem", [NMEM, D]); w_in = din("w_in", [D, INW])
    attn_w_o = din("attn_w_o", [768, D]); rwkv_w_o = din("rwkv_w_o", [768, D]); x_w_o = din("x_w_o", [512, D])
    x_w_kv = din("x_w_kv", [D, 1024]); w_out = din("w_out", [D, D])
    w2cat = din("w2cat", [128, 768]); a2cat = din("a2cat", [128, 768])
    prm_d = din("prm", [128, NPRM]); gqk_d = din("gqk", [128, 1024]); sink_d = din("sinkb", [128, 12])
    identf_d = din("identf", [128, 128]); bones_d = din("bones", [128, 128]); bo64_d = din("bo64", [128, 128])
    cos_d = din("cosr", [128, 16, 32]); sin_d = din("sinr", [128, 16, 32]); maskLR_d = din("maskLR", [128, 2, 128])
    rwm_d = din("rwm", [128, 2, 3, 128]); segm_d = din("segm", [128, 512])
    out = nc.dram_tensor("out", [S, D], F32, kind="ExternalOutput").ap()

    def dscr(name, shape, dt):
        kind = "ExternalOutput" if name in debug else "Internal"
        return nc.dram_tensor(name, list(shape), dt, kind=kind).ap()

    proj_f = dscr("proj_f", [NPF, S], F32)
    qkv_t = dscr("qkv_t", [S, NQKV], F32)
    ybuf = dscr("ybuf", [2048, S], BF16)
    dbg = dscr("dbg", [128, 4096], F32) if "dbg" in debug else None

    NBIG = 52600
    big_cm = nc.sbuf_tensor("big", [128, NBIG], F32)
    big = big_cm.__enter__()
    ps_cms = [nc.psum_tensor("ps%d" % i, [128, 512], F32) for i in range(8)]
    ps = [cm.__enter__() for cm in ps_cms]
    Bps = [Buf("ps%d" % i) for i in range(8)]
    psb = [p[:, :].bitcast(BF16) for p in ps]
    Sc = Sched(nc)
    AR = Arena(big, NBIG)
    V, A_, P_, G_, T_ = nc.vector, nc.scalar, nc.gpsimd, nc.sync, nc.tensor
    bank_rr = [0]
    dumped = set()

    def dump(name, sb_ap, shape, dt, reads):
        if name in debug and name not in dumped:
            dumped.add(name)
            t = nc.dram_tensor(name, list(shape), dt, kind="ExternalOutput").ap()
            Sc.dma("sp", t, sb_ap, reads=reads)

    def nbank(lo=0, hi=8):
        for _ in range(hi - lo):
            b = lo + bank_rr[0] % (hi - lo)
            bank_rr[0] += 1
            if not Bps[b].pending:
                return b
        raise RuntimeError("all PSUM banks in [%d,%d) hold unconsumed data" % (lo, hi))

    def mm_group(out_ap, items, obuf):
        n = len(items)
        for i, (l, r, rd) in enumerate(items):
            first, last = i == 0, i == n - 1
            Sc.op("pe", lambda: T_.matmul(out_ap, lhsT=l, rhs=r, start=first, stop=last), reads=rd,
                  writes=[obuf] if first else [], inc=last, setw=[obuf] if last else [])

    def mm_multi(items, obuf):
        n = len(items)
        for i, (o, l, r, rd) in enumerate(items):
            first, last = i == 0, i == n - 1
            Sc.op("pe", lambda: T_.matmul(o, lhsT=l, rhs=r, start=True, stop=True), reads=rd,
                  writes=[obuf] if first else [], inc=last, setw=[obuf] if last else [])

    def rsqrt_act(out_ap, in_ap, scale, eps, reads, writes, tmp_ap=None):
        t = out_ap if tmp_ap is None else tmp_ap
        Sc.op("act", lambda: A_.activation(out=t, in_=in_ap, func=AF.Ln, bias=eps, scale=scale), reads=reads, writes=writes)
        Sc.op("act", lambda: A_.activation(out=out_ap, in_=t, func=AF.Exp, scale=-0.5), reads=writes, writes=writes)

    identf = AR.f32(128); identb = AR.bf16(128); prm = AR.f32(NPRM)
    kmT = AR.bf16(4, 256); vm = AR.bf16(2, 512)
    Bid, Bprm, Bkm, Bvm = Buf("id"), Buf("prm"), Buf("kmT"), Buf("vm")
    Sc.dma("sp", identf, identf_d[:, :], writes=[Bid])
    Sc.dma("sp", prm[:, 0:OMM], prm_d[:, 0:OMM], writes=[Bprm])
    Sc.op("dve", lambda: V.tensor_copy(out=identb, in_=identf), reads=[Bid], writes=[Bid])
    Sc.op("dve", lambda: V.tensor_scalar(out=prm[:, OMM:OMM + 20], in0=prm[:, MU:MU + 20], scalar1=-1.0, scalar2=1.0, op0=ALU.mult, op1=ALU.add), reads=[Bprm], writes=[Bprm])
    Sc.op("dve", lambda: V.tensor_scalar(out=prm[:, HMU:HMU + 20], in0=prm[:, MU:MU + 20], scalar1=0.5, scalar2=None, op0=ALU.mult), reads=[Bprm], writes=[Bprm])
    Sc.op("dve", lambda: V.tensor_scalar(out=prm[:, OMK:OMK + 6], in0=prm[:, KA:KA + 6], scalar1=-1.0, scalar2=1.0, op0=ALU.mult, op1=ALU.add), reads=[Bprm], writes=[Bprm])
    Sc.op("dve", lambda: V.tensor_scalar(out=prm[:, HW0:HW0 + 24], in0=prm[:, W0:W0 + 24], scalar1=0.5, scalar2=None, op0=ALU.mult), reads=[Bprm], writes=[Bprm])
    Sc.op("dve", lambda: V.tensor_tensor(out=prm[:, XG2:XG2 + 1], in0=prm[:, XQG:XQG + 1], in1=prm[:, XKG:XKG + 1], op=ALU.mult), reads=[Bprm], writes=[Bprm])
    m_persist = AR.mark()

    hT = AR.bf16(16, S)
    BhT = [Buf("hT%d" % g) for g in range(4)]
    memT = AR.bf16(16, NMEM); BmemT = Buf("memT")
    m_ph0 = AR.mark()
    xbuf = [AR.f32(4, D) for _ in range(2)]
    Bx = [[Buf() for _ in range(4)] for _ in range(2)]
    junk = AR.f32(D); Bjunk = Buf("junk")
    evac_rr = [0]

    def build_T(src, nblk_total, gcol, dstT, dst_bufs):
        ngrp = (nblk_total + 3) // 4
        for g in range(ngrp):
            nb = min(4, nblk_total - g * 4)
            xb, bx = xbuf[g % 2], Bx[g % 2]
            ssq = AR.f32(4); rt = AR.f32(4); Bss = Buf("ss")
            Sc.op("dve", lambda: V.memset(ssq, 0.0), writes=[Bss])
            for i in range(nb):
                r0 = (g * 4 + i) * 128
                Sc.dma("sp", xb[:, i, :], src[r0:r0 + 128, :], writes=[bx[i]])
            for i in range(nb):
                Sc.op("act", lambda: A_.activation(out=junk, in_=xb[:, i, :], func=AF.Square, accum_out=ssq[:, i:i + 1]),
                      reads=[bx[i]], writes=[Bjunk, Bss])
            rsqrt_act(rt[:, 0:nb], ssq[:, 0:nb], 1.0 / D, EPS, [Bss], [Bss])
            for i in range(nb):
                Sc.op("dve", lambda: V.tensor_scalar(out=xb[:, i, :], in0=xb[:, i, :], scalar1=rt[:, i:i + 1], scalar2=None, op0=ALU.mult),
                      reads=[bx[i], Bss], writes=[bx[i]])
            for c in range(16):
                b = nbank()
                for i in range(nb):
                    Sc.op("pe", lambda: T_.transpose(out=ps[b][:, i * 128:(i + 1) * 128], in_=xb[:, i, c * 128:(c + 1) * 128], identity=identf),
                          reads=[bx[i], Bid], writes=[Bps[b]] if i == 0 else [], inc=(i == nb - 1), setw=[Bps[b]] if i == nb - 1 else [])
                dst = dstT[:, c, g * 512:g * 512 + nb * 128]
                gc = prm[:, gcol + c:gcol + c + 1]
                if evac_rr[0] % 2 == 0:
                    Sc.op("act", lambda: A_.activation(out=dst, in_=ps[b][:, 0:nb * 128], func=AF.Copy, scale=gc),
                          reads=[Bps[b], Bprm], writes=[dst_bufs[g]])
                else:
                    Sc.op("dve", lambda: V.tensor_scalar(out=dst, in0=ps[b][:, 0:nb * 128], scalar1=gc, scalar2=None, op0=ALU.mult),
                          reads=[Bps[b], Bprm], writes=[dst_bufs[g]])
                evac_rr[0] += 1

    build_T(mem, 2, MG_, memT, [BmemT])
    build_T(x, 16, NG, hT, BhT)

    Sc.barrier()
    AR.release(m_ph0)
    memT2 = memT
    wkv = AR.bf16(16, 1024); Bwkv = [Buf("wkv%d" % q) for q in range(4)]
    kmn = AR.f32(512); Bkmn = Buf("kmn")
    ssk = AR.f32(4); rk_ = AR.f32(4); Bssk = Buf("ssk")
    junk2 = AR.f32(128); Bjunk2 = Buf("junk2")
    wkv_src = x_w_kv.rearrange("(k p) n -> p k n", p=128)
    for q in range(4):
        Sc.dma("pool", wkv[:, q * 4:(q + 1) * 4, :], wkv_src[:, q * 4:(q + 1) * 4, :], writes=[Bwkv[q]])
    for mb in range(2):
        for half in range(2):
            b = nbank()
            mm_group(ps[b][:, :], [(memT2[:, k, mb * 128:(mb + 1) * 128], wkv[:, k, half * 512:(half + 1) * 512], [BmemT, Bwkv[k // 4]]) for k in range(16)], Bps[b])
            if half == 0:
                Sc.op("dve", lambda: V.memset(ssk, 0.0), writes=[Bssk])
                for h in range(4):
                    Sc.op("act", lambda: A_.activation(out=junk2, in_=ps[b][:, h * 128:(h + 1) * 128], func=AF.Square, accum_out=ssk[:, h:h + 1]),
                          reads=[Bps[b]], writes=[Bjunk2, Bssk])
                rsqrt_act(rk_, ssk, 1.0 / 128, EPS, [Bssk], [Bssk])
                for h in range(4):
                    Sc.op("dve", lambda: V.tensor_scalar(out=kmn[:, h * 128:(h + 1) * 128], in0=ps[b][:, h * 128:(h + 1) * 128], scalar1=rk_[:, h:h + 1], scalar2=None, op0=ALU.mult),
                          reads=[Bps[b], Bssk], writes=[Bkmn])
                b2 = nbank()
                for h in range(4):
                    Sc.op("pe", lambda: T_.transpose(out=ps[b2][:, h * 128:(h + 1) * 128], in_=kmn[:, h * 128:(h + 1) * 128], identity=identf),
                          reads=[Bkmn, Bid], writes=[Bps[b2]] if h == 0 else [], inc=(h == 3), setw=[Bps[b2]] if h == 3 else [])
                Sc.op("dve", lambda: V.tensor_scalar(out=kmT[:, :, mb * 128:(mb + 1) * 128], in0=ps[b2][:, :].rearrange("p (h m) -> p h m", m=128),
                                                     scalar1=prm[:, XG2:XG2 + 1], scalar2=None, op0=ALU.mult),
                      reads=[Bps[b2], Bprm], writes=[Bkm])
            else:
                Sc.op("act", lambda: A_.activation(out=vm[:, mb, :], in_=ps[b][:, :], func=AF.Copy), reads=[Bps[b]], writes=[Bvm])
    dump("d_kmT", kmT, [128, 4, 256], BF16, [Bkm]); dump("d_vm", vm, [128, 2, 512], BF16, [Bvm])
    dump("d_hT", hT, [128, 16, S], BF16, BhT)
    Sc.barrier()
    if stop_after == "hT":
        return finish_debug(nc, Sc, locals())

    AR.release(m_ph0)
    NWB = 3
    wt = [AR.bf16(16, 512) for _ in range(NWB)]
    Bwt = [[Buf("wt%d_%d" % (i, q)) for q in range(4)] for i in range(NWB)]
    stf = [AR.f32(S) for _ in range(2)]; Bstf = [Buf("stf%d" % i) for i in range(2)]
    stt = [AR.f32(512) for _ in range(3)]; Bstt = [Buf("stt%d" % i) for i in range(3)]
    Bproj = [Buf("pf%d" % i) for i in range(NPF // 128)]
    Bqkv = Buf("qkv")
    w_src = w_in.rearrange("(k p) n -> p k n", p=128)
    NT = (INW + 511) // 512

    def load_w(t):
        c0 = t * 512
        ncol = min(512, INW - c0)
        for q in range(4):
            Sc.dma("pool", wt[t % NWB][:, q * 4:(q + 1) * 4, 0:ncol], w_src[:, q * 4:(q + 1) * 4, c0:c0 + ncol], writes=[Bwt[t % NWB][q]])

    def act_for(feat):
        if (1280 <= feat < 2048) or (4608 <= feat < 5376) or (5888 <= feat < 6400):
            return "silu"
        if feat >= 6400:
            return "sig"
        return "copy"

    stf_rr = [0]; stt_rr = [0]; ev_rr = [0]
    import os as _os2
    TESTCOPY = bool(_os2.environ.get("TESTCOPY"))
    load_w(0); load_w(1)

    def rr(gens, weights=None):
        act_ = [[g, (weights[i] if weights else 1)] for i, g in enumerate(gens)]
        while act_:
            for ent in list(act_):
                for _ in range(ent[1]):
                    try:
                        next(ent[0])
                        yield
                    except StopIteration:
                        act_.remove(ent)
                        break

    def run(gen):
        for _ in gen:
            pass

    def proj_gen(t_lo, t_hi, bk):
      for t in range(t_lo, t_hi):
          if t + 2 < NT:
              load_w(t + 2)
          c0 = t * 512
          ncol = min(512, INW - c0)
          w = wt[t % NWB]; bw = Bwt[t % NWB]
          ntok = max(0, min(ncol, NQKV - c0))
          if ntok > 0:
              for tb in range(16):
                  b = nbank(*bk)
                  mm_group(ps[b][:, 0:ntok], [(hT[:, k, tb * 128:(tb + 1) * 128], w[:, k, 0:ntok], [BhT[tb // 4], bw[k // 4]]) for k in range(16)], Bps[b])
                  si = stt_rr[0] % 3; stt_rr[0] += 1
                  if ev_rr[0] % 2 == 0:
                      Sc.op("act", lambda: A_.activation(out=stt[si][:, 0:ntok], in_=ps[b][:, 0:ntok], func=AF.Copy), reads=[Bps[b]], writes=[Bstt[si]])
                  else:
                      Sc.op("dve", lambda: V.tensor_copy(out=stt[si][:, 0:ntok], in_=ps[b][:, 0:ntok]), reads=[Bps[b]], writes=[Bstt[si]])
                  ev_rr[0] += 1
                  Sc.dma("sp", qkv_t[tb * 128:(tb + 1) * 128, c0:c0 + ntok], stt[si][:, 0:ntok], reads=[Bstt[si]])
                  yield
          for sub in range(ntok // 128, ncol // 128):
              feat = c0 + sub * 128
              fi = (feat - NQKV) // 128
              kind = act_for(feat)
              si = stf_rr[0] % 2; stf_rr[0] += 1
              for tc in range(4):
                  b = nbank(*bk)
                  mm_group(ps[b][:, :], [(w[:, k, sub * 128:(sub + 1) * 128], hT[:, k, tc * 512:(tc + 1) * 512], [bw[k // 4], BhT[tc]]) for k in range(16)], Bps[b])
                  dst = stf[si][:, tc * 512:(tc + 1) * 512]
                  if kind == "silu":
                      Sc.op("act", lambda: A_.activation(out=dst, in_=ps[b][:, :], func=AF.Silu), reads=[Bps[b]], writes=[Bstf[si]])
                  elif kind == "sig":
                      gcol = GB + (feat - 6400) // 128
                      Sc.op("act", lambda: A_.activation(out=dst, in_=ps[b][:, :], func=(AF.Tanh if TESTCOPY else AF.Sigmoid), bias=prm[:, gcol:gcol + 1], scale=1.0),
                            reads=[Bps[b], Bprm], writes=[Bstf[si]])
                  else:
                      if ev_rr[0] % 2 == 0:
                          Sc.op("act", lambda: A_.activation(out=dst, in_=ps[b][:, :], func=AF.Copy), reads=[Bps[b]], writes=[Bstf[si]])
                      else:
                          Sc.op("dve", lambda: V.tensor_copy(out=dst, in_=ps[b][:, :]), reads=[Bps[b]], writes=[Bstf[si]])
                      ev_rr[0] += 1
                  yield
              Sc.dma("sp", proj_f[fi * 128:(fi + 1) * 128, :], stf[si], reads=[Bstf[si]], writes=[Bproj[fi]])

    TSPLIT = 13
    run(proj_gen(0, TSPLIT, (0, 8)))
    onesf = AR.f32(128); onesb = AR.bf16(128); Bones = Buf("ones")
    Sc.op("pool", lambda: P_.memset(onesf, 1.0), writes=[Bones])
    Sc.op("pool", lambda: P_.tensor_copy(out=onesb, in_=onesf), reads=[Bones], writes=[Bones])
    qTc = [AR.f32(S) for _ in range(2)]; gtc = [AR.f32(S)] * 2
    BqTc = [Buf(), Buf()]; Bgtc = [Buf()] * 2
    sqc = AR.f32(512); sc2 = AR.f32(512); qn_c = AR.bf16(512); pTc = [AR.bf16(512) for _ in range(2)]; rden = AR.f32(512); yo = AR.f32(512)
    yxs = AR.bf16(S)
    Bsqc, Bsc2, Bqnc, BpTc, Brden, Byo, Byxs = Buf(), Buf(), Buf(), [Buf(), Buf()], Buf(), Buf(), Buf()

    c_rr = [0]

    def cbank():
        c_rr[0] += 1
        return 6 + c_rr[0] % 2

    def xattn_gen():
        for h in range(4):
            Sc.dma("sp", qTc[h % 2], proj_f[XQ0 + h * 128:XQ0 + (h + 1) * 128, :], reads=[Bproj[XQ0 // 128 + h]], writes=[BqTc[h % 2]])
            Sc.dma("sp", gtc[h % 2], proj_f[XG0 + h * 128:XG0 + (h + 1) * 128, :], reads=[Bproj[XG0 // 128 + h]], writes=[Bgtc[h % 2]])
            q_ = qTc[h % 2]; bq_ = BqTc[h % 2]
            for tc in range(4):
                sl = slice(tc * 512, (tc + 1) * 512)
                Sc.op("act", lambda: A_.activation(out=sqc, in_=q_[:, sl], func=AF.Square), reads=[bq_], writes=[Bsqc]); yield
                b = cbank()
                mm_group(ps[b][:, :], [(onesf, sqc, [Bones, Bsqc])], Bps[b]); yield
                rsqrt_act(sc2, ps[b][:, :], 1.0, 128.0 * EPS, [Bps[b]], [Bsc2]); yield
                Sc.op("dve", lambda: V.tensor_tensor(out=qn_c, in0=q_[:, sl], in1=sc2, op=ALU.mult), reads=[bq_, Bsc2], writes=[Bqnc]); yield
                for mb in range(2):
                    b = cbank()
                    mm_group(ps[b][:, :], [(kmT[:, h, mb * 128:(mb + 1) * 128], qn_c, [Bkm, Bqnc])], Bps[b]); yield
                    Sc.op("act", lambda: A_.activation(out=pTc[mb], in_=ps[b][:, :], func=AF.Exp), reads=[Bps[b]], writes=[BpTc[mb]]); yield
                bo_ = cbank()
                mm_group(ps[bo_][:, :], [(vm[:, mb, h * 128:(h + 1) * 128], pTc[mb], [Bvm, BpTc[mb]]) for mb in range(2)], Bps[bo_]); yield
                bd_ = cbank()
                mm_group(ps[bd_][:, :], [(onesb, pTc[mb], [Bones, BpTc[mb]]) for mb in range(2)], Bps[bd_]); yield
                Sc.op("act", lambda: A_.activation(out=rden, in_=ps[bd_][:, :], func=AF.Ln), reads=[Bps[bd_]], writes=[Brden]); yield
                Sc.op("act", lambda: A_.activation(out=rden, in_=rden, func=AF.Exp, scale=-1.0), reads=[Brden], writes=[Brden]); yield
                Sc.op("dve", lambda: V.tensor_tensor(out=yo, in0=ps[bo_][:, :], in1=rden, op=ALU.mult), reads=[Bps[bo_], Brden], writes=[Byo]); yield
                Sc.op("pool", lambda: P_.tensor_tensor(out=yxs[:, sl], in0=yo, in1=gtc[h % 2][:, sl], op=ALU.mult), reads=[Byo, Bgtc[h % 2]], writes=[Byxs]); yield
            Sc.dma("sp", ybuf[1536 + h * 128:1536 + (h + 1) * 128, :], yxs, reads=[Byxs]); yield


    run(rr([proj_gen(TSPLIT, NT, (0, 6)), xattn_gen()]))
    Sc.barrier()
    if stop_after == "proj":
        return finish_debug(nc, Sc, locals())

    AR.release(m_persist)
    bones = AR.f32(128); bo64 = AR.f32(128); rwm = AR.bf16(2, 3, 128); segm = AR.f32(512)
    w2b = AR.bf16(768); a2b = AR.bf16(768)
    Bc = Buf("rwconst")
    Sc.dma("sp", bones, bones_d[:, :], writes=[Bc])
    tb1 = Buf(); tb2 = Buf(); tb3 = Buf(); tb4 = Buf(); tb5 = Buf()
    Sc.dma("sp", bo64, bo64_d[:, :], writes=[tb1])
    Sc.dma("sp", segm, segm_d[:, :], writes=[tb2])
    Sc.dma("pool", rwm, rwm_d[:, :, :, :], writes=[tb3])
    Sc.dma("pool", w2b, w2cat[:, :], writes=[tb4])
    Sc.dma("pool", a2b, a2cat[:, :], writes=[tb5])
    lw_t = AR.bf16(S); la_s = AR.bf16(S); Blw = Buf("lw"); Bla = Buf("la")
    r_ = AR.bf16(S); k_ = AR.bf16(S); v_ = AR.bf16(S); kk_ = AR.bf16(S); bon = AR.f32(S); ysum = AR.f32(S)
    Br, Bk, Bv, Bkk, Bbon, Bys = Buf("r"), Buf("k"), Buf("v"), Buf("kk"), Buf("bon"), Buf("ysum")
    Vtok = AR.bf16(32, 128); BVtok = [Buf("vtok%d" % q) for q in range(4)]
    vbd = AR.bf16(8, 128); Bvbd = Buf("vbd")
    rkones = AR.f32(128); Brk = Buf("rkones")
    NTMP = 7168
    tmp_raw = AR._raw(NTMP)
    WD = []
    for d in range(2):
        W = {}
        for nm in ("ARt", "BKtok", "Gb", "Gk"):
            W[nm] = [AR.bf16(8, 2, 128) for _ in range(2)]
        W["Tt"] = [AR.bf16(8, 128) for _ in range(2)]
        W["Pc"] = [AR.f32(8) for _ in range(2)]
        W["BKt"] = AR.bf16(8, 2, 128)
        W["Xs"] = AR.bf16(128); W["Ubf"] = AR.bf16(128); W["Ybd"] = AR.f32(8, 128)
        W["St"] = AR.f32(128); W["Stbf"] = AR.bf16(128); W["tmpS"] = AR.f32(128); W["tot"] = AR.f32(8)
        for nm in ("BBKt", "BXs", "BUbf", "BSt", "BStbf", "BtmpS", "Btot", "BYbd"):
            W[nm] = Buf(nm)
        for nm in ("BARt", "BPc"):
            W[nm] = [Buf(nm + "0"), Buf(nm + "1")]
        for nm in ("BBKtok", "BGb", "BGk", "BTt"):
            W[nm] = [[Buf(), Buf()], [Buf(), Buf()]]
        W["BLab"] = [Buf(), Buf()]
        W["BAn"] = [[Buf(), Buf()], [Buf(), Buf()]]; W["BBn"] = [[Buf(), Buf()], [Buf(), Buf()]]
        TA = Arena(tmp_raw[:, d * (NTMP // 2):(d + 1) * (NTMP // 2)], NTMP // 2)
        for nm in ("a", "ld", "cum", "E1", "E2", "E3", "u"):
            W[nm] = TA.f32(512); W["B" + nm] = Buf(nm)
        asb = lambda ap: ap.bitcast(BF16).rearrange("p (a b) -> p a b", b=128)
        W["An"] = [asb(W["E1"]), asb(W["E2"])]; W["Bn"] = [asb(W["E3"]), asb(W["u"])]; W["Lab"] = asb(W["ld"])
        W["alias_An"] = [W["BE1"], W["BE2"]]; W["alias_Bn"] = [W["BE3"], W["Bu"]]
        WD.append(W)
    for d in range(2):
        W = WD[d]
        for jp in range(2):
            Sc.op("pool", lambda: P_.memset(W["ARt"][jp], 0.0), writes=[W["BARt"][jp]])
        Sc.op("pool", lambda: P_.memset(W["BKt"], 0.0), writes=[W["BBKt"]])
    Sc.op("pool", lambda: P_.memset(vbd, 0.0), writes=[Bvbd])

    rawc = [AR.f32(514) for _ in range(2)]; Brawc = [Buf(), Buf()]
    nbc = AR.f32(512); sqc_ = AR.f32(512); Bnbc, Bsqc_ = Buf(), Buf()
    sqk = nbc; rnk = sqc_; Bsqk, Brnk = Bnbc, Bsqc_
    gatec = AR.f32(512); dtmp = AR.f32(512); sq2 = AR.f32(512); rstd = AR.f32(512); yn = AR.f32(512); ystc = [AR.bf16(512) for _ in range(2)]
    Bgatec, Bdt, Bs2, Brs, Byn, Bystc = Buf(), Buf(), Buf(), Buf(), Buf(), [Buf(), Buf()]
    rc_rr = [0]

    def shift_gen(dst, bdst, row0, mi, func=AF.Copy):
        for tc in range(4):
            sl = slice(tc * 512, (tc + 1) * 512)
            lo = max(0, tc * 512 - 1); hi = min(S, tc * 512 + 513)
            off = lo - (tc * 512 - 1)
            ri = rc_rr[0] % 2; rc_rr[0] += 1
            rc = rawc[ri]; brc = Brawc[ri]
            if tc == 0:
                Sc.op("pool", lambda: P_.memset(rc[:, 0:1], 0.0), writes=[brc])
            if tc == 3:
                Sc.op("pool", lambda: P_.memset(rc[:, 513:514], 0.0), writes=[brc])
            Sc.dma("sp", rc[:, off:off + (hi - lo)], proj_f[row0:row0 + 128, lo:hi], writes=[brc]); yield
            Sc.op("pool", lambda: P_.tensor_tensor(out=nbc, in0=rc[:, 0:512], in1=rc[:, 2:514], op=ALU.add), reads=[brc], writes=[Bnbc]); yield
            Sc.op("act", lambda: A_.activation(out=sqc_, in_=rc[:, 1:513], func=AF.Copy, scale=prm[:, OMM + mi:OMM + mi + 1]), reads=[brc, Bprm], writes=[Bsqc_]); yield
            if func == AF.Copy:
                Sc.op("dve", lambda: V.scalar_tensor_tensor(out=dst[:, sl], in0=nbc, scalar=prm[:, HMU + mi:HMU + mi + 1], in1=sqc_, op0=ALU.mult, op1=ALU.add),
                      reads=[Bnbc, Bprm, Bsqc_], writes=[bdst]); yield
            else:
                Sc.op("dve", lambda: V.scalar_tensor_tensor(out=sqc_, in0=nbc, scalar=prm[:, HMU + mi:HMU + mi + 1], in1=sqc_, op0=ALU.mult, op1=ALU.add),
                      reads=[Bnbc, Bprm, Bsqc_], writes=[Bsqc_]); yield
                Sc.op("act", lambda: A_.activation(out=dst[:, sl], in_=sqc_, func=func), reads=[Bsqc_], writes=[bdst]); yield

    def v3(ap, n=64):
        return ap.rearrange("p (c t) -> p c t", t=n)

    def unit_prep(p, d, sc, jp):
        W = WD[d]
        sl = slice(sc * 512, (sc + 1) * 512)
        dh = slice(d * 64, (d + 1) * 64)
        pc = slice(p * 128, (p + 1) * 128)
        a, ld, cum, E1, E2, E3, u = W["a"], W["ld"], W["cum"], W["E1"], W["E2"], W["E3"], W["u"]
        Ba, Bld, Bcum, BE1, BE2, BE3, Bu = W["Ba"], W["Bld"], W["Bcum"], W["BE1"], W["BE2"], W["BE3"], W["Bu"]
        ARt, BKtok, Gb, Gk, Tt, Pc = W["ARt"][jp], W["BKtok"][jp], W["Gb"][jp], W["Gk"][jp], W["Tt"][jp], W["Pc"][jp]
        BARt, BBKtok, BGb, BGk, BTt, BPc = W["BARt"][jp], W["BBKtok"][jp], W["BGb"][jp], W["BGk"][jp], W["BTt"][jp], W["BPc"][jp]
        BKt, Lab = W["BKt"], W["Lab"]
        b = nbank(2, 8)
        mm_group(ps[b][:, :], [(a2b[dh, pc], la_s[dh, sl], [tb5, Bla])], Bps[b]); yield
        hc = HA0 + d * 6 + p
        Sc.op("act", lambda: A_.activation(out=a, in_=ps[b][:, :], func=AF.Tanh, bias=prm[:, hc:hc + 1], scale=0.5), reads=[Bps[b], Bprm], writes=[Ba]); yield
        Sc.op("dve", lambda: V.tensor_scalar(out=a, in0=a, scalar1=0.5, scalar2=0.5, op0=ALU.mult, op1=ALU.add), reads=[Ba], writes=[Ba]); yield
        b = nbank(2, 8)
        mm_group(ps[b][:, :], [(w2b[dh, pc], lw_t[dh, sl], [tb4, Blw])], Bps[b]); yield
        hc2 = HW0 + d * 6 + p
        Sc.op("act", lambda: A_.activation(out=ld, in_=ps[b][:, :], func=AF.Tanh, bias=prm[:, hc2:hc2 + 1], scale=0.5), reads=[Bps[b], Bprm], writes=[Bld] + W["BLab"]); yield
        Sc.op("dve", lambda: V.tensor_scalar(out=ld, in0=ld, scalar1=C1, scalar2=C1, op0=ALU.mult, op1=ALU.add), reads=[Bld], writes=[Bld]); yield
        Sc.op("dve", lambda: V.tensor_tensor_scan(out=cum, data0=segm, data1=ld, initial=0.0, op0=ALU.mult, op1=ALU.add), reads=[tb2, Bld], writes=[Bcum]); yield
        if d == 1:
            Sc.op("dve", lambda: V.tensor_copy(out=W["tot"], in_=v3(cum)[:, :, 63]), reads=[Bcum], writes=[W["Btot"]]); yield
            Sc.op("dve", lambda: V.tensor_tensor(out=cum, in0=ld, in1=cum, op=ALU.subtract), reads=[Bld, Bcum], writes=[Bcum]); yield
            Sc.op("dve", lambda: V.tensor_tensor(out=v3(cum), in0=v3(cum), in1=W["tot"].unsqueeze(2).to_broadcast([128, 8, 64]), op=ALU.add),
                  reads=[Bcum, W["Btot"]], writes=[Bcum]); yield
        Sc.op("dve", lambda: V.tensor_tensor(out=ld, in0=cum, in1=ld, op=ALU.subtract), reads=[Bcum, Bld], writes=[Bld]); yield
        Sc.op("act", lambda: A_.activation(out=E3, in_=ld, func=AF.Exp), reads=[Bld], writes=[BE3] + W["BBn"][0]); yield
        Sc.op("act", lambda: A_.activation(out=E1, in_=cum, func=AF.Exp), reads=[Bcum], writes=[BE1] + W["BAn"][0]); yield
        Sc.op("act", lambda: A_.activation(out=E2, in_=cum, func=AF.Exp, scale=-1.0), reads=[Bcum], writes=[BE2] + W["BAn"][1]); yield
        pcol = 63 if d == 0 else 0
        Sc.op("dve", lambda: V.tensor_copy(out=Pc, in_=v3(E1)[:, :, pcol]), reads=[BE1], writes=[BPc]); yield
        Sc.op("dve", lambda: V.tensor_scalar(out=u, in0=a, scalar1=prm[:, KA + p:KA + p + 1], scalar2=prm[:, OMK + p:OMK + p + 1], op0=ALU.mult, op1=ALU.add),
              reads=[Ba, Bprm], writes=[Bu] + W["BBn"][1]); yield
        Sc.op("dve", lambda: V.tensor_tensor(out=u, in0=k_[:, sl], in1=u, op=ALU.mult), reads=[Bk, Bu], writes=[Bu]); yield
        Sc.op("pool", lambda: P_.tensor_tensor(out=a, in0=kk_[:, sl], in1=a, op=ALU.mult), reads=[Bkk, Ba], writes=[Ba]); yield
        for half in range(2):
            hs = slice(half * 64, (half + 1) * 64)
            bc = slice(half * 64, (half + 1) * 64)
            Sc.op("dve", lambda: V.scalar_tensor_tensor(out=ARt[hs, :, 0, bc], in0=v3(kk_[hs, sl]), scalar=-1.0, in1=v3(E3[hs, :]), op0=ALU.mult, op1=ALU.mult),
                  reads=[Bkk, BE3], writes=[BARt]); yield
            Sc.op("pool", lambda: P_.tensor_tensor(out=ARt[hs, :, 1, bc], in0=v3(r_[hs, sl]), in1=v3(E1[hs, :]), op=ALU.mult), reads=[Br, BE1], writes=[BARt]); yield
            Sc.op("dve", lambda: V.tensor_tensor(out=BKt[hs, :, 1, bc], in0=v3(u[hs, :]), in1=v3(E2[hs, :]), op=ALU.mult), reads=[Bu, BE2], writes=[W["BBKt"]]); yield
            Sc.op("dve", lambda: V.tensor_tensor(out=BKt[hs, :, 0, bc], in0=v3(a[hs, :]), in1=v3(E2[hs, :]), op=ALU.mult), reads=[Ba, BE2], writes=[W["BBKt"]]); yield
        Sc.op("pool", lambda: P_.tensor_tensor(out=cum, in0=r_[:, sl], in1=u, op=ALU.mult), reads=[Br, Bu, Bcum], writes=[Bcum]); yield
        b = nbank(2, 8)
        mm_group(ps[b][:, :], [(rkones, cum, [Brk, Bcum])], Bps[b]); yield
        Sc.op("dve", lambda: V.tensor_tensor(out=cum, in0=ps[b][:, :], in1=v_[:, sl], op=ALU.mult), reads=[Bps[b], Bv, Bcum], writes=[Bcum]); yield
        Sc.op("pool", lambda: P_.tensor_tensor(out=bon[:, sl], in0=bon[:, sl], in1=cum, op=ALU.add), reads=[Bbon, Bcum], writes=[Bbon]); yield
        for hb in range(2):
            b = nbank(2, 8)
            for ci in range(4):
                for s2 in range(2):
                    j = ci * 2 + s2
                    Sc.op("pe", lambda: T_.transpose(out=psb[b][:, j * 128:(j + 1) * 128], in_=BKt[:, hb * 4 + ci, s2, :], identity=identb),
                          reads=[W["BBKt"], Bid], writes=[Bps[b]] if j == 0 else [], inc=(j == 7), setw=[Bps[b]] if j == 7 else [])
            yield
            dstv = BKtok[:, hb * 4:(hb + 1) * 4, :, :].rearrange("p a b c -> p (a b c)")
            Sc.op("act", lambda: A_.activation(out=dstv, in_=psb[b], func=AF.Copy), reads=[Bps[b]], writes=[BBKtok[hb]]); yield
        M2 = rwm[:, d, 0:2, :].rearrange("p a b -> p (a b)")
        ML = rwm[:, d, 2, :]
        for hb in range(2):
            bL = nbank(2, 8)
            mm_multi([(ps[bL][:, ci * 128:(ci + 1) * 128], ARt[:, hb * 4 + ci, 0, :], BKt[:, hb * 4 + ci, 0, :], [W["BBKt"], BARt]) for ci in range(4)], Bps[bL])
            yield
            Sc.op("dve", lambda: V.tensor_tensor(out=Lab[:, hb * 4:(hb + 1) * 4, :], in0=ps[bL][:, :].rearrange("p (a b) -> p a b", b=128),
                                                 in1=ML.unsqueeze(1).to_broadcast([128, 4, 128]), op=ALU.mult), reads=[Bps[bL], tb3], writes=[W["BLab"][hb], Bld]); yield
            bB = [nbank(2, 8), nbank(2, 8)]
            for i in range(2):
                mm_multi([(ps[bB[i]][:, q2 * 256:(q2 + 1) * 256], BKt[:, hb * 4 + i * 2 + q2, 0, :], ARt[:, hb * 4 + i * 2 + q2, :, :], [W["BBKt"], BARt]) for q2 in range(2)], Bps[bB[i]])
            yield
            for i in range(2):
                c0 = hb * 4 + i * 2
                Sc.op("dve", lambda: V.tensor_tensor(out=Gb[:, c0:c0 + 2, :, :].rearrange("p a b c -> p a (b c)"), in0=ps[bB[i]][:, :].rearrange("p (a b) -> p a b", b=256),
                                                     in1=M2.unsqueeze(1).to_broadcast([128, 2, 256]), op=ALU.mult), reads=[Bps[bB[i]], tb3], writes=[BGb[hb]]); yield
        for hb in range(2):
            cs = slice(hb * 4, (hb + 1) * 4)
            Sc.op("dve", lambda: V.tensor_tensor(out=Tt[:, cs, :], in0=Gb[:, cs, 0, :], in1=identb.unsqueeze(1).to_broadcast([128, 4, 128]), op=ALU.add),
                  reads=[BGb[hb], Bid], writes=[BTt[hb]]); yield
        for lvl in range(0, 6):
            for hb in range(2):
                cs = slice(hb * 4, (hb + 1) * 4)
                if lvl == 0:
                    Aget = lambda c: Gb[:, c, 0, :]
                    Bget = lambda c: Lab[:, c, :]
                    BAsrc, BBsrc = BGb[hb], W["BLab"][hb]
                else:
                    Aprev, Bprev = W["An"][lvl % 2], W["Bn"][lvl % 2]
                    Aget = lambda c: Aprev[:, c, :]
                    Bget = lambda c: Bprev[:, c, :]
                    BAsrc, BBsrc = W["BAn"][lvl % 2][hb], W["BBn"][lvl % 2][hb]
                Anew, Bnew = W["An"][(lvl + 1) % 2], W["Bn"][(lvl + 1) % 2]
                BAnew, BBnew = W["BAn"][(lvl + 1) % 2][hb], W["BBn"][(lvl + 1) % 2][hb]
                if lvl >= 1:
                    bT = nbank(2, 8)
                    mm_multi([(ps[bT][:, ci * 128:(ci + 1) * 128], Bget(hb * 4 + ci), Tt[:, hb * 4 + ci, :], [BBsrc, BTt[hb]]) for ci in range(4)], Bps[bT])
                    yield
                if lvl < 5:
                    if lvl < 4:
                        bA = nbank(2, 8)
                        mm_multi([(ps[bA][:, ci * 128:(ci + 1) * 128], Bget(hb * 4 + ci), Aget(hb * 4 + ci), [BAsrc, BBsrc]) for ci in range(4)], Bps[bA])
                        yield
                    bBm = nbank(2, 8)
                    mm_multi([(ps[bBm][:, ci * 128:(ci + 1) * 128], Aget(hb * 4 + ci), Bget(hb * 4 + ci), [BAsrc, BBsrc]) for ci in range(4)], Bps[bBm])
                    yield
                if lvl >= 1:
                    Sc.op("dve", lambda: V.tensor_tensor(out=Tt[:, cs, :], in0=ps[bT][:, :].rearrange("p (a b) -> p a b", b=128), in1=Tt[:, cs, :], op=ALU.add),
                          reads=[Bps[bT], BTt[hb]], writes=[BTt[hb]]); yield
                if lvl < 5:
                    if lvl < 4:
                        Sc.op("act", lambda: A_.activation(out=Anew[:, cs, :], in_=ps[bA][:, :].rearrange("p (a b) -> p a b", b=128), func=AF.Copy),
                              reads=[Bps[bA]], writes=[BAnew, W["alias_An"][(lvl + 1) % 2]]); yield
                    Sc.op("act", lambda: A_.activation(out=Bnew[:, cs, :], in_=ps[bBm][:, :].rearrange("p (a b) -> p a b", b=128), func=AF.Copy),
                          reads=[Bps[bBm]], writes=[BBnew, W["alias_Bn"][(lvl + 1) % 2]]); yield
        for hb in range(2):
            bK = [nbank(2, 8), nbank(2, 8)]
            for i in range(2):
                mm_multi([(ps[bK[i]][:, q2 * 256:(q2 + 1) * 256], BKt[:, hb * 4 + i * 2 + q2, 1, :], ARt[:, hb * 4 + i * 2 + q2, :, :], [W["BBKt"], BARt]) for q2 in range(2)], Bps[bK[i]])
            yield
            for i in range(2):
                c0 = hb * 4 + i * 2
                Sc.op("dve", lambda: V.tensor_tensor(out=Gk[:, c0:c0 + 2, :, :].rearrange("p a b c -> p a (b c)"), in0=ps[bK[i]][:, :].rearrange("p (a b) -> p a b", b=256),
                                                     in1=M2.unsqueeze(1).to_broadcast([128, 2, 256]), op=ALU.mult), reads=[Bps[bK[i]], tb3], writes=[BGk[hb]]); yield

    def scan_step(p, d, sc, ci, jp):
        W = WD[d]
        hb = ci // 4
        cg = sc * 8 + ci
        sb = d
        bkb = Bps[sb]
        ARt, BKtok, Gb, Gk, Tt, Pc = W["ARt"][jp], W["BKtok"][jp], W["Gb"][jp], W["Gk"][jp], W["Tt"][jp], W["Pc"][jp]
        BARt, BBKtok, BGb, BGk, BTt, BPc = W["BARt"][jp], W["BBKtok"][jp], W["BGb"][jp], W["BGk"][jp], W["BTt"][jp], W["BPc"][jp]
        St, Stbf, Xs, Ubf, tmpS = W["St"], W["Stbf"], W["Xs"], W["Ubf"], W["tmpS"]
        vt = Vtok[:, cg, :]
        bvt = BVtok[cg // 8]
        pcc = Pc[:, ci:ci + 1]
        Sc.op("act", lambda: A_.activation(out=tmpS, in_=St, func=AF.Copy, scale=pcc), reads=[W["BSt"], BPc], writes=[W["BtmpS"]]); yield
        mm_group(ps[sb][:, 0:128], [(ARt[:, ci, 0, :], Stbf, [BARt, W["BStbf"]]), (Gk[:, ci, 0, :], vt, [BGk[hb], bvt])], bkb); yield
        Sc.op("act", lambda: A_.activation(out=Xs, in_=ps[sb][:, 0:128], func=AF.Copy), writes=[bkb, W["BXs"]]); yield
        mm_group(ps[sb][:, 128:256], [(Tt[:, ci, :], Xs, [BTt[hb], W["BXs"]])], bkb); yield
        Sc.op("dve", lambda: V.tensor_copy(out=Ubf, in_=ps[sb][:, 128:256]), writes=[bkb, W["BUbf"]]); yield
        mm_group(ps[sb][:, 384:512], [(BKtok[:, ci, 0, :], Ubf, [BBKtok[hb], W["BUbf"]]), (BKtok[:, ci, 1, :], vt, [BBKtok[hb], bvt])], bkb); yield
        mm_group(ps[sb][:, 256:384], [(Stbf, ARt[:, ci, 1, :], [BARt, W["BStbf"]]), (Ubf, Gb[:, ci, 1, :], [W["BUbf"], BGb[hb]]),
                                      (vt, Gk[:, ci, 1, :], [bvt, BGk[hb]])], bkb); yield
        Sc.op("dve", lambda: V.scalar_tensor_tensor(out=St, in0=ps[sb][:, 384:512], scalar=pcc, in1=tmpS, op0=ALU.mult, op1=ALU.add),
              reads=[BPc, W["BtmpS"]], writes=[bkb, W["BSt"]]); yield
        Sc.op("act", lambda: A_.activation(out=Stbf, in_=St, func=AF.Copy), reads=[W["BSt"]], writes=[W["BStbf"]]); yield
        Sc.op("act", lambda: A_.activation(out=W["Ybd"][:, ci, :], in_=ps[sb][:, 256:384], func=AF.Copy), writes=[bkb, W["BYbd"]]); yield

    def chain(p, d, sc, jp):
        order = range(8) if d == 0 else range(7, -1, -1)
        for ci in order:
            yield from scan_step(p, d, sc, ci, jp)
        W = WD[d]
        sl = slice(sc * 512, (sc + 1) * 512)
        for half in range(2):
            hs = slice(half * 64, (half + 1) * 64)
            Sc.op("pool", lambda: P_.tensor_tensor(out=v3(ysum[hs, sl]), in0=v3(ysum[hs, sl]), in1=W["Ybd"][hs, :, half * 64:(half + 1) * 64], op=ALU.add),
                  reads=[Bys, W["BYbd"]], writes=[Bys]); yield

    run(shift_gen(lw_t, Blw, LW0, 18, func=AF.Tanh))
    run(shift_gen(la_s, Bla, LA0, 19))

    PAIRS = list(range(6))
    if stop_after.startswith("rwkvp"):
        PAIRS = [int(ch) for ch in stop_after[5:]]

    def prologue(p):
        yield from shift_gen(r_, Br, R0 + p * 128, p)
        yield from shift_gen(k_, Bk, K0 + p * 128, 6 + p)
        yield from shift_gen(v_, Bv, V0 + p * 128, 12 + p)
        kcol = prm[:, KK + p:KK + p + 1]
        for tc in range(4):
            sl = slice(tc * 512, (tc + 1) * 512)
            Sc.op("act", lambda: A_.activation(out=sqk, in_=k_[:, sl], func=AF.Square, scale=kcol), reads=[Bk, Bprm], writes=[Bsqk]); yield
            b = nbank(2, 8)
            mm_group(ps[b][:, :], [(bones, sqk, [Bc, Bsqk])], Bps[b]); yield
            rsqrt_act(rnk, ps[b][:, :], 1.0, 1e-24, [Bps[b]], [Brnk]); yield
            Sc.op("dve", lambda: V.scalar_tensor_tensor(out=kk_[:, sl], in0=k_[:, sl], scalar=kcol, in1=rnk, op0=ALU.mult, op1=ALU.mult),
                  reads=[Bk, Bprm, Brnk], writes=[Bkk]); yield
        Sc.op("dve", lambda: V.tensor_scalar(out=rkones, in0=bones, scalar1=prm[:, RK + p:RK + p + 1], scalar2=None, op0=ALU.mult), reads=[Bc, Bprm], writes=[Brk]); yield

    def vtok_build(p):
        for q in range(4):
            for half in range(2):
                hs = slice(half * 64, (half + 1) * 64)
                Sc.op("pool", lambda: P_.tensor_copy(out=vbd[hs, :, half * 64:(half + 1) * 64], in_=v3(v_[hs, q * 512:(q + 1) * 512])), reads=[Bv], writes=[Bvbd]); yield
            b = nbank(2, 8)
            for j in range(8):
                Sc.op("pe", lambda: T_.transpose(out=psb[b][:, j * 128:(j + 1) * 128], in_=vbd[:, j, :], identity=identb),
                      reads=[Bvbd, Bid], writes=[Bps[b]] if j == 0 else [], inc=(j == 7), setw=[Bps[b]] if j == 7 else [])
            yield
            Sc.op("dve", lambda: V.tensor_copy(out=Vtok[:, q * 8:(q + 1) * 8, :].rearrange("p a b -> p (a b)"), in_=psb[b]), reads=[Bps[b]], writes=[BVtok[q]]); yield

    def resets(p):
        Sc.op("pool", lambda: P_.memset(bon, 0.0), writes=[Bbon])
        Sc.op("pool", lambda: P_.memset(ysum, 0.0), writes=[Bys])
        for d in range(2):
            W = WD[d]
            Sc.op("pool", lambda: P_.memset(W["St"], 0.0), writes=[W["BSt"]])
            Sc.op("pool", lambda: P_.memset(W["Stbf"], 0.0), writes=[W["BStbf"]])

    def epilogue(p):
        dump("d_ysum%d" % p, ysum, [128, S], F32, [Bys])
        for tc in range(4):
            sl = slice(tc * 512, (tc + 1) * 512)
            yc = ystc[tc % 2]; byc = Bystc[tc % 2]
            Sc.dma("sp", gatec, proj_f[RG0 + p * 128:RG0 + (p + 1) * 128, sl], writes=[Bgatec])
            b = nbank(2, 8)
            mm_group(ps[b][:, :], [(bo64, ysum[:, sl], [tb1, Bys])], Bps[b]); yield
            Sc.op("dve", lambda: V.tensor_tensor(out=dtmp, in0=ysum[:, sl], in1=ps[b][:, :], op=ALU.subtract), reads=[Bys, Bps[b]], writes=[Bdt]); yield
            Sc.op("act", lambda: A_.activation(out=sq2, in_=dtmp, func=AF.Square), reads=[Bdt], writes=[Bs2]); yield
            b2 = nbank(2, 8)
            mm_group(ps[b2][:, :], [(bo64, sq2, [tb1, Bs2])], Bps[b2]); yield
            rsqrt_act(rstd, ps[b2][:, :], 1.0, GN_EPS, [Bps[b2]], [Brs]); yield
            Sc.op("dve", lambda: V.tensor_tensor(out=yn, in0=dtmp, in1=rstd, op=ALU.mult), reads=[Bdt, Brs], writes=[Byn]); yield
            Sc.op("dve", lambda: V.tensor_scalar(out=yn, in0=yn, scalar1=prm[:, LW + p:LW + p + 1], scalar2=prm[:, LB + p:LB + p + 1], op0=ALU.mult, op1=ALU.add),
                  reads=[Byn, Bprm], writes=[Byn]); yield
            Sc.op("pool", lambda: P_.tensor_tensor(out=yn, in0=yn, in1=bon[:, sl], op=ALU.add), reads=[Byn, Bbon], writes=[Byn]); yield
            Sc.op("dve", lambda: V.tensor_tensor(out=yc, in0=yn, in1=gatec, op=ALU.mult), reads=[Byn, Bgatec], writes=[byc]); yield
            Sc.dma("sp", ybuf[768 + p * 128:768 + (p + 1) * 128, sl], yc, reads=[byc]); yield

    def preps(p, j):
        return [unit_prep(p, 0, j, j % 2), unit_prep(p, 1, 3 - j, j % 2)]

    Sc.barrier()
    run(prologue(PAIRS[0]))
    run(vtok_build(PAIRS[0]))
    resets(PAIRS[0])
    run(rr(preps(PAIRS[0], 0)))
    for i, p in enumerate(PAIRS):
        nxt = PAIRS[i + 1] if i + 1 < len(PAIRS) else None
        for j in range(4):
            gens = [chain(p, 0, j, j % 2), chain(p, 1, 3 - j, j % 2)]
            if j < 3:
                gens += preps(p, j + 1)
            elif nxt is not None:
                gens.append(prologue(nxt))
            run(rr(gens))
        gens = [epilogue(p)]
        if nxt is not None:
            gens.append(vtok_build(nxt))
        run(rr(gens))
        if nxt is not None:
            resets(nxt)
            run(rr(preps(nxt, 0)))
    Sc.barrier()
    if stop_after.startswith("rwkv"):
        return finish_debug(nc, Sc, locals())

    AR.release(m_persist)
    cosr = AR.f32(16, 32); sinr = AR.f32(16, 32); gqk = AR.f32(16, 64); esink = AR.f32(12); mLR = AR.bf16(2, 128)
    Bcs, Bsn, Bgq, Bes, Bml = Buf(), Buf(), Buf(), Buf(), Buf()
    Sc.dma("sp", cosr, cos_d[:, :, :], writes=[Bcs]); Sc.dma("sp", sinr, sin_d[:, :, :], writes=[Bsn])
    Sc.dma("sp", gqk, gqk_d.rearrange("p (a b) -> p a b", b=64), writes=[Bgq]); Sc.dma("sp", esink, sink_d[:, :], writes=[Bes])
    Sc.dma("pool", mLR, maskLR_d[:, :, :], writes=[Bml])
    Sc.op("act", lambda: A_.activation(out=esink, in_=esink, func=AF.Exp), reads=[Bes], writes=[Bes])
    qTa = AR.bf16(12, S); kTa = AR.bf16(4, S); vext = AR.bf16(16, 4, 128); ya = AR.bf16(6, S)
    BqTa = [Buf() for _ in range(16)]; BkTa = [Buf() for _ in range(16)]; Bvx = [Buf() for _ in range(16)]; Bya = [Buf() for _ in range(16)]
    Sc.op("pool", lambda: P_.memset(vext, 1.0), writes=Bvx)
    qkv = [AR.f32(NQKV) for _ in range(2)]; Bqkv2 = [Buf(), Buf()]
    gta = [AR.f32(6, 128) for _ in range(6)]; Bgta = [Buf() for _ in range(6)]
    sqa2 = [AR.f32(1024) for _ in range(2)]; ssa2 = [AR.f32(16) for _ in range(2)]; rsa2 = [AR.f32(16) for _ in range(2)]
    qna2 = [AR.f32(16, 64) for _ in range(2)]; qra2 = [AR.bf16(16, 64) for _ in range(2)]
    rt2 = [[AR.f32(16, 32) for _ in range(4)] for _ in range(2)]
    Bsqa2, Bssa2, Bqna2, Bqra2 = [Buf(), Buf()], [Buf(), Buf()], [Buf(), Buf()], [Buf(), Buf()]
    Brt2 = [[Buf() for _ in range(4)] for _ in range(2)]
    pTa = [AR.bf16(384) for _ in range(6)]; BpTa = [Buf() for _ in range(6)]
    rda = AR.f32(4, 128); Brda = Buf(); yodd = AR.f32(2, 128); Byodd = Buf(); yev = AR.f32(2, 128); Byev = Buf()
    Bo = [Buf("o5"), Buf("o6"), Buf("o7")]
    pta_rr = [0]
    ag_src = proj_f[AG0:AG0 + 768, :].rearrange("(c p) t -> p c t", p=128)

    def prep_blk(tb):
        qk_ = qkv[tb % 2]; bqk = Bqkv2[tb % 2]
        sqa, ssa, rsa, qna, qra, rt_ = sqa2[tb % 2], ssa2[tb % 2], rsa2[tb % 2], qna2[tb % 2], qra2[tb % 2], rt2[tb % 2]
        Bsqa, Bssa, Bqna, Bqra, Brt = Bsqa2[tb % 2], Bssa2[tb % 2], Bqna2[tb % 2], Bqra2[tb % 2], Brt2[tb % 2]
        Sc.dma("sp", qk_, qkv_t[tb * 128:(tb + 1) * 128, :], writes=[bqk])
        Sc.dma("sp", gta[tb % 6], ag_src[:, :, tb * 128:(tb + 1) * 128], writes=[Bgta[tb % 6]])
        Sc.op("act", lambda: A_.activation(out=sqa, in_=qk_[:, 0:1024], func=AF.Square), reads=[bqk], writes=[Bsqa]); yield
        Sc.op("dve", lambda: V.tensor_reduce(out=ssa, in_=sqa.rearrange("p (a b) -> p a b", b=64), axis=AX.X, op=ALU.add), reads=[Bsqa], writes=[Bssa]); yield
        rsqrt_act(rsa, ssa, 1.0 / 64, EPS, [Bssa], [Bssa]); yield
        Sc.op("dve", lambda: V.tensor_tensor(out=qna, in0=qk_[:, 0:1024].rearrange("p (a b) -> p a b", b=64), in1=rsa.unsqueeze(2).to_broadcast([128, 16, 64]), op=ALU.mult),
              reads=[bqk, Bssa], writes=[Bqna]); yield
        Sc.op("pool", lambda: P_.tensor_tensor(out=qna, in0=qna, in1=gqk, op=ALU.mult), reads=[Bqna, Bgq], writes=[Bqna]); yield
        t1 = qna[:, :, 0:32]; t2 = qna[:, :, 32:64]
        cb = cosr[:, tb, :].unsqueeze(1).to_broadcast([128, 16, 32]); sb_ = sinr[:, tb, :].unsqueeze(1).to_broadcast([128, 16, 32])
        Sc.op("dve", lambda: V.tensor_tensor(out=rt_[0], in0=t1, in1=cb, op=ALU.mult), reads=[Bqna, Bcs], writes=[Brt[0]]); yield
        Sc.op("pool", lambda: P_.tensor_tensor(out=rt_[1], in0=t2, in1=sb_, op=ALU.mult), reads=[Bqna, Bsn], writes=[Brt[1]]); yield
        Sc.op("dve", lambda: V.tensor_tensor(out=qra[:, :, 0:32], in0=rt_[0], in1=rt_[1], op=ALU.subtract), reads=[Brt[0], Brt[1]], writes=[Bqra]); yield
        Sc.op("pool", lambda: P_.tensor_tensor(out=rt_[2], in0=t2, in1=cb, op=ALU.mult), reads=[Bqna, Bcs], writes=[Brt[2]]); yield
        Sc.op("dve", lambda: V.tensor_tensor(out=rt_[3], in0=t1, in1=sb_, op=ALU.mult), reads=[Bqna, Bsn], writes=[Brt[3]]); yield
        Sc.op("dve", lambda: V.tensor_tensor(out=qra[:, :, 32:64], in0=rt_[2], in1=rt_[3], op=ALU.add), reads=[Brt[2], Brt[3]], writes=[Bqra]); yield
        Sc.op("act", lambda: A_.activation(out=vext[:, tb, :, 0:64], in_=qk_[:, 1024:1280].rearrange("p (a b) -> p a b", b=64), func=AF.Copy), reads=[bqk], writes=[Bvx[tb]]); yield
        b = [0, 6][tb % 2]
        qflat = qra.rearrange("p a b -> p (a b)")
        for j in range(8):
            Sc.op("pe", lambda: T_.transpose(out=psb[b][:, j * 128:(j + 1) * 128], in_=qflat[:, j * 128:(j + 1) * 128], identity=identb),
                  reads=[Bqra, Bid], writes=[Bps[b]] if j == 0 else [], inc=(j == 7), setw=[Bps[b]] if j == 7 else [])
        yield
        psv = psb[b].rearrange("p (a b) -> p a b", b=128)
        tsl = slice(tb * 128, (tb + 1) * 128)
        qv = qTa.rearrange("p (h two) t -> p h two t", two=2)
        kv = kTa.rearrange("p (h two) t -> p h two t", two=2)
        Sc.op("dve", lambda: V.tensor_copy(out=qv[0:64, :, 0, tsl], in_=psv[0:64, 0:6, :]), reads=[Bps[b]], writes=[BqTa[tb]]); yield
        Sc.op("act", lambda: A_.activation(out=qv[0:64, :, 1, tsl], in_=psv[64:128, 0:6, :], func=AF.Copy), reads=[Bps[b]], writes=[BqTa[tb]]); yield
        Sc.op("dve", lambda: V.tensor_copy(out=kv[0:64, :, 0, tsl], in_=psv[0:64, 6:8, :]), reads=[Bps[b]], writes=[BkTa[tb]]); yield
        Sc.op("act", lambda: A_.activation(out=kv[0:64, :, 1, tsl], in_=psv[64:128, 6:8, :], func=AF.Copy), reads=[Bps[b]], writes=[BkTa[tb]]); yield

    def attend(n):
        qsl = slice(n * 128, (n + 1) * 128)
        gt_ = gta[n % 6]; bgt = Bgta[n % 6]
        seq = []
        for g in range(4):
            kbs = [kb for kb in (n - 1, n, n + 1) if 0 <= kb < 16]
            for kb in kbs:
                seq.append((g, kb, kb == kbs[0], kb == kbs[-1]))
        order = []
        for (g, kb, fst, lst) in seq:
            for hh in range(3):
                order.append((g, kb, hh, fst, lst))
        firsts = {}; lasts = {}
        for idx, (g, kb, hh, fst, lst) in enumerate(order):
            ob = (3 * g + hh) // 4
            firsts.setdefault(ob, idx); lasts[ob] = idx
        idx = 0
        LOOK = 2
        issued = {}

        def score(i):
            g, kb, fst, lst = seq[i]
            b = sbanks[pta_rr[0] % len(sbanks)]
            pi = pta_rr[0] % 6; pta_rr[0] += 1
            mm_group(ps[b][:, 0:384], [(kTa[0:64, g, kb * 128:(kb + 1) * 128], qTa[0:64, 3 * g:3 * g + 3, qsl], [BkTa[kb], BqTa[n]])], Bps[b])
            issued[i] = (b, pi)

        for i in range(min(LOOK, len(seq))):
            score(i)
        yield
        for i, (g, kb, fst, lst) in enumerate(seq):
            if i + LOOK < len(seq):
                score(i + LOOK)
                yield
            b, pi = issued.pop(i)
            Sc.op("act", lambda: A_.activation(out=pTa[pi], in_=ps[b][:, 0:384], func=AF.Exp, scale=0.125), reads=[Bps[b]], writes=[BpTa[pi]]); yield
            if kb != n:
                mk = mLR[:, 0 if kb < n else 1, :].unsqueeze(1).to_broadcast([128, 3, 128])
                Sc.op("dve", lambda: V.tensor_tensor(out=pTa[pi].rearrange("p (a b) -> p a b", b=128), in0=pTa[pi].rearrange("p (a b) -> p a b", b=128), in1=mk, op=ALU.mult),
                      reads=[BpTa[pi], Bml], writes=[BpTa[pi]]); yield
            for hh in range(3):
                head = 3 * g + hh
                ob = head // 4
                col = (head % 4) * 128
                isf = firsts[ob] == idx; isl = lasts[ob] == idx
                st_flag = fst and (hh == 0 or head % 4 == 0)
                Sc.op("pe", lambda: T_.matmul(ps[obanks[ob]][:, col:col + 128], lhsT=vext[:, kb, g, :], rhs=pTa[pi][:, hh * 128:(hh + 1) * 128], start=st_flag, stop=lst),
                      reads=[Bvx[kb], BpTa[pi]], writes=[Bo[ob]] if isf else [], inc=(hh == 2), setw=[Bo[ob]] if isl else [])
                idx += 1
            yield
        for ob in range(3):
            pv = ps[obanks[ob]][:, :].rearrange("p (a b) -> p a b", b=128)
            Sc.op("dve", lambda: V.tensor_tensor(out=rda[0:64, :, :], in0=pv[64:128, :, :], in1=esink[64:128, ob * 4:(ob + 1) * 4].unsqueeze(2).to_broadcast([64, 4, 128]), op=ALU.add),
                  reads=[Bo[ob], Bes], writes=[Brda]); yield
            Sc.op("act", lambda: A_.activation(out=rda[0:64, :, :], in_=rda[0:64, :, :], func=AF.Ln), reads=[Brda], writes=[Brda]); yield
            Sc.op("act", lambda: A_.activation(out=rda[0:64, :, :], in_=rda[0:64, :, :], func=AF.Exp, scale=-1.0), reads=[Brda], writes=[Brda]); yield
            pv2 = pv.rearrange("p (h two) t -> p h two t", two=2)
            rd2 = rda.rearrange("p (h two) t -> p h two t", two=2)
            c0 = ob * 2
            Sc.op("dve", lambda: V.tensor_tensor(out=yev[0:64, :, :], in0=pv2[0:64, :, 0, :], in1=rd2[0:64, :, 0, :], op=ALU.mult), reads=[Bo[ob], Brda], writes=[Byev]); yield
            Sc.op("dve", lambda: V.tensor_tensor(out=yodd[64:128, :, :], in0=pv2[0:64, :, 1, :], in1=rd2[0:64, :, 1, :], op=ALU.mult), reads=[Bo[ob], Brda], writes=[Byodd]); yield
            Sc.op("pool", lambda: P_.tensor_tensor(out=ya[0:64, c0:c0 + 2, qsl], in0=yev[0:64, :, :], in1=gt_[0:64, c0:c0 + 2, :], op=ALU.mult), reads=[Byev, bgt], writes=[Bya[n]]); yield
            Sc.op("pool", lambda: P_.tensor_tensor(out=ya[64:128, c0:c0 + 2, qsl], in0=yodd[64:128, :, :], in1=gt_[64:128, c0:c0 + 2, :], op=ALU.mult), reads=[Byodd, bgt], writes=[Bya[n]]); yield

    sbanks = [1, 2, 7]
    obanks = [3, 4, 5]

    def seq_(gs):
        for g in gs:
            yield from g

    def attn_driver():
        for w in range(8):
            gens = [prep_blk(2 * w), prep_blk(2 * w + 1)]
            att = [attend(n) for n in (2 * w - 3, 2 * w - 2) if n >= 0]
            if att:
                gens.append(seq_(att))
            yield from rr(gens, [1, 1, 2])
        yield from attend(13)
        yield from attend(14)
        yield from attend(15)
        for c in range(6):
            Sc.dma("sp", ybuf[c * 128:(c + 1) * 128, :], ya[:, c, :], reads=Bya); yield

    run(attn_driver())
    Sc.barrier()
    if stop_after in ("attn", "xattn"):
        return finish_debug(nc, Sc, locals())

    AR.release(m_persist)
    mT = AR.bf16(16, S); BmT = [Buf() for _ in range(16)]
    m_p3 = AR.mark()
    yall = AR.bf16(16, S); Byall = [Buf() for _ in range(16)]
    wo = [AR.bf16(16, 256) for _ in range(2)]; Bwo = [[Buf(), Buf(), Buf()] for _ in range(2)]
    gt3 = [AR.f32(S) for _ in range(2)]; Bgt3 = [Buf(), Buf()]
    macc = AR.f32(S); Bmacc = [Buf() for _ in range(4)]
    ptmp = [AR.f32(512) for _ in range(2)]; Bptmp = [Buf(), Buf()]
    for k in range(16):
        Sc.dma("sp", yall[:, k, :], ybuf[k * 128:(k + 1) * 128, :], writes=[Byall[k]])
    wsrcs = [(attn_w_o.rearrange("(k p) n -> p k n", p=128), 0, 6), (rwkv_w_o.rearrange("(k p) n -> p k n", p=128), 6, 6), (x_w_o.rearrange("(k p) n -> p k n", p=128), 12, 4)]
    kranges = [range(0, 6), range(6, 12), range(12, 16)]

    def load_wo(fg):
        for bi, (src, k0, nk) in enumerate(wsrcs):
            Sc.dma("pool", wo[fg % 2][:, k0:k0 + nk, :], src[:, :, fg * 256:(fg + 1) * 256], writes=[Bwo[fg % 2][bi]])

    g3_rr = [0]; pt_rr = [0]
    load_wo(0)
    for fg in range(8):
        if fg + 1 < 8:
            load_wo(fg + 1)
        for fi in range(2):
            f = fg * 2 + fi
            for bi in range(3):
                gi = g3_rr[0] % 2; g3_rr[0] += 1
                r0 = MG0 + bi * 2048 + f * 128
                Sc.dma("sp", gt3[gi], proj_f[r0:r0 + 128, :], writes=[Bgt3[gi]])
                for tc in range(4):
                    sl = slice(tc * 512, (tc + 1) * 512)
                    bk = nbank()
                    mm_group(ps[bk][:, :], [(wo[fg % 2][:, kc, fi * 128:(fi + 1) * 128], yall[:, kc, sl], [Bwo[fg % 2][bi], Byall[kc]]) for kc in kranges[bi]], Bps[bk])
                    if bi == 0:
                        Sc.op("dve", lambda: V.tensor_tensor(out=macc[:, sl], in0=ps[bk][:, :], in1=gt3[gi][:, sl], op=ALU.mult), reads=[Bps[bk], Bgt3[gi]], writes=[Bmacc[tc]])
                    else:
                        pi = pt_rr[0] % 2; pt_rr[0] += 1
                        Sc.op("dve", lambda: V.tensor_tensor(out=ptmp[pi], in0=ps[bk][:, :], in1=gt3[gi][:, sl], op=ALU.mult), reads=[Bps[bk], Bgt3[gi]], writes=[Bptmp[pi]])
                        if bi == 1:
                            Sc.op("pool", lambda: P_.tensor_tensor(out=macc[:, sl], in0=macc[:, sl], in1=ptmp[pi], op=ALU.add), reads=[Bmacc[tc], Bptmp[pi]], writes=[Bmacc[tc]])
                        else:
                            Sc.op("pool", lambda: P_.tensor_tensor(out=mT[:, f, sl], in0=macc[:, sl], in1=ptmp[pi], op=ALU.add), reads=[Bmacc[tc], Bptmp[pi]], writes=[BmT[f]])
    dump("d_mT", mT, [128, 16, S], BF16, BmT)
    Sc.barrier()
    if stop_after == "merge":
        return finish_debug(nc, Sc, locals())
    AR.release(m_p3)
    wout = [AR.bf16(16, 512) for _ in range(2)]; Bwout = [[Buf() for _ in range(4)] for _ in range(2)]
    xres = [AR.f32(512) for _ in range(3)]; Bxres = [Buf() for _ in range(3)]
    ost = [AR.f32(512) for _ in range(3)]; Bost = [Buf() for _ in range(3)]
    wo_src = w_out.rearrange("(k p) n -> p k n", p=128)

    def load_wout(ng):
        for q in range(4):
            Sc.dma("pool", wout[ng % 2][:, q * 4:(q + 1) * 4, :], wo_src[:, q * 4:(q + 1) * 4, ng * 512:(ng + 1) * 512], writes=[Bwout[ng % 2][q]])

    load_wout(0)
    xr_rr = [0]
    final_toks = []
    for ng in range(4):
        if ng + 1 < 4:
            load_wout(ng + 1)
        for tb in range(16):
            xi = xr_rr[0] % 3; xr_rr[0] += 1
            Sc.dma("sp", xres[xi], x[tb * 128:(tb + 1) * 128, ng * 512:(ng + 1) * 512], writes=[Bxres[xi]])
            bk = nbank()
            mm_group(ps[bk][:, :], [(mT[:, f, tb * 128:(tb + 1) * 128], wout[ng % 2][:, f, :], [BmT[f], Bwout[ng % 2][f // 4]]) for f in range(16)], Bps[bk])
            Sc.op("dve", lambda: V.tensor_tensor(out=ost[xi], in0=ps[bk][:, :], in1=xres[xi], op=ALU.add), reads=[Bps[bk], Bxres[xi]], writes=[Bost[xi]])
            final_toks.append(Sc.dma("sp", out[tb * 128:(tb + 1) * 128, ng * 512:(ng + 1) * 512], ost[xi], reads=[Bost[xi]]))
    return finish_debug(nc, Sc, locals())


def finish_debug(nc, Sc, env):
    Sc.barrier()
    ok, stuck, _ = Sc.check_deadlock()
    if not ok:
        raise RuntimeError("logical deadlock in emitted program: %r" % (stuck,))
    Sc.close()
    for cm in reversed(env["ps_cms"]):
        cm.__exit__(None, None, None)
    env["big_cm"].__exit__(None, None, None)
    return nc


def make_in_maps(inputs):
    c = host_consts()
    f = lambda k: np.asarray(inputs[k], dtype=np.float32)
    sq = lambda k: f(k)[0]
    prm = np.zeros((128, NPRM), np.float32)

    def put(col, vec):
        m = vec.size // 128
        prm[:, col:col + m] = vec.reshape(m, 128).T

    put(NG, sq("norm_g")); put(MG_, sq("mem_norm_g")); put(GB, sq("gate_b")); put(MU, sq("rwkv_mu"))
    put(KK, sq("rwkv_k_k")); put(KA, sq("rwkv_k_a")); put(RK, sq("rwkv_r_k").reshape(-1)); put(LW, sq("rwkv_ln_w"))
    put(LB, sq("rwkv_ln_b")); put(W0, sq("rwkv_w0").reshape(-1)); put(A0, sq("rwkv_a0").reshape(-1))
    put(XQG, sq("x_q_norm_g")); put(XKG, sq("x_k_norm_g"))
    gqk = np.concatenate([np.tile(sq("attn_q_norm_g"), 12), np.tile(sq("attn_k_norm_g"), 4)])
    shared = {
        "w_in": sq("w_in"), "attn_w_o": sq("attn_w_o"), "rwkv_w_o": sq("rwkv_w_o"), "x_w_o": sq("x_w_o"),
        "x_w_kv": sq("x_w_kv"), "w_out": sq("w_out"),
        "w2cat": np.ascontiguousarray(sq("rwkv_w2").reshape(128, 768)), "a2cat": np.ascontiguousarray(sq("rwkv_a2").reshape(128, 768)),
        "prm": prm, "gqk": np.ascontiguousarray(np.broadcast_to(gqk[None, :], (128, 1024))),
        "sinkb": np.ascontiguousarray(np.broadcast_to(sq("attn_sink")[None, :], (128, 12))),
    }
    shared.update(c)
    xs = f("x"); ms = f("mem")
    return [dict(shared, x=np.ascontiguousarray(xs[b]), mem=np.ascontiguousarray(ms[b])) for b in range(xs.shape[0])]


_NC_CACHE = {}


def kernel(**inputs):
    in_maps = make_in_maps(inputs)
    if "nc" not in _NC_CACHE:
        _NC_CACHE["nc"] = build_nc()
    nc = _NC_CACHE["nc"]
    res = run_bass_kernel_spmd(nc, in_maps, core_ids=list(range(len(in_maps))))
    return np.stack([np.asarray(r["out"], dtype=np.float32) for r in res.results], axis=0)
```

```python
import math
import numpy as np
import ml_dtypes
import concourse.bass as bass
import concourse.mybir as mybir
from concourse.bass_utils import run_bass_kernel_spmd

F32 = mybir.dt.float32
BF16 = mybir.dt.bfloat16
AF = mybir.ActivationFunctionType
ALU = mybir.AluOpType
AX = mybir.AxisListType

S = 2048
D = 2048
NMEM = 256
INW = 12544
NQKV = 1280
NPF = INW - NQKV
EPS = 1e-6
GN_EPS = 64e-5
C1 = -0.5 * math.exp(-0.5)

AG0 = 0
R0 = 2048 - NQKV
K0 = R0 + 768
V0 = K0 + 768
LW0 = V0 + 768
LA0 = LW0 + 128
RG0 = 4608 - NQKV
XQ0 = 5376 - NQKV
XG0 = 5888 - NQKV
MG0 = 6400 - NQKV

NG, MG_, GB, MU, KK, KA, RK, LW, LB, W0, A0, XQG, XKG = 0, 16, 32, 80, 100, 106, 112, 118, 124, 130, 142, 154, 155
OMM, HMU, OMK, HW0, HA0, XG2 = 156, 176, 196, 202, 214, 226
NPRM = 228


class Buf:
    __slots__ = ("name", "w", "r", "pending")

    def __init__(self, name=""):
        self.name = name
        self.w = None
        self.r = {}
        self.pending = False


class Sched:
    ENG = ("pe", "act", "dve", "pool", "sp")

    def __init__(self, nc, n_dma_sems=40):
        self.nc = nc
        self.eng = {"pe": nc.tensor, "act": nc.scalar, "dve": nc.vector, "pool": nc.gpsimd, "sp": nc.sync}
        self.sem = {}
        self.cnt = {e: 0 for e in self.ENG}
        self.known = {e: {} for e in self.ENG}
        self._cms = []
        for e in self.ENG:
            cm = nc.semaphore("s_" + e)
            self.sem[e] = cm.__enter__()
            self._cms.append(cm)
        self.dsem = []
        for i in range(n_dma_sems):
            cm = nc.semaphore("d%d" % i)
            self.dsem.append([cm.__enter__(), 0])
            self._cms.append(cm)
        self.dnext = 0
        self.nwait = 0
        self.log = {e: [] for e in self.ENG}

    def close(self):
        for cm in reversed(self._cms):
            cm.__exit__(None, None, None)

    def _wait(self, e, key, semh, val):
        k = self.known[e]
        if k.get(key, 0) >= val:
            return
        self.eng[e].wait_ge(semh, val)
        self.nwait += 1
        self.log[e].append(("w", key, val))
        k[key] = val

    def wait_tok(self, e, tok):
        if tok is None:
            return
        kind, a, v = tok
        if kind == "eng":
            self._wait(e, a, self.sem[a], v)
        else:
            self._wait(e, "d%d" % a, self.dsem[a][0], v)

    def deps(self, e, reads, writes):
        for b in reads:
            self.wait_tok(e, b.w)
        for b in writes:
            self.wait_tok(e, b.w)
            for tok in b.r.values():
                self.wait_tok(e, tok)

    def op(self, e, fn, reads=(), writes=(), inc=True, setw=None):
        self.deps(e, reads, writes)
        ins = fn()
        if inc:
            self.cnt[e] += 1
            ins.then_inc(self.sem[e], 1)
            self.log[e].append(("i", e, 1))
            tok = ("eng", e, self.cnt[e])
        else:
            tok = ("eng", e, self.cnt[e] + 1)
        for b in reads:
            b.r[("eng", e)] = tok
            b.pending = False
        for b in (writes if setw is None else setw):
            b.w = tok
            b.r = {}
            b.pending = True
        for b in writes:
            b.pending = True
        return ins

    def dma(self, e, out, in_, reads=(), writes=()):
        idx = self.dnext
        self.dnext = (self.dnext + 1) % len(self.dsem)
        semh, val = self.dsem[idx]
        if val > 0:
            self._wait(e, "d%d" % idx, semh, val)
        self.deps(e, reads, writes)
        ins = self.eng[e].dma_start(out=out, in_=in_)
        val += 16
        ins.then_inc(semh, 16)
        self.log[e].append(("i", "d%d" % idx, 16))
        self.dsem[idx][1] = val
        tok = ("dma", idx, val)
        for b in reads:
            b.r[("dma", idx)] = tok
        for b in writes:
            b.w = tok
            b.r = {}
        return tok

    def check_deadlock(self):
        sem = {}
        pos = {e: 0 for e in self.ENG}
        prog = True
        while prog:
            prog = False
            for e in self.ENG:
                lg = self.log[e]
                while pos[e] < len(lg):
                    kind, key, val = lg[pos[e]]
                    if kind == "w":
                        if sem.get(key, 0) < val:
                            break
                    else:
                        sem[key] = sem.get(key, 0) + val
                    pos[e] += 1
                    prog = True
        stuck = {e: (pos[e], len(self.log[e]), self.log[e][pos[e]] if pos[e] < len(self.log[e]) else None) for e in self.ENG}
        ok = all(pos[e] == len(self.log[e]) for e in self.ENG)
        return ok, stuck, sem

    def barrier(self):
        for e in self.ENG:
            for e2 in self.ENG:
                if e2 != e and self.cnt[e2] > 0:
                    self._wait(e, e2, self.sem[e2], self.cnt[e2])
            for idx, (semh, val) in enumerate(self.dsem):
                if val > 0:
                    self._wait(e, "d%d" % idx, semh, val)


class Arena:
    def __init__(self, big, n):
        self.big = big
        self.n = n
        self.off = 0

    def mark(self):
        return self.off

    def release(self, m):
        self.off = m

    def _raw(self, nf32):
        a = self.off
        self.off += nf32
        assert self.off <= self.n, "SBUF arena overflow %d > %d" % (self.off, self.n)
        return self.big[:, a:a + nf32]

    @staticmethod
    def _shape(ap, dims):
        if len(dims) == 1:
            return ap
        if len(dims) == 2:
            return ap.rearrange("p (a b) -> p a b", b=dims[1])
        if len(dims) == 3:
            return ap.rearrange("p (a b c) -> p a b c", b=dims[1], c=dims[2])
        raise ValueError

    def f32(self, *dims):
        n = int(np.prod(dims))
        return self._shape(self._raw(n), dims)

    def bf16(self, *dims):
        n = int(np.prod(dims))
        assert n % 2 == 0
        return self._shape(self._raw(n // 2).bitcast(BF16), dims)


def host_consts():
    c = {}
    c["identf"] = np.eye(128, dtype=np.float32)
    bo = np.zeros((128, 128), np.float32)
    bo[:64, :64] = 1.0
    bo[64:, 64:] = 1.0
    c["bones"] = bo
    c["bo64"] = bo / 64.0
    half = 32
    inv = (10000.0 ** (-np.arange(half, dtype=np.float64) / half))
    ang = np.arange(S, dtype=np.float64)[:, None] * inv[None, :]
    c["cosr"] = np.ascontiguousarray(np.cos(ang).reshape(16, 128, 32).transpose(1, 0, 2)).astype(np.float32)
    c["sinr"] = np.ascontiguousarray(np.sin(ang).reshape(16, 128, 32).transpose(1, 0, 2)).astype(np.float32)
    j = np.arange(128)[:, None]
    i = np.arange(128)[None, :]
    c["maskLR"] = np.stack([(j >= i), (j <= i)], axis=1).astype(np.float32)
    t = np.arange(64)
    st_f = (t[:, None] < t[None, :]).astype(np.float32)
    in_f = (t[:, None] <= t[None, :]).astype(np.float32)
    def bd(m):
        z = np.zeros((128, 128), np.float32)
        z[:64, :64] = m
        z[64:, 64:] = m
        return z
    rwm = np.zeros((128, 2, 3, 128), np.float32)
    rwm[:, 0, 0] = bd(st_f); rwm[:, 0, 1] = bd(in_f); rwm[:, 0, 2] = bd(st_f.T)
    rwm[:, 1, 0] = bd(st_f.T); rwm[:, 1, 1] = bd(in_f.T); rwm[:, 1, 2] = bd(st_f)
    c["rwm"] = rwm
    seg = np.ones((128, 512), np.float32)
    seg[:, ::64] = 0.0
    c["segm"] = seg
    return c


def build_nc(stop_after="all", debug=()):
    nc = bass.Bass("TRN2", target_bir_lowering=False)

    def din(name, shape):
        return nc.dram_tensor(name, list(shape), F32, kind="ExternalInput").ap()

    x = din("x", [S, D]); mem = din("mem", [NMEM, D]); w_in = din("w_in", [D, INW])
    attn_w_o = din("attn_w_o", [768, D]); rwkv_w_o = din("rwkv_w_o", [768, D]); x_w_o = din("x_w_o", [512, D])
    x_w_kv = din("x_w_kv", [D, 1024]); w_out = din("w_out", [D, D])
    w2cat = din("w2cat", [128, 768]); a2cat = din("a2cat", [128, 768])
    prm_d = din("prm", [128, NPRM]); gqk_d = din("gqk", [128, 1024]); sink_d = din("sinkb", [128, 12])
    identf_d = din("identf", [128, 128]); bones_d = din("bones", [128, 128]); bo64_d = din("bo64", [128, 128])
    cos_d = din("cosr", [128, 16, 32]); sin_d = din("sinr", [128, 16, 32]); maskLR_d = din("maskLR", [128, 2, 128])
    rwm_d = din("rwm", [128, 2, 3, 128]); segm_d = din("segm", [128, 512])
    out = nc.dram_tensor("out", [S, D], F32, kind="ExternalOutput").ap()

    def dscr(name, shape, dt):
        kind = "ExternalOutput" if name in debug else "Internal"
        return nc.dram_tensor(name, list(shape), dt, kind=kind).ap()

    proj_f = dscr("proj_f", [NPF, S], F32)
    qkv_t = dscr("qkv_t", [S, NQKV], F32)
    ybuf = dscr("ybuf", [2048, S], BF16)
    dbg = dscr("dbg", [128, 4096], F32) if "dbg" in debug else None

    NBIG = 52600
    big_cm = nc.sbuf_tensor("big", [128, NBIG], F32)
    big = big_cm.__enter__()
    ps_cms = [nc.psum_tensor("ps%d" % i, [128, 512], F32) for i in range(8)]
    ps = [cm.__enter__() for cm in ps_cms]
    Bps = [Buf("ps%d" % i) for i in range(8)]
    psb = [p[:, :].bitcast(BF16) for p in ps]
    Sc = Sched(nc)
    AR = Arena(big, NBIG)
    V, A_, P_, G_, T_ = nc.vector, nc.scalar, nc.gpsimd, nc.sync, nc.tensor
    bank_rr = [0]
    dumped = set()

    def dump(name, sb_ap, shape, dt, reads):
        if name in debug and name not in dumped:
            dumped.add(name)
            t = nc.dram_tensor(name, list(shape), dt, kind="ExternalOutput").ap()
            Sc.dma("sp", t, sb_ap, reads=reads)

    def nbank(lo=0, hi=8):
        for _ in range(hi - lo):
            b = lo + bank_rr[0] % (hi - lo)
            bank_rr[0] += 1
            if not Bps[b].pending:
                return b
        raise RuntimeError("all PSUM banks in [%d,%d) hold unconsumed data" % (lo, hi))

    def mm_group(out_ap, items, obuf):
        n = len(items)
        for i, (l, r, rd) in enumerate(items):
            first, last = i == 0, i == n - 1
            Sc.op("pe", lambda: T_.matmul(out_ap, lhsT=l, rhs=r, start=first, stop=last), reads=rd,
                  writes=[obuf] if first else [], inc=last, setw=[obuf] if last else [])

    def mm_multi(items, obuf):
        n = len(items)
        for i, (o, l, r, rd) in enumerate(items):
            first, last = i == 0, i == n - 1
            Sc.op("pe", lambda: T_.matmul(o, lhsT=l, rhs=r, start=True, stop=True), reads=rd,
                  writes=[obuf] if first else [], inc=last, setw=[obuf] if last else [])

    def rsqrt_act(out_ap, in_ap, scale, eps, reads, writes, tmp_ap=None):
        t = out_ap if tmp_ap is None else tmp_ap
        Sc.op("act", lambda: A_.activation(out=t, in_=in_ap, func=AF.Ln, bias=eps, scale=scale), reads=reads, writes=writes)
        Sc.op("act", lambda: A_.activation(out=out_ap, in_=t, func=AF.Exp, scale=-0.5), reads=writes, writes=writes)

    identf = AR.f32(128); identb = AR.bf16(128); prm = AR.f32(NPRM)
    kmT = AR.bf16(4, 256); vm = AR.bf16(2, 512)
    Bid, Bprm, Bkm, Bvm = Buf("id"), Buf("prm"), Buf("kmT"), Buf("vm")
    Sc.dma("sp", identf, identf_d[:, :], writes=[Bid])
    Sc.dma("sp", prm[:, 0:OMM], prm_d[:, 0:OMM], writes=[Bprm])
    Sc.op("dve", lambda: V.tensor_copy(out=identb, in_=identf), reads=[Bid], writes=[Bid])
    Sc.op("dve", lambda: V.tensor_scalar(out=prm[:, OMM:OMM + 20], in0=prm[:, MU:MU + 20], scalar1=-1.0, scalar2=1.0, op0=ALU.mult, op1=ALU.add), reads=[Bprm], writes=[Bprm])
    Sc.op("dve", lambda: V.tensor_scalar(out=prm[:, HMU:HMU + 20], in0=prm[:, MU:MU + 20], scalar1=0.5, scalar2=None, op0=ALU.mult), reads=[Bprm], writes=[Bprm])
    Sc.op("dve", lambda: V.tensor_scalar(out=prm[:, OMK:OMK + 6], in0=prm[:, KA:KA + 6], scalar1=-1.0, scalar2=1.0, op0=ALU.mult, op1=ALU.add), reads=[Bprm], writes=[Bprm])
    Sc.op("dve", lambda: V.tensor_scalar(out=prm[:, HW0:HW0 + 24], in0=prm[:, W0:W0 + 24], scalar1=0.5, scalar2=None, op0=ALU.mult), reads=[Bprm], writes=[Bprm])
    Sc.op("dve", lambda: V.tensor_tensor(out=prm[:, XG2:XG2 + 1], in0=prm[:, XQG:XQG + 1], in1=prm[:, XKG:XKG + 1], op=ALU.mult), reads=[Bprm], writes=[Bprm])
    m_persist = AR.mark()

    hT = AR.bf16(16, S)
    BhT = [Buf("hT%d" % g) for g in range(4)]
    memT = AR.bf16(16, NMEM); BmemT = Buf("memT")
    m_ph0 = AR.mark()
    xbuf = [AR.f32(4, D) for _ in range(2)]
    Bx = [[Buf() for _ in range(4)] for _ in range(2)]
    junk = AR.f32(D); Bjunk = Buf("junk")
    evac_rr = [0]

    def build_T(src, nblk_total, gcol, dstT, dst_bufs):
        ngrp = (nblk_total + 3) // 4
        for g in range(ngrp):
            nb = min(4, nblk_total - g * 4)
            xb, bx = xbuf[g % 2], Bx[g % 2]
            ssq = AR.f32(4); rt = AR.f32(4); Bss = Buf("ss")
            Sc.op("dve", lambda: V.memset(ssq, 0.0), writes=[Bss])
            for i in range(nb):
                r0 = (g * 4 + i) * 128
                Sc.dma("sp", xb[:, i, :], src[r0:r0 + 128, :], writes=[bx[i]])
            for i in range(nb):
                Sc.op("act", lambda: A_.activation(out=junk, in_=xb[:, i, :], func=AF.Square, accum_out=ssq[:, i:i + 1]),
                      reads=[bx[i]], writes=[Bjunk, Bss])
            rsqrt_act(rt[:, 0:nb], ssq[:, 0:nb], 1.0 / D, EPS, [Bss], [Bss])
            for i in range(nb):
                Sc.op("dve", lambda: V.tensor_scalar(out=xb[:, i, :], in0=xb[:, i, :], scalar1=rt[:, i:i + 1], scalar2=None, op0=ALU.mult),
                      reads=[bx[i], Bss], writes=[bx[i]])
            for c in range(16):
                b = nbank()
                for i in range(nb):
                    Sc.op("pe", lambda: T_.transpose(out=ps[b][:, i * 128:(i + 1) * 128], in_=xb[:, i, c * 128:(c + 1) * 128], identity=identf),
                          reads=[bx[i], Bid], writes=[Bps[b]] if i == 0 else [], inc=(i == nb - 1), setw=[Bps[b]] if i == nb - 1 else [])
                dst = dstT[:, c, g * 512:g * 512 + nb * 128]
                gc = prm[:, gcol + c:gcol + c + 1]
                if evac_rr[0] % 2 == 0:
                    Sc.op("act", lambda: A_.activation(out=dst, in_=ps[b][:, 0:nb * 128], func=AF.Copy, scale=gc),
                          reads=[Bps[b], Bprm], writes=[dst_bufs[g]])
                else:
                    Sc.op("dve", lambda: V.tensor_scalar(out=dst, in0=ps[b][:, 0:nb * 128], scalar1=gc, scalar2=None, op0=ALU.mult),
                          reads=[Bps[b], Bprm], writes=[dst_bufs[g]])
                evac_rr[0] += 1

    build_T(mem, 2, MG_, memT, [BmemT])
    build_T(x, 16, NG, hT, BhT)

    Sc.barrier()
    AR.release(m_ph0)
    memT2 = memT
    wkv = AR.bf16(16, 1024); Bwkv = [Buf("wkv%d" % q) for q in range(4)]
    kmn = AR.f32(512); Bkmn = Buf("kmn")
    ssk = AR.f32(4); rk_ = AR.f32(4); Bssk = Buf("ssk")
    junk2 = AR.f32(128); Bjunk2 = Buf("junk2")
    wkv_src = x_w_kv.rearrange("(k p) n -> p k n", p=128)
    for q in range(4):
        Sc.dma("pool", wkv[:, q * 4:(q + 1) * 4, :], wkv_src[:, q * 4:(q + 1) * 4, :], writes=[Bwkv[q]])
    for mb in range(2):
        for half in range(2):
            b = nbank()
            mm_group(ps[b][:, :], [(memT2[:, k, mb * 128:(mb + 1) * 128], wkv[:, k, half * 512:(half + 1) * 512], [BmemT, Bwkv[k // 4]]) for k in range(16)], Bps[b])
            if half == 0:
                Sc.op("dve", lambda: V.memset(ssk, 0.0), writes=[Bssk])
                for h in range(4):
                    Sc.op("act", lambda: A_.activation(out=junk2, in_=ps[b][:, h * 128:(h + 1) * 128], func=AF.Square, accum_out=ssk[:, h:h + 1]),
                          reads=[Bps[b]], writes=[Bjunk2, Bssk])
                rsqrt_act(rk_, ssk, 1.0 / 128, EPS, [Bssk], [Bssk])
                for h in range(4):
                    Sc.op("dve", lambda: V.tensor_scalar(out=kmn[:, h * 128:(h + 1) * 128], in0=ps[b][:, h * 128:(h + 1) * 128], scalar1=rk_[:, h:h + 1], scalar2=None, op0=ALU.mult),
                          reads=[Bps[b], Bssk], writes=[Bkmn])
                b2 = nbank()
                for h in range(4):
                    Sc.op("pe", lambda: T_.transpose(out=ps[b2][:, h * 128:(h + 1) * 128], in_=kmn[:, h * 128:(h + 1) * 128], identity=identf),
                          reads=[Bkmn, Bid], writes=[Bps[b2]] if h == 0 else [], inc=(h == 3), setw=[Bps[b2]] if h == 3 else [])
                Sc.op("dve", lambda: V.tensor_scalar(out=kmT[:, :, mb * 128:(mb + 1) * 128], in0=ps[b2][:, :].rearrange("p (h m) -> p h m", m=128),
                                                     scalar1=prm[:, XG2:XG2 + 1], scalar2=None, op0=ALU.mult),
                      reads=[Bps[b2], Bprm], writes=[Bkm])
            else:
                Sc.op("act", lambda: A_.activation(out=vm[:, mb, :], in_=ps[b][:, :], func=AF.Copy), reads=[Bps[b]], writes=[Bvm])
    dump("d_kmT", kmT, [128, 4, 256], BF16, [Bkm]); dump("d_vm", vm, [128, 2, 512], BF16, [Bvm])
    dump("d_hT", hT, [128, 16, S], BF16, BhT)
    Sc.barrier()
    if stop_after == "hT":
        return finish_debug(nc, Sc, locals())

    AR.release(m_ph0)
    NWB = 3
    wt = [AR.bf16(16, 512) for _ in range(NWB)]
    Bwt = [[Buf("wt%d_%d" % (i, q)) for q in range(4)] for i in range(NWB)]
    stf = [AR.f32(S) for _ in range(2)]; Bstf = [Buf("stf%d" % i) for i in range(2)]
    stt = [AR.f32(512) for _ in range(3)]; Bstt = [Buf("stt%d" % i) for i in range(3)]
    Bproj = [Buf("pf%d" % i) for i in range(NPF // 128)]
    Bqkv = Buf("qkv")
    w_src = w_in.rearrange("(k p) n -> p k n", p=128)
    NT = (INW + 511) // 512

    TORDER = [0, 1, 2, 10, 11, 12] + [t for t in range(NT) if t not in (0, 1, 2, 10, 11, 12)]

    def load_w(pos):
        t = TORDER[pos]
        c0 = t * 512
        ncol = min(512, INW - c0)
        for q in range(4):
            Sc.dma("pool", wt[pos % NWB][:, q * 4:(q + 1) * 4, 0:ncol], w_src[:, q * 4:(q + 1) * 4, c0:c0 + ncol], writes=[Bwt[pos % NWB][q]])

    def act_for(feat):
        if (1280 <= feat < 2048) or (4608 <= feat < 5376) or (5888 <= feat < 6400):
            return "silu"
        if feat >= 6400:
            return "sig"
        return "copy"

    stf_rr = [0]; stt_rr = [0]; ev_rr = [0]
    import os as _os2
    TESTCOPY = bool(_os2.environ.get("TESTCOPY"))
    load_w(0); load_w(1)

    def rr(gens, weights=None):
        act_ = [[g, (weights[i] if weights else 1)] for i, g in enumerate(gens)]
        while act_:
            for ent in list(act_):
                for _ in range(ent[1]):
                    try:
                        next(ent[0])
                        yield
                    except StopIteration:
                        act_.remove(ent)
                        break

    def run(gen):
        for _ in gen:
            pass

    def proj_gen(t_lo, t_hi, bk):
      for pos in range(t_lo, t_hi):
          t = TORDER[pos]
          if pos + 2 < NT:
              load_w(pos + 2)
          c0 = t * 512
          ncol = min(512, INW - c0)
          w = wt[pos % NWB]; bw = Bwt[pos % NWB]
          ntok = max(0, min(ncol, NQKV - c0))
          if ntok > 0:
              for tb in range(16):
                  b = nbank(*bk)
                  mm_group(ps[b][:, 0:ntok], [(hT[:, k, tb * 128:(tb + 1) * 128], w[:, k, 0:ntok], [BhT[tb // 4], bw[k // 4]]) for k in range(16)], Bps[b])
                  si = stt_rr[0] % 3; stt_rr[0] += 1
                  if ev_rr[0] % 2 == 0:
                      Sc.op("act", lambda: A_.activation(out=stt[si][:, 0:ntok], in_=ps[b][:, 0:ntok], func=AF.Copy), reads=[Bps[b]], writes=[Bstt[si]])
                  else:
                      Sc.op("dve", lambda: V.tensor_copy(out=stt[si][:, 0:ntok], in_=ps[b][:, 0:ntok]), reads=[Bps[b]], writes=[Bstt[si]])
                  ev_rr[0] += 1
                  Sc.dma("sp", qkv_t[tb * 128:(tb + 1) * 128, c0:c0 + ntok], stt[si][:, 0:ntok], reads=[Bstt[si]])
                  yield
          for sub in range(ntok // 128, ncol // 128):
              feat = c0 + sub * 128
              fi = (feat - NQKV) // 128
              kind = act_for(feat)
              si = stf_rr[0] % 2; stf_rr[0] += 1
              for tc in range(4):
                  b = nbank(*bk)
                  mm_group(ps[b][:, :], [(w[:, k, sub * 128:(sub + 1) * 128], hT[:, k, tc * 512:(tc + 1) * 512], [bw[k // 4], BhT[tc]]) for k in range(16)], Bps[b])
                  dst = stf[si][:, tc * 512:(tc + 1) * 512]
                  if kind == "silu":
                      Sc.op("act", lambda: A_.activation(out=dst, in_=ps[b][:, :], func=AF.Silu), reads=[Bps[b]], writes=[Bstf[si]])
                  elif kind == "sig":
                      gcol = GB + (feat - 6400) // 128
                      Sc.op("act", lambda: A_.activation(out=dst, in_=ps[b][:, :], func=(AF.Tanh if TESTCOPY else AF.Sigmoid), bias=prm[:, gcol:gcol + 1], scale=1.0),
                            reads=[Bps[b], Bprm], writes=[Bstf[si]])
                  else:
                      if ev_rr[0] % 2 == 0:
                          Sc.op("act", lambda: A_.activation(out=dst, in_=ps[b][:, :], func=AF.Copy), reads=[Bps[b]], writes=[Bstf[si]])
                      else:
                          Sc.op("dve", lambda: V.tensor_copy(out=dst, in_=ps[b][:, :]), reads=[Bps[b]], writes=[Bstf[si]])
                      ev_rr[0] += 1
                  yield
              Sc.dma("sp", proj_f[fi * 128:(fi + 1) * 128, :], stf[si], reads=[Bstf[si]], writes=[Bproj[fi]])

    TSPLIT = 6
    run(proj_gen(0, TSPLIT, (0, 8)))
    onesf = AR.f32(128); onesb = AR.bf16(128); Bones = Buf("ones")
    Sc.op("pool", lambda: P_.memset(onesf, 1.0), writes=[Bones])
    Sc.op("pool", lambda: P_.tensor_copy(out=onesb, in_=onesf), reads=[Bones], writes=[Bones])
    qTc = [AR.f32(S) for _ in range(2)]; gtc = [AR.f32(S)] * 2
    BqTc = [Buf(), Buf()]; Bgtc = [Buf()] * 2
    sqc = AR.f32(512); sc2 = AR.f32(512); qn_c = AR.bf16(512); pTc = [AR.bf16(512) for _ in range(2)]; rden = AR.f32(512); yo = AR.f32(512)
    yxs = AR.bf16(S)
    Bsqc, Bsc2, Bqnc, BpTc, Brden, Byo, Byxs = Buf(), Buf(), Buf(), [Buf(), Buf()], Buf(), Buf(), Buf()

    c_rr = [0]

    def cbank():
        c_rr[0] += 1
        return 6 + c_rr[0] % 2

    def xattn_gen():
        for h in range(4):
            Sc.dma("sp", qTc[h % 2], proj_f[XQ0 + h * 128:XQ0 + (h + 1) * 128, :], reads=[Bproj[XQ0 // 128 + h]], writes=[BqTc[h % 2]])
            Sc.dma("sp", gtc[h % 2], proj_f[XG0 + h * 128:XG0 + (h + 1) * 128, :], reads=[Bproj[XG0 // 128 + h]], writes=[Bgtc[h % 2]])
            q_ = qTc[h % 2]; bq_ = BqTc[h % 2]
            for tc in range(4):
                sl = slice(tc * 512, (tc + 1) * 512)
                Sc.op("act", lambda: A_.activation(out=sqc, in_=q_[:, sl], func=AF.Square), reads=[bq_], writes=[Bsqc]); yield
                yield; yield
                b = cbank()
                mm_group(ps[b][:, :], [(onesf, sqc, [Bones, Bsqc])], Bps[b]); yield
                rsqrt_act(sc2, ps[b][:, :], 1.0, 128.0 * EPS, [Bps[b]], [Bsc2]); yield
                Sc.op("dve", lambda: V.tensor_tensor(out=qn_c, in0=q_[:, sl], in1=sc2, op=ALU.mult), reads=[bq_, Bsc2], writes=[Bqnc]); yield
                for mb in range(2):
                    yield; yield
                    b = cbank()
                    mm_group(ps[b][:, :], [(kmT[:, h, mb * 128:(mb + 1) * 128], qn_c, [Bkm, Bqnc])], Bps[b]); yield
                    Sc.op("act", lambda: A_.activation(out=pTc[mb], in_=ps[b][:, :], func=AF.Exp), reads=[Bps[b]], writes=[BpTc[mb]]); yield
                yield; yield
                bo_ = cbank()
                mm_group(ps[bo_][:, :], [(vm[:, mb, h * 128:(h + 1) * 128], pTc[mb], [Bvm, BpTc[mb]]) for mb in range(2)], Bps[bo_]); yield
                bd_ = cbank()
                mm_group(ps[bd_][:, :], [(onesb, pTc[mb], [Bones, BpTc[mb]]) for mb in range(2)], Bps[bd_]); yield
                Sc.op("act", lambda: A_.activation(out=rden, in_=ps[bd_][:, :], func=AF.Ln), reads=[Bps[bd_]], writes=[Brden]); yield
                Sc.op("act", lambda: A_.activation(out=rden, in_=rden, func=AF.Exp, scale=-1.0), reads=[Brden], writes=[Brden]); yield
                Sc.op("dve", lambda: V.tensor_tensor(out=yo, in0=ps[bo_][:, :], in1=rden, op=ALU.mult), reads=[Bps[bo_], Brden], writes=[Byo]); yield
                Sc.op("pool", lambda: P_.tensor_tensor(out=yxs[:, sl], in0=yo, in1=gtc[h % 2][:, sl], op=ALU.mult), reads=[Byo, Bgtc[h % 2]], writes=[Byxs]); yield
            Sc.dma("sp", ybuf[1536 + h * 128:1536 + (h + 1) * 128, :], yxs, reads=[Byxs]); yield


    run(rr([proj_gen(TSPLIT, NT, (0, 6)), xattn_gen()]))
    Sc.barrier()
    if stop_after == "proj":
        return finish_debug(nc, Sc, locals())

    AR.release(m_persist)
    bones = AR.f32(128); bo64 = AR.f32(128); rwm = AR.bf16(2, 3, 128); segm = AR.f32(512)
    w2b = AR.bf16(768); a2b = AR.bf16(768)
    Bc = Buf("rwconst")
    Sc.dma("sp", bones, bones_d[:, :], writes=[Bc])
    tb1 = Buf(); tb2 = Buf(); tb3 = Buf(); tb4 = Buf(); tb5 = Buf()
    Sc.dma("sp", bo64, bo64_d[:, :], writes=[tb1])
    Sc.dma("sp", segm, segm_d[:, :], writes=[tb2])
    Sc.dma("pool", rwm, rwm_d[:, :, :, :], writes=[tb3])
    Sc.dma("pool", w2b, w2cat[:, :], writes=[tb4])
    Sc.dma("pool", a2b, a2cat[:, :], writes=[tb5])
    lw_t = AR.bf16(S); la_s = AR.bf16(S); Blw = Buf("lw"); Bla = Buf("la")
    r_ = AR.bf16(S); k_ = AR.bf16(S); v_ = AR.bf16(S); kk_ = AR.bf16(S); bon = AR.f32(S); ysum = AR.f32(S)
    Br, Bk, Bv, Bkk, Bbon, Bys = Buf("r"), Buf("k"), Buf("v"), Buf("kk"), Buf("bon"), Buf("ysum")
    Vtok = AR.bf16(32, 128); BVtok = [Buf("vtok%d" % q) for q in range(4)]
    vbd = AR.bf16(8, 128); Bvbd = Buf("vbd")
    rkones = AR.f32(128); Brk = Buf("rkones")
    NTMP = 7168
    tmp_raw = AR._raw(NTMP)
    WD = []
    for d in range(2):
        W = {}
        for nm in ("ARt", "BKtok", "Gb", "Gk"):
            W[nm] = [AR.bf16(8, 2, 128) for _ in range(2)]
        W["Tt"] = [AR.bf16(8, 128) for _ in range(2)]
        W["Pc"] = [AR.f32(8) for _ in range(2)]
        W["BKt"] = AR.bf16(8, 2, 128)
        W["Xs"] = AR.bf16(128); W["Ubf"] = AR.bf16(128); W["Ybd"] = AR.f32(8, 128)
        W["St"] = AR.f32(128); W["Stbf"] = AR.bf16(128); W["tmpS"] = AR.f32(128); W["tot"] = AR.f32(8)
        for nm in ("BBKt", "BXs", "BUbf", "BSt", "BStbf", "BtmpS", "Btot", "BYbd"):
            W[nm] = Buf(nm)
        for nm in ("BARt", "BPc"):
            W[nm] = [Buf(nm + "0"), Buf(nm + "1")]
        for nm in ("BBKtok", "BGb", "BGk", "BTt"):
            W[nm] = [[Buf(), Buf()], [Buf(), Buf()]]
        W["BLab"] = [Buf(), Buf()]
        W["BAn"] = [[Buf(), Buf()], [Buf(), Buf()]]; W["BBn"] = [[Buf(), Buf()], [Buf(), Buf()]]
        TA = Arena(tmp_raw[:, d * (NTMP // 2):(d + 1) * (NTMP // 2)], NTMP // 2)
        for nm in ("a", "ld", "cum", "E1", "E2", "E3", "u"):
            W[nm] = TA.f32(512); W["B" + nm] = Buf(nm)
        asb = lambda ap: ap.bitcast(BF16).rearrange("p (a b) -> p a b", b=128)
        W["An"] = [asb(W["E1"]), asb(W["E2"])]; W["Bn"] = [asb(W["E3"]), asb(W["u"])]; W["Lab"] = asb(W["ld"])
        W["alias_An"] = [W["BE1"], W["BE2"]]; W["alias_Bn"] = [W["BE3"], W["Bu"]]
        WD.append(W)
    for d in range(2):
        W = WD[d]
        for jp in range(2):
            Sc.op("pool", lambda: P_.memset(W["ARt"][jp], 0.0), writes=[W["BARt"][jp]])
        Sc.op("pool", lambda: P_.memset(W["BKt"], 0.0), writes=[W["BBKt"]])
    Sc.op("pool", lambda: P_.memset(vbd, 0.0), writes=[Bvbd])

    rawc = [AR.f32(514) for _ in range(2)]; Brawc = [Buf(), Buf()]
    nbc = AR.f32(512); sqc_ = AR.f32(512); Bnbc, Bsqc_ = Buf(), Buf()
    sqk = nbc; rnk = sqc_; Bsqk, Brnk = Bnbc, Bsqc_
    gatec = AR.f32(512); dtmp = AR.f32(512); sq2 = AR.f32(512); rstd = AR.f32(512); yn = AR.f32(512); ystc = [AR.bf16(512) for _ in range(2)]
    Bgatec, Bdt, Bs2, Brs, Byn, Bystc = Buf(), Buf(), Buf(), Buf(), Buf(), [Buf(), Buf()]
    rc_rr = [0]

    def shift_gen(dst, bdst, row0, mi, func=AF.Copy):
        for tc in range(4):
            sl = slice(tc * 512, (tc + 1) * 512)
            lo = max(0, tc * 512 - 1); hi = min(S, tc * 512 + 513)
            off = lo - (tc * 512 - 1)
            ri = rc_rr[0] % 2; rc_rr[0] += 1
            rc = rawc[ri]; brc = Brawc[ri]
            if tc == 0:
                Sc.op("pool", lambda: P_.memset(rc[:, 0:1], 0.0), writes=[brc])
            if tc == 3:
                Sc.op("pool", lambda: P_.memset(rc[:, 513:514], 0.0), writes=[brc])
            Sc.dma("sp", rc[:, off:off + (hi - lo)], proj_f[row0:row0 + 128, lo:hi], writes=[brc]); yield
            Sc.op("pool", lambda: P_.tensor_tensor(out=nbc, in0=rc[:, 0:512], in1=rc[:, 2:514], op=ALU.add), reads=[brc], writes=[Bnbc]); yield
            Sc.op("act", lambda: A_.activation(out=sqc_, in_=rc[:, 1:513], func=AF.Copy, scale=prm[:, OMM + mi:OMM + mi + 1]), reads=[brc, Bprm], writes=[Bsqc_]); yield
            if func == AF.Copy:
                Sc.op("dve", lambda: V.scalar_tensor_tensor(out=dst[:, sl], in0=nbc, scalar=prm[:, HMU + mi:HMU + mi + 1], in1=sqc_, op0=ALU.mult, op1=ALU.add),
                      reads=[Bnbc, Bprm, Bsqc_], writes=[bdst]); yield
            else:
                Sc.op("dve", lambda: V.scalar_tensor_tensor(out=sqc_, in0=nbc, scalar=prm[:, HMU + mi:HMU + mi + 1], in1=sqc_, op0=ALU.mult, op1=ALU.add),
                      reads=[Bnbc, Bprm, Bsqc_], writes=[Bsqc_]); yield
                Sc.op("act", lambda: A_.activation(out=dst[:, sl], in_=sqc_, func=func), reads=[Bsqc_], writes=[bdst]); yield

    def v3(ap, n=64):
        return ap.rearrange("p (c t) -> p c t", t=n)

    def unit_prep(p, d, sc, jp):
        W = WD[d]
        sl = slice(sc * 512, (sc + 1) * 512)
        dh = slice(d * 64, (d + 1) * 64)
        pc = slice(p * 128, (p + 1) * 128)
        a, ld, cum, E1, E2, E3, u = W["a"], W["ld"], W["cum"], W["E1"], W["E2"], W["E3"], W["u"]
        Ba, Bld, Bcum, BE1, BE2, BE3, Bu = W["Ba"], W["Bld"], W["Bcum"], W["BE1"], W["BE2"], W["BE3"], W["Bu"]
        ARt, BKtok, Gb, Gk, Tt, Pc = W["ARt"][jp], W["BKtok"][jp], W["Gb"][jp], W["Gk"][jp], W["Tt"][jp], W["Pc"][jp]
        BARt, BBKtok, BGb, BGk, BTt, BPc = W["BARt"][jp], W["BBKtok"][jp], W["BGb"][jp], W["BGk"][jp], W["BTt"][jp], W["BPc"][jp]
        BKt, Lab = W["BKt"], W["Lab"]
        b = nbank(2, 8)
        mm_group(ps[b][:, :], [(a2b[dh, pc], la_s[dh, sl], [tb5, Bla])], Bps[b]); yield
        hc = HA0 + d * 6 + p
        Sc.op("act", lambda: A_.activation(out=a, in_=ps[b][:, :], func=AF.Tanh, bias=prm[:, hc:hc + 1], scale=0.5), reads=[Bps[b], Bprm], writes=[Ba]); yield
        Sc.op("dve", lambda: V.tensor_scalar(out=a, in0=a, scalar1=0.5, scalar2=0.5, op0=ALU.mult, op1=ALU.add), reads=[Ba], writes=[Ba]); yield
        b = nbank(2, 8)
        mm_group(ps[b][:, :], [(w2b[dh, pc], lw_t[dh, sl], [tb4, Blw])], Bps[b]); yield
        hc2 = HW0 + d * 6 + p
        Sc.op("act", lambda: A_.activation(out=ld, in_=ps[b][:, :], func=AF.Tanh, bias=prm[:, hc2:hc2 + 1], scale=0.5), reads=[Bps[b], Bprm], writes=[Bld] + W["BLab"]); yield
        Sc.op("dve", lambda: V.tensor_scalar(out=ld, in0=ld, scalar1=C1, scalar2=C1, op0=ALU.mult, op1=ALU.add), reads=[Bld], writes=[Bld]); yield
        Sc.op("dve", lambda: V.tensor_tensor_scan(out=cum, data0=segm, data1=ld, initial=0.0, op0=ALU.mult, op1=ALU.add), reads=[tb2, Bld], writes=[Bcum]); yield
        if d == 1:
            Sc.op("dve", lambda: V.tensor_copy(out=W["tot"], in_=v3(cum)[:, :, 63]), reads=[Bcum], writes=[W["Btot"]]); yield
            Sc.op("dve", lambda: V.tensor_tensor(out=cum, in0=ld, in1=cum, op=ALU.subtract), reads=[Bld, Bcum], writes=[Bcum]); yield
            Sc.op("dve", lambda: V.tensor_tensor(out=v3(cum), in0=v3(cum), in1=W["tot"].unsqueeze(2).to_broadcast([128, 8, 64]), op=ALU.add),
                  reads=[Bcum, W["Btot"]], writes=[Bcum]); yield
        Sc.op("dve", lambda: V.tensor_tensor(out=ld, in0=cum, in1=ld, op=ALU.subtract), reads=[Bcum, Bld], writes=[Bld]); yield
        Sc.op("act", lambda: A_.activation(out=E3, in_=ld, func=AF.Exp), reads=[Bld], writes=[BE3] + W["BBn"][0]); yield
        Sc.op("act", lambda: A_.activation(out=E1, in_=cum, func=AF.Exp), reads=[Bcum], writes=[BE1] + W["BAn"][0]); yield
        Sc.op("act", lambda: A_.activation(out=E2, in_=cum, func=AF.Exp, scale=-1.0), reads=[Bcum], writes=[BE2] + W["BAn"][1]); yield
        pcol = 63 if d == 0 else 0
        Sc.op("dve", lambda: V.tensor_copy(out=Pc, in_=v3(E1)[:, :, pcol]), reads=[BE1], writes=[BPc]); yield
        Sc.op("dve", lambda: V.tensor_scalar(out=u, in0=a, scalar1=prm[:, KA + p:KA + p + 1], scalar2=prm[:, OMK + p:OMK + p + 1], op0=ALU.mult, op1=ALU.add),
              reads=[Ba, Bprm], writes=[Bu] + W["BBn"][1]); yield
        Sc.op("dve", lambda: V.tensor_tensor(out=u, in0=k_[:, sl], in1=u, op=ALU.mult), reads=[Bk, Bu], writes=[Bu]); yield
        Sc.op("pool", lambda: P_.tensor_tensor(out=a, in0=kk_[:, sl], in1=a, op=ALU.mult), reads=[Bkk, Ba], writes=[Ba]); yield
        for half in range(2):
            hs = slice(half * 64, (half + 1) * 64)
            bc = slice(half * 64, (half + 1) * 64)
            Sc.op("dve", lambda: V.scalar_tensor_tensor(out=ARt[hs, :, 0, bc], in0=v3(kk_[hs, sl]), scalar=-1.0, in1=v3(E3[hs, :]), op0=ALU.mult, op1=ALU.mult),
                  reads=[Bkk, BE3], writes=[BARt]); yield
            Sc.op("pool", lambda: P_.tensor_tensor(out=ARt[hs, :, 1, bc], in0=v3(r_[hs, sl]), in1=v3(E1[hs, :]), op=ALU.mult), reads=[Br, BE1], writes=[BARt]); yield
            Sc.op("dve", lambda: V.tensor_tensor(out=BKt[hs, :, 1, bc], in0=v3(u[hs, :]), in1=v3(E2[hs, :]), op=ALU.mult), reads=[Bu, BE2], writes=[W["BBKt"]]); yield
            Sc.op("dve", lambda: V.tensor_tensor(out=BKt[hs, :, 0, bc], in0=v3(a[hs, :]), in1=v3(E2[hs, :]), op=ALU.mult), reads=[Ba, BE2], writes=[W["BBKt"]]); yield
        Sc.op("pool", lambda: P_.tensor_tensor(out=cum, in0=r_[:, sl], in1=u, op=ALU.mult), reads=[Br, Bu, Bcum], writes=[Bcum]); yield
        b = nbank(2, 8)
        mm_group(ps[b][:, :], [(rkones, cum, [Brk, Bcum])], Bps[b]); yield
        Sc.op("dve", lambda: V.tensor_tensor(out=cum, in0=ps[b][:, :], in1=v_[:, sl], op=ALU.mult), reads=[Bps[b], Bv, Bcum], writes=[Bcum]); yield
        Sc.op("pool", lambda: P_.tensor_tensor(out=bon[:, sl], in0=bon[:, sl], in1=cum, op=ALU.add), reads=[Bbon, Bcum], writes=[Bbon]); yield
        for hb in range(2):
            b = nbank(2, 8)
            for ci in range(4):
                for s2 in range(2):
                    j = ci * 2 + s2
                    Sc.op("pe", lambda: T_.transpose(out=psb[b][:, j * 128:(j + 1) * 128], in_=BKt[:, hb * 4 + ci, s2, :], identity=identb),
                          reads=[W["BBKt"], Bid], writes=[Bps[b]] if j == 0 else [], inc=(j == 7), setw=[Bps[b]] if j == 7 else [])
            yield
            dstv = BKtok[:, hb * 4:(hb + 1) * 4, :, :].rearrange("p a b c -> p (a b c)")
            Sc.op("act", lambda: A_.activation(out=dstv, in_=psb[b], func=AF.Copy), reads=[Bps[b]], writes=[BBKtok[hb]]); yield
        M2 = rwm[:, d, 0:2, :].rearrange("p a b -> p (a b)")
        ML = rwm[:, d, 2, :]
        for hb in range(2):
            bL = nbank(2, 8)
            mm_multi([(ps[bL][:, ci * 128:(ci + 1) * 128], ARt[:, hb * 4 + ci, 0, :], BKt[:, hb * 4 + ci, 0, :], [W["BBKt"], BARt]) for ci in range(4)], Bps[bL])
            yield
            Sc.op("dve", lambda: V.tensor_tensor(out=Lab[:, hb * 4:(hb + 1) * 4, :], in0=ps[bL][:, :].rearrange("p (a b) -> p a b", b=128),
                                                 in1=ML.unsqueeze(1).to_broadcast([128, 4, 128]), op=ALU.mult), reads=[Bps[bL], tb3], writes=[W["BLab"][hb], Bld]); yield
            bB = [nbank(2, 8), nbank(2, 8)]
            for i in range(2):
                mm_multi([(ps[bB[i]][:, q2 * 256:(q2 + 1) * 256], BKt[:, hb * 4 + i * 2 + q2, 0, :], ARt[:, hb * 4 + i * 2 + q2, :, :], [W["BBKt"], BARt]) for q2 in range(2)], Bps[bB[i]])
            yield
            for i in range(2):
                c0 = hb * 4 + i * 2
                Sc.op("dve", lambda: V.tensor_tensor(out=Gb[:, c0:c0 + 2, :, :].rearrange("p a b c -> p a (b c)"), in0=ps[bB[i]][:, :].rearrange("p (a b) -> p a b", b=256),
                                                     in1=M2.unsqueeze(1).to_broadcast([128, 2, 256]), op=ALU.mult), reads=[Bps[bB[i]], tb3], writes=[BGb[hb]]); yield
        for hb in range(2):
            cs = slice(hb * 4, (hb + 1) * 4)
            Sc.op("dve", lambda: V.tensor_tensor(out=Tt[:, cs, :], in0=Gb[:, cs, 0, :], in1=identb.unsqueeze(1).to_broadcast([128, 4, 128]), op=ALU.add),
                  reads=[BGb[hb], Bid], writes=[BTt[hb]]); yield
        for lvl in range(0, 6):
            for hb in range(2):
                cs = slice(hb * 4, (hb + 1) * 4)
                if lvl == 0:
                    Aget = lambda c: Gb[:, c, 0, :]
                    Bget = lambda c: Lab[:, c, :]
                    BAsrc, BBsrc = BGb[hb], W["BLab"][hb]
                else:
                    Aprev, Bprev = W["An"][lvl % 2], W["Bn"][lvl % 2]
                    Aget = lambda c: Aprev[:, c, :]
                    Bget = lambda c: Bprev[:, c, :]
                    BAsrc, BBsrc = W["BAn"][lvl % 2][hb], W["BBn"][lvl % 2][hb]
                Anew, Bnew = W["An"][(lvl + 1) % 2], W["Bn"][(lvl + 1) % 2]
                BAnew, BBnew = W["BAn"][(lvl + 1) % 2][hb], W["BBn"][(lvl + 1) % 2][hb]
                if lvl >= 1:
                    bT = nbank(2, 8)
                    mm_multi([(ps[bT][:, ci * 128:(ci + 1) * 128], Bget(hb * 4 + ci), Tt[:, hb * 4 + ci, :], [BBsrc, BTt[hb]]) for ci in range(4)], Bps[bT])
                    yield
                if lvl < 5:
                    if lvl < 4:
                        bA = nbank(2, 8)
                        mm_multi([(ps[bA][:, ci * 128:(ci + 1) * 128], Bget(hb * 4 + ci), Aget(hb * 4 + ci), [BAsrc, BBsrc]) for ci in range(4)], Bps[bA])
                        yield
                    bBm = nbank(2, 8)
                    mm_multi([(ps[bBm][:, ci * 128:(ci + 1) * 128], Aget(hb * 4 + ci), Bget(hb * 4 + ci), [BAsrc, BBsrc]) for ci in range(4)], Bps[bBm])
                    yield
                if lvl >= 1:
                    Sc.op("dve", lambda: V.tensor_tensor(out=Tt[:, cs, :], in0=ps[bT][:, :].rearrange("p (a b) -> p a b", b=128), in1=Tt[:, cs, :], op=ALU.add),
                          reads=[Bps[bT], BTt[hb]], writes=[BTt[hb]]); yield
                if lvl < 5:
                    if lvl < 4:
                        Sc.op("act", lambda: A_.activation(out=Anew[:, cs, :], in_=ps[bA][:, :].rearrange("p (a b) -> p a b", b=128), func=AF.Copy),
                              reads=[Bps[bA]], writes=[BAnew, W["alias_An"][(lvl + 1) % 2]]); yield
                    Sc.op("act", lambda: A_.activation(out=Bnew[:, cs, :], in_=ps[bBm][:, :].rearrange("p (a b) -> p a b", b=128), func=AF.Copy),
                          reads=[Bps[bBm]], writes=[BBnew, W["alias_Bn"][(lvl + 1) % 2]]); yield
        for hb in range(2):
            bK = [nbank(2, 8), nbank(2, 8)]
            for i in range(2):
                mm_multi([(ps[bK[i]][:, q2 * 256:(q2 + 1) * 256], BKt[:, hb * 4 + i * 2 + q2, 1, :], ARt[:, hb * 4 + i * 2 + q2, :, :], [W["BBKt"], BARt]) for q2 in range(2)], Bps[bK[i]])
            yield
            for i in range(2):
                c0 = hb * 4 + i * 2
                Sc.op("dve", lambda: V.tensor_tensor(out=Gk[:, c0:c0 + 2, :, :].rearrange("p a b c -> p a (b c)"), in0=ps[bK[i]][:, :].rearrange("p (a b) -> p a b", b=256),
                                                     in1=M2.unsqueeze(1).to_broadcast([128, 2, 256]), op=ALU.mult), reads=[Bps[bK[i]], tb3], writes=[BGk[hb]]); yield

    import os as _os3
    SLK = int(_os3.environ.get('SCAN_SLACK', '0'))

    def scan_step(p, d, sc, ci, jp):
        W = WD[d]
        hb = ci // 4
        cg = sc * 8 + ci
        sb = d
        bkb = Bps[sb]
        ARt, BKtok, Gb, Gk, Tt, Pc = W["ARt"][jp], W["BKtok"][jp], W["Gb"][jp], W["Gk"][jp], W["Tt"][jp], W["Pc"][jp]
        BARt, BBKtok, BGb, BGk, BTt, BPc = W["BARt"][jp], W["BBKtok"][jp], W["BGb"][jp], W["BGk"][jp], W["BTt"][jp], W["BPc"][jp]
        St, Stbf, Xs, Ubf, tmpS = W["St"], W["Stbf"], W["Xs"], W["Ubf"], W["tmpS"]
        vt = Vtok[:, cg, :]
        bvt = BVtok[cg // 8]
        pcc = Pc[:, ci:ci + 1]
        Sc.op("act", lambda: A_.activation(out=tmpS, in_=St, func=AF.Copy, scale=pcc), reads=[W["BSt"], BPc], writes=[W["BtmpS"]]); yield
        for _s in range(SLK): yield
        mm_group(ps[sb][:, 0:128], [(ARt[:, ci, 0, :], Stbf, [BARt, W["BStbf"]]), (Gk[:, ci, 0, :], vt, [BGk[hb], bvt])], bkb); yield
        Sc.op("act", lambda: A_.activation(out=Xs, in_=ps[sb][:, 0:128], func=AF.Copy), writes=[bkb, W["BXs"]]); yield
        for _s in range(SLK): yield
        mm_group(ps[sb][:, 128:256], [(Tt[:, ci, :], Xs, [BTt[hb], W["BXs"]])], bkb); yield
        Sc.op("dve", lambda: V.tensor_copy(out=Ubf, in_=ps[sb][:, 128:256]), writes=[bkb, W["BUbf"]]); yield
        for _s in range(SLK): yield
        mm_group(ps[sb][:, 384:512], [(BKtok[:, ci, 0, :], Ubf, [BBKtok[hb], W["BUbf"]]), (BKtok[:, ci, 1, :], vt, [BBKtok[hb], bvt])], bkb); yield
        mm_group(ps[sb][:, 256:384], [(Stbf, ARt[:, ci, 1, :], [BARt, W["BStbf"]]), (Ubf, Gb[:, ci, 1, :], [W["BUbf"], BGb[hb]]),
                                      (vt, Gk[:, ci, 1, :], [bvt, BGk[hb]])], bkb); yield
        Sc.op("dve", lambda: V.scalar_tensor_tensor(out=St, in0=ps[sb][:, 384:512], scalar=pcc, in1=tmpS, op0=ALU.mult, op1=ALU.add),
              reads=[BPc, W["BtmpS"]], writes=[bkb, W["BSt"]]); yield
        Sc.op("act", lambda: A_.activation(out=Stbf, in_=St, func=AF.Copy), reads=[W["BSt"]], writes=[W["BStbf"]]); yield
        Sc.op("act", lambda: A_.activation(out=W["Ybd"][:, ci, :], in_=ps[sb][:, 256:384], func=AF.Copy), writes=[bkb, W["BYbd"]]); yield

    def chain(p, d, sc, jp):
        order = range(8) if d == 0 else range(7, -1, -1)
        for ci in order:
            yield from scan_step(p, d, sc, ci, jp)
        W = WD[d]
        sl = slice(sc * 512, (sc + 1) * 512)
        for half in range(2):
            hs = slice(half * 64, (half + 1) * 64)
            Sc.op("pool", lambda: P_.tensor_tensor(out=v3(ysum[hs, sl]), in0=v3(ysum[hs, sl]), in1=W["Ybd"][hs, :, half * 64:(half + 1) * 64], op=ALU.add),
                  reads=[Bys, W["BYbd"]], writes=[Bys]); yield

    run(shift_gen(lw_t, Blw, LW0, 18, func=AF.Tanh))
    run(shift_gen(la_s, Bla, LA0, 19))

    PAIRS = list(range(6))
    if stop_after.startswith("rwkvp"):
        PAIRS = [int(ch) for ch in stop_after[5:]]

    def prologue(p):
        yield from shift_gen(r_, Br, R0 + p * 128, p)
        yield from shift_gen(k_, Bk, K0 + p * 128, 6 + p)
        yield from shift_gen(v_, Bv, V0 + p * 128, 12 + p)
        kcol = prm[:, KK + p:KK + p + 1]
        for tc in range(4):
            sl = slice(tc * 512, (tc + 1) * 512)
            Sc.op("act", lambda: A_.activation(out=sqk, in_=k_[:, sl], func=AF.Square, scale=kcol), reads=[Bk, Bprm], writes=[Bsqk]); yield
            b = nbank(2, 8)
            mm_group(ps[b][:, :], [(bones, sqk, [Bc, Bsqk])], Bps[b]); yield
            rsqrt_act(rnk, ps[b][:, :], 1.0, 1e-24, [Bps[b]], [Brnk]); yield
            Sc.op("dve", lambda: V.scalar_tensor_tensor(out=kk_[:, sl], in0=k_[:, sl], scalar=kcol, in1=rnk, op0=ALU.mult, op1=ALU.mult),
                  reads=[Bk, Bprm, Brnk], writes=[Bkk]); yield
        Sc.op("dve", lambda: V.tensor_scalar(out=rkones, in0=bones, scalar1=prm[:, RK + p:RK + p + 1], scalar2=None, op0=ALU.mult), reads=[Bc, Bprm], writes=[Brk]); yield

    def vtok_build(p):
        for q in range(4):
            for half in range(2):
                hs = slice(half * 64, (half + 1) * 64)
                Sc.op("pool", lambda: P_.tensor_copy(out=vbd[hs, :, half * 64:(half + 1) * 64], in_=v3(v_[hs, q * 512:(q + 1) * 512])), reads=[Bv], writes=[Bvbd]); yield
            b = nbank(2, 8)
            for j in range(8):
                Sc.op("pe", lambda: T_.transpose(out=psb[b][:, j * 128:(j + 1) * 128], in_=vbd[:, j, :], identity=identb),
                      reads=[Bvbd, Bid], writes=[Bps[b]] if j == 0 else [], inc=(j == 7), setw=[Bps[b]] if j == 7 else [])
            yield
            Sc.op("dve", lambda: V.tensor_copy(out=Vtok[:, q * 8:(q + 1) * 8, :].rearrange("p a b -> p (a b)"), in_=psb[b]), reads=[Bps[b]], writes=[BVtok[q]]); yield

    def resets(p):
        Sc.op("pool", lambda: P_.memset(bon, 0.0), writes=[Bbon])
        Sc.op("pool", lambda: P_.memset(ysum, 0.0), writes=[Bys])
        for d in range(2):
            W = WD[d]
            Sc.op("pool", lambda: P_.memset(W["St"], 0.0), writes=[W["BSt"]])
            Sc.op("pool", lambda: P_.memset(W["Stbf"], 0.0), writes=[W["BStbf"]])

    def epilogue(p):
        dump("d_ysum%d" % p, ysum, [128, S], F32, [Bys])
        for tc in range(4):
            sl = slice(tc * 512, (tc + 1) * 512)
            yc = ystc[tc % 2]; byc = Bystc[tc % 2]
            Sc.dma("sp", gatec, proj_f[RG0 + p * 128:RG0 + (p + 1) * 128, sl], writes=[Bgatec])
            b = nbank(2, 8)
            mm_group(ps[b][:, :], [(bo64, ysum[:, sl], [tb1, Bys])], Bps[b]); yield
            Sc.op("dve", lambda: V.tensor_tensor(out=dtmp, in0=ysum[:, sl], in1=ps[b][:, :], op=ALU.subtract), reads=[Bys, Bps[b]], writes=[Bdt]); yield
            Sc.op("act", lambda: A_.activation(out=sq2, in_=dtmp, func=AF.Square), reads=[Bdt], writes=[Bs2]); yield
            b2 = nbank(2, 8)
            mm_group(ps[b2][:, :], [(bo64, sq2, [tb1, Bs2])], Bps[b2]); yield
            rsqrt_act(rstd, ps[b2][:, :], 1.0, GN_EPS, [Bps[b2]], [Brs]); yield
            Sc.op("dve", lambda: V.tensor_tensor(out=yn, in0=dtmp, in1=rstd, op=ALU.mult), reads=[Bdt, Brs], writes=[Byn]); yield
            Sc.op("dve", lambda: V.tensor_scalar(out=yn, in0=yn, scalar1=prm[:, LW + p:LW + p + 1], scalar2=prm[:, LB + p:LB + p + 1], op0=ALU.mult, op1=ALU.add),
                  reads=[Byn, Bprm], writes=[Byn]); yield
            Sc.op("pool", lambda: P_.tensor_tensor(out=yn, in0=yn, in1=bon[:, sl], op=ALU.add), reads=[Byn, Bbon], writes=[Byn]); yield
            Sc.op("dve", lambda: V.tensor_tensor(out=yc, in0=yn, in1=gatec, op=ALU.mult), reads=[Byn, Bgatec], writes=[byc]); yield
            Sc.dma("sp", ybuf[768 + p * 128:768 + (p + 1) * 128, sl], yc, reads=[byc]); yield

    def preps(p, j):
        return [unit_prep(p, 0, j, j % 2), unit_prep(p, 1, 3 - j, j % 2)]

    Sc.barrier()
    run(prologue(PAIRS[0]))
    run(vtok_build(PAIRS[0]))
    resets(PAIRS[0])
    run(rr(preps(PAIRS[0], 0)))
    for i, p in enumerate(PAIRS):
        nxt = PAIRS[i + 1] if i + 1 < len(PAIRS) else None
        for j in range(4):
            gens = [chain(p, 0, j, j % 2), chain(p, 1, 3 - j, j % 2)]
            if j < 3:
                gens += preps(p, j + 1)
            elif nxt is not None:
                gens.append(prologue(nxt))
            run(rr(gens))
        gens = [epilogue(p)]
        if nxt is not None:
            gens.append(vtok_build(nxt))
        run(rr(gens))
        if nxt is not None:
            resets(nxt)
            run(rr(preps(nxt, 0)))
    Sc.barrier()
    if stop_after.startswith("rwkv"):
        return finish_debug(nc, Sc, locals())

    AR.release(m_persist)
    cosr = AR.f32(16, 32); sinr = AR.f32(16, 32); gqk = AR.f32(16, 64); esink = AR.f32(12); mLR = AR.bf16(2, 128)
    Bcs, Bsn, Bgq, Bes, Bml = Buf(), Buf(), Buf(), Buf(), Buf()
    Sc.dma("sp", cosr, cos_d[:, :, :], writes=[Bcs]); Sc.dma("sp", sinr, sin_d[:, :, :], writes=[Bsn])
    Sc.dma("sp", gqk, gqk_d.rearrange("p (a b) -> p a b", b=64), writes=[Bgq]); Sc.dma("sp", esink, sink_d[:, :], writes=[Bes])
    Sc.dma("pool", mLR, maskLR_d[:, :, :], writes=[Bml])
    Sc.op("act", lambda: A_.activation(out=esink, in_=esink, func=AF.Exp), reads=[Bes], writes=[Bes])
    qTa = AR.bf16(12, S); kTa = AR.bf16(4, S); vext = AR.bf16(16, 4, 128); ya = AR.bf16(6, S)
    BqTa = [Buf() for _ in range(16)]; BkTa = [Buf() for _ in range(16)]; Bvx = [Buf() for _ in range(16)]; Bya = [Buf() for _ in range(16)]
    Sc.op("pool", lambda: P_.memset(vext, 1.0), writes=Bvx)
    qkv = [AR.f32(NQKV) for _ in range(2)]; Bqkv2 = [Buf(), Buf()]
    gta = [AR.f32(6, 128) for _ in range(6)]; Bgta = [Buf() for _ in range(6)]
    sqa2 = [AR.f32(1024) for _ in range(2)]; ssa2 = [AR.f32(16) for _ in range(2)]; rsa2 = [AR.f32(16) for _ in range(2)]
    qna2 = [AR.f32(16, 64) for _ in range(2)]; qra2 = [AR.bf16(16, 64) for _ in range(2)]
    rt2 = [[AR.f32(16, 32) for _ in range(4)] for _ in range(2)]
    Bsqa2, Bssa2, Bqna2, Bqra2 = [Buf(), Buf()], [Buf(), Buf()], [Buf(), Buf()], [Buf(), Buf()]
    Brt2 = [[Buf() for _ in range(4)] for _ in range(2)]
    pTa = [AR.bf16(384) for _ in range(6)]; BpTa = [Buf() for _ in range(6)]
    rda = AR.f32(4, 128); Brda = Buf(); yodd = AR.f32(2, 128); Byodd = Buf(); yev = AR.f32(2, 128); Byev = Buf()
    Bo = [Buf("o5"), Buf("o6"), Buf("o7")]
    pta_rr = [0]
    ag_src = proj_f[AG0:AG0 + 768, :].rearrange("(c p) t -> p c t", p=128)

    def prep_blk(tb):
        qk_ = qkv[tb % 2]; bqk = Bqkv2[tb % 2]
        sqa, ssa, rsa, qna, qra, rt_ = sqa2[tb % 2], ssa2[tb % 2], rsa2[tb % 2], qna2[tb % 2], qra2[tb % 2], rt2[tb % 2]
        Bsqa, Bssa, Bqna, Bqra, Brt = Bsqa2[tb % 2], Bssa2[tb % 2], Bqna2[tb % 2], Bqra2[tb % 2], Brt2[tb % 2]
        Sc.dma("sp", qk_, qkv_t[tb * 128:(tb + 1) * 128, :], writes=[bqk])
        Sc.dma("sp", gta[tb % 6], ag_src[:, :, tb * 128:(tb + 1) * 128], writes=[Bgta[tb % 6]])
        Sc.op("act", lambda: A_.activation(out=sqa, in_=qk_[:, 0:1024], func=AF.Square), reads=[bqk], writes=[Bsqa]); yield
        Sc.op("dve", lambda: V.tensor_reduce(out=ssa, in_=sqa.rearrange("p (a b) -> p a b", b=64), axis=AX.X, op=ALU.add), reads=[Bsqa], writes=[Bssa]); yield
        rsqrt_act(rsa, ssa, 1.0 / 64, EPS, [Bssa], [Bssa]); yield
        Sc.op("dve", lambda: V.tensor_tensor(out=qna, in0=qk_[:, 0:1024].rearrange("p (a b) -> p a b", b=64), in1=rsa.unsqueeze(2).to_broadcast([128, 16, 64]), op=ALU.mult),
              reads=[bqk, Bssa], writes=[Bqna]); yield
        Sc.op("pool", lambda: P_.tensor_tensor(out=qna, in0=qna, in1=gqk, op=ALU.mult), reads=[Bqna, Bgq], writes=[Bqna]); yield
        t1 = qna[:, :, 0:32]; t2 = qna[:, :, 32:64]
        cb = cosr[:, tb, :].unsqueeze(1).to_broadcast([128, 16, 32]); sb_ = sinr[:, tb, :].unsqueeze(1).to_broadcast([128, 16, 32])
        Sc.op("dve", lambda: V.tensor_tensor(out=rt_[0], in0=t1, in1=cb, op=ALU.mult), reads=[Bqna, Bcs], writes=[Brt[0]]); yield
        Sc.op("pool", lambda: P_.tensor_tensor(out=rt_[1], in0=t2, in1=sb_, op=ALU.mult), reads=[Bqna, Bsn], writes=[Brt[1]]); yield
        Sc.op("dve", lambda: V.tensor_tensor(out=qra[:, :, 0:32], in0=rt_[0], in1=rt_[1], op=ALU.subtract), reads=[Brt[0], Brt[1]], writes=[Bqra]); yield
        Sc.op("pool", lambda: P_.tensor_tensor(out=rt_[2], in0=t2, in1=cb, op=ALU.mult), reads=[Bqna, Bcs], writes=[Brt[2]]); yield
        Sc.op("dve", lambda: V.tensor_tensor(out=rt_[3], in0=t1, in1=sb_, op=ALU.mult), reads=[Bqna, Bsn], writes=[Brt[3]]); yield
        Sc.op("dve", lambda: V.tensor_tensor(out=qra[:, :, 32:64], in0=rt_[2], in1=rt_[3], op=ALU.add), reads=[Brt[2], Brt[3]], writes=[Bqra]); yield
        Sc.op("act", lambda: A_.activation(out=vext[:, tb, :, 0:64], in_=qk_[:, 1024:1280].rearrange("p (a b) -> p a b", b=64), func=AF.Copy), reads=[bqk], writes=[Bvx[tb]]); yield
        b = [0, 6][tb % 2]
        qflat = qra.rearrange("p a b -> p (a b)")
        for j in range(8):
            Sc.op("pe", lambda: T_.transpose(out=psb[b][:, j * 128:(j + 1) * 128], in_=qflat[:, j * 128:(j + 1) * 128], identity=identb),
                  reads=[Bqra, Bid], writes=[Bps[b]] if j == 0 else [], inc=(j == 7), setw=[Bps[b]] if j == 7 else [])
        yield
        psv = psb[b].rearrange("p (a b) -> p a b", b=128)
        tsl = slice(tb * 128, (tb + 1) * 128)
        qv = qTa.rearrange("p (h two) t -> p h two t", two=2)
        kv = kTa.rearrange("p (h two) t -> p h two t", two=2)
        Sc.op("dve", lambda: V.tensor_copy(out=qv[0:64, :, 0, tsl], in_=psv[0:64, 0:6, :]), reads=[Bps[b]], writes=[BqTa[tb]]); yield
        Sc.op("act", lambda: A_.activation(out=qv[0:64, :, 1, tsl], in_=psv[64:128, 0:6, :], func=AF.Copy), reads=[Bps[b]], writes=[BqTa[tb]]); yield
        Sc.op("dve", lambda: V.tensor_copy(out=kv[0:64, :, 0, tsl], in_=psv[0:64, 6:8, :]), reads=[Bps[b]], writes=[BkTa[tb]]); yield
        Sc.op("act", lambda: A_.activation(out=kv[0:64, :, 1, tsl], in_=psv[64:128, 6:8, :], func=AF.Copy), reads=[Bps[b]], writes=[BkTa[tb]]); yield

    def attend(n):
        qsl = slice(n * 128, (n + 1) * 128)
        gt_ = gta[n % 6]; bgt = Bgta[n % 6]
        seq = []
        for g in range(4):
            kbs = [kb for kb in (n - 1, n, n + 1) if 0 <= kb < 16]
            for kb in kbs:
                seq.append((g, kb, kb == kbs[0], kb == kbs[-1]))
        order = []
        for (g, kb, fst, lst) in seq:
            for hh in range(3):
                order.append((g, kb, hh, fst, lst))
        firsts = {}; lasts = {}
        for idx, (g, kb, hh, fst, lst) in enumerate(order):
            ob = (3 * g + hh) // 4
            firsts.setdefault(ob, idx); lasts[ob] = idx
        idx = 0
        LOOK = 2
        issued = {}

        def score(i):
            g, kb, fst, lst = seq[i]
            b = sbanks[pta_rr[0] % len(sbanks)]
            pi = pta_rr[0] % 6; pta_rr[0] += 1
            mm_group(ps[b][:, 0:384], [(kTa[0:64, g, kb * 128:(kb + 1) * 128], qTa[0:64, 3 * g:3 * g + 3, qsl], [BkTa[kb], BqTa[n]])], Bps[b])
            issued[i] = (b, pi)

        for i in range(min(LOOK, len(seq))):
            score(i)
        yield
        for i, (g, kb, fst, lst) in enumerate(seq):
            if i + LOOK < len(seq):
                score(i + LOOK)
                yield
            b, pi = issued.pop(i)
            Sc.op("act", lambda: A_.activation(out=pTa[pi], in_=ps[b][:, 0:384], func=AF.Exp, scale=0.125), reads=[Bps[b]], writes=[BpTa[pi]]); yield
            if kb != n:
                mk = mLR[:, 0 if kb < n else 1, :].unsqueeze(1).to_broadcast([128, 3, 128])
                Sc.op("dve", lambda: V.tensor_tensor(out=pTa[pi].rearrange("p (a b) -> p a b", b=128), in0=pTa[pi].rearrange("p (a b) -> p a b", b=128), in1=mk, op=ALU.mult),
                      reads=[BpTa[pi], Bml], writes=[BpTa[pi]]); yield
            for hh in range(3):
                head = 3 * g + hh
                ob = head // 4
                col = (head % 4) * 128
                isf = firsts[ob] == idx; isl = lasts[ob] == idx
                st_flag = fst and (hh == 0 or head % 4 == 0)
                Sc.op("pe", lambda: T_.matmul(ps[obanks[ob]][:, col:col + 128], lhsT=vext[:, kb, g, :], rhs=pTa[pi][:, hh * 128:(hh + 1) * 128], start=st_flag, stop=lst),
                      reads=[Bvx[kb], BpTa[pi]], writes=[Bo[ob]] if isf else [], inc=(hh == 2), setw=[Bo[ob]] if isl else [])
                idx += 1
            yield
        for ob in range(3):
            pv = ps[obanks[ob]][:, :].rearrange("p (a b) -> p a b", b=128)
            Sc.op("dve", lambda: V.tensor_tensor(out=rda[0:64, :, :], in0=pv[64:128, :, :], in1=esink[64:128, ob * 4:(ob + 1) * 4].unsqueeze(2).to_broadcast([64, 4, 128]), op=ALU.add),
                  reads=[Bo[ob], Bes], writes=[Brda]); yield
            Sc.op("act", lambda: A_.activation(out=rda[0:64, :, :], in_=rda[0:64, :, :], func=AF.Ln), reads=[Brda], writes=[Brda]); yield
            Sc.op("act", lambda: A_.activation(out=rda[0:64, :, :], in_=rda[0:64, :, :], func=AF.Exp, scale=-1.0), reads=[Brda], writes=[Brda]); yield
            pv2 = pv.rearrange("p (h two) t -> p h two t", two=2)
            rd2 = rda.rearrange("p (h two) t -> p h two t", two=2)
            c0 = ob * 2
            Sc.op("dve", lambda: V.tensor_tensor(out=yev[0:64, :, :], in0=pv2[0:64, :, 0, :], in1=rd2[0:64, :, 0, :], op=ALU.mult), reads=[Bo[ob], Brda], writes=[Byev]); yield
            Sc.op("dve", lambda: V.tensor_tensor(out=yodd[64:128, :, :], in0=pv2[0:64, :, 1, :], in1=rd2[0:64, :, 1, :], op=ALU.mult), reads=[Bo[ob], Brda], writes=[Byodd]); yield
            Sc.op("pool", lambda: P_.tensor_tensor(out=ya[0:64, c0:c0 + 2, qsl], in0=yev[0:64, :, :], in1=gt_[0:64, c0:c0 + 2, :], op=ALU.mult), reads=[Byev, bgt], writes=[Bya[n]]); yield
            Sc.op("pool", lambda: P_.tensor_tensor(out=ya[64:128, c0:c0 + 2, qsl], in0=yodd[64:128, :, :], in1=gt_[64:128, c0:c0 + 2, :], op=ALU.mult), reads=[Byodd, bgt], writes=[Bya[n]]); yield

    sbanks = [1, 2, 7]
    obanks = [3, 4, 5]

    def seq_(gs):
        for g in gs:
            yield from g

    def attn_driver():
        for w in range(8):
            gens = [prep_blk(2 * w), prep_blk(2 * w + 1)]
            att = [attend(n) for n in (2 * w - 3, 2 * w - 2) if n >= 0]
            if att:
                gens.append(seq_(att))
            yield from rr(gens, [1, 1, 2])
        yield from attend(13)
        yield from attend(14)
        yield from attend(15)
        for c in range(6):
            Sc.dma("sp", ybuf[c * 128:(c + 1) * 128, :], ya[:, c, :], reads=Bya); yield

    run(attn_driver())
    Sc.barrier()
    if stop_after in ("attn", "xattn"):
        return finish_debug(nc, Sc, locals())

    AR.release(m_persist)
    mT = AR.bf16(16, S); BmT = [Buf() for _ in range(16)]
    m_p3 = AR.mark()
    yall = AR.bf16(16, S); Byall = [Buf() for _ in range(16)]
    wo = [AR.bf16(16, 256) for _ in range(2)]; Bwo = [[Buf(), Buf(), Buf()] for _ in range(2)]
    gt3 = [AR.f32(S) for _ in range(2)]; Bgt3 = [Buf(), Buf()]
    macc = AR.f32(S); Bmacc = [Buf() for _ in range(4)]
    ptmp = [AR.f32(512) for _ in range(2)]; Bptmp = [Buf(), Buf()]
    for k in range(16):
        Sc.dma("sp", yall[:, k, :], ybuf[k * 128:(k + 1) * 128, :], writes=[Byall[k]])
    wsrcs = [(attn_w_o.rearrange("(k p) n -> p k n", p=128), 0, 6), (rwkv_w_o.rearrange("(k p) n -> p k n", p=128), 6, 6), (x_w_o.rearrange("(k p) n -> p k n", p=128), 12, 4)]
    kranges = [range(0, 6), range(6, 12), range(12, 16)]

    def load_wo(fg):
        for bi, (src, k0, nk) in enumerate(wsrcs):
            Sc.dma("pool", wo[fg % 2][:, k0:k0 + nk, :], src[:, :, fg * 256:(fg + 1) * 256], writes=[Bwo[fg % 2][bi]])

    g3_rr = [0]; pt_rr = [0]
    load_wo(0)
    for fg in range(8):
        if fg + 1 < 8:
            load_wo(fg + 1)
        for fi in range(2):
            f = fg * 2 + fi
            for bi in range(3):
                gi = g3_rr[0] % 2; g3_rr[0] += 1
                r0 = MG0 + bi * 2048 + f * 128
                Sc.dma("sp", gt3[gi], proj_f[r0:r0 + 128, :], writes=[Bgt3[gi]])
                for tc in range(4):
                    sl = slice(tc * 512, (tc + 1) * 512)
                    bk = nbank()
                    mm_group(ps[bk][:, :], [(wo[fg % 2][:, kc, fi * 128:(fi + 1) * 128], yall[:, kc, sl], [Bwo[fg % 2][bi], Byall[kc]]) for kc in kranges[bi]], Bps[bk])
                    if bi == 0:
                        Sc.op("dve", lambda: V.tensor_tensor(out=macc[:, sl], in0=ps[bk][:, :], in1=gt3[gi][:, sl], op=ALU.mult), reads=[Bps[bk], Bgt3[gi]], writes=[Bmacc[tc]])
                    else:
                        pi = pt_rr[0] % 2; pt_rr[0] += 1
                        Sc.op("dve", lambda: V.tensor_tensor(out=ptmp[pi], in0=ps[bk][:, :], in1=gt3[gi][:, sl], op=ALU.mult), reads=[Bps[bk], Bgt3[gi]], writes=[Bptmp[pi]])
                        if bi == 1:
                            Sc.op("pool", lambda: P_.tensor_tensor(out=macc[:, sl], in0=macc[:, sl], in1=ptmp[pi], op=ALU.add), reads=[Bmacc[tc], Bptmp[pi]], writes=[Bmacc[tc]])
                        else:
                            Sc.op("pool", lambda: P_.tensor_tensor(out=mT[:, f, sl], in0=macc[:, sl], in1=ptmp[pi], op=ALU.add), reads=[Bmacc[tc], Bptmp[pi]], writes=[BmT[f]])
    dump("d_mT", mT, [128, 16, S], BF16, BmT)
    Sc.barrier()
    if stop_after == "merge":
        return finish_debug(nc, Sc, locals())
    AR.release(m_p3)
    wout = [AR.bf16(16, 512) for _ in range(2)]; Bwout = [[Buf() for _ in range(4)] for _ in range(2)]
    xres = [AR.f32(512) for _ in range(3)]; Bxres = [Buf() for _ in range(3)]
    ost = [AR.f32(512) for _ in range(3)]; Bost = [Buf() for _ in range(3)]
    wo_src = w_out.rearrange("(k p) n -> p k n", p=128)

    def load_wout(ng):
        for q in range(4):
            Sc.dma("pool", wout[ng % 2][:, q * 4:(q + 1) * 4, :], wo_src[:, q * 4:(q + 1) * 4, ng * 512:(ng + 1) * 512], writes=[Bwout[ng % 2][q]])

    load_wout(0)
    xr_rr = [0]
    final_toks = []
    for ng in range(4):
        if ng + 1 < 4:
            load_wout(ng + 1)
        for tb in range(16):
            xi = xr_rr[0] % 3; xr_rr[0] += 1
            Sc.dma("sp", xres[xi], x[tb * 128:(tb + 1) * 128, ng * 512:(ng + 1) * 512], writes=[Bxres[xi]])
            bk = nbank()
            mm_group(ps[bk][:, :], [(mT[:, f, tb * 128:(tb + 1) * 128], wout[ng % 2][:, f, :], [BmT[f], Bwout[ng % 2][f // 4]]) for f in range(16)], Bps[bk])
            Sc.op("dve", lambda: V.tensor_tensor(out=ost[xi], in0=ps[bk][:, :], in1=xres[xi], op=ALU.add), reads=[Bps[bk], Bxres[xi]], writes=[Bost[xi]])
            final_toks.append(Sc.dma("sp", out[tb * 128:(tb + 1) * 128, ng * 512:(ng + 1) * 512], ost[xi], reads=[Bost[xi]]))
    return finish_debug(nc, Sc, locals())


def finish_debug(nc, Sc, env):
    Sc.barrier()
    ok, stuck, _ = Sc.check_deadlock()
    if not ok:
        raise RuntimeError("logical deadlock in emitted program: %r" % (stuck,))
    Sc.close()
    for cm in reversed(env["ps_cms"]):
        cm.__exit__(None, None, None)
    env["big_cm"].__exit__(None, None, None)
    return nc


def make_in_maps(inputs):
    c = host_consts()
    f = lambda k: np.asarray(inputs[k], dtype=np.float32)
    sq = lambda k: f(k)[0]
    prm = np.zeros((128, NPRM), np.float32)

    def put(col, vec):
        m = vec.size // 128
        prm[:, col:col + m] = vec.reshape(m, 128).T

    put(NG, sq("norm_g")); put(MG_, sq("mem_norm_g")); put(GB, sq("gate_b")); put(MU, sq("rwkv_mu"))
    put(KK, sq("rwkv_k_k")); put(KA, sq("rwkv_k_a")); put(RK, sq("rwkv_r_k").reshape(-1)); put(LW, sq("rwkv_ln_w"))
    put(LB, sq("rwkv_ln_b")); put(W0, sq("rwkv_w0").reshape(-1)); put(A0, sq("rwkv_a0").reshape(-1))
    put(XQG, sq("x_q_norm_g")); put(XKG, sq("x_k_norm_g"))
    gqk = np.concatenate([np.tile(sq("attn_q_norm_g"), 12), np.tile(sq("attn_k_norm_g"), 4)])
    shared = {
        "w_in": sq("w_in"), "attn_w_o": sq("attn_w_o"), "rwkv_w_o": sq("rwkv_w_o"), "x_w_o": sq("x_w_o"),
        "x_w_kv": sq("x_w_kv"), "w_out": sq("w_out"),
        "w2cat": np.ascontiguousarray(sq("rwkv_w2").reshape(128, 768)), "a2cat": np.ascontiguousarray(sq("rwkv_a2").reshape(128, 768)),
        "prm": prm, "gqk": np.ascontiguousarray(np.broadcast_to(gqk[None, :], (128, 1024))),
        "sinkb": np.ascontiguousarray(np.broadcast_to(sq("attn_sink")[None, :], (128, 12))),
    }
    shared.update(c)
    xs = f("x"); ms = f("mem")
    return [dict(shared, x=np.ascontiguousarray(xs[b]), mem=np.ascontiguousarray(ms[b])) for b in range(xs.shape[0])]


_NC_CACHE = {}


def kernel(**inputs):
    in_maps = make_in_maps(inputs)
    if "nc" not in _NC_CACHE:
        _NC_CACHE["nc"] = build_nc()
    nc = _NC_CACHE["nc"]
    res = run_bass_kernel_spmd(nc, in_maps, core_ids=list(range(len(in_maps))))
    return np.stack([np.asarray(r["out"], dtype=np.float32) for r in res.results], axis=0)
```

```python
import math
import numpy as np
import ml_dtypes
import concourse.bass as bass
import concourse.mybir as mybir
from concourse.bass_utils import run_bass_kernel_spmd

F32 = mybir.dt.float32
BF16 = mybir.dt.bfloat16
AF = mybir.ActivationFunctionType
ALU = mybir.AluOpType
AX = mybir.AxisListType

S = 2048
D = 2048
NMEM = 256
INW = 12544
NQKV = 1280
NPF = INW - NQKV
EPS = 1e-6
GN_EPS = 64e-5
C1 = -0.5 * math.exp(-0.5)

AG0 = 0
R0 = 2048 - NQKV
K0 = R0 + 768
V0 = K0 + 768
LW0 = V0 + 768
LA0 = LW0 + 128
RG0 = 4608 - NQKV
XQ0 = 5376 - NQKV
XG0 = 5888 - NQKV
MG0 = 6400 - NQKV

NG, MG_, GB, MU, KK, KA, RK, LW, LB, W0, A0, XQG, XKG = 0, 16, 32, 80, 100, 106, 112, 118, 124, 130, 142, 154, 155
OMM, HMU, OMK, HW0, HA0, XG2 = 156, 176, 196, 202, 214, 226
NPRM = 228


class Buf:
    __slots__ = ("name", "w", "r", "pending")

    def __init__(self, name=""):
        self.name = name
        self.w = None
        self.r = {}
        self.pending = False


class Sched:
    ENG = ("pe", "act", "dve", "pool", "sp")

    def __init__(self, nc, n_dma_sems=40):
        self.nc = nc
        self.eng = {"pe": nc.tensor, "act": nc.scalar, "dve": nc.vector, "pool": nc.gpsimd, "sp": nc.sync}
        self.sem = {}
        self.cnt = {e: 0 for e in self.ENG}
        self.known = {e: {} for e in self.ENG}
        self._cms = []
        for e in self.ENG:
            cm = nc.semaphore("s_" + e)
            self.sem[e] = cm.__enter__()
            self._cms.append(cm)
        self.dsem = []
        for i in range(n_dma_sems):
            cm = nc.semaphore("d%d" % i)
            self.dsem.append([cm.__enter__(), 0])
            self._cms.append(cm)
        self.dnext = 0
        self.nwait = 0
        self.log = {e: [] for e in self.ENG}

    def close(self):
        for cm in reversed(self._cms):
            cm.__exit__(None, None, None)

    def _wait(self, e, key, semh, val):
        k = self.known[e]
        if k.get(key, 0) >= val:
            return
        self.eng[e].wait_ge(semh, val)
        self.nwait += 1
        self.log[e].append(("w", key, val))
        k[key] = val

    def wait_tok(self, e, tok):
        if tok is None:
            return
        kind, a, v = tok
        if kind == "eng":
            self._wait(e, a, self.sem[a], v)
        else:
            self._wait(e, "d%d" % a, self.dsem[a][0], v)

    def deps(self, e, reads, writes):
        for b in reads:
            self.wait_tok(e, b.w)
        for b in writes:
            self.wait_tok(e, b.w)
            for tok in b.r.values():
                self.wait_tok(e, tok)

    def op(self, e, fn, reads=(), writes=(), inc=True, setw=None):
        self.deps(e, reads, writes)
        ins = fn()
        if inc:
            self.cnt[e] += 1
            ins.then_inc(self.sem[e], 1)
            self.log[e].append(("i", e, 1))
            tok = ("eng", e, self.cnt[e])
        else:
            tok = ("eng", e, self.cnt[e] + 1)
        for b in reads:
            b.r[("eng", e)] = tok
            b.pending = False
        for b in (writes if setw is None else setw):
            b.w = tok
            b.r = {}
            b.pending = True
        for b in writes:
            b.pending = True
        return ins

    def dma(self, e, out, in_, reads=(), writes=()):
        idx = self.dnext
        self.dnext = (self.dnext + 1) % len(self.dsem)
        semh, val = self.dsem[idx]
        if val > 0:
            self._wait(e, "d%d" % idx, semh, val)
        self.deps(e, reads, writes)
        ins = self.eng[e].dma_start(out=out, in_=in_)
        val += 16
        ins.then_inc(semh, 16)
        self.log[e].append(("i", "d%d" % idx, 16))
        self.dsem[idx][1] = val
        tok = ("dma", idx, val)
        for b in reads:
            b.r[("dma", idx)] = tok
        for b in writes:
            b.w = tok
            b.r = {}
        return tok

    def check_deadlock(self):
        sem = {}
        pos = {e: 0 for e in self.ENG}
        prog = True
        while prog:
            prog = False
            for e in self.ENG:
                lg = self.log[e]
                while pos[e] < len(lg):
                    kind, key, val = lg[pos[e]]
                    if kind == "w":
                        if sem.get(key, 0) < val:
                            break
                    else:
                        sem[key] = sem.get(key, 0) + val
                    pos[e] += 1
                    prog = True
        stuck = {e: (pos[e], len(self.log[e]), self.log[e][pos[e]] if pos[e] < len(self.log[e]) else None) for e in self.ENG}
        ok = all(pos[e] == len(self.log[e]) for e in self.ENG)
        return ok, stuck, sem

    def barrier(self):
        for e in self.ENG:
            for e2 in self.ENG:
                if e2 != e and self.cnt[e2] > 0:
                    self._wait(e, e2, self.sem[e2], self.cnt[e2])
            for idx, (semh, val) in enumerate(self.dsem):
                if val > 0:
                    self._wait(e, "d%d" % idx, semh, val)


class Arena:
    def __init__(self, big, n):
        self.big = big
        self.n = n
        self.off = 0

    def mark(self):
        return self.off

    def release(self, m):
        self.off = m

    def _raw(self, nf32):
        a = self.off
        self.off += nf32
        assert self.off <= self.n, "SBUF arena overflow %d > %d" % (self.off, self.n)
        return self.big[:, a:a + nf32]

    @staticmethod
    def _shape(ap, dims):
        if len(dims) == 1:
            return ap
        if len(dims) == 2:
            return ap.rearrange("p (a b) -> p a b", b=dims[1])
        if len(dims) == 3:
            return ap.rearrange("p (a b c) -> p a b c", b=dims[1], c=dims[2])
        raise ValueError

    def f32(self, *dims):
        n = int(np.prod(dims))
        return self._shape(self._raw(n), dims)

    def bf16(self, *dims):
        n = int(np.prod(dims))
        assert n % 2 == 0
        return self._shape(self._raw(n // 2).bitcast(BF16), dims)


def host_consts():
    c = {}
    c["identf"] = np.eye(128, dtype=np.float32)
    bo = np.zeros((128, 128), np.float32)
    bo[:64, :64] = 1.0
    bo[64:, 64:] = 1.0
    c["bones"] = bo
    c["bo64"] = bo / 64.0
    half = 32
    inv = (10000.0 ** (-np.arange(half, dtype=np.float64) / half))
    ang = np.arange(S, dtype=np.float64)[:, None] * inv[None, :]
    c["cosr"] = np.ascontiguousarray(np.cos(ang).reshape(16, 128, 32).transpose(1, 0, 2)).astype(np.float32)
    c["sinr"] = np.ascontiguousarray(np.sin(ang).reshape(16, 128, 32).transpose(1, 0, 2)).astype(np.float32)
    j = np.arange(128)[:, None]
    i = np.arange(128)[None, :]
    c["maskLR"] = np.stack([(j >= i), (j <= i)], axis=1).astype(np.float32)
    t = np.arange(64)
    st_f = (t[:, None] < t[None, :]).astype(np.float32)
    in_f = (t[:, None] <= t[None, :]).astype(np.float32)
    def bd(m):
        z = np.zeros((128, 128), np.float32)
        z[:64, :64] = m
        z[64:, 64:] = m
        return z
    rwm = np.zeros((128, 2, 3, 128), np.float32)
    rwm[:, 0, 0] = bd(st_f); rwm[:, 0, 1] = bd(in_f); rwm[:, 0, 2] = bd(st_f.T)
    rwm[:, 1, 0] = bd(st_f.T); rwm[:, 1, 1] = bd(in_f.T); rwm[:, 1, 2] = bd(st_f)
    c["rwm"] = rwm
    seg = np.ones((128, 512), np.float32)
    seg[:, ::64] = 0.0
    c["segm"] = seg
    return c


def build_nc(stop_after="all", debug=()):
    nc = bass.Bass("TRN2", target_bir_lowering=False)

    def din(name, shape):
        return nc.dram_tensor(name, list(shape), F32, kind="ExternalInput").ap()

    x = din("x", [S, D]); mem = din("mem", [NMEM, D]); w_in = din("w_in", [D, INW])
    attn_w_o = din("attn_w_o", [768, D]); rwkv_w_o = din("rwkv_w_o", [768, D]); x_w_o = din("x_w_o", [512, D])
    x_w_kv = din("x_w_kv", [D, 1024]); w_out = din("w_out", [D, D])
    w2cat = din("w2cat", [128, 768]); a2cat = din("a2cat", [128, 768])
    prm_d = din("prm", [128, NPRM]); gqk_d = din("gqk", [128, 1024]); sink_d = din("sinkb", [128, 12])
    identf_d = din("identf", [128, 128]); bones_d = din("bones", [128, 128]); bo64_d = din("bo64", [128, 128])
    cos_d = din("cosr", [128, 16, 32]); sin_d = din("sinr", [128, 16, 32]); maskLR_d = din("maskLR", [128, 2, 128])
    rwm_d = din("rwm", [128, 2, 3, 128]); segm_d = din("segm", [128, 512])
    out = nc.dram_tensor("out", [S, D], F32, kind="ExternalOutput").ap()

    def dscr(name, shape, dt):
        kind = "ExternalOutput" if name in debug else "Internal"
        return nc.dram_tensor(name, list(shape), dt, kind=kind).ap()

    proj_f = dscr("proj_f", [NPF, S], F32)
    qkv_t = dscr("qkv_t", [S, NQKV], F32)
    ybuf = dscr("ybuf", [2048, S], BF16)
    dbg = dscr("dbg", [128, 4096], F32) if "dbg" in debug else None

    NBIG = 52600
    big_cm = nc.sbuf_tensor("big", [128, NBIG], F32)
    big = big_cm.__enter__()
    ps_cms = [nc.psum_tensor("ps%d" % i, [128, 512], F32) for i in range(8)]
    ps = [cm.__enter__() for cm in ps_cms]
    Bps = [Buf("ps%d" % i) for i in range(8)]
    psb = [p[:, :].bitcast(BF16) for p in ps]
    Sc = Sched(nc)
    AR = Arena(big, NBIG)
    V, A_, P_, G_, T_ = nc.vector, nc.scalar, nc.gpsimd, nc.sync, nc.tensor
    bank_rr = [0]
    dumped = set()

    def dump(name, sb_ap, shape, dt, reads):
        if name in debug and name not in dumped:
            dumped.add(name)
            t = nc.dram_tensor(name, list(shape), dt, kind="ExternalOutput").ap()
            Sc.dma("sp", t, sb_ap, reads=reads)

    def nbank(lo=0, hi=8):
        for _ in range(hi - lo):
            b = lo + bank_rr[0] % (hi - lo)
            bank_rr[0] += 1
            if not Bps[b].pending:
                return b
        raise RuntimeError("all PSUM banks in [%d,%d) hold unconsumed data" % (lo, hi))

    def mm_group(out_ap, items, obuf):
        n = len(items)
        for i, (l, r, rd) in enumerate(items):
            first, last = i == 0, i == n - 1
            Sc.op("pe", lambda: T_.matmul(out_ap, lhsT=l, rhs=r, start=first, stop=last), reads=rd,
                  writes=[obuf] if first else [], inc=last, setw=[obuf] if last else [])

    def mm_multi(items, obuf):
        n = len(items)
        for i, (o, l, r, rd) in enumerate(items):
            first, last = i == 0, i == n - 1
            Sc.op("pe", lambda: T_.matmul(o, lhsT=l, rhs=r, start=True, stop=True), reads=rd,
                  writes=[obuf] if first else [], inc=last, setw=[obuf] if last else [])

    def rsqrt_act(out_ap, in_ap, scale, eps, reads, writes, tmp_ap=None):
        t = out_ap if tmp_ap is None else tmp_ap
        Sc.op("act", lambda: A_.activation(out=t, in_=in_ap, func=AF.Ln, bias=eps, scale=scale), reads=reads, writes=writes)
        Sc.op("act", lambda: A_.activation(out=out_ap, in_=t, func=AF.Exp, scale=-0.5), reads=writes, writes=writes)

    identf = AR.f32(128); identb = AR.bf16(128); prm = AR.f32(NPRM)
    kmT = AR.bf16(4, 256); vm = AR.bf16(2, 512)
    Bid, Bprm, Bkm, Bvm = Buf("id"), Buf("prm"), Buf("kmT"), Buf("vm")
    Sc.dma("sp", identf, identf_d[:, :], writes=[Bid])
    Sc.dma("sp", prm[:, 0:OMM], prm_d[:, 0:OMM], writes=[Bprm])
    Sc.op("dve", lambda: V.tensor_copy(out=identb, in_=identf), reads=[Bid], writes=[Bid])
    Sc.op("dve", lambda: V.tensor_scalar(out=prm[:, OMM:OMM + 20], in0=prm[:, MU:MU + 20], scalar1=-1.0, scalar2=1.0, op0=ALU.mult, op1=ALU.add), reads=[Bprm], writes=[Bprm])
    Sc.op("dve", lambda: V.tensor_scalar(out=prm[:, HMU:HMU + 20], in0=prm[:, MU:MU + 20], scalar1=0.5, scalar2=None, op0=ALU.mult), reads=[Bprm], writes=[Bprm])
    Sc.op("dve", lambda: V.tensor_scalar(out=prm[:, OMK:OMK + 6], in0=prm[:, KA:KA + 6], scalar1=-1.0, scalar2=1.0, op0=ALU.mult, op1=ALU.add), reads=[Bprm], writes=[Bprm])
    Sc.op("dve", lambda: V.tensor_scalar(out=prm[:, HW0:HW0 + 24], in0=prm[:, W0:W0 + 24], scalar1=0.5, scalar2=None, op0=ALU.mult), reads=[Bprm], writes=[Bprm])
    Sc.op("dve", lambda: V.tensor_tensor(out=prm[:, XG2:XG2 + 1], in0=prm[:, XQG:XQG + 1], in1=prm[:, XKG:XKG + 1], op=ALU.mult), reads=[Bprm], writes=[Bprm])
    m_persist = AR.mark()

    hT = AR.bf16(16, S)
    BhT = [Buf("hT%d" % g) for g in range(4)]
    memT = AR.bf16(16, NMEM); BmemT = Buf("memT")
    m_ph0 = AR.mark()
    xbuf = [AR.f32(4, D) for _ in range(2)]
    Bx = [[Buf() for _ in range(4)] for _ in range(2)]
    junk = AR.f32(D); Bjunk = Buf("junk")
    evac_rr = [0]

    def build_T(src, nblk_total, gcol, dstT, dst_bufs):
        ngrp = (nblk_total + 3) // 4
        for g in range(ngrp):
            nb = min(4, nblk_total - g * 4)
            xb, bx = xbuf[g % 2], Bx[g % 2]
            ssq = AR.f32(4); rt = AR.f32(4); Bss = Buf("ss")
            Sc.op("dve", lambda: V.memset(ssq, 0.0), writes=[Bss])
            for i in range(nb):
                r0 = (g * 4 + i) * 128
                Sc.dma("sp", xb[:, i, :], src[r0:r0 + 128, :], writes=[bx[i]])
            for i in range(nb):
                Sc.op("act", lambda: A_.activation(out=junk, in_=xb[:, i, :], func=AF.Square, accum_out=ssq[:, i:i + 1]),
                      reads=[bx[i]], writes=[Bjunk, Bss])
            rsqrt_act(rt[:, 0:nb], ssq[:, 0:nb], 1.0 / D, EPS, [Bss], [Bss])
            for i in range(nb):
                Sc.op("dve", lambda: V.tensor_scalar(out=xb[:, i, :], in0=xb[:, i, :], scalar1=rt[:, i:i + 1], scalar2=None, op0=ALU.mult),
                      reads=[bx[i], Bss], writes=[bx[i]])
            for c in range(16):
                b = nbank()
                for i in range(nb):
                    Sc.op("pe", lambda: T_.transpose(out=ps[b][:, i * 128:(i + 1) * 128], in_=xb[:, i, c * 128:(c + 1) * 128], identity=identf),
                          reads=[bx[i], Bid], writes=[Bps[b]] if i == 0 else [], inc=(i == nb - 1), setw=[Bps[b]] if i == nb - 1 else [])
                dst = dstT[:, c, g * 512:g * 512 + nb * 128]
                gc = prm[:, gcol + c:gcol + c + 1]
                if evac_rr[0] % 2 == 0:
                    Sc.op("act", lambda: A_.activation(out=dst, in_=ps[b][:, 0:nb * 128], func=AF.Copy, scale=gc),
                          reads=[Bps[b], Bprm], writes=[dst_bufs[g]])
                else:
                    Sc.op("dve", lambda: V.tensor_scalar(out=dst, in0=ps[b][:, 0:nb * 128], scalar1=gc, scalar2=None, op0=ALU.mult),
                          reads=[Bps[b], Bprm], writes=[dst_bufs[g]])
                evac_rr[0] += 1

    build_T(mem, 2, MG_, memT, [BmemT])
    build_T(x, 16, NG, hT, BhT)

    Sc.barrier()
    AR.release(m_ph0)
    memT2 = memT
    wkv = AR.bf16(16, 1024); Bwkv = [Buf("wkv%d" % q) for q in range(4)]
    kmn = AR.f32(512); Bkmn = Buf("kmn")
    ssk = AR.f32(4); rk_ = AR.f32(4); Bssk = Buf("ssk")
    junk2 = AR.f32(128); Bjunk2 = Buf("junk2")
    wkv_src = x_w_kv.rearrange("(k p) n -> p k n", p=128)
    for q in range(4):
        Sc.dma("pool", wkv[:, q * 4:(q + 1) * 4, :], wkv_src[:, q * 4:(q + 1) * 4, :], writes=[Bwkv[q]])
    for mb in range(2):
        for half in range(2):
            b = nbank()
            mm_group(ps[b][:, :], [(memT2[:, k, mb * 128:(mb + 1) * 128], wkv[:, k, half * 512:(half + 1) * 512], [BmemT, Bwkv[k // 4]]) for k in range(16)], Bps[b])
            if half == 0:
                Sc.op("dve", lambda: V.memset(ssk, 0.0), writes=[Bssk])
                for h in range(4):
                    Sc.op("act", lambda: A_.activation(out=junk2, in_=ps[b][:, h * 128:(h + 1) * 128], func=AF.Square, accum_out=ssk[:, h:h + 1]),
                          reads=[Bps[b]], writes=[Bjunk2, Bssk])
                rsqrt_act(rk_, ssk, 1.0 / 128, EPS, [Bssk], [Bssk])
                for h in range(4):
                    Sc.op("dve", lambda: V.tensor_scalar(out=kmn[:, h * 128:(h + 1) * 128], in0=ps[b][:, h * 128:(h + 1) * 128], scalar1=rk_[:, h:h + 1], scalar2=None, op0=ALU.mult),
                          reads=[Bps[b], Bssk], writes=[Bkmn])
                b2 = nbank()
                for h in range(4):
                    Sc.op("pe", lambda: T_.transpose(out=ps[b2][:, h * 128:(h + 1) * 128], in_=kmn[:, h * 128:(h + 1) * 128], identity=identf),
                          reads=[Bkmn, Bid], writes=[Bps[b2]] if h == 0 else [], inc=(h == 3), setw=[Bps[b2]] if h == 3 else [])
                Sc.op("dve", lambda: V.tensor_scalar(out=kmT[:, :, mb * 128:(mb + 1) * 128], in0=ps[b2][:, :].rearrange("p (h m) -> p h m", m=128),
                                                     scalar1=prm[:, XG2:XG2 + 1], scalar2=None, op0=ALU.mult),
                      reads=[Bps[b2], Bprm], writes=[Bkm])
            else:
                Sc.op("act", lambda: A_.activation(out=vm[:, mb, :], in_=ps[b][:, :], func=AF.Copy), reads=[Bps[b]], writes=[Bvm])
    dump("d_kmT", kmT, [128, 4, 256], BF16, [Bkm]); dump("d_vm", vm, [128, 2, 512], BF16, [Bvm])
    dump("d_hT", hT, [128, 16, S], BF16, BhT)
    Sc.barrier()
    if stop_after == "hT":
        return finish_debug(nc, Sc, locals())

    AR.release(m_ph0)
    NWB = 3
    wt = [AR.bf16(16, 512) for _ in range(NWB)]
    Bwt = [[Buf("wt%d_%d" % (i, q)) for q in range(4)] for i in range(NWB)]
    stf = [AR.f32(S) for _ in range(2)]; Bstf = [Buf("stf%d" % i) for i in range(2)]
    stt = [AR.f32(512) for _ in range(3)]; Bstt = [Buf("stt%d" % i) for i in range(3)]
    Bproj = [Buf("pf%d" % i) for i in range(NPF // 128)]
    Bqkv = Buf("qkv")
    w_src = w_in.rearrange("(k p) n -> p k n", p=128)
    NT = (INW + 511) // 512

    TORDER = [0, 1, 2, 10, 11, 12] + [t for t in range(NT) if t not in (0, 1, 2, 10, 11, 12)]

    def load_w(pos):
        t = TORDER[pos]
        c0 = t * 512
        ncol = min(512, INW - c0)
        for q in range(4):
            Sc.dma("pool", wt[pos % NWB][:, q * 4:(q + 1) * 4, 0:ncol], w_src[:, q * 4:(q + 1) * 4, c0:c0 + ncol], writes=[Bwt[pos % NWB][q]])

    def act_for(feat):
        if (1280 <= feat < 2048) or (4608 <= feat < 5376) or (5888 <= feat < 6400):
            return "silu"
        if feat >= 6400:
            return "sig"
        return "copy"

    stf_rr = [0]; stt_rr = [0]; ev_rr = [0]
    import os as _os2
    TESTCOPY = bool(_os2.environ.get("TESTCOPY"))
    load_w(0); load_w(1)

    def rr(gens, weights=None):
        act_ = [[g, (weights[i] if weights else 1)] for i, g in enumerate(gens)]
        while act_:
            for ent in list(act_):
                for _ in range(ent[1]):
                    try:
                        next(ent[0])
                        yield
                    except StopIteration:
                        act_.remove(ent)
                        break

    def run(gen):
        for _ in gen:
            pass

    def proj_gen(t_lo, t_hi, bk):
      for pos in range(t_lo, t_hi):
          t = TORDER[pos]
          if pos + 2 < NT:
              load_w(pos + 2)
          c0 = t * 512
          ncol = min(512, INW - c0)
          w = wt[pos % NWB]; bw = Bwt[pos % NWB]
          ntok = max(0, min(ncol, NQKV - c0))
          if ntok > 0:
              for tb in range(16):
                  b = nbank(*bk)
                  mm_group(ps[b][:, 0:ntok], [(hT[:, k, tb * 128:(tb + 1) * 128], w[:, k, 0:ntok], [BhT[tb // 4], bw[k // 4]]) for k in range(16)], Bps[b])
                  si = stt_rr[0] % 3; stt_rr[0] += 1
                  if ev_rr[0] % 2 == 0:
                      Sc.op("act", lambda: A_.activation(out=stt[si][:, 0:ntok], in_=ps[b][:, 0:ntok], func=AF.Copy), reads=[Bps[b]], writes=[Bstt[si]])
                  else:
                      Sc.op("dve", lambda: V.tensor_copy(out=stt[si][:, 0:ntok], in_=ps[b][:, 0:ntok]), reads=[Bps[b]], writes=[Bstt[si]])
                  ev_rr[0] += 1
                  Sc.dma("sp", qkv_t[tb * 128:(tb + 1) * 128, c0:c0 + ntok], stt[si][:, 0:ntok], reads=[Bstt[si]])
                  yield
          for sub in range(ntok // 128, ncol // 128):
              feat = c0 + sub * 128
              fi = (feat - NQKV) // 128
              kind = act_for(feat)
              si = stf_rr[0] % 2; stf_rr[0] += 1
              for tc in range(4):
                  b = nbank(*bk)
                  mm_group(ps[b][:, :], [(w[:, k, sub * 128:(sub + 1) * 128], hT[:, k, tc * 512:(tc + 1) * 512], [bw[k // 4], BhT[tc]]) for k in range(16)], Bps[b])
                  dst = stf[si][:, tc * 512:(tc + 1) * 512]
                  if kind == "silu":
                      Sc.op("act", lambda: A_.activation(out=dst, in_=ps[b][:, :], func=AF.Silu), reads=[Bps[b]], writes=[Bstf[si]])
                  elif kind == "sig":
                      gcol = GB + (feat - 6400) // 128
                      Sc.op("act", lambda: A_.activation(out=dst, in_=ps[b][:, :], func=(AF.Tanh if TESTCOPY else AF.Sigmoid), bias=prm[:, gcol:gcol + 1], scale=1.0),
                            reads=[Bps[b], Bprm], writes=[Bstf[si]])
                  else:
                      if ev_rr[0] % 2 == 0:
                          Sc.op("act", lambda: A_.activation(out=dst, in_=ps[b][:, :], func=AF.Copy), reads=[Bps[b]], writes=[Bstf[si]])
                      else:
                          Sc.op("dve", lambda: V.tensor_copy(out=dst, in_=ps[b][:, :]), reads=[Bps[b]], writes=[Bstf[si]])
                      ev_rr[0] += 1
                  yield
              Sc.dma("sp", proj_f[fi * 128:(fi + 1) * 128, :], stf[si], reads=[Bstf[si]], writes=[Bproj[fi]])

    TSPLIT = 6
    run(proj_gen(0, TSPLIT, (0, 8)))
    onesf = AR.f32(128); onesb = AR.bf16(128); Bones = Buf("ones")
    Sc.op("pool", lambda: P_.memset(onesf, 1.0), writes=[Bones])
    Sc.op("pool", lambda: P_.tensor_copy(out=onesb, in_=onesf), reads=[Bones], writes=[Bones])
    qTc = [AR.f32(S) for _ in range(2)]; gtc = [AR.f32(S)] * 2
    BqTc = [Buf(), Buf()]; Bgtc = [Buf()] * 2
    sqc = AR.f32(512); sc2 = AR.f32(512); qn_c = AR.bf16(512); pTc = [AR.bf16(512) for _ in range(2)]; rden = AR.f32(512); yo = AR.f32(512)
    yxs = AR.bf16(S)
    Bsqc, Bsc2, Bqnc, BpTc, Brden, Byo, Byxs = Buf(), Buf(), Buf(), [Buf(), Buf()], Buf(), Buf(), Buf()

    c_rr = [0]

    def cbank():
        c_rr[0] += 1
        return 6 + c_rr[0] % 2

    def xattn_gen():
        for h in range(4):
            Sc.dma("sp", qTc[h % 2], proj_f[XQ0 + h * 128:XQ0 + (h + 1) * 128, :], reads=[Bproj[XQ0 // 128 + h]], writes=[BqTc[h % 2]])
            Sc.dma("sp", gtc[h % 2], proj_f[XG0 + h * 128:XG0 + (h + 1) * 128, :], reads=[Bproj[XG0 // 128 + h]], writes=[Bgtc[h % 2]])
            q_ = qTc[h % 2]; bq_ = BqTc[h % 2]
            for tc in range(4):
                sl = slice(tc * 512, (tc + 1) * 512)
                Sc.op("act", lambda: A_.activation(out=sqc, in_=q_[:, sl], func=AF.Square), reads=[bq_], writes=[Bsqc]); yield
                yield; yield
                b = cbank()
                mm_group(ps[b][:, :], [(onesf, sqc, [Bones, Bsqc])], Bps[b]); yield
                rsqrt_act(sc2, ps[b][:, :], 1.0, 128.0 * EPS, [Bps[b]], [Bsc2]); yield
                Sc.op("dve", lambda: V.tensor_tensor(out=qn_c, in0=q_[:, sl], in1=sc2, op=ALU.mult), reads=[bq_, Bsc2], writes=[Bqnc]); yield
                for mb in range(2):
                    yield; yield
                    b = cbank()
                    mm_group(ps[b][:, :], [(kmT[:, h, mb * 128:(mb + 1) * 128], qn_c, [Bkm, Bqnc])], Bps[b]); yield
                    Sc.op("act", lambda: A_.activation(out=pTc[mb], in_=ps[b][:, :], func=AF.Exp), reads=[Bps[b]], writes=[BpTc[mb]]); yield
                yield; yield
                bo_ = cbank()
                mm_group(ps[bo_][:, :], [(vm[:, mb, h * 128:(h + 1) * 128], pTc[mb], [Bvm, BpTc[mb]]) for mb in range(2)], Bps[bo_]); yield
                bd_ = cbank()
                mm_group(ps[bd_][:, :], [(onesb, pTc[mb], [Bones, BpTc[mb]]) for mb in range(2)], Bps[bd_]); yield
                Sc.op("act", lambda: A_.activation(out=rden, in_=ps[bd_][:, :], func=AF.Ln), reads=[Bps[bd_]], writes=[Brden]); yield
                Sc.op("act", lambda: A_.activation(out=rden, in_=rden, func=AF.Exp, scale=-1.0), reads=[Brden], writes=[Brden]); yield
                Sc.op("dve", lambda: V.tensor_tensor(out=yo, in0=ps[bo_][:, :], in1=rden, op=ALU.mult), reads=[Bps[bo_], Brden], writes=[Byo]); yield
                Sc.op("pool", lambda: P_.tensor_tensor(out=yxs[:, sl], in0=yo, in1=gtc[h % 2][:, sl], op=ALU.mult), reads=[Byo, Bgtc[h % 2]], writes=[Byxs]); yield
            Sc.dma("sp", ybuf[1536 + h * 128:1536 + (h + 1) * 128, :], yxs, reads=[Byxs]); yield


    run(rr([proj_gen(TSPLIT, NT, (0, 6)), xattn_gen()]))
    Sc.barrier()
    if stop_after == "proj":
        return finish_debug(nc, Sc, locals())

    AR.release(m_persist)
    bones = AR.f32(128); bo64 = AR.f32(128); rwm = AR.bf16(2, 3, 128); segm = AR.f32(512)
    w2b = AR.bf16(768); a2b = AR.bf16(768)
    Bc = Buf("rwconst")
    Sc.dma("sp", bones, bones_d[:, :], writes=[Bc])
    tb1 = Buf(); tb2 = Buf(); tb3 = Buf(); tb4 = Buf(); tb5 = Buf()
    Sc.dma("sp", bo64, bo64_d[:, :], writes=[tb1])
    Sc.dma("sp", segm, segm_d[:, :], writes=[tb2])
    Sc.dma("pool", rwm, rwm_d[:, :, :, :], writes=[tb3])
    Sc.dma("pool", w2b, w2cat[:, :], writes=[tb4])
    Sc.dma("pool", a2b, a2cat[:, :], writes=[tb5])
    lw_t = AR.bf16(S); la_s = AR.bf16(S); Blw = Buf("lw"); Bla = Buf("la")
    r_ = AR.bf16(S); k_ = AR.bf16(S); v_ = AR.bf16(S); kk_ = AR.bf16(S); bon = AR.f32(S); ysum = AR.f32(S)
    Br, Bk, Bv, Bkk, Bbon, Bys = Buf("r"), Buf("k"), Buf("v"), Buf("kk"), Buf("bon"), Buf("ysum")
    Vtok = AR.bf16(32, 128); BVtok = [Buf("vtok%d" % q) for q in range(4)]
    vbd = AR.bf16(8, 128); Bvbd = Buf("vbd")
    rkones = AR.f32(128); Brk = Buf("rkones")
    NTMP = 7168
    tmp_raw = AR._raw(NTMP)
    WD = []
    for d in range(2):
        W = {}
        for nm in ("ARt", "BKtok", "Gb", "Gk"):
            W[nm] = [AR.bf16(8, 2, 128) for _ in range(2)]
        W["Tt"] = [AR.bf16(8, 128) for _ in range(2)]
        W["Pc"] = [AR.f32(8) for _ in range(2)]
        W["BKt"] = AR.bf16(8, 2, 128)
        W["Xs"] = AR.bf16(128); W["Ubf"] = AR.bf16(128); W["Ybd"] = AR.f32(8, 128)
        W["St"] = AR.f32(128); W["Stbf"] = AR.bf16(128); W["tmpS"] = AR.f32(128); W["tot"] = AR.f32(8)
        for nm in ("BBKt", "BXs", "BUbf", "BSt", "BStbf", "BtmpS", "Btot", "BYbd"):
            W[nm] = Buf(nm)
        for nm in ("BARt", "BPc"):
            W[nm] = [Buf(nm + "0"), Buf(nm + "1")]
        for nm in ("BBKtok", "BGb", "BGk", "BTt"):
            W[nm] = [[Buf(), Buf()], [Buf(), Buf()]]
        W["BLab"] = [Buf(), Buf()]
        W["BAn"] = [[Buf(), Buf()], [Buf(), Buf()]]; W["BBn"] = [[Buf(), Buf()], [Buf(), Buf()]]
        TA = Arena(tmp_raw[:, d * (NTMP // 2):(d + 1) * (NTMP // 2)], NTMP // 2)
        for nm in ("a", "ld", "cum", "E1", "E2", "E3", "u"):
            W[nm] = TA.f32(512); W["B" + nm] = Buf(nm)
        asb = lambda ap: ap.bitcast(BF16).rearrange("p (a b) -> p a b", b=128)
        W["An"] = [asb(W["E1"]), asb(W["E2"])]; W["Bn"] = [asb(W["E3"]), asb(W["u"])]; W["Lab"] = asb(W["ld"])
        W["alias_An"] = [W["BE1"], W["BE2"]]; W["alias_Bn"] = [W["BE3"], W["Bu"]]
        WD.append(W)
    for d in range(2):
        W = WD[d]
        for jp in range(2):
            Sc.op("pool", lambda: P_.memset(W["ARt"][jp], 0.0), writes=[W["BARt"][jp]])
        Sc.op("pool", lambda: P_.memset(W["BKt"], 0.0), writes=[W["BBKt"]])
    Sc.op("pool", lambda: P_.memset(vbd, 0.0), writes=[Bvbd])

    rawc = [AR.f32(514) for _ in range(2)]; Brawc = [Buf(), Buf()]
    nbc = AR.f32(512); sqc_ = AR.f32(512); Bnbc, Bsqc_ = Buf(), Buf()
    sqk = nbc; rnk = sqc_; Bsqk, Brnk = Bnbc, Bsqc_
    gatec = AR.f32(512); dtmp = AR.f32(512); sq2 = AR.f32(512); rstd = AR.f32(512); yn = AR.f32(512); ystc = [AR.bf16(512) for _ in range(2)]
    Bgatec, Bdt, Bs2, Brs, Byn, Bystc = Buf(), Buf(), Buf(), Buf(), Buf(), [Buf(), Buf()]
    rc_rr = [0]

    def shift_gen(dst, bdst, row0, mi, func=AF.Copy):
        for tc in range(4):
            sl = slice(tc * 512, (tc + 1) * 512)
            lo = max(0, tc * 512 - 1); hi = min(S, tc * 512 + 513)
            off = lo - (tc * 512 - 1)
            ri = rc_rr[0] % 2; rc_rr[0] += 1
            rc = rawc[ri]; brc = Brawc[ri]
            if tc == 0:
                Sc.op("pool", lambda: P_.memset(rc[:, 0:1], 0.0), writes=[brc])
            if tc == 3:
                Sc.op("pool", lambda: P_.memset(rc[:, 513:514], 0.0), writes=[brc])
            Sc.dma("sp", rc[:, off:off + (hi - lo)], proj_f[row0:row0 + 128, lo:hi], writes=[brc]); yield
            Sc.op("pool", lambda: P_.tensor_tensor(out=nbc, in0=rc[:, 0:512], in1=rc[:, 2:514], op=ALU.add), reads=[brc], writes=[Bnbc]); yield
            Sc.op("act", lambda: A_.activation(out=sqc_, in_=rc[:, 1:513], func=AF.Copy, scale=prm[:, OMM + mi:OMM + mi + 1]), reads=[brc, Bprm], writes=[Bsqc_]); yield
            if func == AF.Copy:
                Sc.op("dve", lambda: V.scalar_tensor_tensor(out=dst[:, sl], in0=nbc, scalar=prm[:, HMU + mi:HMU + mi + 1], in1=sqc_, op0=ALU.mult, op1=ALU.add),
                      reads=[Bnbc, Bprm, Bsqc_], writes=[bdst]); yield
            else:
                Sc.op("dve", lambda: V.scalar_tensor_tensor(out=sqc_, in0=nbc, scalar=prm[:, HMU + mi:HMU + mi + 1], in1=sqc_, op0=ALU.mult, op1=ALU.add),
                      reads=[Bnbc, Bprm, Bsqc_], writes=[Bsqc_]); yield
                Sc.op("act", lambda: A_.activation(out=dst[:, sl], in_=sqc_, func=func), reads=[Bsqc_], writes=[bdst]); yield

    def v3(ap, n=64):
        return ap.rearrange("p (c t) -> p c t", t=n)

    def unit_prep(p, d, sc, jp):
        W = WD[d]
        sl = slice(sc * 512, (sc + 1) * 512)
        dh = slice(d * 64, (d + 1) * 64)
        pc = slice(p * 128, (p + 1) * 128)
        a, ld, cum, E1, E2, E3, u = W["a"], W["ld"], W["cum"], W["E1"], W["E2"], W["E3"], W["u"]
        Ba, Bld, Bcum, BE1, BE2, BE3, Bu = W["Ba"], W["Bld"], W["Bcum"], W["BE1"], W["BE2"], W["BE3"], W["Bu"]
        ARt, BKtok, Gb, Gk, Tt, Pc = W["ARt"][jp], W["BKtok"][jp], W["Gb"][jp], W["Gk"][jp], W["Tt"][jp], W["Pc"][jp]
        BARt, BBKtok, BGb, BGk, BTt, BPc = W["BARt"][jp], W["BBKtok"][jp], W["BGb"][jp], W["BGk"][jp], W["BTt"][jp], W["BPc"][jp]
        BKt, Lab = W["BKt"], W["Lab"]
        b = nbank(2, 8)
        mm_group(ps[b][:, :], [(a2b[dh, pc], la_s[dh, sl], [tb5, Bla])], Bps[b]); yield
        hc = HA0 + d * 6 + p
        Sc.op("act", lambda: A_.activation(out=a, in_=ps[b][:, :], func=AF.Tanh, bias=prm[:, hc:hc + 1], scale=0.5), reads=[Bps[b], Bprm], writes=[Ba]); yield
        Sc.op("dve", lambda: V.tensor_scalar(out=a, in0=a, scalar1=0.5, scalar2=0.5, op0=ALU.mult, op1=ALU.add), reads=[Ba], writes=[Ba]); yield
        b = nbank(2, 8)
        mm_group(ps[b][:, :], [(w2b[dh, pc], lw_t[dh, sl], [tb4, Blw])], Bps[b]); yield
        hc2 = HW0 + d * 6 + p
        Sc.op("act", lambda: A_.activation(out=ld, in_=ps[b][:, :], func=AF.Tanh, bias=prm[:, hc2:hc2 + 1], scale=0.5), reads=[Bps[b], Bprm], writes=[Bld] + W["BLab"]); yield
        Sc.op("dve", lambda: V.tensor_scalar(out=ld, in0=ld, scalar1=C1, scalar2=C1, op0=ALU.mult, op1=ALU.add), reads=[Bld], writes=[Bld]); yield
        Sc.op("dve", lambda: V.tensor_tensor_scan(out=cum, data0=segm, data1=ld, initial=0.0, op0=ALU.mult, op1=ALU.add), reads=[tb2, Bld], writes=[Bcum]); yield
        if d == 1:
            Sc.op("dve", lambda: V.tensor_copy(out=W["tot"], in_=v3(cum)[:, :, 63]), reads=[Bcum], writes=[W["Btot"]]); yield
            Sc.op("dve", lambda: V.tensor_tensor(out=cum, in0=ld, in1=cum, op=ALU.subtract), reads=[Bld, Bcum], writes=[Bcum]); yield
            Sc.op("dve", lambda: V.tensor_tensor(out=v3(cum), in0=v3(cum), in1=W["tot"].unsqueeze(2).to_broadcast([128, 8, 64]), op=ALU.add),
                  reads=[Bcum, W["Btot"]], writes=[Bcum]); yield
        Sc.op("dve", lambda: V.tensor_tensor(out=ld, in0=cum, in1=ld, op=ALU.subtract), reads=[Bcum, Bld], writes=[Bld]); yield
        Sc.op("act", lambda: A_.activation(out=E3, in_=ld, func=AF.Exp), reads=[Bld], writes=[BE3] + W["BBn"][0]); yield
        Sc.op("act", lambda: A_.activation(out=E1, in_=cum, func=AF.Exp), reads=[Bcum], writes=[BE1] + W["BAn"][0]); yield
        Sc.op("act", lambda: A_.activation(out=E2, in_=cum, func=AF.Exp, scale=-1.0), reads=[Bcum], writes=[BE2] + W["BAn"][1]); yield
        pcol = 63 if d == 0 else 0
        Sc.op("dve", lambda: V.tensor_copy(out=Pc, in_=v3(E1)[:, :, pcol]), reads=[BE1], writes=[BPc]); yield
        Sc.op("dve", lambda: V.tensor_scalar(out=u, in0=a, scalar1=prm[:, KA + p:KA + p + 1], scalar2=prm[:, OMK + p:OMK + p + 1], op0=ALU.mult, op1=ALU.add),
              reads=[Ba, Bprm], writes=[Bu] + W["BBn"][1]); yield
        Sc.op("dve", lambda: V.tensor_tensor(out=u, in0=k_[:, sl], in1=u, op=ALU.mult), reads=[Bk, Bu], writes=[Bu]); yield
        Sc.op("pool", lambda: P_.tensor_tensor(out=a, in0=kk_[:, sl], in1=a, op=ALU.mult), reads=[Bkk, Ba], writes=[Ba]); yield
        for half in range(2):
            hs = slice(half * 64, (half + 1) * 64)
            bc = slice(half * 64, (half + 1) * 64)
            Sc.op("dve", lambda: V.scalar_tensor_tensor(out=ARt[hs, :, 0, bc], in0=v3(kk_[hs, sl]), scalar=-1.0, in1=v3(E3[hs, :]), op0=ALU.mult, op1=ALU.mult),
                  reads=[Bkk, BE3], writes=[BARt]); yield
            Sc.op("pool", lambda: P_.tensor_tensor(out=ARt[hs, :, 1, bc], in0=v3(r_[hs, sl]), in1=v3(E1[hs, :]), op=ALU.mult), reads=[Br, BE1], writes=[BARt]); yield
            Sc.op("dve", lambda: V.tensor_tensor(out=BKt[hs, :, 1, bc], in0=v3(u[hs, :]), in1=v3(E2[hs, :]), op=ALU.mult), reads=[Bu, BE2], writes=[W["BBKt"]]); yield
            Sc.op("dve", lambda: V.tensor_tensor(out=BKt[hs, :, 0, bc], in0=v3(a[hs, :]), in1=v3(E2[hs, :]), op=ALU.mult), reads=[Ba, BE2], writes=[W["BBKt"]]); yield
        Sc.op("pool", lambda: P_.tensor_tensor(out=cum, in0=r_[:, sl], in1=u, op=ALU.mult), reads=[Br, Bu, Bcum], writes=[Bcum]); yield
        b = nbank(2, 8)
        mm_group(ps[b][:, :], [(rkones, cum, [Brk, Bcum])], Bps[b]); yield
        Sc.op("dve", lambda: V.tensor_tensor(out=cum, in0=ps[b][:, :], in1=v_[:, sl], op=ALU.mult), reads=[Bps[b], Bv, Bcum], writes=[Bcum]); yield
        Sc.op("pool", lambda: P_.tensor_tensor(out=bon[:, sl], in0=bon[:, sl], in1=cum, op=ALU.add), reads=[Bbon, Bcum], writes=[Bbon]); yield
        for hb in range(2):
            b = nbank(2, 8)
            for ci in range(4):
                for s2 in range(2):
                    j = ci * 2 + s2
                    Sc.op("pe", lambda: T_.transpose(out=psb[b][:, j * 128:(j + 1) * 128], in_=BKt[:, hb * 4 + ci, s2, :], identity=identb),
                          reads=[W["BBKt"], Bid], writes=[Bps[b]] if j == 0 else [], inc=(j == 7), setw=[Bps[b]] if j == 7 else [])
            yield
            dstv = BKtok[:, hb * 4:(hb + 1) * 4, :, :].rearrange("p a b c -> p (a b c)")
            Sc.op("act", lambda: A_.activation(out=dstv, in_=psb[b], func=AF.Copy), reads=[Bps[b]], writes=[BBKtok[hb]]); yield
        M2 = rwm[:, d, 0:2, :].rearrange("p a b -> p (a b)")
        ML = rwm[:, d, 2, :]
        for hb in range(2):
            bL = nbank(2, 8)
            mm_multi([(ps[bL][:, ci * 128:(ci + 1) * 128], ARt[:, hb * 4 + ci, 0, :], BKt[:, hb * 4 + ci, 0, :], [W["BBKt"], BARt]) for ci in range(4)], Bps[bL])
            yield
            Sc.op("dve", lambda: V.tensor_tensor(out=Lab[:, hb * 4:(hb + 1) * 4, :], in0=ps[bL][:, :].rearrange("p (a b) -> p a b", b=128),
                                                 in1=ML.unsqueeze(1).to_broadcast([128, 4, 128]), op=ALU.mult), reads=[Bps[bL], tb3], writes=[W["BLab"][hb], Bld]); yield
            bB = [nbank(2, 8), nbank(2, 8)]
            for i in range(2):
                mm_multi([(ps[bB[i]][:, q2 * 256:(q2 + 1) * 256], BKt[:, hb * 4 + i * 2 + q2, 0, :], ARt[:, hb * 4 + i * 2 + q2, :, :], [W["BBKt"], BARt]) for q2 in range(2)], Bps[bB[i]])
            yield
            for i in range(2):
                c0 = hb * 4 + i * 2
                Sc.op("dve", lambda: V.tensor_tensor(out=Gb[:, c0:c0 + 2, :, :].rearrange("p a b c -> p a (b c)"), in0=ps[bB[i]][:, :].rearrange("p (a b) -> p a b", b=256),
                                                     in1=M2.unsqueeze(1).to_broadcast([128, 2, 256]), op=ALU.mult), reads=[Bps[bB[i]], tb3], writes=[BGb[hb]]); yield
        for hb in range(2):
            cs = slice(hb * 4, (hb + 1) * 4)
            Sc.op("dve", lambda: V.tensor_tensor(out=Tt[:, cs, :], in0=Lab[:, cs, :], in1=identb.unsqueeze(1).to_broadcast([128, 4, 128]), op=ALU.add),
                  reads=[W["BLab"][hb], Bid], writes=[BTt[hb]]); yield
        for lvl in range(0, 6):
            for hb in range(2):
                cs = slice(hb * 4, (hb + 1) * 4)
                if lvl == 0:
                    Aget = lambda c: Gb[:, c, 0, :]
                    Bget = lambda c: Lab[:, c, :]
                    BAsrc, BBsrc = BGb[hb], W["BLab"][hb]
                else:
                    Aprev, Bprev = W["An"][lvl % 2], W["Bn"][lvl % 2]
                    Aget = lambda c: Aprev[:, c, :]
                    Bget = lambda c: Bprev[:, c, :]
                    BAsrc, BBsrc = W["BAn"][lvl % 2][hb], W["BBn"][lvl % 2][hb]
                Anew, Bnew = W["An"][(lvl + 1) % 2], W["Bn"][(lvl + 1) % 2]
                BAnew, BBnew = W["BAn"][(lvl + 1) % 2][hb], W["BBn"][(lvl + 1) % 2][hb]
                if lvl >= 1:
                    bT = nbank(2, 8)
                    mm_multi([(ps[bT][:, ci * 128:(ci + 1) * 128], Aget(hb * 4 + ci), Tt[:, hb * 4 + ci, :], [BAsrc, BTt[hb]]) for ci in range(4)], Bps[bT])
                    yield
                if lvl < 5:
                    if lvl < 4:
                        bA = nbank(2, 8)
                        mm_multi([(ps[bA][:, ci * 128:(ci + 1) * 128], Bget(hb * 4 + ci), Aget(hb * 4 + ci), [BAsrc, BBsrc]) for ci in range(4)], Bps[bA])
                        yield
                    bBm = nbank(2, 8)
                    mm_multi([(ps[bBm][:, ci * 128:(ci + 1) * 128], Aget(hb * 4 + ci), Bget(hb * 4 + ci), [BAsrc, BBsrc]) for ci in range(4)], Bps[bBm])
                    yield
                if lvl >= 1:
                    Sc.op("dve", lambda: V.tensor_tensor(out=Tt[:, cs, :], in0=ps[bT][:, :].rearrange("p (a b) -> p a b", b=128), in1=Tt[:, cs, :], op=ALU.add),
                          reads=[Bps[bT], BTt[hb]], writes=[BTt[hb]]); yield
                if lvl < 5:
                    if lvl < 4:
                        Sc.op("act", lambda: A_.activation(out=Anew[:, cs, :], in_=ps[bA][:, :].rearrange("p (a b) -> p a b", b=128), func=AF.Copy),
                              reads=[Bps[bA]], writes=[BAnew, W["alias_An"][(lvl + 1) % 2]]); yield
                    Sc.op("act", lambda: A_.activation(out=Bnew[:, cs, :], in_=ps[bBm][:, :].rearrange("p (a b) -> p a b", b=128), func=AF.Copy),
                          reads=[Bps[bBm]], writes=[BBnew, W["alias_Bn"][(lvl + 1) % 2]]); yield
        for hb in range(2):
            cs = slice(hb * 4, (hb + 1) * 4)
            bX = nbank(2, 8)
            mm_multi([(ps[bX][:, ci * 128:(ci + 1) * 128], Tt[:, hb * 4 + ci, :], BKtok[:, hb * 4 + ci, 0, :], [BTt[hb], BBKtok[hb]]) for ci in range(4)], Bps[bX])
            yield
            bM = nbank(2, 8)
            mm_multi([(ps[bM][:, ci * 128:(ci + 1) * 128], Tt[:, hb * 4 + ci, :], Gb[:, hb * 4 + ci, 1, :], [BTt[hb], BGb[hb]]) for ci in range(4)], Bps[bM])
            yield
            Sc.op("act", lambda: A_.activation(out=BKtok[:, cs, 0, :], in_=ps[bX][:, :].rearrange("p (a b) -> p a b", b=128), func=AF.Copy), reads=[Bps[bX]], writes=[BBKtok[hb]]); yield
            Sc.op("dve", lambda: V.tensor_copy(out=Gb[:, cs, 1, :], in_=ps[bM][:, :].rearrange("p (a b) -> p a b", b=128)), reads=[Bps[bM]], writes=[BGb[hb]]); yield
        for hb in range(2):
            bK = [nbank(2, 8), nbank(2, 8)]
            for i in range(2):
                mm_multi([(ps[bK[i]][:, q2 * 256:(q2 + 1) * 256], BKt[:, hb * 4 + i * 2 + q2, 1, :], ARt[:, hb * 4 + i * 2 + q2, :, :], [W["BBKt"], BARt]) for q2 in range(2)], Bps[bK[i]])
            yield
            for i in range(2):
                c0 = hb * 4 + i * 2
                Sc.op("dve", lambda: V.tensor_tensor(out=Gk[:, c0:c0 + 2, :, :].rearrange("p a b c -> p a (b c)"), in0=ps[bK[i]][:, :].rearrange("p (a b) -> p a b", b=256),
                                                     in1=M2.unsqueeze(1).to_broadcast([128, 2, 256]), op=ALU.mult), reads=[Bps[bK[i]], tb3], writes=[BGk[hb]]); yield

    import os as _os3
    SLK = int(_os3.environ.get('SCAN_SLACK', '0'))

    def scan_step(p, d, sc, ci, jp):
        W = WD[d]
        hb = ci // 4
        cg = sc * 8 + ci
        sb = d
        bkb = Bps[sb]
        ARt, BKtok, Gb, Gk, Tt, Pc = W["ARt"][jp], W["BKtok"][jp], W["Gb"][jp], W["Gk"][jp], W["Tt"][jp], W["Pc"][jp]
        BARt, BBKtok, BGb, BGk, BTt, BPc = W["BARt"][jp], W["BBKtok"][jp], W["BGb"][jp], W["BGk"][jp], W["BTt"][jp], W["BPc"][jp]
        St, Stbf, Xs, Ubf, tmpS = W["St"], W["Stbf"], W["Xs"], W["Ubf"], W["tmpS"]
        vt = Vtok[:, cg, :]
        bvt = BVtok[cg // 8]
        pcc = Pc[:, ci:ci + 1]
        Sc.op("act", lambda: A_.activation(out=tmpS, in_=Stbf, func=AF.Copy, scale=pcc), reads=[W["BStbf"], BPc], writes=[W["BtmpS"]]); yield
        for _s in range(SLK): yield
        mm_group(ps[sb][:, 0:128], [(ARt[:, ci, 0, :], Stbf, [BARt, W["BStbf"]]), (Gk[:, ci, 0, :], vt, [BGk[hb], bvt])], bkb); yield
        Sc.op("act", lambda: A_.activation(out=Xs, in_=ps[sb][:, 0:128], func=AF.Copy), writes=[bkb, W["BXs"]]); yield
        for _s in range(SLK): yield
        mm_group(ps[sb][:, 384:512], [(BKtok[:, ci, 0, :], Xs, [BBKtok[hb], W["BXs"]]), (BKtok[:, ci, 1, :], vt, [BBKtok[hb], bvt])], bkb); yield
        mm_group(ps[sb][:, 256:384], [(Stbf, ARt[:, ci, 1, :], [BARt, W["BStbf"]]), (Xs, Gb[:, ci, 1, :], [W["BXs"], BGb[hb]]),
                                      (vt, Gk[:, ci, 1, :], [bvt, BGk[hb]])], bkb); yield
        Sc.op("dve", lambda: V.scalar_tensor_tensor(out=Stbf, in0=ps[sb][:, 384:512], scalar=pcc, in1=tmpS, op0=ALU.mult, op1=ALU.add),
              reads=[BPc, W["BtmpS"]], writes=[bkb, W["BStbf"]]); yield
        Sc.op("act", lambda: A_.activation(out=W["Ybd"][:, ci, :], in_=ps[sb][:, 256:384], func=AF.Copy), writes=[bkb, W["BYbd"]]); yield

    def chain(p, d, sc, jp):
        order = range(8) if d == 0 else range(7, -1, -1)
        for ci in order:
            yield from scan_step(p, d, sc, ci, jp)
        W = WD[d]
        sl = slice(sc * 512, (sc + 1) * 512)
        for half in range(2):
            hs = slice(half * 64, (half + 1) * 64)
            Sc.op("pool", lambda: P_.tensor_tensor(out=v3(ysum[hs, sl]), in0=v3(ysum[hs, sl]), in1=W["Ybd"][hs, :, half * 64:(half + 1) * 64], op=ALU.add),
                  reads=[Bys, W["BYbd"]], writes=[Bys]); yield

    run(shift_gen(lw_t, Blw, LW0, 18, func=AF.Tanh))
    run(shift_gen(la_s, Bla, LA0, 19))

    PAIRS = list(range(6))
    if stop_after.startswith("rwkvp"):
        PAIRS = [int(ch) for ch in stop_after[5:]]

    def prologue(p):
        yield from shift_gen(r_, Br, R0 + p * 128, p)
        yield from shift_gen(k_, Bk, K0 + p * 128, 6 + p)
        yield from shift_gen(v_, Bv, V0 + p * 128, 12 + p)
        kcol = prm[:, KK + p:KK + p + 1]
        for tc in range(4):
            sl = slice(tc * 512, (tc + 1) * 512)
            Sc.op("act", lambda: A_.activation(out=sqk, in_=k_[:, sl], func=AF.Square, scale=kcol), reads=[Bk, Bprm], writes=[Bsqk]); yield
            b = nbank(2, 8)
            mm_group(ps[b][:, :], [(bones, sqk, [Bc, Bsqk])], Bps[b]); yield
            rsqrt_act(rnk, ps[b][:, :], 1.0, 1e-24, [Bps[b]], [Brnk]); yield
            Sc.op("dve", lambda: V.scalar_tensor_tensor(out=kk_[:, sl], in0=k_[:, sl], scalar=kcol, in1=rnk, op0=ALU.mult, op1=ALU.mult),
                  reads=[Bk, Bprm, Brnk], writes=[Bkk]); yield
        Sc.op("dve", lambda: V.tensor_scalar(out=rkones, in0=bones, scalar1=prm[:, RK + p:RK + p + 1], scalar2=None, op0=ALU.mult), reads=[Bc, Bprm], writes=[Brk]); yield

    def vtok_build(p):
        for q in range(4):
            for half in range(2):
                hs = slice(half * 64, (half + 1) * 64)
                Sc.op("pool", lambda: P_.tensor_copy(out=vbd[hs, :, half * 64:(half + 1) * 64], in_=v3(v_[hs, q * 512:(q + 1) * 512])), reads=[Bv], writes=[Bvbd]); yield
            b = nbank(2, 8)
            for j in range(8):
                Sc.op("pe", lambda: T_.transpose(out=psb[b][:, j * 128:(j + 1) * 128], in_=vbd[:, j, :], identity=identb),
                      reads=[Bvbd, Bid], writes=[Bps[b]] if j == 0 else [], inc=(j == 7), setw=[Bps[b]] if j == 7 else [])
            yield
            Sc.op("dve", lambda: V.tensor_copy(out=Vtok[:, q * 8:(q + 1) * 8, :].rearrange("p a b -> p (a b)"), in_=psb[b]), reads=[Bps[b]], writes=[BVtok[q]]); yield

    def resets(p):
        Sc.op("pool", lambda: P_.memset(bon, 0.0), writes=[Bbon])
        Sc.op("pool", lambda: P_.memset(ysum, 0.0), writes=[Bys])
        for d in range(2):
            W = WD[d]
            Sc.op("pool", lambda: P_.memset(W["St"], 0.0), writes=[W["BSt"]])
            Sc.op("pool", lambda: P_.memset(W["Stbf"], 0.0), writes=[W["BStbf"]])

    def epilogue(p):
        dump("d_ysum%d" % p, ysum, [128, S], F32, [Bys])
        for tc in range(4):
            sl = slice(tc * 512, (tc + 1) * 512)
            yc = ystc[tc % 2]; byc = Bystc[tc % 2]
            Sc.dma("sp", gatec, proj_f[RG0 + p * 128:RG0 + (p + 1) * 128, sl], writes=[Bgatec])
            b = nbank(2, 8)
            mm_group(ps[b][:, :], [(bo64, ysum[:, sl], [tb1, Bys])], Bps[b]); yield
            Sc.op("dve", lambda: V.tensor_tensor(out=dtmp, in0=ysum[:, sl], in1=ps[b][:, :], op=ALU.subtract), reads=[Bys, Bps[b]], writes=[Bdt]); yield
            Sc.op("act", lambda: A_.activation(out=sq2, in_=dtmp, func=AF.Square), reads=[Bdt], writes=[Bs2]); yield
            b2 = nbank(2, 8)
            mm_group(ps[b2][:, :], [(bo64, sq2, [tb1, Bs2])], Bps[b2]); yield
            rsqrt_act(rstd, ps[b2][:, :], 1.0, GN_EPS, [Bps[b2]], [Brs]); yield
            Sc.op("dve", lambda: V.tensor_tensor(out=yn, in0=dtmp, in1=rstd, op=ALU.mult), reads=[Bdt, Brs], writes=[Byn]); yield
            Sc.op("dve", lambda: V.tensor_scalar(out=yn, in0=yn, scalar1=prm[:, LW + p:LW + p + 1], scalar2=prm[:, LB + p:LB + p + 1], op0=ALU.mult, op1=ALU.add),
                  reads=[Byn, Bprm], writes=[Byn]); yield
            Sc.op("pool", lambda: P_.tensor_tensor(out=yn, in0=yn, in1=bon[:, sl], op=ALU.add), reads=[Byn, Bbon], writes=[Byn]); yield
            Sc.op("dve", lambda: V.tensor_tensor(out=yc, in0=yn, in1=gatec, op=ALU.mult), reads=[Byn, Bgatec], writes=[byc]); yield
            Sc.dma("sp", ybuf[768 + p * 128:768 + (p + 1) * 128, sl], yc, reads=[byc]); yield

    def preps(p, j):
        return [unit_prep(p, 0, j, j % 2), unit_prep(p, 1, 3 - j, j % 2)]

    Sc.barrier()
    run(prologue(PAIRS[0]))
    run(vtok_build(PAIRS[0]))
    resets(PAIRS[0])
    run(rr(preps(PAIRS[0], 0)))
    for i, p in enumerate(PAIRS):
        nxt = PAIRS[i + 1] if i + 1 < len(PAIRS) else None
        for j in range(4):
            gens = [chain(p, 0, j, j % 2), chain(p, 1, 3 - j, j % 2)]
            if j < 3:
                gens += preps(p, j + 1)
            elif nxt is not None:
                gens.append(prologue(nxt))
            run(rr(gens))
        gens = [epilogue(p)]
        if nxt is not None:
            gens.append(vtok_build(nxt))
        run(rr(gens))
        if nxt is not None:
            resets(nxt)
            run(rr(preps(nxt, 0)))
    Sc.barrier()
    if stop_after.startswith("rwkv"):
        return finish_debug(nc, Sc, locals())

    AR.release(m_persist)
    cosr = AR.f32(16, 32); sinr = AR.f32(16, 32); gqk = AR.f32(16, 64); esink = AR.f32(12); mLR = AR.bf16(2, 128)
    Bcs, Bsn, Bgq, Bes, Bml = Buf(), Buf(), Buf(), Buf(), Buf()
    Sc.dma("sp", cosr, cos_d[:, :, :], writes=[Bcs]); Sc.dma("sp", sinr, sin_d[:, :, :], writes=[Bsn])
    Sc.dma("sp", gqk, gqk_d.rearrange("p (a b) -> p a b", b=64), writes=[Bgq]); Sc.dma("sp", esink, sink_d[:, :], writes=[Bes])
    Sc.dma("pool", mLR, maskLR_d[:, :, :], writes=[Bml])
    Sc.op("act", lambda: A_.activation(out=esink, in_=esink, func=AF.Exp), reads=[Bes], writes=[Bes])
    qTa = AR.bf16(12, S); kTa = AR.bf16(4, S); vext = AR.bf16(16, 4, 128); ya = AR.bf16(6, S)
    BqTa = [Buf() for _ in range(16)]; BkTa = [Buf() for _ in range(16)]; Bvx = [Buf() for _ in range(16)]; Bya = [Buf() for _ in range(16)]
    Sc.op("pool", lambda: P_.memset(vext, 1.0), writes=Bvx)
    qkv = [AR.f32(NQKV) for _ in range(2)]; Bqkv2 = [Buf(), Buf()]
    gta = [AR.f32(6, 128) for _ in range(6)]; Bgta = [Buf() for _ in range(6)]
    sqa2 = [AR.f32(1024) for _ in range(2)]; ssa2 = [AR.f32(16) for _ in range(2)]; rsa2 = [AR.f32(16) for _ in range(2)]
    qna2 = [AR.f32(16, 64) for _ in range(2)]; qra2 = [AR.bf16(16, 64) for _ in range(2)]
    rt2 = [[AR.f32(16, 32) for _ in range(4)] for _ in range(2)]
    Bsqa2, Bssa2, Bqna2, Bqra2 = [Buf(), Buf()], [Buf(), Buf()], [Buf(), Buf()], [Buf(), Buf()]
    Brt2 = [[Buf() for _ in range(4)] for _ in range(2)]
    pTa = [AR.bf16(384) for _ in range(6)]; BpTa = [Buf() for _ in range(6)]
    rda = AR.f32(4, 128); Brda = Buf(); yodd = AR.f32(2, 128); Byodd = Buf(); yev = AR.f32(2, 128); Byev = Buf()
    Bo = [Buf("o5"), Buf("o6"), Buf("o7")]
    pta_rr = [0]
    ag_src = proj_f[AG0:AG0 + 768, :].rearrange("(c p) t -> p c t", p=128)

    def prep_blk(tb):
        qk_ = qkv[tb % 2]; bqk = Bqkv2[tb % 2]
        sqa, ssa, rsa, qna, qra, rt_ = sqa2[tb % 2], ssa2[tb % 2], rsa2[tb % 2], qna2[tb % 2], qra2[tb % 2], rt2[tb % 2]
        Bsqa, Bssa, Bqna, Bqra, Brt = Bsqa2[tb % 2], Bssa2[tb % 2], Bqna2[tb % 2], Bqra2[tb % 2], Brt2[tb % 2]
        Sc.dma("sp", qk_, qkv_t[tb * 128:(tb + 1) * 128, :], writes=[bqk])
        Sc.dma("sp", gta[tb % 6], ag_src[:, :, tb * 128:(tb + 1) * 128], writes=[Bgta[tb % 6]])
        Sc.op("act", lambda: A_.activation(out=sqa, in_=qk_[:, 0:1024], func=AF.Square), reads=[bqk], writes=[Bsqa]); yield
        Sc.op("dve", lambda: V.tensor_reduce(out=ssa, in_=sqa.rearrange("p (a b) -> p a b", b=64), axis=AX.X, op=ALU.add), reads=[Bsqa], writes=[Bssa]); yield
        rsqrt_act(rsa, ssa, 1.0 / 64, EPS, [Bssa], [Bssa]); yield
        Sc.op("dve", lambda: V.tensor_tensor(out=qna, in0=qk_[:, 0:1024].rearrange("p (a b) -> p a b", b=64), in1=rsa.unsqueeze(2).to_broadcast([128, 16, 64]), op=ALU.mult),
              reads=[bqk, Bssa], writes=[Bqna]); yield
        Sc.op("pool", lambda: P_.tensor_tensor(out=qna, in0=qna, in1=gqk, op=ALU.mult), reads=[Bqna, Bgq], writes=[Bqna]); yield
        t1 = qna[:, :, 0:32]; t2 = qna[:, :, 32:64]
        cb = cosr[:, tb, :].unsqueeze(1).to_broadcast([128, 16, 32]); sb_ = sinr[:, tb, :].unsqueeze(1).to_broadcast([128, 16, 32])
        Sc.op("dve", lambda: V.tensor_tensor(out=rt_[0], in0=t1, in1=cb, op=ALU.mult), reads=[Bqna, Bcs], writes=[Brt[0]]); yield
        Sc.op("pool", lambda: P_.tensor_tensor(out=rt_[1], in0=t2, in1=sb_, op=ALU.mult), reads=[Bqna, Bsn], writes=[Brt[1]]); yield
        Sc.op("dve", lambda: V.tensor_tensor(out=qra[:, :, 0:32], in0=rt_[0], in1=rt_[1], op=ALU.subtract), reads=[Brt[0], Brt[1]], writes=[Bqra]); yield
        Sc.op("pool", lambda: P_.tensor_tensor(out=rt_[2], in0=t2, in1=cb, op=ALU.mult), reads=[Bqna, Bcs], writes=[Brt[2]]); yield
        Sc.op("dve", lambda: V.tensor_tensor(out=rt_[3], in0=t1, in1=sb_, op=ALU.mult), reads=[Bqna, Bsn], writes=[Brt[3]]); yield
        Sc.op("dve", lambda: V.tensor_tensor(out=qra[:, :, 32:64], in0=rt_[2], in1=rt_[3], op=ALU.add), reads=[Brt[2], Brt[3]], writes=[Bqra]); yield
        Sc.op("act", lambda: A_.activation(out=vext[:, tb, :, 0:64], in_=qk_[:, 1024:1280].rearrange("p (a b) -> p a b", b=64), func=AF.Copy), reads=[bqk], writes=[Bvx[tb]]); yield
        b = [0, 6][tb % 2]
        qflat = qra.rearrange("p a b -> p (a b)")
        for j in range(8):
            Sc.op("pe", lambda: T_.transpose(out=psb[b][:, j * 128:(j + 1) * 128], in_=qflat[:, j * 128:(j + 1) * 128], identity=identb),
                  reads=[Bqra, Bid], writes=[Bps[b]] if j == 0 else [], inc=(j == 7), setw=[Bps[b]] if j == 7 else [])
        yield
        psv = psb[b].rearrange("p (a b) -> p a b", b=128)
        tsl = slice(tb * 128, (tb + 1) * 128)
        qv = qTa.rearrange("p (h two) t -> p h two t", two=2)
        kv = kTa.rearrange("p (h two) t -> p h two t", two=2)
        Sc.op("dve", lambda: V.tensor_copy(out=qv[0:64, :, 0, tsl], in_=psv[0:64, 0:6, :]), reads=[Bps[b]], writes=[BqTa[tb]]); yield
        Sc.op("act", lambda: A_.activation(out=qv[0:64, :, 1, tsl], in_=psv[64:128, 0:6, :], func=AF.Copy), reads=[Bps[b]], writes=[BqTa[tb]]); yield
        Sc.op("dve", lambda: V.tensor_copy(out=kv[0:64, :, 0, tsl], in_=psv[0:64, 6:8, :]), reads=[Bps[b]], writes=[BkTa[tb]]); yield
        Sc.op("act", lambda: A_.activation(out=kv[0:64, :, 1, tsl], in_=psv[64:128, 6:8, :], func=AF.Copy), reads=[Bps[b]], writes=[BkTa[tb]]); yield

    def attend(n):
        qsl = slice(n * 128, (n + 1) * 128)
        gt_ = gta[n % 6]; bgt = Bgta[n % 6]
        seq = []
        for g in range(4):
            kbs = [kb for kb in (n - 1, n, n + 1) if 0 <= kb < 16]
            for kb in kbs:
                seq.append((g, kb, kb == kbs[0], kb == kbs[-1]))
        order = []
        for (g, kb, fst, lst) in seq:
            for hh in range(3):
                order.append((g, kb, hh, fst, lst))
        firsts = {}; lasts = {}
        for idx, (g, kb, hh, fst, lst) in enumerate(order):
            ob = (3 * g + hh) // 4
            firsts.setdefault(ob, idx); lasts[ob] = idx
        idx = 0
        LOOK = 2
        issued = {}

        def score(i):
            g, kb, fst, lst = seq[i]
            b = sbanks[pta_rr[0] % len(sbanks)]
            pi = pta_rr[0] % 6; pta_rr[0] += 1
            mm_group(ps[b][:, 0:384], [(kTa[0:64, g, kb * 128:(kb + 1) * 128], qTa[0:64, 3 * g:3 * g + 3, qsl], [BkTa[kb], BqTa[n]])], Bps[b])
            issued[i] = (b, pi)

        for i in range(min(LOOK, len(seq))):
            score(i)
        yield
        for i, (g, kb, fst, lst) in enumerate(seq):
            if i + LOOK < len(seq):
                score(i + LOOK)
                yield
            b, pi = issued.pop(i)
            Sc.op("act", lambda: A_.activation(out=pTa[pi], in_=ps[b][:, 0:384], func=AF.Exp, scale=0.125), reads=[Bps[b]], writes=[BpTa[pi]]); yield
            if kb != n:
                mk = mLR[:, 0 if kb < n else 1, :].unsqueeze(1).to_broadcast([128, 3, 128])
                Sc.op("dve", lambda: V.tensor_tensor(out=pTa[pi].rearrange("p (a b) -> p a b", b=128), in0=pTa[pi].rearrange("p (a b) -> p a b", b=128), in1=mk, op=ALU.mult),
                      reads=[BpTa[pi], Bml], writes=[BpTa[pi]]); yield
            for hh in range(3):
                head = 3 * g + hh
                ob = head // 4
                col = (head % 4) * 128
                isf = firsts[ob] == idx; isl = lasts[ob] == idx
                st_flag = fst and (hh == 0 or head % 4 == 0)
                Sc.op("pe", lambda: T_.matmul(ps[obanks[ob]][:, col:col + 128], lhsT=vext[:, kb, g, :], rhs=pTa[pi][:, hh * 128:(hh + 1) * 128], start=st_flag, stop=lst),
                      reads=[Bvx[kb], BpTa[pi]], writes=[Bo[ob]] if isf else [], inc=(hh == 2), setw=[Bo[ob]] if isl else [])
                idx += 1
            yield
        for ob in range(3):
            pv = ps[obanks[ob]][:, :].rearrange("p (a b) -> p a b", b=128)
            Sc.op("dve", lambda: V.tensor_tensor(out=rda[0:64, :, :], in0=pv[64:128, :, :], in1=esink[64:128, ob * 4:(ob + 1) * 4].unsqueeze(2).to_broadcast([64, 4, 128]), op=ALU.add),
                  reads=[Bo[ob], Bes], writes=[Brda]); yield
            Sc.op("act", lambda: A_.activation(out=rda[0:64, :, :], in_=rda[0:64, :, :], func=AF.Ln), reads=[Brda], writes=[Brda]); yield
            Sc.op("act", lambda: A_.activation(out=rda[0:64, :, :], in_=rda[0:64, :, :], func=AF.Exp, scale=-1.0), reads=[Brda], writes=[Brda]); yield
            pv2 = pv.rearrange("p (h two) t -> p h two t", two=2)
            rd2 = rda.rearrange("p (h two) t -> p h two t", two=2)
            c0 = ob * 2
            Sc.op("dve", lambda: V.tensor_tensor(out=yev[0:64, :, :], in0=pv2[0:64, :, 0, :], in1=rd2[0:64, :, 0, :], op=ALU.mult), reads=[Bo[ob], Brda], writes=[Byev]); yield
            Sc.op("dve", lambda: V.tensor_tensor(out=yodd[64:128, :, :], in0=pv2[0:64, :, 1, :], in1=rd2[0:64, :, 1, :], op=ALU.mult), reads=[Bo[ob], Brda], writes=[Byodd]); yield
            Sc.op("pool", lambda: P_.tensor_tensor(out=ya[0:64, c0:c0 + 2, qsl], in0=yev[0:64, :, :], in1=gt_[0:64, c0:c0 + 2, :], op=ALU.mult), reads=[Byev, bgt], writes=[Bya[n]]); yield
            Sc.op("pool", lambda: P_.tensor_tensor(out=ya[64:128, c0:c0 + 2, qsl], in0=yodd[64:128, :, :], in1=gt_[64:128, c0:c0 + 2, :], op=ALU.mult), reads=[Byodd, bgt], writes=[Bya[n]]); yield

    sbanks = [1, 2, 7]
    obanks = [3, 4, 5]

    def seq_(gs):
        for g in gs:
            yield from g

    def attn_driver():
        for w in range(8):
            gens = [prep_blk(2 * w), prep_blk(2 * w + 1)]
            att = [attend(n) for n in (2 * w - 3, 2 * w - 2) if n >= 0]
            if att:
                gens.append(seq_(att))
            yield from rr(gens, [1, 1, 2])
        yield from attend(13)
        yield from attend(14)
        yield from attend(15)
        for c in range(6):
            Sc.dma("sp", ybuf[c * 128:(c + 1) * 128, :], ya[:, c, :], reads=Bya); yield

    run(attn_driver())
    Sc.barrier()
    if stop_after in ("attn", "xattn"):
        return finish_debug(nc, Sc, locals())

    AR.release(m_persist)
    mT = AR.bf16(16, S); BmT = [Buf() for _ in range(16)]
    m_p3 = AR.mark()
    yall = AR.bf16(16, S); Byall = [Buf() for _ in range(16)]
    wo = [AR.bf16(16, 256) for _ in range(2)]; Bwo = [[Buf(), Buf(), Buf()] for _ in range(2)]
    gt3 = [AR.f32(S) for _ in range(2)]; Bgt3 = [Buf(), Buf()]
    macc = AR.f32(S); Bmacc = [Buf() for _ in range(4)]
    ptmp = [AR.f32(512) for _ in range(2)]; Bptmp = [Buf(), Buf()]
    for k in range(16):
        Sc.dma("sp", yall[:, k, :], ybuf[k * 128:(k + 1) * 128, :], writes=[Byall[k]])
    wsrcs = [(attn_w_o.rearrange("(k p) n -> p k n", p=128), 0, 6), (rwkv_w_o.rearrange("(k p) n -> p k n", p=128), 6, 6), (x_w_o.rearrange("(k p) n -> p k n", p=128), 12, 4)]
    kranges = [range(0, 6), range(6, 12), range(12, 16)]

    def load_wo(fg):
        for bi, (src, k0, nk) in enumerate(wsrcs):
            Sc.dma("pool", wo[fg % 2][:, k0:k0 + nk, :], src[:, :, fg * 256:(fg + 1) * 256], writes=[Bwo[fg % 2][bi]])

    g3_rr = [0]; pt_rr = [0]
    load_wo(0)
    for fg in range(8):
        if fg + 1 < 8:
            load_wo(fg + 1)
        for fi in range(2):
            f = fg * 2 + fi
            for bi in range(3):
                gi = g3_rr[0] % 2; g3_rr[0] += 1
                r0 = MG0 + bi * 2048 + f * 128
                Sc.dma("sp", gt3[gi], proj_f[r0:r0 + 128, :], writes=[Bgt3[gi]])
                for tc in range(4):
                    sl = slice(tc * 512, (tc + 1) * 512)
                    bk = nbank()
                    mm_group(ps[bk][:, :], [(wo[fg % 2][:, kc, fi * 128:(fi + 1) * 128], yall[:, kc, sl], [Bwo[fg % 2][bi], Byall[kc]]) for kc in kranges[bi]], Bps[bk])
                    if bi == 0:
                        Sc.op("dve", lambda: V.tensor_tensor(out=macc[:, sl], in0=ps[bk][:, :], in1=gt3[gi][:, sl], op=ALU.mult), reads=[Bps[bk], Bgt3[gi]], writes=[Bmacc[tc]])
                    else:
                        pi = pt_rr[0] % 2; pt_rr[0] += 1
                        Sc.op("dve", lambda: V.tensor_tensor(out=ptmp[pi], in0=ps[bk][:, :], in1=gt3[gi][:, sl], op=ALU.mult), reads=[Bps[bk], Bgt3[gi]], writes=[Bptmp[pi]])
                        if bi == 1:
                            Sc.op("pool", lambda: P_.tensor_tensor(out=macc[:, sl], in0=macc[:, sl], in1=ptmp[pi], op=ALU.add), reads=[Bmacc[tc], Bptmp[pi]], writes=[Bmacc[tc]])
                        else:
                            Sc.op("pool", lambda: P_.tensor_tensor(out=mT[:, f, sl], in0=macc[:, sl], in1=ptmp[pi], op=ALU.add), reads=[Bmacc[tc], Bptmp[pi]], writes=[BmT[f]])
    dump("d_mT", mT, [128, 16, S], BF16, BmT)
    Sc.barrier()
    if stop_after == "merge":
        return finish_debug(nc, Sc, locals())
    AR.release(m_p3)
    wout = [AR.bf16(16, 512) for _ in range(2)]; Bwout = [[Buf() for _ in range(4)] for _ in range(2)]
    xres = [AR.f32(512) for _ in range(3)]; Bxres = [Buf() for _ in range(3)]
    ost = [AR.f32(512) for _ in range(3)]; Bost = [Buf() for _ in range(3)]
    wo_src = w_out.rearrange("(k p) n -> p k n", p=128)

    def load_wout(ng):
        for q in range(4):
            Sc.dma("pool", wout[ng % 2][:, q * 4:(q + 1) * 4, :], wo_src[:, q * 4:(q + 1) * 4, ng * 512:(ng + 1) * 512], writes=[Bwout[ng % 2][q]])

    load_wout(0)
    xr_rr = [0]
    final_toks = []
    for ng in range(4):
        if ng + 1 < 4:
            load_wout(ng + 1)
        for tb in range(16):
            xi = xr_rr[0] % 3; xr_rr[0] += 1
            Sc.dma("sp", xres[xi], x[tb * 128:(tb + 1) * 128, ng * 512:(ng + 1) * 512], writes=[Bxres[xi]])
            bk = nbank()
            mm_group(ps[bk][:, :], [(mT[:, f, tb * 128:(tb + 1) * 128], wout[ng % 2][:, f, :], [BmT[f], Bwout[ng % 2][f // 4]]) for f in range(16)], Bps[bk])
            Sc.op("dve", lambda: V.tensor_tensor(out=ost[xi], in0=ps[bk][:, :], in1=xres[xi], op=ALU.add), reads=[Bps[bk], Bxres[xi]], writes=[Bost[xi]])
            final_toks.append(Sc.dma("sp", out[tb * 128:(tb + 1) * 128, ng * 512:(ng + 1) * 512], ost[xi], reads=[Bost[xi]]))
    return finish_debug(nc, Sc, locals())


def finish_debug(nc, Sc, env):
    Sc.barrier()
    ok, stuck, _ = Sc.check_deadlock()
    if not ok:
        raise RuntimeError("logical deadlock in emitted program: %r" % (stuck,))
    Sc.close()
    for cm in reversed(env["ps_cms"]):
        cm.__exit__(None, None, None)
    env["big_cm"].__exit__(None, None, None)
    return nc


def make_in_maps(inputs):
    c = host_consts()
    f = lambda k: np.asarray(inputs[k], dtype=np.float32)
    sq = lambda k: f(k)[0]
    prm = np.zeros((128, NPRM), np.float32)

    def put(col, vec):
        m = vec.size // 128
        prm[:, col:col + m] = vec.reshape(m, 128).T

    put(NG, sq("norm_g")); put(MG_, sq("mem_norm_g")); put(GB, sq("gate_b")); put(MU, sq("rwkv_mu"))
    put(KK, sq("rwkv_k_k")); put(KA, sq("rwkv_k_a")); put(RK, sq("rwkv_r_k").reshape(-1)); put(LW, sq("rwkv_ln_w"))
    put(LB, sq("rwkv_ln_b")); put(W0, sq("rwkv_w0").reshape(-1)); put(A0, sq("rwkv_a0").reshape(-1))
    put(XQG, sq("x_q_norm_g")); put(XKG, sq("x_k_norm_g"))
    gqk = np.concatenate([np.tile(sq("attn_q_norm_g"), 12), np.tile(sq("attn_k_norm_g"), 4)])
    shared = {
        "w_in": sq("w_in"), "attn_w_o": sq("attn_w_o"), "rwkv_w_o": sq("rwkv_w_o"), "x_w_o": sq("x_w_o"),
        "x_w_kv": sq("x_w_kv"), "w_out": sq("w_out"),
        "w2cat": np.ascontiguousarray(sq("rwkv_w2").reshape(128, 768)), "a2cat": np.ascontiguousarray(sq("rwkv_a2").reshape(128, 768)),
        "prm": prm, "gqk": np.ascontiguousarray(np.broadcast_to(gqk[None, :], (128, 1024))),
        "sinkb": np.ascontiguousarray(np.broadcast_to(sq("attn_sink")[None, :], (128, 12))),
    }
    shared.update(c)
    xs = f("x"); ms = f("mem")
    return [dict(shared, x=np.ascontiguousarray(xs[b]), mem=np.ascontiguousarray(ms[b])) for b in range(xs.shape[0])]


_NC_CACHE = {}


def kernel(**inputs):
    in_maps = make_in_maps(inputs)
    if "nc" not in _NC_CACHE:
        _NC_CACHE["nc"] = build_nc()
    nc = _NC_CACHE["nc"]
    res = run_bass_kernel_spmd(nc, in_maps, core_ids=list(range(len(in_maps))))
    return np.stack([np.asarray(r["out"], dtype=np.float32) for r in res.results], axis=0)
```

```python
import math
import numpy as np
import ml_dtypes
import concourse.bass as bass
import concourse.mybir as mybir
from concourse.bass_utils import run_bass_kernel_spmd

F32 = mybir.dt.float32
BF16 = mybir.dt.bfloat16
AF = mybir.ActivationFunctionType
ALU = mybir.AluOpType
AX = mybir.AxisListType

S = 2048
D = 2048
NMEM = 256
INW = 12544
NQKV = 1280
NPF = INW - NQKV
EPS = 1e-6
GN_EPS = 64e-5
C1 = -0.5 * math.exp(-0.5)

AG0 = 0
R0 = 2048 - NQKV
K0 = R0 + 768
V0 = K0 + 768
LW0 = V0 + 768
LA0 = LW0 + 128
RG0 = 4608 - NQKV
XQ0 = 5376 - NQKV
XG0 = 5888 - NQKV
MG0 = 6400 - NQKV

NG, MG_, GB, MU, KK, KA, RK, LW, LB, W0, A0, XQG, XKG = 0, 16, 32, 80, 100, 106, 112, 118, 124, 130, 142, 154, 155
OMM, HMU, OMK, HW0, HA0, XG2 = 156, 176, 196, 202, 214, 226
NPRM = 228


class Buf:
    __slots__ = ("name", "w", "r", "pending")

    def __init__(self, name=""):
        self.name = name
        self.w = None
        self.r = {}
        self.pending = False


class Sched:
    ENG = ("pe", "act", "dve", "pool", "sp")

    def __init__(self, nc, n_dma_sems=40):
        self.nc = nc
        self.eng = {"pe": nc.tensor, "act": nc.scalar, "dve": nc.vector, "pool": nc.gpsimd, "sp": nc.sync}
        self.sem = {}
        self.cnt = {e: 0 for e in self.ENG}
        self.known = {e: {} for e in self.ENG}
        self._cms = []
        for e in self.ENG:
            cm = nc.semaphore("s_" + e)
            self.sem[e] = cm.__enter__()
            self._cms.append(cm)
        self.dsem = []
        for i in range(n_dma_sems):
            cm = nc.semaphore("d%d" % i)
            self.dsem.append([cm.__enter__(), 0])
            self._cms.append(cm)
        self.dnext = 0
        self.nwait = 0
        self.log = {e: [] for e in self.ENG}

    def close(self):
        for cm in reversed(self._cms):
            cm.__exit__(None, None, None)

    def _wait(self, e, key, semh, val):
        k = self.known[e]
        if k.get(key, 0) >= val:
            return
        self.eng[e].wait_ge(semh, val)
        self.nwait += 1
        self.log[e].append(("w", key, val))
        k[key] = val

    def wait_tok(self, e, tok):
        if tok is None:
            return
        kind, a, v = tok
        if kind == "eng":
            self._wait(e, a, self.sem[a], v)
        else:
            self._wait(e, "d%d" % a, self.dsem[a][0], v)

    def deps(self, e, reads, writes):
        for b in reads:
            self.wait_tok(e, b.w)
        for b in writes:
            self.wait_tok(e, b.w)
            for tok in b.r.values():
                self.wait_tok(e, tok)

    def op(self, e, fn, reads=(), writes=(), inc=True, setw=None):
        self.deps(e, reads, writes)
        ins = fn()
        if inc:
            self.cnt[e] += 1
            ins.then_inc(self.sem[e], 1)
            self.log[e].append(("i", e, 1))
            tok = ("eng", e, self.cnt[e])
        else:
            tok = ("eng", e, self.cnt[e] + 1)
        for b in reads:
            b.r[("eng", e)] = tok
            b.pending = False
        for b in (writes if setw is None else setw):
            b.w = tok
            b.r = {}
            b.pending = True
        for b in writes:
            b.pending = True
        return ins

    def dma(self, e, out, in_, reads=(), writes=()):
        idx = self.dnext
        self.dnext = (self.dnext + 1) % len(self.dsem)
        semh, val = self.dsem[idx]
        if val > 0:
            self._wait(e, "d%d" % idx, semh, val)
        self.deps(e, reads, writes)
        ins = self.eng[e].dma_start(out=out, in_=in_)
        val += 16
        ins.then_inc(semh, 16)
        self.log[e].append(("i", "d%d" % idx, 16))
        self.dsem[idx][1] = val
        tok = ("dma", idx, val)
        for b in reads:
            b.r[("dma", idx)] = tok
        for b in writes:
            b.w = tok
            b.r = {}
        return tok

    def check_deadlock(self):
        sem = {}
        pos = {e: 0 for e in self.ENG}
        prog = True
        while prog:
            prog = False
            for e in self.ENG:
                lg = self.log[e]
                while pos[e] < len(lg):
                    kind, key, val = lg[pos[e]]
                    if kind == "w":
                        if sem.get(key, 0) < val:
                            break
                    else:
                        sem[key] = sem.get(key, 0) + val
                    pos[e] += 1
                    prog = True
        stuck = {e: (pos[e], len(self.log[e]), self.log[e][pos[e]] if pos[e] < len(self.log[e]) else None) for e in self.ENG}
        ok = all(pos[e] == len(self.log[e]) for e in self.ENG)
        return ok, stuck, sem

    def barrier(self):
        for e in self.ENG:
            for e2 in self.ENG:
                if e2 != e and self.cnt[e2] > 0:
                    self._wait(e, e2, self.sem[e2], self.cnt[e2])
            for idx, (semh, val) in enumerate(self.dsem):
                if val > 0:
                    self._wait(e, "d%d" % idx, semh, val)


class Arena:
    def __init__(self, big, n):
        self.big = big
        self.n = n
        self.off = 0

    def mark(self):
        return self.off

    def release(self, m):
        self.off = m

    def _raw(self, nf32):
        a = self.off
        self.off += nf32
        assert self.off <= self.n, "SBUF arena overflow %d > %d" % (self.off, self.n)
        return self.big[:, a:a + nf32]

    @staticmethod
    def _shape(ap, dims):
        if len(dims) == 1:
            return ap
        if len(dims) == 2:
            return ap.rearrange("p (a b) -> p a b", b=dims[1])
        if len(dims) == 3:
            return ap.rearrange("p (a b c) -> p a b c", b=dims[1], c=dims[2])
        raise ValueError

    def f32(self, *dims):
        n = int(np.prod(dims))
        return self._shape(self._raw(n), dims)

    def bf16(self, *dims):
        n = int(np.prod(dims))
        assert n % 2 == 0
        return self._shape(self._raw(n // 2).bitcast(BF16), dims)


def host_consts():
    c = {}
    c["identf"] = np.eye(128, dtype=np.float32)
    bo = np.zeros((128, 128), np.float32)
    bo[:64, :64] = 1.0
    bo[64:, 64:] = 1.0
    c["bones"] = bo
    c["bo64"] = bo / 64.0
    half = 32
    inv = (10000.0 ** (-np.arange(half, dtype=np.float64) / half))
    ang = np.arange(S, dtype=np.float64)[:, None] * inv[None, :]
    c["cosr"] = np.ascontiguousarray(np.cos(ang).reshape(16, 128, 32).transpose(1, 0, 2)).astype(np.float32)
    c["sinr"] = np.ascontiguousarray(np.sin(ang).reshape(16, 128, 32).transpose(1, 0, 2)).astype(np.float32)
    j = np.arange(128)[:, None]
    i = np.arange(128)[None, :]
    c["maskLR"] = np.stack([(j >= i), (j <= i)], axis=1).astype(np.float32)
    t = np.arange(64)
    st_f = (t[:, None] < t[None, :]).astype(np.float32)
    in_f = (t[:, None] <= t[None, :]).astype(np.float32)
    def bd(m):
        z = np.zeros((128, 128), np.float32)
        z[:64, :64] = m
        z[64:, 64:] = m
        return z
    rwm = np.zeros((128, 2, 3, 128), np.float32)
    rwm[:, 0, 0] = bd(st_f); rwm[:, 0, 1] = bd(in_f); rwm[:, 0, 2] = bd(st_f.T)
    rwm[:, 1, 0] = bd(st_f.T); rwm[:, 1, 1] = bd(in_f.T); rwm[:, 1, 2] = bd(st_f)
    c["rwm"] = rwm
    seg = np.ones((128, 512), np.float32)
    seg[:, ::64] = 0.0
    c["segm"] = seg
    return c


def build_nc(stop_after="all", debug=()):
    nc = bass.Bass("TRN2", target_bir_lowering=False)

    def din(name, shape):
        return nc.dram_tensor(name, list(shape), F32, kind="ExternalInput").ap()

    x = din("x", [S, D]); mem = din("mem", [NMEM, D]); w_in = din("w_in", [D, INW])
    attn_w_o = din("attn_w_o", [768, D]); rwkv_w_o = din("rwkv_w_o", [768, D]); x_w_o = din("x_w_o", [512, D])
    x_w_kv = din("x_w_kv", [D, 1024]); w_out = din("w_out", [D, D])
    w2cat = din("w2cat", [128, 768]); a2cat = din("a2cat", [128, 768])
    prm_d = din("prm", [128, NPRM]); gqk_d = din("gqk", [128, 1024]); sink_d = din("sinkb", [128, 12])
    identf_d = din("identf", [128, 128]); bones_d = din("bones", [128, 128]); bo64_d = din("bo64", [128, 128])
    cos_d = din("cosr", [128, 16, 32]); sin_d = din("sinr", [128, 16, 32]); maskLR_d = din("maskLR", [128, 2, 128])
    rwm_d = din("rwm", [128, 2, 3, 128]); segm_d = din("segm", [128, 512])
    out = nc.dram_tensor("out", [S, D], F32, kind="ExternalOutput").ap()

    def dscr(name, shape, dt):
        kind = "ExternalOutput" if name in debug else "Internal"
        return nc.dram_tensor(name, list(shape), dt, kind=kind).ap()

    proj_f = dscr("proj_f", [NPF, S], F32)
    qkv_t = dscr("qkv_t", [S, NQKV], F32)
    ybuf = dscr("ybuf", [2048, S], BF16)
    dbg = dscr("dbg", [128, 4096], F32) if "dbg" in debug else None

    NBIG = 52600
    big_cm = nc.sbuf_tensor("big", [128, NBIG], F32)
    big = big_cm.__enter__()
    ps_cms = [nc.psum_tensor("ps%d" % i, [128, 512], F32) for i in range(8)]
    ps = [cm.__enter__() for cm in ps_cms]
    Bps = [Buf("ps%d" % i) for i in range(8)]
    psb = [p[:, :].bitcast(BF16) for p in ps]
    Sc = Sched(nc)
    AR = Arena(big, NBIG)
    V, A_, P_, G_, T_ = nc.vector, nc.scalar, nc.gpsimd, nc.sync, nc.tensor
    bank_rr = [0]
    dumped = set()

    def dump(name, sb_ap, shape, dt, reads):
        if name in debug and name not in dumped:
            dumped.add(name)
            t = nc.dram_tensor(name, list(shape), dt, kind="ExternalOutput").ap()
            Sc.dma("sp", t, sb_ap, reads=reads)

    def nbank(lo=0, hi=8):
        for _ in range(hi - lo):
            b = lo + bank_rr[0] % (hi - lo)
            bank_rr[0] += 1
            if not Bps[b].pending:
                return b
        raise RuntimeError("all PSUM banks in [%d,%d) hold unconsumed data" % (lo, hi))

    def mm_group(out_ap, items, obuf):
        n = len(items)
        for i, (l, r, rd) in enumerate(items):
            first, last = i == 0, i == n - 1
            Sc.op("pe", lambda: T_.matmul(out_ap, lhsT=l, rhs=r, start=first, stop=last), reads=rd,
                  writes=[obuf] if first else [], inc=last, setw=[obuf] if last else [])

    def mm_multi(items, obuf):
        n = len(items)
        for i, (o, l, r, rd) in enumerate(items):
            first, last = i == 0, i == n - 1
            Sc.op("pe", lambda: T_.matmul(o, lhsT=l, rhs=r, start=True, stop=True), reads=rd,
                  writes=[obuf] if first else [], inc=last, setw=[obuf] if last else [])

    def rsqrt_act(out_ap, in_ap, scale, eps, reads, writes, tmp_ap=None):
        t = out_ap if tmp_ap is None else tmp_ap
        Sc.op("act", lambda: A_.activation(out=t, in_=in_ap, func=AF.Ln, bias=eps, scale=scale), reads=reads, writes=writes)
        Sc.op("act", lambda: A_.activation(out=out_ap, in_=t, func=AF.Exp, scale=-0.5), reads=writes, writes=writes)

    identf = AR.f32(128); identb = AR.bf16(128); prm = AR.f32(NPRM)
    kmT = AR.bf16(4, 256); vm = AR.bf16(2, 512)
    Bid, Bprm, Bkm, Bvm = Buf("id"), Buf("prm"), Buf("kmT"), Buf("vm")
    Sc.dma("sp", identf, identf_d[:, :], writes=[Bid])
    Sc.dma("sp", prm[:, 0:OMM], prm_d[:, 0:OMM], writes=[Bprm])
    Sc.op("dve", lambda: V.tensor_copy(out=identb, in_=identf), reads=[Bid], writes=[Bid])
    Sc.op("dve", lambda: V.tensor_scalar(out=prm[:, OMM:OMM + 20], in0=prm[:, MU:MU + 20], scalar1=-1.0, scalar2=1.0, op0=ALU.mult, op1=ALU.add), reads=[Bprm], writes=[Bprm])
    Sc.op("dve", lambda: V.tensor_scalar(out=prm[:, HMU:HMU + 20], in0=prm[:, MU:MU + 20], scalar1=0.5, scalar2=None, op0=ALU.mult), reads=[Bprm], writes=[Bprm])
    Sc.op("dve", lambda: V.tensor_scalar(out=prm[:, OMK:OMK + 6], in0=prm[:, KA:KA + 6], scalar1=-1.0, scalar2=1.0, op0=ALU.mult, op1=ALU.add), reads=[Bprm], writes=[Bprm])
    Sc.op("dve", lambda: V.tensor_scalar(out=prm[:, HW0:HW0 + 24], in0=prm[:, W0:W0 + 24], scalar1=0.5, scalar2=None, op0=ALU.mult), reads=[Bprm], writes=[Bprm])
    Sc.op("dve", lambda: V.tensor_tensor(out=prm[:, XG2:XG2 + 1], in0=prm[:, XQG:XQG + 1], in1=prm[:, XKG:XKG + 1], op=ALU.mult), reads=[Bprm], writes=[Bprm])
    m_persist = AR.mark()

    hT = AR.bf16(16, S)
    BhT = [Buf("hT%d" % g) for g in range(4)]
    memT = AR.bf16(16, NMEM); BmemT = Buf("memT")
    m_ph0 = AR.mark()
    xbuf = [AR.f32(4, D) for _ in range(2)]
    Bx = [[Buf() for _ in range(4)] for _ in range(2)]
    junk = AR.f32(D); Bjunk = Buf("junk")
    evac_rr = [0]

    def build_T(src, nblk_total, gcol, dstT, dst_bufs):
        ngrp = (nblk_total + 3) // 4
        for g in range(ngrp):
            nb = min(4, nblk_total - g * 4)
            xb, bx = xbuf[g % 2], Bx[g % 2]
            ssq = AR.f32(4); rt = AR.f32(4); Bss = Buf("ss")
            Sc.op("dve", lambda: V.memset(ssq, 0.0), writes=[Bss])
            for i in range(nb):
                r0 = (g * 4 + i) * 128
                Sc.dma("sp", xb[:, i, :], src[r0:r0 + 128, :], writes=[bx[i]])
            for i in range(nb):
                Sc.op("act", lambda: A_.activation(out=junk, in_=xb[:, i, :], func=AF.Square, accum_out=ssq[:, i:i + 1]),
                      reads=[bx[i]], writes=[Bjunk, Bss])
            rsqrt_act(rt[:, 0:nb], ssq[:, 0:nb], 1.0 / D, EPS, [Bss], [Bss])
            for i in range(nb):
                Sc.op("dve", lambda: V.tensor_scalar(out=xb[:, i, :], in0=xb[:, i, :], scalar1=rt[:, i:i + 1], scalar2=None, op0=ALU.mult),
                      reads=[bx[i], Bss], writes=[bx[i]])
            for c in range(16):
                b = nbank()
                for i in range(nb):
                    Sc.op("pe", lambda: T_.transpose(out=ps[b][:, i * 128:(i + 1) * 128], in_=xb[:, i, c * 128:(c + 1) * 128], identity=identf),
                          reads=[bx[i], Bid], writes=[Bps[b]] if i == 0 else [], inc=(i == nb - 1), setw=[Bps[b]] if i == nb - 1 else [])
                dst = dstT[:, c, g * 512:g * 512 + nb * 128]
                gc = prm[:, gcol + c:gcol + c + 1]
                if evac_rr[0] % 2 == 0:
                    Sc.op("act", lambda: A_.activation(out=dst, in_=ps[b][:, 0:nb * 128], func=AF.Copy, scale=gc),
                          reads=[Bps[b], Bprm], writes=[dst_bufs[g]])
                else:
                    Sc.op("dve", lambda: V.tensor_scalar(out=dst, in0=ps[b][:, 0:nb * 128], scalar1=gc, scalar2=None, op0=ALU.mult),
                          reads=[Bps[b], Bprm], writes=[dst_bufs[g]])
                evac_rr[0] += 1

    build_T(mem, 2, MG_, memT, [BmemT])
    build_T(x, 16, NG, hT, BhT)

    Sc.barrier()
    AR.release(m_ph0)
    memT2 = memT
    wkv = AR.bf16(16, 1024); Bwkv = [Buf("wkv%d" % q) for q in range(4)]
    kmn = AR.f32(512); Bkmn = Buf("kmn")
    ssk = AR.f32(4); rk_ = AR.f32(4); Bssk = Buf("ssk")
    junk2 = AR.f32(128); Bjunk2 = Buf("junk2")
    wkv_src = x_w_kv.rearrange("(k p) n -> p k n", p=128)
    for q in range(4):
        Sc.dma("pool", wkv[:, q * 4:(q + 1) * 4, :], wkv_src[:, q * 4:(q + 1) * 4, :], writes=[Bwkv[q]])
    for mb in range(2):
        for half in range(2):
            b = nbank()
            mm_group(ps[b][:, :], [(memT2[:, k, mb * 128:(mb + 1) * 128], wkv[:, k, half * 512:(half + 1) * 512], [BmemT, Bwkv[k // 4]]) for k in range(16)], Bps[b])
            if half == 0:
                Sc.op("dve", lambda: V.memset(ssk, 0.0), writes=[Bssk])
                for h in range(4):
                    Sc.op("act", lambda: A_.activation(out=junk2, in_=ps[b][:, h * 128:(h + 1) * 128], func=AF.Square, accum_out=ssk[:, h:h + 1]),
                          reads=[Bps[b]], writes=[Bjunk2, Bssk])
                rsqrt_act(rk_, ssk, 1.0 / 128, EPS, [Bssk], [Bssk])
                for h in range(4):
                    Sc.op("dve", lambda: V.tensor_scalar(out=kmn[:, h * 128:(h + 1) * 128], in0=ps[b][:, h * 128:(h + 1) * 128], scalar1=rk_[:, h:h + 1], scalar2=None, op0=ALU.mult),
                          reads=[Bps[b], Bssk], writes=[Bkmn])
                b2 = nbank()
                for h in range(4):
                    Sc.op("pe", lambda: T_.transpose(out=ps[b2][:, h * 128:(h + 1) * 128], in_=kmn[:, h * 128:(h + 1) * 128], identity=identf),
                          reads=[Bkmn, Bid], writes=[Bps[b2]] if h == 0 else [], inc=(h == 3), setw=[Bps[b2]] if h == 3 else [])
                Sc.op("dve", lambda: V.tensor_scalar(out=kmT[:, :, mb * 128:(mb + 1) * 128], in0=ps[b2][:, :].rearrange("p (h m) -> p h m", m=128),
                                                     scalar1=prm[:, XG2:XG2 + 1], scalar2=None, op0=ALU.mult),
                      reads=[Bps[b2], Bprm], writes=[Bkm])
            else:
                Sc.op("act", lambda: A_.activation(out=vm[:, mb, :], in_=ps[b][:, :], func=AF.Copy), reads=[Bps[b]], writes=[Bvm])
    dump("d_kmT", kmT, [128, 4, 256], BF16, [Bkm]); dump("d_vm", vm, [128, 2, 512], BF16, [Bvm])
    dump("d_hT", hT, [128, 16, S], BF16, BhT)
    Sc.barrier()
    if stop_after == "hT":
        return finish_debug(nc, Sc, locals())

    AR.release(m_ph0)
    NWB = 3
    wt = [AR.bf16(16, 512) for _ in range(NWB)]
    Bwt = [[Buf("wt%d_%d" % (i, q)) for q in range(4)] for i in range(NWB)]
    stf = [AR.f32(S) for _ in range(2)]; Bstf = [Buf("stf%d" % i) for i in range(2)]
    stt = [AR.f32(512) for _ in range(3)]; Bstt = [Buf("stt%d" % i) for i in range(3)]
    Bproj = [Buf("pf%d" % i) for i in range(NPF // 128)]
    Bqkv = Buf("qkv")
    w_src = w_in.rearrange("(k p) n -> p k n", p=128)
    NT = (INW + 511) // 512

    TORDER = [0, 1, 2, 10, 11, 12] + [t for t in range(NT) if t not in (0, 1, 2, 10, 11, 12)]

    def load_w(pos):
        t = TORDER[pos]
        c0 = t * 512
        ncol = min(512, INW - c0)
        for q in range(4):
            Sc.dma("pool", wt[pos % NWB][:, q * 4:(q + 1) * 4, 0:ncol], w_src[:, q * 4:(q + 1) * 4, c0:c0 + ncol], writes=[Bwt[pos % NWB][q]])

    def act_for(feat):
        if (1280 <= feat < 2048) or (4608 <= feat < 5376) or (5888 <= feat < 6400):
            return "silu"
        if feat >= 6400:
            return "sig"
        return "copy"

    stf_rr = [0]; stt_rr = [0]; ev_rr = [0]
    import os as _os2
    TESTCOPY = bool(_os2.environ.get("TESTCOPY"))
    load_w(0); load_w(1)

    def rr(gens, weights=None):
        act_ = [[g, (weights[i] if weights else 1)] for i, g in enumerate(gens)]
        while act_:
            for ent in list(act_):
                for _ in range(ent[1]):
                    try:
                        next(ent[0])
                        yield
                    except StopIteration:
                        act_.remove(ent)
                        break

    def run(gen):
        for _ in gen:
            pass

    def proj_gen(t_lo, t_hi, bk):
      for pos in range(t_lo, t_hi):
          t = TORDER[pos]
          if pos + 2 < NT:
              load_w(pos + 2)
          c0 = t * 512
          ncol = min(512, INW - c0)
          w = wt[pos % NWB]; bw = Bwt[pos % NWB]
          ntok = max(0, min(ncol, NQKV - c0))
          if ntok > 0:
              for tb in range(16):
                  b = nbank(*bk)
                  mm_group(ps[b][:, 0:ntok], [(hT[:, k, tb * 128:(tb + 1) * 128], w[:, k, 0:ntok], [BhT[tb // 4], bw[k // 4]]) for k in range(16)], Bps[b])
                  si = stt_rr[0] % 3; stt_rr[0] += 1
                  if ev_rr[0] % 2 == 0:
                      Sc.op("act", lambda: A_.activation(out=stt[si][:, 0:ntok], in_=ps[b][:, 0:ntok], func=AF.Copy), reads=[Bps[b]], writes=[Bstt[si]])
                  else:
                      Sc.op("dve", lambda: V.tensor_copy(out=stt[si][:, 0:ntok], in_=ps[b][:, 0:ntok]), reads=[Bps[b]], writes=[Bstt[si]])
                  ev_rr[0] += 1
                  Sc.dma("sp", qkv_t[tb * 128:(tb + 1) * 128, c0:c0 + ntok], stt[si][:, 0:ntok], reads=[Bstt[si]])
                  yield
          for sub in range(ntok // 128, ncol // 128):
              feat = c0 + sub * 128
              fi = (feat - NQKV) // 128
              kind = act_for(feat)
              si = stf_rr[0] % 2; stf_rr[0] += 1
              for tc in range(4):
                  b = nbank(*bk)
                  mm_group(ps[b][:, :], [(w[:, k, sub * 128:(sub + 1) * 128], hT[:, k, tc * 512:(tc + 1) * 512], [bw[k // 4], BhT[tc]]) for k in range(16)], Bps[b])
                  dst = stf[si][:, tc * 512:(tc + 1) * 512]
                  if kind == "silu":
                      Sc.op("act", lambda: A_.activation(out=dst, in_=ps[b][:, :], func=AF.Silu), reads=[Bps[b]], writes=[Bstf[si]])
                  elif kind == "sig":
                      gcol = GB + (feat - 6400) // 128
                      Sc.op("act", lambda: A_.activation(out=dst, in_=ps[b][:, :], func=(AF.Tanh if TESTCOPY else AF.Sigmoid), bias=prm[:, gcol:gcol + 1], scale=1.0),
                            reads=[Bps[b], Bprm], writes=[Bstf[si]])
                  else:
                      if ev_rr[0] % 2 == 0:
                          Sc.op("act", lambda: A_.activation(out=dst, in_=ps[b][:, :], func=AF.Copy), reads=[Bps[b]], writes=[Bstf[si]])
                      else:
                          Sc.op("dve", lambda: V.tensor_copy(out=dst, in_=ps[b][:, :]), reads=[Bps[b]], writes=[Bstf[si]])
                      ev_rr[0] += 1
                  yield
              Sc.dma("sp", proj_f[fi * 128:(fi + 1) * 128, :], stf[si], reads=[Bstf[si]], writes=[Bproj[fi]])

    TSPLIT = 6
    run(proj_gen(0, TSPLIT, (0, 8)))
    onesf = AR.f32(128); onesb = AR.bf16(128); Bones = Buf("ones")
    Sc.op("pool", lambda: P_.memset(onesf, 1.0), writes=[Bones])
    Sc.op("pool", lambda: P_.tensor_copy(out=onesb, in_=onesf), reads=[Bones], writes=[Bones])
    qTc = [AR.f32(S) for _ in range(2)]; gtc = [AR.f32(S)] * 2
    BqTc = [Buf(), Buf()]; Bgtc = [Buf()] * 2
    sqc = AR.f32(512); sc2 = AR.f32(512); qn_c = AR.bf16(512); pTc = [AR.bf16(512) for _ in range(2)]; rden = AR.f32(512); yo = AR.f32(512)
    yxs = AR.bf16(S)
    Bsqc, Bsc2, Bqnc, BpTc, Brden, Byo, Byxs = Buf(), Buf(), Buf(), [Buf(), Buf()], Buf(), Buf(), Buf()

    c_rr = [0]

    def cbank():
        c_rr[0] += 1
        return 6 + c_rr[0] % 2

    def xattn_gen():
        for h in range(4):
            Sc.dma("sp", qTc[h % 2], proj_f[XQ0 + h * 128:XQ0 + (h + 1) * 128, :], reads=[Bproj[XQ0 // 128 + h]], writes=[BqTc[h % 2]])
            Sc.dma("sp", gtc[h % 2], proj_f[XG0 + h * 128:XG0 + (h + 1) * 128, :], reads=[Bproj[XG0 // 128 + h]], writes=[Bgtc[h % 2]])
            q_ = qTc[h % 2]; bq_ = BqTc[h % 2]
            for tc in range(4):
                sl = slice(tc * 512, (tc + 1) * 512)
                Sc.op("act", lambda: A_.activation(out=sqc, in_=q_[:, sl], func=AF.Square), reads=[bq_], writes=[Bsqc]); yield
                yield; yield
                b = cbank()
                mm_group(ps[b][:, :], [(onesf, sqc, [Bones, Bsqc])], Bps[b]); yield
                rsqrt_act(sc2, ps[b][:, :], 1.0, 128.0 * EPS, [Bps[b]], [Bsc2]); yield
                Sc.op("dve", lambda: V.tensor_tensor(out=qn_c, in0=q_[:, sl], in1=sc2, op=ALU.mult), reads=[bq_, Bsc2], writes=[Bqnc]); yield
                for mb in range(2):
                    yield; yield
                    b = cbank()
                    mm_group(ps[b][:, :], [(kmT[:, h, mb * 128:(mb + 1) * 128], qn_c, [Bkm, Bqnc])], Bps[b]); yield
                    Sc.op("act", lambda: A_.activation(out=pTc[mb], in_=ps[b][:, :], func=AF.Exp), reads=[Bps[b]], writes=[BpTc[mb]]); yield
                yield; yield
                bo_ = cbank()
                mm_group(ps[bo_][:, :], [(vm[:, mb, h * 128:(h + 1) * 128], pTc[mb], [Bvm, BpTc[mb]]) for mb in range(2)], Bps[bo_]); yield
                bd_ = cbank()
                mm_group(ps[bd_][:, :], [(onesb, pTc[mb], [Bones, BpTc[mb]]) for mb in range(2)], Bps[bd_]); yield
                Sc.op("act", lambda: A_.activation(out=rden, in_=ps[bd_][:, :], func=AF.Ln), reads=[Bps[bd_]], writes=[Brden]); yield
                Sc.op("act", lambda: A_.activation(out=rden, in_=rden, func=AF.Exp, scale=-1.0), reads=[Brden], writes=[Brden]); yield
                Sc.op("dve", lambda: V.tensor_tensor(out=yo, in0=ps[bo_][:, :], in1=rden, op=ALU.mult), reads=[Bps[bo_], Brden], writes=[Byo]); yield
                Sc.op("pool", lambda: P_.tensor_tensor(out=yxs[:, sl], in0=yo, in1=gtc[h % 2][:, sl], op=ALU.mult), reads=[Byo, Bgtc[h % 2]], writes=[Byxs]); yield
            Sc.dma("sp", ybuf[1536 + h * 128:1536 + (h + 1) * 128, :], yxs, reads=[Byxs]); yield


    run(rr([proj_gen(TSPLIT, NT, (0, 6)), xattn_gen()]))
    Sc.barrier()
    if stop_after == "proj":
        return finish_debug(nc, Sc, locals())

    AR.release(m_persist)
    bones = AR.f32(128); bo64 = AR.f32(128); rwm = AR.bf16(2, 3, 128); segm = AR.f32(512)
    w2b = AR.bf16(768); a2b = AR.bf16(768)
    Bc = Buf("rwconst")
    Sc.dma("sp", bones, bones_d[:, :], writes=[Bc])
    tb1 = Buf(); tb2 = Buf(); tb3 = Buf(); tb4 = Buf(); tb5 = Buf()
    Sc.dma("sp", bo64, bo64_d[:, :], writes=[tb1])
    Sc.dma("sp", segm, segm_d[:, :], writes=[tb2])
    Sc.dma("pool", rwm, rwm_d[:, :, :, :], writes=[tb3])
    Sc.dma("pool", w2b, w2cat[:, :], writes=[tb4])
    Sc.dma("pool", a2b, a2cat[:, :], writes=[tb5])
    lw_t = AR.bf16(S); la_s = AR.bf16(S); Blw = Buf("lw"); Bla = Buf("la")
    r_ = AR.bf16(S); k_ = AR.bf16(S); v_ = AR.bf16(S); kk_ = AR.bf16(S); bon = AR.f32(S); ysum = AR.f32(S)
    Br, Bk, Bv, Bkk, Bbon, Bys = Buf("r"), Buf("k"), Buf("v"), Buf("kk"), Buf("bon"), Buf("ysum")
    Vtok = AR.bf16(32, 128); BVtok = [Buf("vtok%d" % q) for q in range(4)]
    vbd = AR.bf16(8, 128); Bvbd = Buf("vbd")
    rkones = AR.f32(128); Brk = Buf("rkones")
    NTMP = 7168
    tmp_raw = AR._raw(NTMP)
    WD = []
    for d in range(2):
        W = {}
        for nm in ("ARt", "BKtok", "Gb", "Gk"):
            W[nm] = [AR.bf16(8, 2, 128) for _ in range(2)]
        W["Tt"] = [AR.bf16(8, 128) for _ in range(2)]
        W["Pc"] = [AR.f32(8) for _ in range(2)]
        W["BKt"] = AR.bf16(8, 2, 128)
        W["Xs"] = AR.bf16(128); W["Ubf"] = AR.bf16(128); W["Ybd"] = AR.f32(8, 128)
        W["St"] = AR.f32(128); W["Stbf"] = AR.bf16(128); W["tmpS"] = AR.f32(128); W["tot"] = AR.f32(8)
        for nm in ("BBKt", "BXs", "BUbf", "BSt", "BStbf", "BtmpS", "Btot", "BYbd"):
            W[nm] = Buf(nm)
        for nm in ("BARt", "BPc"):
            W[nm] = [Buf(nm + "0"), Buf(nm + "1")]
        for nm in ("BBKtok", "BGb", "BGk", "BTt"):
            W[nm] = [[Buf(), Buf()], [Buf(), Buf()]]
        W["BLab"] = [Buf(), Buf()]
        W["BAn"] = [[Buf(), Buf()], [Buf(), Buf()]]; W["BBn"] = [[Buf(), Buf()], [Buf(), Buf()]]
        TA = Arena(tmp_raw[:, d * (NTMP // 2):(d + 1) * (NTMP // 2)], NTMP // 2)
        for nm in ("a", "ld", "cum", "E1", "E2", "E3", "u"):
            W[nm] = TA.f32(512); W["B" + nm] = Buf(nm)
        asb = lambda ap: ap.bitcast(BF16).rearrange("p (a b) -> p a b", b=128)
        W["An"] = [asb(W["E1"]), asb(W["E2"])]; W["Bn"] = [asb(W["E3"]), asb(W["u"])]; W["Lab"] = asb(W["ld"])
        W["alias_An"] = [W["BE1"], W["BE2"]]; W["alias_Bn"] = [W["BE3"], W["Bu"]]
        WD.append(W)
    for d in range(2):
        W = WD[d]
        for jp in range(2):
            Sc.op("pool", lambda: P_.memset(W["ARt"][jp], 0.0), writes=[W["BARt"][jp]])
        Sc.op("pool", lambda: P_.memset(W["BKt"], 0.0), writes=[W["BBKt"]])
    Sc.op("pool", lambda: P_.memset(vbd, 0.0), writes=[Bvbd])

    rawc = [AR.f32(514) for _ in range(2)]; Brawc = [Buf(), Buf()]
    nbc = AR.f32(512); sqc_ = AR.f32(512); Bnbc, Bsqc_ = Buf(), Buf()
    sqk = nbc; rnk = sqc_; Bsqk, Brnk = Bnbc, Bsqc_
    gatec = AR.f32(512); dtmp = AR.f32(512); sq2 = AR.f32(512); rstd = AR.f32(512); yn = AR.f32(512); ystc = [AR.bf16(512) for _ in range(2)]
    Bgatec, Bdt, Bs2, Brs, Byn, Bystc = Buf(), Buf(), Buf(), Buf(), Buf(), [Buf(), Buf()]
    rc_rr = [0]

    def shift_gen(dst, bdst, row0, mi, func=AF.Copy):
        for tc in range(4):
            sl = slice(tc * 512, (tc + 1) * 512)
            lo = max(0, tc * 512 - 1); hi = min(S, tc * 512 + 513)
            off = lo - (tc * 512 - 1)
            ri = rc_rr[0] % 2; rc_rr[0] += 1
            rc = rawc[ri]; brc = Brawc[ri]
            if tc == 0:
                Sc.op("pool", lambda: P_.memset(rc[:, 0:1], 0.0), writes=[brc])
            if tc == 3:
                Sc.op("pool", lambda: P_.memset(rc[:, 513:514], 0.0), writes=[brc])
            Sc.dma("sp", rc[:, off:off + (hi - lo)], proj_f[row0:row0 + 128, lo:hi], writes=[brc]); yield
            Sc.op("pool", lambda: P_.tensor_tensor(out=nbc, in0=rc[:, 0:512], in1=rc[:, 2:514], op=ALU.add), reads=[brc], writes=[Bnbc]); yield
            Sc.op("act", lambda: A_.activation(out=sqc_, in_=rc[:, 1:513], func=AF.Copy, scale=prm[:, OMM + mi:OMM + mi + 1]), reads=[brc, Bprm], writes=[Bsqc_]); yield
            if func == AF.Copy:
                Sc.op("dve", lambda: V.scalar_tensor_tensor(out=dst[:, sl], in0=nbc, scalar=prm[:, HMU + mi:HMU + mi + 1], in1=sqc_, op0=ALU.mult, op1=ALU.add),
                      reads=[Bnbc, Bprm, Bsqc_], writes=[bdst]); yield
            else:
                Sc.op("dve", lambda: V.scalar_tensor_tensor(out=sqc_, in0=nbc, scalar=prm[:, HMU + mi:HMU + mi + 1], in1=sqc_, op0=ALU.mult, op1=ALU.add),
                      reads=[Bnbc, Bprm, Bsqc_], writes=[Bsqc_]); yield
                Sc.op("act", lambda: A_.activation(out=dst[:, sl], in_=sqc_, func=func), reads=[Bsqc_], writes=[bdst]); yield

    def v3(ap, n=64):
        return ap.rearrange("p (c t) -> p c t", t=n)

    def unit_prep(p, d, sc, jp):
        W = WD[d]
        sl = slice(sc * 512, (sc + 1) * 512)
        dh = slice(d * 64, (d + 1) * 64)
        pc = slice(p * 128, (p + 1) * 128)
        a, ld, cum, E1, E2, E3, u = W["a"], W["ld"], W["cum"], W["E1"], W["E2"], W["E3"], W["u"]
        Ba, Bld, Bcum, BE1, BE2, BE3, Bu = W["Ba"], W["Bld"], W["Bcum"], W["BE1"], W["BE2"], W["BE3"], W["Bu"]
        ARt, BKtok, Gb, Gk, Tt, Pc = W["ARt"][jp], W["BKtok"][jp], W["Gb"][jp], W["Gk"][jp], W["Tt"][jp], W["Pc"][jp]
        BARt, BBKtok, BGb, BGk, BTt, BPc = W["BARt"][jp], W["BBKtok"][jp], W["BGb"][jp], W["BGk"][jp], W["BTt"][jp], W["BPc"][jp]
        BKt, Lab = W["BKt"], W["Lab"]
        b = nbank(2, 8)
        mm_group(ps[b][:, :], [(a2b[dh, pc], la_s[dh, sl], [tb5, Bla])], Bps[b]); yield
        hc = HA0 + d * 6 + p
        Sc.op("act", lambda: A_.activation(out=a, in_=ps[b][:, :], func=AF.Tanh, bias=prm[:, hc:hc + 1], scale=0.5), reads=[Bps[b], Bprm], writes=[Ba]); yield
        Sc.op("dve", lambda: V.tensor_scalar(out=a, in0=a, scalar1=0.5, scalar2=0.5, op0=ALU.mult, op1=ALU.add), reads=[Ba], writes=[Ba]); yield
        b = nbank(2, 8)
        mm_group(ps[b][:, :], [(w2b[dh, pc], lw_t[dh, sl], [tb4, Blw])], Bps[b]); yield
        hc2 = HW0 + d * 6 + p
        Sc.op("act", lambda: A_.activation(out=ld, in_=ps[b][:, :], func=AF.Tanh, bias=prm[:, hc2:hc2 + 1], scale=0.5), reads=[Bps[b], Bprm], writes=[Bld] + W["BLab"]); yield
        Sc.op("dve", lambda: V.tensor_scalar(out=ld, in0=ld, scalar1=C1, scalar2=C1, op0=ALU.mult, op1=ALU.add), reads=[Bld], writes=[Bld]); yield
        Sc.op("dve", lambda: V.tensor_tensor_scan(out=cum, data0=segm, data1=ld, initial=0.0, op0=ALU.mult, op1=ALU.add), reads=[tb2, Bld], writes=[Bcum]); yield
        if d == 1:
            Sc.op("dve", lambda: V.tensor_copy(out=W["tot"], in_=v3(cum)[:, :, 63]), reads=[Bcum], writes=[W["Btot"]]); yield
            Sc.op("dve", lambda: V.tensor_tensor(out=cum, in0=ld, in1=cum, op=ALU.subtract), reads=[Bld, Bcum], writes=[Bcum]); yield
            Sc.op("dve", lambda: V.tensor_tensor(out=v3(cum), in0=v3(cum), in1=W["tot"].unsqueeze(2).to_broadcast([128, 8, 64]), op=ALU.add),
                  reads=[Bcum, W["Btot"]], writes=[Bcum]); yield
        Sc.op("dve", lambda: V.tensor_tensor(out=ld, in0=cum, in1=ld, op=ALU.subtract), reads=[Bcum, Bld], writes=[Bld]); yield
        Sc.op("act", lambda: A_.activation(out=E3, in_=ld, func=AF.Exp), reads=[Bld], writes=[BE3] + W["BBn"][0]); yield
        Sc.op("act", lambda: A_.activation(out=E1, in_=cum, func=AF.Exp), reads=[Bcum], writes=[BE1] + W["BAn"][0]); yield
        Sc.op("act", lambda: A_.activation(out=E2, in_=cum, func=AF.Exp, scale=-1.0), reads=[Bcum], writes=[BE2] + W["BAn"][1]); yield
        pcol = 63 if d == 0 else 0
        Sc.op("dve", lambda: V.tensor_copy(out=Pc, in_=v3(E1)[:, :, pcol]), reads=[BE1], writes=[BPc]); yield
        Sc.op("dve", lambda: V.tensor_scalar(out=u, in0=a, scalar1=prm[:, KA + p:KA + p + 1], scalar2=prm[:, OMK + p:OMK + p + 1], op0=ALU.mult, op1=ALU.add),
              reads=[Ba, Bprm], writes=[Bu] + W["BBn"][1]); yield
        Sc.op("dve", lambda: V.tensor_tensor(out=u, in0=k_[:, sl], in1=u, op=ALU.mult), reads=[Bk, Bu], writes=[Bu]); yield
        Sc.op("pool", lambda: P_.tensor_tensor(out=a, in0=kk_[:, sl], in1=a, op=ALU.mult), reads=[Bkk, Ba], writes=[Ba]); yield
        for half in range(2):
            hs = slice(half * 64, (half + 1) * 64)
            bc = slice(half * 64, (half + 1) * 64)
            Sc.op("dve", lambda: V.scalar_tensor_tensor(out=ARt[hs, :, 0, bc], in0=v3(kk_[hs, sl]), scalar=-1.0, in1=v3(E3[hs, :]), op0=ALU.mult, op1=ALU.mult),
                  reads=[Bkk, BE3], writes=[BARt]); yield
            Sc.op("pool", lambda: P_.tensor_tensor(out=ARt[hs, :, 1, bc], in0=v3(r_[hs, sl]), in1=v3(E1[hs, :]), op=ALU.mult), reads=[Br, BE1], writes=[BARt]); yield
            Sc.op("dve", lambda: V.tensor_tensor(out=BKt[hs, :, 1, bc], in0=v3(u[hs, :]), in1=v3(E2[hs, :]), op=ALU.mult), reads=[Bu, BE2], writes=[W["BBKt"]]); yield
            Sc.op("dve", lambda: V.tensor_tensor(out=BKt[hs, :, 0, bc], in0=v3(a[hs, :]), in1=v3(E2[hs, :]), op=ALU.mult), reads=[Ba, BE2], writes=[W["BBKt"]]); yield
        Sc.op("pool", lambda: P_.tensor_tensor(out=cum, in0=r_[:, sl], in1=u, op=ALU.mult), reads=[Br, Bu, Bcum], writes=[Bcum]); yield
        b = nbank(2, 8)
        mm_group(ps[b][:, :], [(rkones, cum, [Brk, Bcum])], Bps[b]); yield
        Sc.op("dve", lambda: V.tensor_tensor(out=cum, in0=ps[b][:, :], in1=v_[:, sl], op=ALU.mult), reads=[Bps[b], Bv, Bcum], writes=[Bcum]); yield
        Sc.op("pool", lambda: P_.tensor_tensor(out=bon[:, sl], in0=bon[:, sl], in1=cum, op=ALU.add), reads=[Bbon, Bcum], writes=[Bbon]); yield
        for hb in range(2):
            b = nbank(2, 8)
            for ci in range(4):
                for s2 in range(2):
                    j = ci * 2 + s2
                    Sc.op("pe", lambda: T_.transpose(out=psb[b][:, j * 128:(j + 1) * 128], in_=BKt[:, hb * 4 + ci, s2, :], identity=identb),
                          reads=[W["BBKt"], Bid], writes=[Bps[b]] if j == 0 else [], inc=(j == 7), setw=[Bps[b]] if j == 7 else [])
            yield
            dstv = BKtok[:, hb * 4:(hb + 1) * 4, :, :].rearrange("p a b c -> p (a b c)")
            Sc.op("act", lambda: A_.activation(out=dstv, in_=psb[b], func=AF.Copy), reads=[Bps[b]], writes=[BBKtok[hb]]); yield
        M2 = rwm[:, d, 0:2, :].rearrange("p a b -> p (a b)")
        ML = rwm[:, d, 2, :]
        for hb in range(2):
            bL = nbank(2, 8)
            mm_multi([(ps[bL][:, ci * 128:(ci + 1) * 128], ARt[:, hb * 4 + ci, 0, :], BKt[:, hb * 4 + ci, 0, :], [W["BBKt"], BARt]) for ci in range(4)], Bps[bL])
            yield
            Sc.op("dve", lambda: V.tensor_tensor(out=Lab[:, hb * 4:(hb + 1) * 4, :], in0=ps[bL][:, :].rearrange("p (a b) -> p a b", b=128),
                                                 in1=ML.unsqueeze(1).to_broadcast([128, 4, 128]), op=ALU.mult), reads=[Bps[bL], tb3], writes=[W["BLab"][hb], Bld]); yield
            bB = [nbank(2, 8), nbank(2, 8)]
            for i in range(2):
                mm_multi([(ps[bB[i]][:, q2 * 256:(q2 + 1) * 256], BKt[:, hb * 4 + i * 2 + q2, 0, :], ARt[:, hb * 4 + i * 2 + q2, :, :], [W["BBKt"], BARt]) for q2 in range(2)], Bps[bB[i]])
            yield
            for i in range(2):
                c0 = hb * 4 + i * 2
                Sc.op("dve", lambda: V.tensor_tensor(out=Gb[:, c0:c0 + 2, :, :].rearrange("p a b c -> p a (b c)"), in0=ps[bB[i]][:, :].rearrange("p (a b) -> p a b", b=256),
                                                     in1=M2.unsqueeze(1).to_broadcast([128, 2, 256]), op=ALU.mult), reads=[Bps[bB[i]], tb3], writes=[BGb[hb]]); yield
        for hb in range(2):
            cs = slice(hb * 4, (hb + 1) * 4)
            Sc.op("dve", lambda: V.tensor_tensor(out=Tt[:, cs, :], in0=Lab[:, cs, :], in1=identb.unsqueeze(1).to_broadcast([128, 4, 128]), op=ALU.add),
                  reads=[W["BLab"][hb], Bid], writes=[BTt[hb]]); yield
        for lvl in range(0, 6):
            for hb in range(2):
                cs = slice(hb * 4, (hb + 1) * 4)
                if lvl == 0:
                    Aget = lambda c: Gb[:, c, 0, :]
                    Bget = lambda c: Lab[:, c, :]
                    BAsrc, BBsrc = BGb[hb], W["BLab"][hb]
                else:
                    Aprev, Bprev = W["An"][lvl % 2], W["Bn"][lvl % 2]
                    Aget = lambda c: Aprev[:, c, :]
                    Bget = lambda c: Bprev[:, c, :]
                    BAsrc, BBsrc = W["BAn"][lvl % 2][hb], W["BBn"][lvl % 2][hb]
                Anew, Bnew = W["An"][(lvl + 1) % 2], W["Bn"][(lvl + 1) % 2]
                BAnew, BBnew = W["BAn"][(lvl + 1) % 2][hb], W["BBn"][(lvl + 1) % 2][hb]
                if lvl >= 1:
                    bT = nbank(2, 8)
                    mm_multi([(ps[bT][:, ci * 128:(ci + 1) * 128], Aget(hb * 4 + ci), Tt[:, hb * 4 + ci, :], [BAsrc, BTt[hb]]) for ci in range(4)], Bps[bT])
                    yield
                if lvl < 5:
                    if lvl < 4:
                        bA = nbank(2, 8)
                        mm_multi([(ps[bA][:, ci * 128:(ci + 1) * 128], Bget(hb * 4 + ci), Aget(hb * 4 + ci), [BAsrc, BBsrc]) for ci in range(4)], Bps[bA])
                        yield
                    bBm = nbank(2, 8)
                    mm_multi([(ps[bBm][:, ci * 128:(ci + 1) * 128], Aget(hb * 4 + ci), Bget(hb * 4 + ci), [BAsrc, BBsrc]) for ci in range(4)], Bps[bBm])
                    yield
                if lvl >= 1:
                    Sc.op("dve", lambda: V.tensor_tensor(out=Tt[:, cs, :], in0=ps[bT][:, :].rearrange("p (a b) -> p a b", b=128), in1=Tt[:, cs, :], op=ALU.add),
                          reads=[Bps[bT], BTt[hb]], writes=[BTt[hb]]); yield
                if lvl < 5:
                    if lvl < 4:
                        Sc.op("act", lambda: A_.activation(out=Anew[:, cs, :], in_=ps[bA][:, :].rearrange("p (a b) -> p a b", b=128), func=AF.Copy),
                              reads=[Bps[bA]], writes=[BAnew, W["alias_An"][(lvl + 1) % 2]]); yield
                    Sc.op("act", lambda: A_.activation(out=Bnew[:, cs, :], in_=ps[bBm][:, :].rearrange("p (a b) -> p a b", b=128), func=AF.Copy),
                          reads=[Bps[bBm]], writes=[BBnew, W["alias_Bn"][(lvl + 1) % 2]]); yield
        for hb in range(2):
            cs = slice(hb * 4, (hb + 1) * 4)
            bX = nbank(2, 8)
            mm_multi([(ps[bX][:, ci * 128:(ci + 1) * 128], Tt[:, hb * 4 + ci, :], BKtok[:, hb * 4 + ci, 0, :], [BTt[hb], BBKtok[hb]]) for ci in range(4)], Bps[bX])
            yield
            bM = nbank(2, 8)
            mm_multi([(ps[bM][:, ci * 128:(ci + 1) * 128], Tt[:, hb * 4 + ci, :], Gb[:, hb * 4 + ci, 1, :], [BTt[hb], BGb[hb]]) for ci in range(4)], Bps[bM])
            yield
            Sc.op("act", lambda: A_.activation(out=BKtok[:, cs, 0, :], in_=ps[bX][:, :].rearrange("p (a b) -> p a b", b=128), func=AF.Copy), reads=[Bps[bX]], writes=[BBKtok[hb]]); yield
            Sc.op("dve", lambda: V.tensor_copy(out=Gb[:, cs, 1, :], in_=ps[bM][:, :].rearrange("p (a b) -> p a b", b=128)), reads=[Bps[bM]], writes=[BGb[hb]]); yield
        for hb in range(2):
            bK = [nbank(2, 8), nbank(2, 8)]
            for i in range(2):
                mm_multi([(ps[bK[i]][:, q2 * 256:(q2 + 1) * 256], BKt[:, hb * 4 + i * 2 + q2, 1, :], ARt[:, hb * 4 + i * 2 + q2, :, :], [W["BBKt"], BARt]) for q2 in range(2)], Bps[bK[i]])
            yield
            for i in range(2):
                c0 = hb * 4 + i * 2
                Sc.op("dve", lambda: V.tensor_tensor(out=Gk[:, c0:c0 + 2, :, :].rearrange("p a b c -> p a (b c)"), in0=ps[bK[i]][:, :].rearrange("p (a b) -> p a b", b=256),
                                                     in1=M2.unsqueeze(1).to_broadcast([128, 2, 256]), op=ALU.mult), reads=[Bps[bK[i]], tb3], writes=[BGk[hb]]); yield

    import os as _os3
    SLK = int(_os3.environ.get('SCAN_SLACK', '0'))

    def scan_step(p, d, sc, ci, jp):
        W = WD[d]
        hb = ci // 4
        cg = sc * 8 + ci
        sb = d
        bkb = Bps[sb]
        ARt, BKtok, Gb, Gk, Tt, Pc = W["ARt"][jp], W["BKtok"][jp], W["Gb"][jp], W["Gk"][jp], W["Tt"][jp], W["Pc"][jp]
        BARt, BBKtok, BGb, BGk, BTt, BPc = W["BARt"][jp], W["BBKtok"][jp], W["BGb"][jp], W["BGk"][jp], W["BTt"][jp], W["BPc"][jp]
        St, Stbf, Xs, Ubf, tmpS = W["St"], W["Stbf"], W["Xs"], W["Ubf"], W["tmpS"]
        vt = Vtok[:, cg, :]
        bvt = BVtok[cg // 8]
        pcc = Pc[:, ci:ci + 1]
        for _s in range(SLK): yield
        mm_group(ps[sb][:, 0:128], [(ARt[:, ci, 0, :], Stbf, [BARt, W["BStbf"]]), (Gk[:, ci, 0, :], vt, [BGk[hb], bvt])], bkb); yield
        Sc.op("act", lambda: A_.activation(out=Xs, in_=ps[sb][:, 0:128], func=AF.Copy), writes=[bkb, W["BXs"]]); yield
        for _s in range(SLK): yield
        mm_group(ps[sb][:, 384:512], [(BKtok[:, ci, 1, :], vt, [BBKtok[hb], bvt]), (identb, Stbf, [Bid, W["BStbf"]]),
                                      (BKtok[:, ci, 0, :], Xs, [BBKtok[hb], W["BXs"]])], bkb); yield
        mm_group(ps[sb][:, 256:384], [(Stbf, ARt[:, ci, 1, :], [BARt, W["BStbf"]]), (Xs, Gb[:, ci, 1, :], [W["BXs"], BGb[hb]]),
                                      (vt, Gk[:, ci, 1, :], [bvt, BGk[hb]])], bkb); yield
        Sc.op("dve", lambda: V.tensor_scalar(out=Stbf, in0=ps[sb][:, 384:512], scalar1=pcc, scalar2=None, op0=ALU.mult),
              reads=[BPc], writes=[bkb, W["BStbf"]]); yield
        Sc.op("act", lambda: A_.activation(out=W["Ybd"][:, ci, :], in_=ps[sb][:, 256:384], func=AF.Copy), writes=[bkb, W["BYbd"]]); yield

    def chain(p, d, sc, jp):
        order = range(8) if d == 0 else range(7, -1, -1)
        for ci in order:
            yield from scan_step(p, d, sc, ci, jp)
        W = WD[d]
        sl = slice(sc * 512, (sc + 1) * 512)
        for half in range(2):
            hs = slice(half * 64, (half + 1) * 64)
            Sc.op("pool", lambda: P_.tensor_tensor(out=v3(ysum[hs, sl]), in0=v3(ysum[hs, sl]), in1=W["Ybd"][hs, :, half * 64:(half + 1) * 64], op=ALU.add),
                  reads=[Bys, W["BYbd"]], writes=[Bys]); yield

    run(shift_gen(lw_t, Blw, LW0, 18, func=AF.Tanh))
    run(shift_gen(la_s, Bla, LA0, 19))

    PAIRS = list(range(6))
    if stop_after.startswith("rwkvp"):
        PAIRS = [int(ch) for ch in stop_after[5:]]

    def prologue(p):
        yield from shift_gen(r_, Br, R0 + p * 128, p)
        yield from shift_gen(k_, Bk, K0 + p * 128, 6 + p)
        yield from shift_gen(v_, Bv, V0 + p * 128, 12 + p)
        kcol = prm[:, KK + p:KK + p + 1]
        for tc in range(4):
            sl = slice(tc * 512, (tc + 1) * 512)
            Sc.op("act", lambda: A_.activation(out=sqk, in_=k_[:, sl], func=AF.Square, scale=kcol), reads=[Bk, Bprm], writes=[Bsqk]); yield
            b = nbank(2, 8)
            mm_group(ps[b][:, :], [(bones, sqk, [Bc, Bsqk])], Bps[b]); yield
            rsqrt_act(rnk, ps[b][:, :], 1.0, 1e-24, [Bps[b]], [Brnk]); yield
            Sc.op("dve", lambda: V.scalar_tensor_tensor(out=kk_[:, sl], in0=k_[:, sl], scalar=kcol, in1=rnk, op0=ALU.mult, op1=ALU.mult),
                  reads=[Bk, Bprm, Brnk], writes=[Bkk]); yield
        Sc.op("dve", lambda: V.tensor_scalar(out=rkones, in0=bones, scalar1=prm[:, RK + p:RK + p + 1], scalar2=None, op0=ALU.mult), reads=[Bc, Bprm], writes=[Brk]); yield

    def vtok_build(p):
        for q in range(4):
            for half in range(2):
                hs = slice(half * 64, (half + 1) * 64)
                Sc.op("pool", lambda: P_.tensor_copy(out=vbd[hs, :, half * 64:(half + 1) * 64], in_=v3(v_[hs, q * 512:(q + 1) * 512])), reads=[Bv], writes=[Bvbd]); yield
            b = nbank(2, 8)
            for j in range(8):
                Sc.op("pe", lambda: T_.transpose(out=psb[b][:, j * 128:(j + 1) * 128], in_=vbd[:, j, :], identity=identb),
                      reads=[Bvbd, Bid], writes=[Bps[b]] if j == 0 else [], inc=(j == 7), setw=[Bps[b]] if j == 7 else [])
            yield
            Sc.op("dve", lambda: V.tensor_copy(out=Vtok[:, q * 8:(q + 1) * 8, :].rearrange("p a b -> p (a b)"), in_=psb[b]), reads=[Bps[b]], writes=[BVtok[q]]); yield

    def resets(p):
        Sc.op("pool", lambda: P_.memset(bon, 0.0), writes=[Bbon])
        Sc.op("pool", lambda: P_.memset(ysum, 0.0), writes=[Bys])
        for d in range(2):
            W = WD[d]
            Sc.op("pool", lambda: P_.memset(W["St"], 0.0), writes=[W["BSt"]])
            Sc.op("pool", lambda: P_.memset(W["Stbf"], 0.0), writes=[W["BStbf"]])

    def epilogue(p):
        dump("d_ysum%d" % p, ysum, [128, S], F32, [Bys])
        for tc in range(4):
            sl = slice(tc * 512, (tc + 1) * 512)
            yc = ystc[tc % 2]; byc = Bystc[tc % 2]
            Sc.dma("sp", gatec, proj_f[RG0 + p * 128:RG0 + (p + 1) * 128, sl], writes=[Bgatec])
            b = nbank(2, 8)
            mm_group(ps[b][:, :], [(bo64, ysum[:, sl], [tb1, Bys])], Bps[b]); yield
            Sc.op("dve", lambda: V.tensor_tensor(out=dtmp, in0=ysum[:, sl], in1=ps[b][:, :], op=ALU.subtract), reads=[Bys, Bps[b]], writes=[Bdt]); yield
            Sc.op("act", lambda: A_.activation(out=sq2, in_=dtmp, func=AF.Square), reads=[Bdt], writes=[Bs2]); yield
            b2 = nbank(2, 8)
            mm_group(ps[b2][:, :], [(bo64, sq2, [tb1, Bs2])], Bps[b2]); yield
            rsqrt_act(rstd, ps[b2][:, :], 1.0, GN_EPS, [Bps[b2]], [Brs]); yield
            Sc.op("dve", lambda: V.tensor_tensor(out=yn, in0=dtmp, in1=rstd, op=ALU.mult), reads=[Bdt, Brs], writes=[Byn]); yield
            Sc.op("dve", lambda: V.tensor_scalar(out=yn, in0=yn, scalar1=prm[:, LW + p:LW + p + 1], scalar2=prm[:, LB + p:LB + p + 1], op0=ALU.mult, op1=ALU.add),
                  reads=[Byn, Bprm], writes=[Byn]); yield
            Sc.op("pool", lambda: P_.tensor_tensor(out=yn, in0=yn, in1=bon[:, sl], op=ALU.add), reads=[Byn, Bbon], writes=[Byn]); yield
            Sc.op("dve", lambda: V.tensor_tensor(out=yc, in0=yn, in1=gatec, op=ALU.mult), reads=[Byn, Bgatec], writes=[byc]); yield
            Sc.dma("sp", ybuf[768 + p * 128:768 + (p + 1) * 128, sl], yc, reads=[byc]); yield

    def preps(p, j):
        return [unit_prep(p, 0, j, j % 2), unit_prep(p, 1, 3 - j, j % 2)]

    Sc.barrier()
    run(prologue(PAIRS[0]))
    run(vtok_build(PAIRS[0]))
    resets(PAIRS[0])
    run(rr(preps(PAIRS[0], 0)))
    for i, p in enumerate(PAIRS):
        nxt = PAIRS[i + 1] if i + 1 < len(PAIRS) else None
        for j in range(4):
            gens = [chain(p, 0, j, j % 2), chain(p, 1, 3 - j, j % 2)]
            if j < 3:
                gens += preps(p, j + 1)
            elif nxt is not None:
                gens.append(prologue(nxt))
            run(rr(gens))
        gens = [epilogue(p)]
        if nxt is not None:
            gens.append(vtok_build(nxt))
        run(rr(gens))
        if nxt is not None:
            resets(nxt)
            run(rr(preps(nxt, 0)))
    Sc.barrier()
    if stop_after.startswith("rwkv"):
        return finish_debug(nc, Sc, locals())

    AR.release(m_persist)
    cosr = AR.f32(16, 32); sinr = AR.f32(16, 32); gqk = AR.f32(16, 64); esink = AR.f32(12); mLR = AR.bf16(2, 128)
    Bcs, Bsn, Bgq, Bes, Bml = Buf(), Buf(), Buf(), Buf(), Buf()
    Sc.dma("sp", cosr, cos_d[:, :, :], writes=[Bcs]); Sc.dma("sp", sinr, sin_d[:, :, :], writes=[Bsn])
    Sc.dma("sp", gqk, gqk_d.rearrange("p (a b) -> p a b", b=64), writes=[Bgq]); Sc.dma("sp", esink, sink_d[:, :], writes=[Bes])
    Sc.dma("pool", mLR, maskLR_d[:, :, :], writes=[Bml])
    Sc.op("act", lambda: A_.activation(out=esink, in_=esink, func=AF.Exp), reads=[Bes], writes=[Bes])
    qTa = AR.bf16(12, S); kTa = AR.bf16(4, S); vext = AR.bf16(16, 4, 128); ya = AR.bf16(6, S)
    BqTa = [Buf() for _ in range(16)]; BkTa = [Buf() for _ in range(16)]; Bvx = [Buf() for _ in range(16)]; Bya = [Buf() for _ in range(16)]
    Sc.op("pool", lambda: P_.memset(vext, 1.0), writes=Bvx)
    qkv = [AR.f32(NQKV) for _ in range(2)]; Bqkv2 = [Buf(), Buf()]
    gta = [AR.f32(6, 128) for _ in range(6)]; Bgta = [Buf() for _ in range(6)]
    sqa2 = [AR.f32(1024) for _ in range(2)]; ssa2 = [AR.f32(16) for _ in range(2)]; rsa2 = [AR.f32(16) for _ in range(2)]
    qna2 = [AR.f32(16, 64) for _ in range(2)]; qra2 = [AR.bf16(16, 64) for _ in range(2)]
    rt2 = [[AR.f32(16, 32) for _ in range(4)] for _ in range(2)]
    Bsqa2, Bssa2, Bqna2, Bqra2 = [Buf(), Buf()], [Buf(), Buf()], [Buf(), Buf()], [Buf(), Buf()]
    Brt2 = [[Buf() for _ in range(4)] for _ in range(2)]
    pTa = [AR.bf16(384) for _ in range(6)]; BpTa = [Buf() for _ in range(6)]
    rda = AR.f32(4, 128); Brda = Buf(); yodd = AR.f32(2, 128); Byodd = Buf(); yev = AR.f32(2, 128); Byev = Buf()
    Bo = [Buf("o5"), Buf("o6"), Buf("o7")]
    pta_rr = [0]
    ag_src = proj_f[AG0:AG0 + 768, :].rearrange("(c p) t -> p c t", p=128)

    def prep_blk(tb):
        qk_ = qkv[tb % 2]; bqk = Bqkv2[tb % 2]
        sqa, ssa, rsa, qna, qra, rt_ = sqa2[tb % 2], ssa2[tb % 2], rsa2[tb % 2], qna2[tb % 2], qra2[tb % 2], rt2[tb % 2]
        Bsqa, Bssa, Bqna, Bqra, Brt = Bsqa2[tb % 2], Bssa2[tb % 2], Bqna2[tb % 2], Bqra2[tb % 2], Brt2[tb % 2]
        Sc.dma("sp", qk_, qkv_t[tb * 128:(tb + 1) * 128, :], writes=[bqk])
        Sc.dma("sp", gta[tb % 6], ag_src[:, :, tb * 128:(tb + 1) * 128], writes=[Bgta[tb % 6]])
        Sc.op("act", lambda: A_.activation(out=sqa, in_=qk_[:, 0:1024], func=AF.Square), reads=[bqk], writes=[Bsqa]); yield
        Sc.op("dve", lambda: V.tensor_reduce(out=ssa, in_=sqa.rearrange("p (a b) -> p a b", b=64), axis=AX.X, op=ALU.add), reads=[Bsqa], writes=[Bssa]); yield
        rsqrt_act(rsa, ssa, 1.0 / 64, EPS, [Bssa], [Bssa]); yield
        Sc.op("dve", lambda: V.tensor_tensor(out=qna, in0=qk_[:, 0:1024].rearrange("p (a b) -> p a b", b=64), in1=rsa.unsqueeze(2).to_broadcast([128, 16, 64]), op=ALU.mult),
              reads=[bqk, Bssa], writes=[Bqna]); yield
        Sc.op("pool", lambda: P_.tensor_tensor(out=qna, in0=qna, in1=gqk, op=ALU.mult), reads=[Bqna, Bgq], writes=[Bqna]); yield
        t1 = qna[:, :, 0:32]; t2 = qna[:, :, 32:64]
        cb = cosr[:, tb, :].unsqueeze(1).to_broadcast([128, 16, 32]); sb_ = sinr[:, tb, :].unsqueeze(1).to_broadcast([128, 16, 32])
        Sc.op("dve", lambda: V.tensor_tensor(out=rt_[0], in0=t1, in1=cb, op=ALU.mult), reads=[Bqna, Bcs], writes=[Brt[0]]); yield
        Sc.op("pool", lambda: P_.tensor_tensor(out=rt_[1], in0=t2, in1=sb_, op=ALU.mult), reads=[Bqna, Bsn], writes=[Brt[1]]); yield
        Sc.op("dve", lambda: V.tensor_tensor(out=qra[:, :, 0:32], in0=rt_[0], in1=rt_[1], op=ALU.subtract), reads=[Brt[0], Brt[1]], writes=[Bqra]); yield
        Sc.op("pool", lambda: P_.tensor_tensor(out=rt_[2], in0=t2, in1=cb, op=ALU.mult), reads=[Bqna, Bcs], writes=[Brt[2]]); yield
        Sc.op("dve", lambda: V.tensor_tensor(out=rt_[3], in0=t1, in1=sb_, op=ALU.mult), reads=[Bqna, Bsn], writes=[Brt[3]]); yield
        Sc.op("dve", lambda: V.tensor_tensor(out=qra[:, :, 32:64], in0=rt_[2], in1=rt_[3], op=ALU.add), reads=[Brt[2], Brt[3]], writes=[Bqra]); yield
        Sc.op("act", lambda: A_.activation(out=vext[:, tb, :, 0:64], in_=qk_[:, 1024:1280].rearrange("p (a b) -> p a b", b=64), func=AF.Copy), reads=[bqk], writes=[Bvx[tb]]); yield
        b = [0, 6][tb % 2]
        qflat = qra.rearrange("p a b -> p (a b)")
        for j in range(8):
            Sc.op("pe", lambda: T_.transpose(out=psb[b][:, j * 128:(j + 1) * 128], in_=qflat[:, j * 128:(j + 1) * 128], identity=identb),
                  reads=[Bqra, Bid], writes=[Bps[b]] if j == 0 else [], inc=(j == 7), setw=[Bps[b]] if j == 7 else [])
        yield
        psv = psb[b].rearrange("p (a b) -> p a b", b=128)
        tsl = slice(tb * 128, (tb + 1) * 128)
        qv = qTa.rearrange("p (h two) t -> p h two t", two=2)
        kv = kTa.rearrange("p (h two) t -> p h two t", two=2)
        Sc.op("dve", lambda: V.tensor_copy(out=qv[0:64, :, 0, tsl], in_=psv[0:64, 0:6, :]), reads=[Bps[b]], writes=[BqTa[tb]]); yield
        Sc.op("act", lambda: A_.activation(out=qv[0:64, :, 1, tsl], in_=psv[64:128, 0:6, :], func=AF.Copy), reads=[Bps[b]], writes=[BqTa[tb]]); yield
        Sc.op("dve", lambda: V.tensor_copy(out=kv[0:64, :, 0, tsl], in_=psv[0:64, 6:8, :]), reads=[Bps[b]], writes=[BkTa[tb]]); yield
        Sc.op("act", lambda: A_.activation(out=kv[0:64, :, 1, tsl], in_=psv[64:128, 6:8, :], func=AF.Copy), reads=[Bps[b]], writes=[BkTa[tb]]); yield

    def attend(n):
        qsl = slice(n * 128, (n + 1) * 128)
        gt_ = gta[n % 6]; bgt = Bgta[n % 6]
        seq = []
        for g in range(4):
            kbs = [kb for kb in (n - 1, n, n + 1) if 0 <= kb < 16]
            for kb in kbs:
                seq.append((g, kb, kb == kbs[0], kb == kbs[-1]))
        order = []
        for (g, kb, fst, lst) in seq:
            for hh in range(3):
                order.append((g, kb, hh, fst, lst))
        firsts = {}; lasts = {}
        for idx, (g, kb, hh, fst, lst) in enumerate(order):
            ob = (3 * g + hh) // 4
            firsts.setdefault(ob, idx); lasts[ob] = idx
        idx = 0
        LOOK = 2
        issued = {}

        def score(i):
            g, kb, fst, lst = seq[i]
            b = sbanks[pta_rr[0] % len(sbanks)]
            pi = pta_rr[0] % 6; pta_rr[0] += 1
            mm_group(ps[b][:, 0:384], [(kTa[0:64, g, kb * 128:(kb + 1) * 128], qTa[0:64, 3 * g:3 * g + 3, qsl], [BkTa[kb], BqTa[n]])], Bps[b])
            issued[i] = (b, pi)

        for i in range(min(LOOK, len(seq))):
            score(i)
        yield
        for i, (g, kb, fst, lst) in enumerate(seq):
            if i + LOOK < len(seq):
                score(i + LOOK)
                yield
            b, pi = issued.pop(i)
            Sc.op("act", lambda: A_.activation(out=pTa[pi], in_=ps[b][:, 0:384], func=AF.Exp, scale=0.125), reads=[Bps[b]], writes=[BpTa[pi]]); yield
            if kb != n:
                mk = mLR[:, 0 if kb < n else 1, :].unsqueeze(1).to_broadcast([128, 3, 128])
                Sc.op("dve", lambda: V.tensor_tensor(out=pTa[pi].rearrange("p (a b) -> p a b", b=128), in0=pTa[pi].rearrange("p (a b) -> p a b", b=128), in1=mk, op=ALU.mult),
                      reads=[BpTa[pi], Bml], writes=[BpTa[pi]]); yield
            for hh in range(3):
                head = 3 * g + hh
                ob = head // 4
                col = (head % 4) * 128
                isf = firsts[ob] == idx; isl = lasts[ob] == idx
                st_flag = fst and (hh == 0 or head % 4 == 0)
                Sc.op("pe", lambda: T_.matmul(ps[obanks[ob]][:, col:col + 128], lhsT=vext[:, kb, g, :], rhs=pTa[pi][:, hh * 128:(hh + 1) * 128], start=st_flag, stop=lst),
                      reads=[Bvx[kb], BpTa[pi]], writes=[Bo[ob]] if isf else [], inc=(hh == 2), setw=[Bo[ob]] if isl else [])
                idx += 1
            yield
        for ob in range(3):
            pv = ps[obanks[ob]][:, :].rearrange("p (a b) -> p a b", b=128)
            Sc.op("dve", lambda: V.tensor_tensor(out=rda[0:64, :, :], in0=pv[64:128, :, :], in1=esink[64:128, ob * 4:(ob + 1) * 4].unsqueeze(2).to_broadcast([64, 4, 128]), op=ALU.add),
                  reads=[Bo[ob], Bes], writes=[Brda]); yield
            Sc.op("act", lambda: A_.activation(out=rda[0:64, :, :], in_=rda[0:64, :, :], func=AF.Ln), reads=[Brda], writes=[Brda]); yield
            Sc.op("act", lambda: A_.activation(out=rda[0:64, :, :], in_=rda[0:64, :, :], func=AF.Exp, scale=-1.0), reads=[Brda], writes=[Brda]); yield
            pv2 = pv.rearrange("p (h two) t -> p h two t", two=2)
            rd2 = rda.rearrange("p (h two) t -> p h two t", two=2)
            c0 = ob * 2
            Sc.op("dve", lambda: V.tensor_tensor(out=yev[0:64, :, :], in0=pv2[0:64, :, 0, :], in1=rd2[0:64, :, 0, :], op=ALU.mult), reads=[Bo[ob], Brda], writes=[Byev]); yield
            Sc.op("dve", lambda: V.tensor_tensor(out=yodd[64:128, :, :], in0=pv2[0:64, :, 1, :], in1=rd2[0:64, :, 1, :], op=ALU.mult), reads=[Bo[ob], Brda], writes=[Byodd]); yield
            Sc.op("pool", lambda: P_.tensor_tensor(out=ya[0:64, c0:c0 + 2, qsl], in0=yev[0:64, :, :], in1=gt_[0:64, c0:c0 + 2, :], op=ALU.mult), reads=[Byev, bgt], writes=[Bya[n]]); yield
            Sc.op("pool", lambda: P_.tensor_tensor(out=ya[64:128, c0:c0 + 2, qsl], in0=yodd[64:128, :, :], in1=gt_[64:128, c0:c0 + 2, :], op=ALU.mult), reads=[Byodd, bgt], writes=[Bya[n]]); yield

    sbanks = [1, 2, 7]
    obanks = [3, 4, 5]

    def seq_(gs):
        for g in gs:
            yield from g

    def attn_driver():
        for w in range(8):
            gens = [prep_blk(2 * w), prep_blk(2 * w + 1)]
            att = [attend(n) for n in (2 * w - 3, 2 * w - 2) if n >= 0]
            if att:
                gens.append(seq_(att))
            yield from rr(gens, [1, 1, 2])
        yield from attend(13)
        yield from attend(14)
        yield from attend(15)
        for c in range(6):
            Sc.dma("sp", ybuf[c * 128:(c + 1) * 128, :], ya[:, c, :], reads=Bya); yield

    run(attn_driver())
    Sc.barrier()
    if stop_after in ("attn", "xattn"):
        return finish_debug(nc, Sc, locals())

    AR.release(m_persist)
    mT = AR.bf16(16, S); BmT = [Buf() for _ in range(16)]
    m_p3 = AR.mark()
    yall = AR.bf16(16, S); Byall = [Buf() for _ in range(16)]
    wo = [AR.bf16(16, 256) for _ in range(2)]; Bwo = [[Buf(), Buf(), Buf()] for _ in range(2)]
    gt3 = [AR.f32(S) for _ in range(2)]; Bgt3 = [Buf(), Buf()]
    macc = AR.f32(S); Bmacc = [Buf() for _ in range(4)]
    ptmp = [AR.f32(512) for _ in range(2)]; Bptmp = [Buf(), Buf()]
    for k in range(16):
        Sc.dma("sp", yall[:, k, :], ybuf[k * 128:(k + 1) * 128, :], writes=[Byall[k]])
    wsrcs = [(attn_w_o.rearrange("(k p) n -> p k n", p=128), 0, 6), (rwkv_w_o.rearrange("(k p) n -> p k n", p=128), 6, 6), (x_w_o.rearrange("(k p) n -> p k n", p=128), 12, 4)]
    kranges = [range(0, 6), range(6, 12), range(12, 16)]

    def load_wo(fg):
        for bi, (src, k0, nk) in enumerate(wsrcs):
            Sc.dma("pool", wo[fg % 2][:, k0:k0 + nk, :], src[:, :, fg * 256:(fg + 1) * 256], writes=[Bwo[fg % 2][bi]])

    g3_rr = [0]; pt_rr = [0]
    load_wo(0)
    for fg in range(8):
        if fg + 1 < 8:
            load_wo(fg + 1)
        for fi in range(2):
            f = fg * 2 + fi
            for bi in range(3):
                gi = g3_rr[0] % 2; g3_rr[0] += 1
                r0 = MG0 + bi * 2048 + f * 128
                Sc.dma("sp", gt3[gi], proj_f[r0:r0 + 128, :], writes=[Bgt3[gi]])
                for tc in range(4):
                    sl = slice(tc * 512, (tc + 1) * 512)
                    bk = nbank()
                    mm_group(ps[bk][:, :], [(wo[fg % 2][:, kc, fi * 128:(fi + 1) * 128], yall[:, kc, sl], [Bwo[fg % 2][bi], Byall[kc]]) for kc in kranges[bi]], Bps[bk])
                    if bi == 0:
                        Sc.op("dve", lambda: V.tensor_tensor(out=macc[:, sl], in0=ps[bk][:, :], in1=gt3[gi][:, sl], op=ALU.mult), reads=[Bps[bk], Bgt3[gi]], writes=[Bmacc[tc]])
                    else:
                        pi = pt_rr[0] % 2; pt_rr[0] += 1
                        Sc.op("dve", lambda: V.tensor_tensor(out=ptmp[pi], in0=ps[bk][:, :], in1=gt3[gi][:, sl], op=ALU.mult), reads=[Bps[bk], Bgt3[gi]], writes=[Bptmp[pi]])
                        if bi == 1:
                            Sc.op("pool", lambda: P_.tensor_tensor(out=macc[:, sl], in0=macc[:, sl], in1=ptmp[pi], op=ALU.add), reads=[Bmacc[tc], Bptmp[pi]], writes=[Bmacc[tc]])
                        else:
                            Sc.op("pool", lambda: P_.tensor_tensor(out=mT[:, f, sl], in0=macc[:, sl], in1=ptmp[pi], op=ALU.add), reads=[Bmacc[tc], Bptmp[pi]], writes=[BmT[f]])
    dump("d_mT", mT, [128, 16, S], BF16, BmT)
    Sc.barrier()
    if stop_after == "merge":
        return finish_debug(nc, Sc, locals())
    AR.release(m_p3)
    wout = [AR.bf16(16, 512) for _ in range(2)]; Bwout = [[Buf() for _ in range(4)] for _ in range(2)]
    xres = [AR.f32(512) for _ in range(3)]; Bxres = [Buf() for _ in range(3)]
    ost = [AR.f32(512) for _ in range(3)]; Bost = [Buf() for _ in range(3)]
    wo_src = w_out.rearrange("(k p) n -> p k n", p=128)

    def load_wout(ng):
        for q in range(4):
            Sc.dma("pool", wout[ng % 2][:, q * 4:(q + 1) * 4, :], wo_src[:, q * 4:(q + 1) * 4, ng * 512:(ng + 1) * 512], writes=[Bwout[ng % 2][q]])

    load_wout(0)
    xr_rr = [0]
    final_toks = []
    for ng in range(4):
        if ng + 1 < 4:
            load_wout(ng + 1)
        for tb in range(16):
            xi = xr_rr[0] % 3; xr_rr[0] += 1
            Sc.dma("sp", xres[xi], x[tb * 128:(tb + 1) * 128, ng * 512:(ng + 1) * 512], writes=[Bxres[xi]])
            bk = nbank()
            mm_group(ps[bk][:, :], [(mT[:, f, tb * 128:(tb + 1) * 128], wout[ng % 2][:, f, :], [BmT[f], Bwout[ng % 2][f // 4]]) for f in range(16)], Bps[bk])
            Sc.op("dve", lambda: V.tensor_tensor(out=ost[xi], in0=ps[bk][:, :], in1=xres[xi], op=ALU.add), reads=[Bps[bk], Bxres[xi]], writes=[Bost[xi]])
            final_toks.append(Sc.dma("sp", out[tb * 128:(tb + 1) * 128, ng * 512:(ng + 1) * 512], ost[xi], reads=[Bost[xi]]))
    return finish_debug(nc, Sc, locals())


def finish_debug(nc, Sc, env):
    Sc.barrier()
    ok, stuck, _ = Sc.check_deadlock()
    if not ok:
        raise RuntimeError("logical deadlock in emitted program: %r" % (stuck,))
    Sc.close()
    for cm in reversed(env["ps_cms"]):
        cm.__exit__(None, None, None)
    env["big_cm"].__exit__(None, None, None)
    return nc


def make_in_maps(inputs):
    c = host_consts()
    f = lambda k: np.asarray(inputs[k], dtype=np.float32)
    sq = lambda k: f(k)[0]
    prm = np.zeros((128, NPRM), np.float32)

    def put(col, vec):
        m = vec.size // 128
        prm[:, col:col + m] = vec.reshape(m, 128).T

    put(NG, sq("norm_g")); put(MG_, sq("mem_norm_g")); put(GB, sq("gate_b")); put(MU, sq("rwkv_mu"))
    put(KK, sq("rwkv_k_k")); put(KA, sq("rwkv_k_a")); put(RK, sq("rwkv_r_k").reshape(-1)); put(LW, sq("rwkv_ln_w"))
    put(LB, sq("rwkv_ln_b")); put(W0, sq("rwkv_w0").reshape(-1)); put(A0, sq("rwkv_a0").reshape(-1))
    put(XQG, sq("x_q_norm_g")); put(XKG, sq("x_k_norm_g"))
    gqk = np.concatenate([np.tile(sq("attn_q_norm_g"), 12), np.tile(sq("attn_k_norm_g"), 4)])
    shared = {
        "w_in": sq("w_in"), "attn_w_o": sq("attn_w_o"), "rwkv_w_o": sq("rwkv_w_o"), "x_w_o": sq("x_w_o"),
        "x_w_kv": sq("x_w_kv"), "w_out": sq("w_out"),
        "w2cat": np.ascontiguousarray(sq("rwkv_w2").reshape(128, 768)), "a2cat": np.ascontiguousarray(sq("rwkv_a2").reshape(128, 768)),
        "prm": prm, "gqk": np.ascontiguousarray(np.broadcast_to(gqk[None, :], (128, 1024))),
        "sinkb": np.ascontiguousarray(np.broadcast_to(sq("attn_sink")[None, :], (128, 12))),
    }
    shared.update(c)
    xs = f("x"); ms = f("mem")
    return [dict(shared, x=np.ascontiguousarray(xs[b]), mem=np.ascontiguousarray(ms[b])) for b in range(xs.shape[0])]


_NC_CACHE = {}


def kernel(**inputs):
    in_maps = make_in_maps(inputs)
    if "nc" not in _NC_CACHE:
        _NC_CACHE["nc"] = build_nc()
    nc = _NC_CACHE["nc"]
    res = run_bass_kernel_spmd(nc, in_maps, core_ids=list(range(len(in_maps))))
    return np.stack([np.asarray(r["out"], dtype=np.float32) for r in res.results], axis=0)
```

```python
import math
import numpy as np
import ml_dtypes
import concourse.bass as bass
import concourse.mybir as mybir
from concourse.bass_utils import run_bass_kernel_spmd

F32 = mybir.dt.float32
BF16 = mybir.dt.bfloat16
AF = mybir.ActivationFunctionType
ALU = mybir.AluOpType
AX = mybir.AxisListType

S = 2048
D = 2048
NMEM = 256
INW = 12544
NQKV = 1280
NPF = INW - NQKV
EPS = 1e-6
GN_EPS = 64e-5
C1 = -0.5 * math.exp(-0.5)

AG0 = 0
R0 = 2048 - NQKV
K0 = R0 + 768
V0 = K0 + 768
LW0 = V0 + 768
LA0 = LW0 + 128
RG0 = 4608 - NQKV
XQ0 = 5376 - NQKV
XG0 = 5888 - NQKV
MG0 = 6400 - NQKV

NG, MG_, GB, MU, KK, KA, RK, LW, LB, W0, A0, XQG, XKG = 0, 16, 32, 80, 100, 106, 112, 118, 124, 130, 142, 154, 155
OMM, HMU, OMK, HW0, HA0, XG2 = 156, 176, 196, 202, 214, 226
NPRM = 228


class Buf:
    __slots__ = ("name", "w", "r", "pending")

    def __init__(self, name=""):
        self.name = name
        self.w = None
        self.r = {}
        self.pending = False


class Sched:
    ENG = ("pe", "act", "dve", "pool", "sp")

    def __init__(self, nc, n_dma_sems=40):
        self.nc = nc
        self.eng = {"pe": nc.tensor, "act": nc.scalar, "dve": nc.vector, "pool": nc.gpsimd, "sp": nc.sync}
        self.sem = {}
        self.cnt = {e: 0 for e in self.ENG}
        self.known = {e: {} for e in self.ENG}
        self._cms = []
        for e in self.ENG:
            cm = nc.semaphore("s_" + e)
            self.sem[e] = cm.__enter__()
            self._cms.append(cm)
        self.dsem = []
        for i in range(n_dma_sems):
            cm = nc.semaphore("d%d" % i)
            self.dsem.append([cm.__enter__(), 0])
            self._cms.append(cm)
        self.dnext = 0
        self.nwait = 0
        self.log = {e: [] for e in self.ENG}

    def close(self):
        for cm in reversed(self._cms):
            cm.__exit__(None, None, None)

    def _wait(self, e, key, semh, val):
        k = self.known[e]
        if k.get(key, 0) >= val:
            return
        self.eng[e].wait_ge(semh, val)
        self.nwait += 1
        self.log[e].append(("w", key, val))
        k[key] = val

    def wait_tok(self, e, tok):
        if tok is None:
            return
        kind, a, v = tok
        if kind == "eng":
            self._wait(e, a, self.sem[a], v)
        else:
            self._wait(e, "d%d" % a, self.dsem[a][0], v)

    def deps(self, e, reads, writes):
        for b in reads:
            self.wait_tok(e, b.w)
        for b in writes:
            self.wait_tok(e, b.w)
            for tok in b.r.values():
                self.wait_tok(e, tok)

    def op(self, e, fn, reads=(), writes=(), inc=True, setw=None):
        self.deps(e, reads, writes)
        ins = fn()
        if inc:
            self.cnt[e] += 1
            ins.then_inc(self.sem[e], 1)
            self.log[e].append(("i", e, 1))
            tok = ("eng", e, self.cnt[e])
        else:
            tok = ("eng", e, self.cnt[e] + 1)
        for b in reads:
            b.r[("eng", e)] = tok
            b.pending = False
        for b in (writes if setw is None else setw):
            b.w = tok
            b.r = {}
            b.pending = True
        for b in writes:
            b.pending = True
        return ins

    def dma(self, e, out, in_, reads=(), writes=()):
        idx = self.dnext
        self.dnext = (self.dnext + 1) % len(self.dsem)
        semh, val = self.dsem[idx]
        if val > 0:
            self._wait(e, "d%d" % idx, semh, val)
        self.deps(e, reads, writes)
        ins = self.eng[e].dma_start(out=out, in_=in_)
        val += 16
        ins.then_inc(semh, 16)
        self.log[e].append(("i", "d%d" % idx, 16))
        self.dsem[idx][1] = val
        tok = ("dma", idx, val)
        for b in reads:
            b.r[("dma", idx)] = tok
        for b in writes:
            b.w = tok
            b.r = {}
        return tok

    def check_deadlock(self):
        sem = {}
        pos = {e: 0 for e in self.ENG}
        prog = True
        while prog:
            prog = False
            for e in self.ENG:
                lg = self.log[e]
                while pos[e] < len(lg):
                    kind, key, val = lg[pos[e]]
                    if kind == "w":
                        if sem.get(key, 0) < val:
                            break
                    else:
                        sem[key] = sem.get(key, 0) + val
                    pos[e] += 1
                    prog = True
        stuck = {e: (pos[e], len(self.log[e]), self.log[e][pos[e]] if pos[e] < len(self.log[e]) else None) for e in self.ENG}
        ok = all(pos[e] == len(self.log[e]) for e in self.ENG)
        return ok, stuck, sem

    def barrier(self):
        for e in self.ENG:
            for e2 in self.ENG:
                if e2 != e and self.cnt[e2] > 0:
                    self._wait(e, e2, self.sem[e2], self.cnt[e2])
            for idx, (semh, val) in enumerate(self.dsem):
                if val > 0:
                    self._wait(e, "d%d" % idx, semh, val)


class Arena:
    def __init__(self, big, n):
        self.big = big
        self.n = n
        self.off = 0

    def mark(self):
        return self.off

    def release(self, m):
        self.off = m

    def _raw(self, nf32):
        a = self.off
        self.off += nf32
        assert self.off <= self.n, "SBUF arena overflow %d > %d" % (self.off, self.n)
        return self.big[:, a:a + nf32]

    @staticmethod
    def _shape(ap, dims):
        if len(dims) == 1:
            return ap
        if len(dims) == 2:
            return ap.rearrange("p (a b) -> p a b", b=dims[1])
        if len(dims) == 3:
            return ap.rearrange("p (a b c) -> p a b c", b=dims[1], c=dims[2])
        raise ValueError

    def f32(self, *dims):
        n = int(np.prod(dims))
        return self._shape(self._raw(n), dims)

    def bf16(self, *dims):
        n = int(np.prod(dims))
        assert n % 2 == 0
        return self._shape(self._raw(n // 2).bitcast(BF16), dims)


def host_consts():
    c = {}
    c["identf"] = np.eye(128, dtype=np.float32)
    bo = np.zeros((128, 128), np.float32)
    bo[:64, :64] = 1.0
    bo[64:, 64:] = 1.0
    c["bones"] = bo
    c["bo64"] = bo / 64.0
    half = 32
    inv = (10000.0 ** (-np.arange(half, dtype=np.float64) / half))
    ang = np.arange(S, dtype=np.float64)[:, None] * inv[None, :]
    c["cosr"] = np.ascontiguousarray(np.cos(ang).reshape(16, 128, 32).transpose(1, 0, 2)).astype(np.float32)
    c["sinr"] = np.ascontiguousarray(np.sin(ang).reshape(16, 128, 32).transpose(1, 0, 2)).astype(np.float32)
    j = np.arange(128)[:, None]
    i = np.arange(128)[None, :]
    c["maskLR"] = np.stack([(j >= i), (j <= i)], axis=1).astype(np.float32)
    t = np.arange(64)
    st_f = (t[:, None] < t[None, :]).astype(np.float32)
    in_f = (t[:, None] <= t[None, :]).astype(np.float32)
    def bd(m):
        z = np.zeros((128, 128), np.float32)
        z[:64, :64] = m
        z[64:, 64:] = m
        return z
    rwm = np.zeros((128, 2, 3, 128), np.float32)
    rwm[:, 0, 0] = bd(st_f); rwm[:, 0, 1] = bd(in_f); rwm[:, 0, 2] = bd(st_f.T)
    rwm[:, 1, 0] = bd(st_f.T); rwm[:, 1, 1] = bd(in_f.T); rwm[:, 1, 2] = bd(st_f)
    c["rwm"] = rwm
    seg = np.ones((128, 512), np.float32)
    seg[:, ::64] = 0.0
    c["segm"] = seg
    return c


def build_nc(stop_after="all", debug=()):
    nc = bass.Bass("TRN2", target_bir_lowering=False)

    def din(name, shape):
        return nc.dram_tensor(name, list(shape), F32, kind="ExternalInput").ap()

    x = din("x", [S, D]); mem = din("mem", [NMEM, D]); w_in = din("w_in", [D, INW])
    attn_w_o = din("attn_w_o", [768, D]); rwkv_w_o = din("rwkv_w_o", [768, D]); x_w_o = din("x_w_o", [512, D])
    x_w_kv = din("x_w_kv", [D, 1024]); w_out = din("w_out", [D, D])
    w2cat = din("w2cat", [128, 768]); a2cat = din("a2cat", [128, 768])
    prm_d = din("prm", [128, NPRM]); gqk_d = din("gqk", [128, 1024]); sink_d = din("sinkb", [128, 12])
    identf_d = din("identf", [128, 128]); bones_d = din("bones", [128, 128]); bo64_d = din("bo64", [128, 128])
    cos_d = din("cosr", [128, 16, 32]); sin_d = din("sinr", [128, 16, 32]); maskLR_d = din("maskLR", [128, 2, 128])
    rwm_d = din("rwm", [128, 2, 3, 128]); segm_d = din("segm", [128, 512])
    out = nc.dram_tensor("out", [S, D], F32, kind="ExternalOutput").ap()

    def dscr(name, shape, dt):
        kind = "ExternalOutput" if name in debug else "Internal"
        return nc.dram_tensor(name, list(shape), dt, kind=kind).ap()

    proj_f = dscr("proj_f", [NPF, S], F32)
    qkv_t = dscr("qkv_t", [S, NQKV], F32)
    ybuf = dscr("ybuf", [2048, S], BF16)
    dbg = dscr("dbg", [128, 4096], F32) if "dbg" in debug else None

    NBIG = 52600
    big_cm = nc.sbuf_tensor("big", [128, NBIG], F32)
    big = big_cm.__enter__()
    ps_cms = [nc.psum_tensor("ps%d" % i, [128, 512], F32) for i in range(8)]
    ps = [cm.__enter__() for cm in ps_cms]
    Bps = [Buf("ps%d" % i) for i in range(8)]
    psb = [p[:, :].bitcast(BF16) for p in ps]
    Sc = Sched(nc)
    AR = Arena(big, NBIG)
    V, A_, P_, G_, T_ = nc.vector, nc.scalar, nc.gpsimd, nc.sync, nc.tensor
    bank_rr = [0]
    dumped = set()

    def dump(name, sb_ap, shape, dt, reads):
        if name in debug and name not in dumped:
            dumped.add(name)
            t = nc.dram_tensor(name, list(shape), dt, kind="ExternalOutput").ap()
            Sc.dma("sp", t, sb_ap, reads=reads)

    def nbank(lo=0, hi=8):
        for _ in range(hi - lo):
            b = lo + bank_rr[0] % (hi - lo)
            bank_rr[0] += 1
            if not Bps[b].pending:
                return b
        raise RuntimeError("all PSUM banks in [%d,%d) hold unconsumed data" % (lo, hi))

    def mm_group(out_ap, items, obuf):
        n = len(items)
        for i, (l, r, rd) in enumerate(items):
            first, last = i == 0, i == n - 1
            Sc.op("pe", lambda: T_.matmul(out_ap, lhsT=l, rhs=r, start=first, stop=last), reads=rd,
                  writes=[obuf] if first else [], inc=last, setw=[obuf] if last else [])

    def mm_multi(items, obuf):
        n = len(items)
        for i, (o, l, r, rd) in enumerate(items):
            first, last = i == 0, i == n - 1
            Sc.op("pe", lambda: T_.matmul(o, lhsT=l, rhs=r, start=True, stop=True), reads=rd,
                  writes=[obuf] if first else [], inc=last, setw=[obuf] if last else [])

    def rsqrt_act(out_ap, in_ap, scale, eps, reads, writes, tmp_ap=None):
        t = out_ap if tmp_ap is None else tmp_ap
        Sc.op("act", lambda: A_.activation(out=t, in_=in_ap, func=AF.Ln, bias=eps, scale=scale), reads=reads, writes=writes)
        Sc.op("act", lambda: A_.activation(out=out_ap, in_=t, func=AF.Exp, scale=-0.5), reads=writes, writes=writes)

    identf = AR.f32(128); identb = AR.bf16(128); prm = AR.f32(NPRM)
    kmT = AR.bf16(4, 256); vm = AR.bf16(2, 512)
    Bid, Bprm, Bkm, Bvm = Buf("id"), Buf("prm"), Buf("kmT"), Buf("vm")
    Sc.dma("sp", identf, identf_d[:, :], writes=[Bid])
    Sc.dma("sp", prm[:, 0:OMM], prm_d[:, 0:OMM], writes=[Bprm])
    Sc.op("dve", lambda: V.tensor_copy(out=identb, in_=identf), reads=[Bid], writes=[Bid])
    Sc.op("dve", lambda: V.tensor_scalar(out=prm[:, OMM:OMM + 20], in0=prm[:, MU:MU + 20], scalar1=-1.0, scalar2=1.0, op0=ALU.mult, op1=ALU.add), reads=[Bprm], writes=[Bprm])
    Sc.op("dve", lambda: V.tensor_scalar(out=prm[:, HMU:HMU + 20], in0=prm[:, MU:MU + 20], scalar1=0.5, scalar2=None, op0=ALU.mult), reads=[Bprm], writes=[Bprm])
    Sc.op("dve", lambda: V.tensor_scalar(out=prm[:, OMK:OMK + 6], in0=prm[:, KA:KA + 6], scalar1=-1.0, scalar2=1.0, op0=ALU.mult, op1=ALU.add), reads=[Bprm], writes=[Bprm])
    Sc.op("dve", lambda: V.tensor_scalar(out=prm[:, HW0:HW0 + 24], in0=prm[:, W0:W0 + 24], scalar1=0.5, scalar2=None, op0=ALU.mult), reads=[Bprm], writes=[Bprm])
    Sc.op("dve", lambda: V.tensor_tensor(out=prm[:, XG2:XG2 + 1], in0=prm[:, XQG:XQG + 1], in1=prm[:, XKG:XKG + 1], op=ALU.mult), reads=[Bprm], writes=[Bprm])
    m_persist = AR.mark()

    hT = AR.bf16(16, S)
    BhT = [Buf("hT%d" % g) for g in range(4)]
    memT = AR.bf16(16, NMEM); BmemT = Buf("memT")
    m_ph0 = AR.mark()
    xbuf = [AR.f32(4, D) for _ in range(2)]
    Bx = [[Buf() for _ in range(4)] for _ in range(2)]
    junk = AR.f32(D); Bjunk = Buf("junk")
    evac_rr = [0]

    def build_T(src, nblk_total, gcol, dstT, dst_bufs):
        ngrp = (nblk_total + 3) // 4
        for g in range(ngrp):
            nb = min(4, nblk_total - g * 4)
            xb, bx = xbuf[g % 2], Bx[g % 2]
            ssq = AR.f32(4); rt = AR.f32(4); Bss = Buf("ss")
            Sc.op("dve", lambda: V.memset(ssq, 0.0), writes=[Bss])
            for i in range(nb):
                r0 = (g * 4 + i) * 128
                Sc.dma("sp", xb[:, i, :], src[r0:r0 + 128, :], writes=[bx[i]])
            for i in range(nb):
                Sc.op("act", lambda: A_.activation(out=junk, in_=xb[:, i, :], func=AF.Square, accum_out=ssq[:, i:i + 1]),
                      reads=[bx[i]], writes=[Bjunk, Bss])
            rsqrt_act(rt[:, 0:nb], ssq[:, 0:nb], 1.0 / D, EPS, [Bss], [Bss])
            for i in range(nb):
                Sc.op("dve", lambda: V.tensor_scalar(out=xb[:, i, :], in0=xb[:, i, :], scalar1=rt[:, i:i + 1], scalar2=None, op0=ALU.mult),
                      reads=[bx[i], Bss], writes=[bx[i]])
            for c in range(16):
                b = nbank()
                for i in range(nb):
                    Sc.op("pe", lambda: T_.transpose(out=ps[b][:, i * 128:(i + 1) * 128], in_=xb[:, i, c * 128:(c + 1) * 128], identity=identf),
                          reads=[bx[i], Bid], writes=[Bps[b]] if i == 0 else [], inc=(i == nb - 1), setw=[Bps[b]] if i == nb - 1 else [])
                dst = dstT[:, c, g * 512:g * 512 + nb * 128]
                gc = prm[:, gcol + c:gcol + c + 1]
                if evac_rr[0] % 2 == 0:
                    Sc.op("act", lambda: A_.activation(out=dst, in_=ps[b][:, 0:nb * 128], func=AF.Copy, scale=gc),
                          reads=[Bps[b], Bprm], writes=[dst_bufs[g]])
                else:
                    Sc.op("dve", lambda: V.tensor_scalar(out=dst, in0=ps[b][:, 0:nb * 128], scalar1=gc, scalar2=None, op0=ALU.mult),
                          reads=[Bps[b], Bprm], writes=[dst_bufs[g]])
                evac_rr[0] += 1

    build_T(mem, 2, MG_, memT, [BmemT])
    build_T(x, 16, NG, hT, BhT)

    Sc.barrier()
    AR.release(m_ph0)
    memT2 = memT
    wkv = AR.bf16(16, 1024); Bwkv = [Buf("wkv%d" % q) for q in range(4)]
    kmn = AR.f32(512); Bkmn = Buf("kmn")
    ssk = AR.f32(4); rk_ = AR.f32(4); Bssk = Buf("ssk")
    junk2 = AR.f32(128); Bjunk2 = Buf("junk2")
    wkv_src = x_w_kv.rearrange("(k p) n -> p k n", p=128)
    for q in range(4):
        Sc.dma("pool", wkv[:, q * 4:(q + 1) * 4, :], wkv_src[:, q * 4:(q + 1) * 4, :], writes=[Bwkv[q]])
    for mb in range(2):
        for half in range(2):
            b = nbank()
            mm_group(ps[b][:, :], [(memT2[:, k, mb * 128:(mb + 1) * 128], wkv[:, k, half * 512:(half + 1) * 512], [BmemT, Bwkv[k // 4]]) for k in range(16)], Bps[b])
            if half == 0:
                Sc.op("dve", lambda: V.memset(ssk, 0.0), writes=[Bssk])
                for h in range(4):
                    Sc.op("act", lambda: A_.activation(out=junk2, in_=ps[b][:, h * 128:(h + 1) * 128], func=AF.Square, accum_out=ssk[:, h:h + 1]),
                          reads=[Bps[b]], writes=[Bjunk2, Bssk])
                rsqrt_act(rk_, ssk, 1.0 / 128, EPS, [Bssk], [Bssk])
                for h in range(4):
                    Sc.op("dve", lambda: V.tensor_scalar(out=kmn[:, h * 128:(h + 1) * 128], in0=ps[b][:, h * 128:(h + 1) * 128], scalar1=rk_[:, h:h + 1], scalar2=None, op0=ALU.mult),
                          reads=[Bps[b], Bssk], writes=[Bkmn])
                b2 = nbank()
                for h in range(4):
                    Sc.op("pe", lambda: T_.transpose(out=ps[b2][:, h * 128:(h + 1) * 128], in_=kmn[:, h * 128:(h + 1) * 128], identity=identf),
                          reads=[Bkmn, Bid], writes=[Bps[b2]] if h == 0 else [], inc=(h == 3), setw=[Bps[b2]] if h == 3 else [])
                Sc.op("dve", lambda: V.tensor_scalar(out=kmT[:, :, mb * 128:(mb + 1) * 128], in0=ps[b2][:, :].rearrange("p (h m) -> p h m", m=128),
                                                     scalar1=prm[:, XG2:XG2 + 1], scalar2=None, op0=ALU.mult),
                      reads=[Bps[b2], Bprm], writes=[Bkm])
            else:
                Sc.op("act", lambda: A_.activation(out=vm[:, mb, :], in_=ps[b][:, :], func=AF.Copy), reads=[Bps[b]], writes=[Bvm])
    dump("d_kmT", kmT, [128, 4, 256], BF16, [Bkm]); dump("d_vm", vm, [128, 2, 512], BF16, [Bvm])
    dump("d_hT", hT, [128, 16, S], BF16, BhT)
    Sc.barrier()
    if stop_after == "hT":
        return finish_debug(nc, Sc, locals())

    AR.release(m_ph0)
    NWB = 3
    wt = [AR.bf16(16, 512) for _ in range(NWB)]
    Bwt = [[Buf("wt%d_%d" % (i, q)) for q in range(4)] for i in range(NWB)]
    stf = [AR.f32(S) for _ in range(2)]; Bstf = [Buf("stf%d" % i) for i in range(2)]
    stt = [AR.f32(512) for _ in range(3)]; Bstt = [Buf("stt%d" % i) for i in range(3)]
    Bproj = [Buf("pf%d" % i) for i in range(NPF // 128)]
    Bqkv = Buf("qkv")
    w_src = w_in.rearrange("(k p) n -> p k n", p=128)
    NT = (INW + 511) // 512

    TORDER = [0, 1, 2, 10, 11, 12] + [t for t in range(NT) if t not in (0, 1, 2, 10, 11, 12)]

    def load_w(pos):
        t = TORDER[pos]
        c0 = t * 512
        ncol = min(512, INW - c0)
        for q in range(4):
            Sc.dma("pool", wt[pos % NWB][:, q * 4:(q + 1) * 4, 0:ncol], w_src[:, q * 4:(q + 1) * 4, c0:c0 + ncol], writes=[Bwt[pos % NWB][q]])

    def act_for(feat):
        if (1280 <= feat < 2048) or (4608 <= feat < 5376) or (5888 <= feat < 6400):
            return "silu"
        if feat >= 6400:
            return "sig"
        return "copy"

    stf_rr = [0]; stt_rr = [0]; ev_rr = [0]
    import os as _os2
    TESTCOPY = bool(_os2.environ.get("TESTCOPY"))
    load_w(0); load_w(1)

    def rr(gens, weights=None):
        act_ = [[g, (weights[i] if weights else 1)] for i, g in enumerate(gens)]
        while act_:
            for ent in list(act_):
                for _ in range(ent[1]):
                    try:
                        next(ent[0])
                        yield
                    except StopIteration:
                        act_.remove(ent)
                        break

    def run(gen):
        for _ in gen:
            pass

    def proj_gen(t_lo, t_hi, bk):
      for pos in range(t_lo, t_hi):
          t = TORDER[pos]
          if pos + 2 < NT:
              load_w(pos + 2)
          c0 = t * 512
          ncol = min(512, INW - c0)
          w = wt[pos % NWB]; bw = Bwt[pos % NWB]
          ntok = max(0, min(ncol, NQKV - c0))
          if ntok > 0:
              for tb in range(16):
                  b = nbank(*bk)
                  mm_group(ps[b][:, 0:ntok], [(hT[:, k, tb * 128:(tb + 1) * 128], w[:, k, 0:ntok], [BhT[tb // 4], bw[k // 4]]) for k in range(16)], Bps[b])
                  si = stt_rr[0] % 3; stt_rr[0] += 1
                  if ev_rr[0] % 2 == 0:
                      Sc.op("act", lambda: A_.activation(out=stt[si][:, 0:ntok], in_=ps[b][:, 0:ntok], func=AF.Copy), reads=[Bps[b]], writes=[Bstt[si]])
                  else:
                      Sc.op("dve", lambda: V.tensor_copy(out=stt[si][:, 0:ntok], in_=ps[b][:, 0:ntok]), reads=[Bps[b]], writes=[Bstt[si]])
                  ev_rr[0] += 1
                  Sc.dma("sp", qkv_t[tb * 128:(tb + 1) * 128, c0:c0 + ntok], stt[si][:, 0:ntok], reads=[Bstt[si]])
                  yield
          for sub in range(ntok // 128, ncol // 128):
              feat = c0 + sub * 128
              fi = (feat - NQKV) // 128
              kind = act_for(feat)
              si = stf_rr[0] % 2; stf_rr[0] += 1
              for tc in range(4):
                  b = nbank(*bk)
                  mm_group(ps[b][:, :], [(w[:, k, sub * 128:(sub + 1) * 128], hT[:, k, tc * 512:(tc + 1) * 512], [bw[k // 4], BhT[tc]]) for k in range(16)], Bps[b])
                  dst = stf[si][:, tc * 512:(tc + 1) * 512]
                  if kind == "silu":
                      Sc.op("act", lambda: A_.activation(out=dst, in_=ps[b][:, :], func=AF.Silu), reads=[Bps[b]], writes=[Bstf[si]])
                  elif kind == "sig":
                      gcol = GB + (feat - 6400) // 128
                      Sc.op("act", lambda: A_.activation(out=dst, in_=ps[b][:, :], func=(AF.Tanh if TESTCOPY else AF.Sigmoid), bias=prm[:, gcol:gcol + 1], scale=1.0),
                            reads=[Bps[b], Bprm], writes=[Bstf[si]])
                  else:
                      if ev_rr[0] % 2 == 0:
                          Sc.op("act", lambda: A_.activation(out=dst, in_=ps[b][:, :], func=AF.Copy), reads=[Bps[b]], writes=[Bstf[si]])
                      else:
                          Sc.op("dve", lambda: V.tensor_copy(out=dst, in_=ps[b][:, :]), reads=[Bps[b]], writes=[Bstf[si]])
                      ev_rr[0] += 1
                  yield
              Sc.dma("sp", proj_f[fi * 128:(fi + 1) * 128, :], stf[si], reads=[Bstf[si]], writes=[Bproj[fi]])

    TSPLIT = 6
    run(proj_gen(0, TSPLIT, (0, 8)))
    onesf = AR.f32(128); onesb = AR.bf16(128); Bones = Buf("ones")
    Sc.op("pool", lambda: P_.memset(onesf, 1.0), writes=[Bones])
    Sc.op("pool", lambda: P_.tensor_copy(out=onesb, in_=onesf), reads=[Bones], writes=[Bones])
    qTc = [AR.f32(S) for _ in range(2)]; gtc = [AR.f32(S)] * 2
    BqTc = [Buf(), Buf()]; Bgtc = [Buf()] * 2
    sqc = AR.f32(512); sc2 = AR.f32(512); qn_c = AR.bf16(512); pTc = [AR.bf16(512) for _ in range(2)]; rden = AR.f32(512); yo = AR.f32(512)
    yxs = AR.bf16(S)
    Bsqc, Bsc2, Bqnc, BpTc, Brden, Byo, Byxs = Buf(), Buf(), Buf(), [Buf(), Buf()], Buf(), Buf(), Buf()

    c_rr = [0]

    def cbank():
        c_rr[0] += 1
        return 6 + c_rr[0] % 2

    def xattn_gen():
        for h in range(4):
            Sc.dma("sp", qTc[h % 2], proj_f[XQ0 + h * 128:XQ0 + (h + 1) * 128, :], reads=[Bproj[XQ0 // 128 + h]], writes=[BqTc[h % 2]])
            Sc.dma("sp", gtc[h % 2], proj_f[XG0 + h * 128:XG0 + (h + 1) * 128, :], reads=[Bproj[XG0 // 128 + h]], writes=[Bgtc[h % 2]])
            q_ = qTc[h % 2]; bq_ = BqTc[h % 2]
            for tc in range(4):
                sl = slice(tc * 512, (tc + 1) * 512)
                Sc.op("act", lambda: A_.activation(out=sqc, in_=q_[:, sl], func=AF.Square), reads=[bq_], writes=[Bsqc]); yield
                yield; yield
                b = cbank()
                mm_group(ps[b][:, :], [(onesf, sqc, [Bones, Bsqc])], Bps[b]); yield
                rsqrt_act(sc2, ps[b][:, :], 1.0, 128.0 * EPS, [Bps[b]], [Bsc2]); yield
                Sc.op("dve", lambda: V.tensor_tensor(out=qn_c, in0=q_[:, sl], in1=sc2, op=ALU.mult), reads=[bq_, Bsc2], writes=[Bqnc]); yield
                for mb in range(2):
                    yield; yield
                    b = cbank()
                    mm_group(ps[b][:, :], [(kmT[:, h, mb * 128:(mb + 1) * 128], qn_c, [Bkm, Bqnc])], Bps[b]); yield
                    Sc.op("act", lambda: A_.activation(out=pTc[mb], in_=ps[b][:, :], func=AF.Exp), reads=[Bps[b]], writes=[BpTc[mb]]); yield
                yield; yield
                bo_ = cbank()
                mm_group(ps[bo_][:, :], [(vm[:, mb, h * 128:(h + 1) * 128], pTc[mb], [Bvm, BpTc[mb]]) for mb in range(2)], Bps[bo_]); yield
                bd_ = cbank()
                mm_group(ps[bd_][:, :], [(onesb, pTc[mb], [Bones, BpTc[mb]]) for mb in range(2)], Bps[bd_]); yield
                Sc.op("act", lambda: A_.activation(out=rden, in_=ps[bd_][:, :], func=AF.Ln), reads=[Bps[bd_]], writes=[Brden]); yield
                Sc.op("act", lambda: A_.activation(out=rden, in_=rden, func=AF.Exp, scale=-1.0), reads=[Brden], writes=[Brden]); yield
                Sc.op("dve", lambda: V.tensor_tensor(out=yo, in0=ps[bo_][:, :], in1=rden, op=ALU.mult), reads=[Bps[bo_], Brden], writes=[Byo]); yield
                Sc.op("pool", lambda: P_.tensor_tensor(out=yxs[:, sl], in0=yo, in1=gtc[h % 2][:, sl], op=ALU.mult), reads=[Byo, Bgtc[h % 2]], writes=[Byxs]); yield
            Sc.dma("sp", ybuf[1536 + h * 128:1536 + (h + 1) * 128, :], yxs, reads=[Byxs]); yield


    run(rr([proj_gen(TSPLIT, NT, (0, 6)), xattn_gen()]))
    Sc.barrier()
    if stop_after == "proj":
        return finish_debug(nc, Sc, locals())

    AR.release(m_persist)
    bones = AR.f32(128); bo64 = AR.f32(128); rwm = AR.bf16(2, 3, 128); segm = AR.f32(512)
    w2b = AR.bf16(768); a2b = AR.bf16(768)
    Bc = Buf("rwconst")
    Sc.dma("sp", bones, bones_d[:, :], writes=[Bc])
    tb1 = Buf(); tb2 = Buf(); tb3 = Buf(); tb4 = Buf(); tb5 = Buf()
    Sc.dma("sp", bo64, bo64_d[:, :], writes=[tb1])
    Sc.dma("sp", segm, segm_d[:, :], writes=[tb2])
    Sc.dma("pool", rwm, rwm_d[:, :, :, :], writes=[tb3])
    Sc.dma("pool", w2b, w2cat[:, :], writes=[tb4])
    Sc.dma("pool", a2b, a2cat[:, :], writes=[tb5])
    lw_t = AR.bf16(S); la_s = AR.bf16(S); Blw = Buf("lw"); Bla = Buf("la")
    r_ = AR.bf16(S); k_ = AR.bf16(S); v_ = AR.bf16(S); kk_ = AR.bf16(S); bon = AR.f32(S); ysum = AR.f32(S)
    Br, Bk, Bv, Bkk, Bbon, Bys = Buf("r"), Buf("k"), Buf("v"), Buf("kk"), Buf("bon"), Buf("ysum")
    Vtok = AR.bf16(32, 128); BVtok = [Buf("vtok%d" % q) for q in range(4)]
    vbd = AR.bf16(8, 128); Bvbd = Buf("vbd")
    rkones = AR.f32(128); Brk = Buf("rkones")
    NTMP = 7168
    tmp_raw = AR._raw(NTMP)
    WD = []
    for d in range(2):
        W = {}
        for nm in ("ARt", "BKtok", "Gb", "Gk"):
            W[nm] = [AR.bf16(8, 2, 128) for _ in range(2)]
        W["Tt"] = [AR.bf16(8, 128) for _ in range(2)]
        W["Pc"] = [AR.f32(8) for _ in range(2)]
        W["BKt"] = AR.bf16(8, 2, 128)
        W["Xs"] = AR.bf16(128); W["Ubf"] = AR.bf16(128); W["Ybd"] = AR.f32(8, 128)
        W["St"] = AR.f32(128); W["Stbf"] = AR.bf16(128); W["tmpS"] = AR.f32(128); W["tot"] = AR.f32(8)
        for nm in ("BBKt", "BXs", "BUbf", "BSt", "BStbf", "BtmpS", "Btot", "BYbd"):
            W[nm] = Buf(nm)
        for nm in ("BARt", "BPc"):
            W[nm] = [Buf(nm + "0"), Buf(nm + "1")]
        for nm in ("BBKtok", "BGb", "BGk", "BTt"):
            W[nm] = [[Buf(), Buf()], [Buf(), Buf()]]
        W["BLab"] = [Buf(), Buf()]
        W["BAn"] = [[Buf(), Buf()], [Buf(), Buf()]]; W["BBn"] = [[Buf(), Buf()], [Buf(), Buf()]]
        TA = Arena(tmp_raw[:, d * (NTMP // 2):(d + 1) * (NTMP // 2)], NTMP // 2)
        for nm in ("a", "ld", "cum", "E1", "E2", "E3", "u"):
            W[nm] = TA.f32(512); W["B" + nm] = Buf(nm)
        asb = lambda ap: ap.bitcast(BF16).rearrange("p (a b) -> p a b", b=128)
        W["An"] = [asb(W["E1"]), asb(W["E2"])]; W["Bn"] = [asb(W["E3"]), asb(W["u"])]; W["Lab"] = asb(W["ld"])
        W["alias_An"] = [W["BE1"], W["BE2"]]; W["alias_Bn"] = [W["BE3"], W["Bu"]]
        WD.append(W)
    for d in range(2):
        W = WD[d]
        for jp in range(2):
            Sc.op("pool", lambda: P_.memset(W["ARt"][jp], 0.0), writes=[W["BARt"][jp]])
        Sc.op("pool", lambda: P_.memset(W["BKt"], 0.0), writes=[W["BBKt"]])
    Sc.op("pool", lambda: P_.memset(vbd, 0.0), writes=[Bvbd])

    rawc = [AR.f32(514) for _ in range(2)]; Brawc = [Buf(), Buf()]
    nbc = AR.f32(512); sqc_ = AR.f32(512); Bnbc, Bsqc_ = Buf(), Buf()
    sqk = nbc; rnk = sqc_; Bsqk, Brnk = Bnbc, Bsqc_
    gatec = AR.f32(512); dtmp = AR.f32(512); sq2 = AR.f32(512); rstd = AR.f32(512); yn = AR.f32(512); ystc = [AR.bf16(512) for _ in range(2)]
    Bgatec, Bdt, Bs2, Brs, Byn, Bystc = Buf(), Buf(), Buf(), Buf(), Buf(), [Buf(), Buf()]
    rc_rr = [0]

    def shift_gen(dst, bdst, row0, mi, func=AF.Copy):
        for tc in range(4):
            sl = slice(tc * 512, (tc + 1) * 512)
            lo = max(0, tc * 512 - 1); hi = min(S, tc * 512 + 513)
            off = lo - (tc * 512 - 1)
            ri = rc_rr[0] % 2; rc_rr[0] += 1
            rc = rawc[ri]; brc = Brawc[ri]
            if tc == 0:
                Sc.op("pool", lambda: P_.memset(rc[:, 0:1], 0.0), writes=[brc])
            if tc == 3:
                Sc.op("pool", lambda: P_.memset(rc[:, 513:514], 0.0), writes=[brc])
            Sc.dma("sp", rc[:, off:off + (hi - lo)], proj_f[row0:row0 + 128, lo:hi], writes=[brc]); yield
            Sc.op("pool", lambda: P_.tensor_tensor(out=nbc, in0=rc[:, 0:512], in1=rc[:, 2:514], op=ALU.add), reads=[brc], writes=[Bnbc]); yield
            Sc.op("act", lambda: A_.activation(out=sqc_, in_=rc[:, 1:513], func=AF.Copy, scale=prm[:, OMM + mi:OMM + mi + 1]), reads=[brc, Bprm], writes=[Bsqc_]); yield
            if func == AF.Copy:
                Sc.op("dve", lambda: V.scalar_tensor_tensor(out=dst[:, sl], in0=nbc, scalar=prm[:, HMU + mi:HMU + mi + 1], in1=sqc_, op0=ALU.mult, op1=ALU.add),
                      reads=[Bnbc, Bprm, Bsqc_], writes=[bdst]); yield
            else:
                Sc.op("dve", lambda: V.scalar_tensor_tensor(out=sqc_, in0=nbc, scalar=prm[:, HMU + mi:HMU + mi + 1], in1=sqc_, op0=ALU.mult, op1=ALU.add),
                      reads=[Bnbc, Bprm, Bsqc_], writes=[Bsqc_]); yield
                Sc.op("act", lambda: A_.activation(out=dst[:, sl], in_=sqc_, func=func), reads=[Bsqc_], writes=[bdst]); yield

    def v3(ap, n=64):
        return ap.rearrange("p (c t) -> p c t", t=n)

    def unit_prep(p, d, sc, jp):
        W = WD[d]
        sl = slice(sc * 512, (sc + 1) * 512)
        dh = slice(d * 64, (d + 1) * 64)
        pc = slice(p * 128, (p + 1) * 128)
        a, ld, cum, E1, E2, E3, u = W["a"], W["ld"], W["cum"], W["E1"], W["E2"], W["E3"], W["u"]
        Ba, Bld, Bcum, BE1, BE2, BE3, Bu = W["Ba"], W["Bld"], W["Bcum"], W["BE1"], W["BE2"], W["BE3"], W["Bu"]
        ARt, BKtok, Gb, Gk, Tt, Pc = W["ARt"][jp], W["BKtok"][jp], W["Gb"][jp], W["Gk"][jp], W["Tt"][jp], W["Pc"][jp]
        BARt, BBKtok, BGb, BGk, BTt, BPc = W["BARt"][jp], W["BBKtok"][jp], W["BGb"][jp], W["BGk"][jp], W["BTt"][jp], W["BPc"][jp]
        BKt, Lab = W["BKt"], W["Lab"]
        b = nbank(2, 8)
        mm_group(ps[b][:, :], [(a2b[dh, pc], la_s[dh, sl], [tb5, Bla])], Bps[b]); yield
        hc = HA0 + d * 6 + p
        Sc.op("act", lambda: A_.activation(out=a, in_=ps[b][:, :], func=AF.Tanh, bias=prm[:, hc:hc + 1], scale=0.5), reads=[Bps[b], Bprm], writes=[Ba]); yield
        Sc.op("dve", lambda: V.tensor_scalar(out=a, in0=a, scalar1=0.5, scalar2=0.5, op0=ALU.mult, op1=ALU.add), reads=[Ba], writes=[Ba]); yield
        b = nbank(2, 8)
        mm_group(ps[b][:, :], [(w2b[dh, pc], lw_t[dh, sl], [tb4, Blw])], Bps[b]); yield
        hc2 = HW0 + d * 6 + p
        Sc.op("act", lambda: A_.activation(out=ld, in_=ps[b][:, :], func=AF.Tanh, bias=prm[:, hc2:hc2 + 1], scale=0.5), reads=[Bps[b], Bprm], writes=[Bld] + W["BLab"]); yield
        Sc.op("dve", lambda: V.tensor_scalar(out=ld, in0=ld, scalar1=C1, scalar2=C1, op0=ALU.mult, op1=ALU.add), reads=[Bld], writes=[Bld]); yield
        Sc.op("dve", lambda: V.tensor_tensor_scan(out=cum, data0=segm, data1=ld, initial=0.0, op0=ALU.mult, op1=ALU.add), reads=[tb2, Bld], writes=[Bcum]); yield
        if d == 1:
            Sc.op("dve", lambda: V.tensor_copy(out=W["tot"], in_=v3(cum)[:, :, 63]), reads=[Bcum], writes=[W["Btot"]]); yield
            Sc.op("dve", lambda: V.tensor_tensor(out=cum, in0=ld, in1=cum, op=ALU.subtract), reads=[Bld, Bcum], writes=[Bcum]); yield
            Sc.op("dve", lambda: V.tensor_tensor(out=v3(cum), in0=v3(cum), in1=W["tot"].unsqueeze(2).to_broadcast([128, 8, 64]), op=ALU.add),
                  reads=[Bcum, W["Btot"]], writes=[Bcum]); yield
        Sc.op("dve", lambda: V.tensor_tensor(out=ld, in0=cum, in1=ld, op=ALU.subtract), reads=[Bcum, Bld], writes=[Bld]); yield
        Sc.op("act", lambda: A_.activation(out=E3, in_=ld, func=AF.Exp), reads=[Bld], writes=[BE3] + W["BBn"][0]); yield
        Sc.op("act", lambda: A_.activation(out=E1, in_=cum, func=AF.Exp), reads=[Bcum], writes=[BE1] + W["BAn"][0]); yield
        Sc.op("act", lambda: A_.activation(out=E2, in_=cum, func=AF.Exp, scale=-1.0), reads=[Bcum], writes=[BE2] + W["BAn"][1]); yield
        pcol = 63 if d == 0 else 0
        Sc.op("dve", lambda: V.tensor_copy(out=Pc, in_=v3(E1)[:, :, pcol]), reads=[BE1], writes=[BPc]); yield
        Sc.op("dve", lambda: V.tensor_scalar(out=u, in0=a, scalar1=prm[:, KA + p:KA + p + 1], scalar2=prm[:, OMK + p:OMK + p + 1], op0=ALU.mult, op1=ALU.add),
              reads=[Ba, Bprm], writes=[Bu] + W["BBn"][1]); yield
        Sc.op("dve", lambda: V.tensor_tensor(out=u, in0=k_[:, sl], in1=u, op=ALU.mult), reads=[Bk, Bu], writes=[Bu]); yield
        Sc.op("pool", lambda: P_.tensor_tensor(out=a, in0=kk_[:, sl], in1=a, op=ALU.mult), reads=[Bkk, Ba], writes=[Ba]); yield
        for half in range(2):
            hs = slice(half * 64, (half + 1) * 64)
            bc = slice(half * 64, (half + 1) * 64)
            Sc.op("dve", lambda: V.scalar_tensor_tensor(out=ARt[hs, :, 0, bc], in0=v3(kk_[hs, sl]), scalar=-1.0, in1=v3(E3[hs, :]), op0=ALU.mult, op1=ALU.mult),
                  reads=[Bkk, BE3], writes=[BARt]); yield
            Sc.op("pool", lambda: P_.tensor_tensor(out=ARt[hs, :, 1, bc], in0=v3(r_[hs, sl]), in1=v3(E1[hs, :]), op=ALU.mult), reads=[Br, BE1], writes=[BARt]); yield
            Sc.op("dve", lambda: V.tensor_tensor(out=BKt[hs, :, 1, bc], in0=v3(u[hs, :]), in1=v3(E2[hs, :]), op=ALU.mult), reads=[Bu, BE2], writes=[W["BBKt"]]); yield
            Sc.op("dve", lambda: V.tensor_tensor(out=BKt[hs, :, 0, bc], in0=v3(a[hs, :]), in1=v3(E2[hs, :]), op=ALU.mult), reads=[Ba, BE2], writes=[W["BBKt"]]); yield
        Sc.op("pool", lambda: P_.tensor_tensor(out=cum, in0=r_[:, sl], in1=u, op=ALU.mult), reads=[Br, Bu, Bcum], writes=[Bcum]); yield
        b = nbank(2, 8)
        mm_group(ps[b][:, :], [(rkones, cum, [Brk, Bcum])], Bps[b]); yield
        Sc.op("dve", lambda: V.tensor_tensor(out=cum, in0=ps[b][:, :], in1=v_[:, sl], op=ALU.mult), reads=[Bps[b], Bv, Bcum], writes=[Bcum]); yield
        Sc.op("pool", lambda: P_.tensor_tensor(out=bon[:, sl], in0=bon[:, sl], in1=cum, op=ALU.add), reads=[Bbon, Bcum], writes=[Bbon]); yield
        for hb in range(2):
            b = nbank(2, 8)
            for ci in range(4):
                for s2 in range(2):
                    j = ci * 2 + s2
                    Sc.op("pe", lambda: T_.transpose(out=psb[b][:, j * 128:(j + 1) * 128], in_=BKt[:, hb * 4 + ci, s2, :], identity=identb),
                          reads=[W["BBKt"], Bid], writes=[Bps[b]] if j == 0 else [], inc=(j == 7), setw=[Bps[b]] if j == 7 else [])
            yield
            dstv = BKtok[:, hb * 4:(hb + 1) * 4, :, :].rearrange("p a b c -> p (a b c)")
            Sc.op("act", lambda: A_.activation(out=dstv, in_=psb[b], func=AF.Copy), reads=[Bps[b]], writes=[BBKtok[hb]]); yield
        M2 = rwm[:, d, 0:2, :].rearrange("p a b -> p (a b)")
        ML = rwm[:, d, 2, :]
        for hb in range(2):
            bL = nbank(2, 8)
            mm_multi([(ps[bL][:, ci * 128:(ci + 1) * 128], ARt[:, hb * 4 + ci, 0, :], BKt[:, hb * 4 + ci, 0, :], [W["BBKt"], BARt]) for ci in range(4)], Bps[bL])
            yield
            Sc.op("dve", lambda: V.tensor_tensor(out=Lab[:, hb * 4:(hb + 1) * 4, :], in0=ps[bL][:, :].rearrange("p (a b) -> p a b", b=128),
                                                 in1=ML.unsqueeze(1).to_broadcast([128, 4, 128]), op=ALU.mult), reads=[Bps[bL], tb3], writes=[W["BLab"][hb], Bld]); yield
            bB = [nbank(2, 8), nbank(2, 8)]
            for i in range(2):
                mm_multi([(ps[bB[i]][:, q2 * 256:(q2 + 1) * 256], BKt[:, hb * 4 + i * 2 + q2, 0, :], ARt[:, hb * 4 + i * 2 + q2, :, :], [W["BBKt"], BARt]) for q2 in range(2)], Bps[bB[i]])
            yield
            for i in range(2):
                c0 = hb * 4 + i * 2
                Sc.op("dve", lambda: V.tensor_tensor(out=Gb[:, c0:c0 + 2, :, :].rearrange("p a b c -> p a (b c)"), in0=ps[bB[i]][:, :].rearrange("p (a b) -> p a b", b=256),
                                                     in1=M2.unsqueeze(1).to_broadcast([128, 2, 256]), op=ALU.mult), reads=[Bps[bB[i]], tb3], writes=[BGb[hb]]); yield
        for hb in range(2):
            cs = slice(hb * 4, (hb + 1) * 4)
            Sc.op("dve", lambda: V.tensor_tensor(out=Tt[:, cs, :], in0=Lab[:, cs, :], in1=identb.unsqueeze(1).to_broadcast([128, 4, 128]), op=ALU.add),
                  reads=[W["BLab"][hb], Bid], writes=[BTt[hb]]); yield
        for lvl in range(0, 6):
            for hb in range(2):
                cs = slice(hb * 4, (hb + 1) * 4)
                if lvl == 0:
                    Aget = lambda c: Gb[:, c, 0, :]
                    Bget = lambda c: Lab[:, c, :]
                    BAsrc, BBsrc = BGb[hb], W["BLab"][hb]
                else:
                    Aprev, Bprev = W["An"][lvl % 2], W["Bn"][lvl % 2]
                    Aget = lambda c: Aprev[:, c, :]
                    Bget = lambda c: Bprev[:, c, :]
                    BAsrc, BBsrc = W["BAn"][lvl % 2][hb], W["BBn"][lvl % 2][hb]
                Anew, Bnew = W["An"][(lvl + 1) % 2], W["Bn"][(lvl + 1) % 2]
                BAnew, BBnew = W["BAn"][(lvl + 1) % 2][hb], W["BBn"][(lvl + 1) % 2][hb]
                if lvl >= 1:
                    bT = nbank(2, 8)
                    mm_multi([(ps[bT][:, ci * 128:(ci + 1) * 128], Aget(hb * 4 + ci), Tt[:, hb * 4 + ci, :], [BAsrc, BTt[hb]]) for ci in range(4)], Bps[bT])
                    yield
                if lvl < 5:
                    if lvl < 4:
                        bA = nbank(2, 8)
                        mm_multi([(ps[bA][:, ci * 128:(ci + 1) * 128], Bget(hb * 4 + ci), Aget(hb * 4 + ci), [BAsrc, BBsrc]) for ci in range(4)], Bps[bA])
                        yield
                    bBm = nbank(2, 8)
                    mm_multi([(ps[bBm][:, ci * 128:(ci + 1) * 128], Aget(hb * 4 + ci), Bget(hb * 4 + ci), [BAsrc, BBsrc]) for ci in range(4)], Bps[bBm])
                    yield
                if lvl >= 1:
                    Sc.op("dve", lambda: V.tensor_tensor(out=Tt[:, cs, :], in0=ps[bT][:, :].rearrange("p (a b) -> p a b", b=128), in1=Tt[:, cs, :], op=ALU.add),
                          reads=[Bps[bT], BTt[hb]], writes=[BTt[hb]]); yield
                if lvl < 5:
                    if lvl < 4:
                        Sc.op("act", lambda: A_.activation(out=Anew[:, cs, :], in_=ps[bA][:, :].rearrange("p (a b) -> p a b", b=128), func=AF.Copy),
                              reads=[Bps[bA]], writes=[BAnew, W["alias_An"][(lvl + 1) % 2]]); yield
                    Sc.op("act", lambda: A_.activation(out=Bnew[:, cs, :], in_=ps[bBm][:, :].rearrange("p (a b) -> p a b", b=128), func=AF.Copy),
                          reads=[Bps[bBm]], writes=[BBnew, W["alias_Bn"][(lvl + 1) % 2]]); yield
        for hb in range(2):
            cs = slice(hb * 4, (hb + 1) * 4)
            bX = nbank(2, 8)
            mm_multi([(ps[bX][:, ci * 128:(ci + 1) * 128], Tt[:, hb * 4 + ci, :], BKtok[:, hb * 4 + ci, 0, :], [BTt[hb], BBKtok[hb]]) for ci in range(4)], Bps[bX])
            yield
            bM = nbank(2, 8)
            mm_multi([(ps[bM][:, ci * 128:(ci + 1) * 128], Tt[:, hb * 4 + ci, :], Gb[:, hb * 4 + ci, 1, :], [BTt[hb], BGb[hb]]) for ci in range(4)], Bps[bM])
            yield
            Sc.op("act", lambda: A_.activation(out=BKtok[:, cs, 0, :], in_=ps[bX][:, :].rearrange("p (a b) -> p a b", b=128), func=AF.Copy), reads=[Bps[bX]], writes=[BBKtok[hb]]); yield
            Sc.op("dve", lambda: V.tensor_copy(out=Gb[:, cs, 1, :], in_=ps[bM][:, :].rearrange("p (a b) -> p a b", b=128)), reads=[Bps[bM]], writes=[BGb[hb]]); yield
        for hb in range(2):
            bK = [nbank(2, 8), nbank(2, 8)]
            for i in range(2):
                mm_multi([(ps[bK[i]][:, q2 * 256:(q2 + 1) * 256], BKt[:, hb * 4 + i * 2 + q2, 1, :], ARt[:, hb * 4 + i * 2 + q2, :, :], [W["BBKt"], BARt]) for q2 in range(2)], Bps[bK[i]])
            yield
            for i in range(2):
                c0 = hb * 4 + i * 2
                Sc.op("dve", lambda: V.tensor_tensor(out=Gk[:, c0:c0 + 2, :, :].rearrange("p a b c -> p a (b c)"), in0=ps[bK[i]][:, :].rearrange("p (a b) -> p a b", b=256),
                                                     in1=M2.unsqueeze(1).to_broadcast([128, 2, 256]), op=ALU.mult), reads=[Bps[bK[i]], tb3], writes=[BGk[hb]]); yield

    import os as _os3
    SLK = int(_os3.environ.get('SCAN_SLACK', '0'))

    def scan_step(p, d, sc, ci, jp):
        W = WD[d]
        hb = ci // 4
        cg = sc * 8 + ci
        sb = d
        bkb = Bps[sb]
        ARt, BKtok, Gb, Gk, Tt, Pc = W["ARt"][jp], W["BKtok"][jp], W["Gb"][jp], W["Gk"][jp], W["Tt"][jp], W["Pc"][jp]
        BARt, BBKtok, BGb, BGk, BTt, BPc = W["BARt"][jp], W["BBKtok"][jp], W["BGb"][jp], W["BGk"][jp], W["BTt"][jp], W["BPc"][jp]
        St, Stbf, Xs, Ubf, tmpS = W["St"], W["Stbf"], W["Xs"], W["Ubf"], W["tmpS"]
        vt = Vtok[:, cg, :]
        bvt = BVtok[cg // 8]
        pcc = Pc[:, ci:ci + 1]
        for _s in range(SLK): yield
        mm_group(ps[sb][:, 0:128], [(ARt[:, ci, 0, :], Stbf, [BARt, W["BStbf"]]), (Gk[:, ci, 0, :], vt, [BGk[hb], bvt])], bkb); yield
        Sc.op("act", lambda: A_.activation(out=Xs, in_=ps[sb][:, 0:128], func=AF.Copy), writes=[bkb, W["BXs"]]); yield
        for _s in range(SLK): yield
        mm_group(ps[sb][:, 384:512], [(BKtok[:, ci, 1, :], vt, [BBKtok[hb], bvt]), (identb, Stbf, [Bid, W["BStbf"]]),
                                      (BKtok[:, ci, 0, :], Xs, [BBKtok[hb], W["BXs"]])], bkb); yield
        mm_group(ps[sb][:, 256:384], [(Stbf, ARt[:, ci, 1, :], [BARt, W["BStbf"]]), (Xs, Gb[:, ci, 1, :], [W["BXs"], BGb[hb]]),
                                      (vt, Gk[:, ci, 1, :], [bvt, BGk[hb]])], bkb); yield
        Sc.op("dve", lambda: V.tensor_scalar(out=Stbf, in0=ps[sb][:, 384:512], scalar1=pcc, scalar2=None, op0=ALU.mult),
              reads=[BPc], writes=[bkb, W["BStbf"]]); yield
        Sc.op("act", lambda: A_.activation(out=W["Ybd"][:, ci, :], in_=ps[sb][:, 256:384], func=AF.Copy), writes=[bkb, W["BYbd"]]); yield

    def chain(p, d, sc, jp):
        order = range(8) if d == 0 else range(7, -1, -1)
        for ci in order:
            yield from scan_step(p, d, sc, ci, jp)
        W = WD[d]
        sl = slice(sc * 512, (sc + 1) * 512)
        for half in range(2):
            hs = slice(half * 64, (half + 1) * 64)
            Sc.op("pool", lambda: P_.tensor_tensor(out=v3(ysum[hs, sl]), in0=v3(ysum[hs, sl]), in1=W["Ybd"][hs, :, half * 64:(half + 1) * 64], op=ALU.add),
                  reads=[Bys, W["BYbd"]], writes=[Bys]); yield

    run(shift_gen(lw_t, Blw, LW0, 18, func=AF.Tanh))
    run(shift_gen(la_s, Bla, LA0, 19))

    PAIRS = list(range(6))
    if stop_after.startswith("rwkvp"):
        PAIRS = [int(ch) for ch in stop_after[5:]]

    def prologue(p):
        yield from shift_gen(r_, Br, R0 + p * 128, p)
        yield from shift_gen(k_, Bk, K0 + p * 128, 6 + p)
        yield from shift_gen(v_, Bv, V0 + p * 128, 12 + p)
        kcol = prm[:, KK + p:KK + p + 1]
        for tc in range(4):
            sl = slice(tc * 512, (tc + 1) * 512)
            Sc.op("act", lambda: A_.activation(out=sqk, in_=k_[:, sl], func=AF.Square, scale=kcol), reads=[Bk, Bprm], writes=[Bsqk]); yield
            b = nbank(2, 8)
            mm_group(ps[b][:, :], [(bones, sqk, [Bc, Bsqk])], Bps[b]); yield
            rsqrt_act(rnk, ps[b][:, :], 1.0, 1e-24, [Bps[b]], [Brnk]); yield
            Sc.op("dve", lambda: V.scalar_tensor_tensor(out=kk_[:, sl], in0=k_[:, sl], scalar=kcol, in1=rnk, op0=ALU.mult, op1=ALU.mult),
                  reads=[Bk, Bprm, Brnk], writes=[Bkk]); yield
        Sc.op("dve", lambda: V.tensor_scalar(out=rkones, in0=bones, scalar1=prm[:, RK + p:RK + p + 1], scalar2=None, op0=ALU.mult), reads=[Bc, Bprm], writes=[Brk]); yield

    def vtok_build(p):
        for q in range(4):
            for half in range(2):
                hs = slice(half * 64, (half + 1) * 64)
                Sc.op("pool", lambda: P_.tensor_copy(out=vbd[hs, :, half * 64:(half + 1) * 64], in_=v3(v_[hs, q * 512:(q + 1) * 512])), reads=[Bv], writes=[Bvbd]); yield
            b = nbank(2, 8)
            for j in range(8):
                Sc.op("pe", lambda: T_.transpose(out=psb[b][:, j * 128:(j + 1) * 128], in_=vbd[:, j, :], identity=identb),
                      reads=[Bvbd, Bid], writes=[Bps[b]] if j == 0 else [], inc=(j == 7), setw=[Bps[b]] if j == 7 else [])
            yield
            Sc.op("dve", lambda: V.tensor_copy(out=Vtok[:, q * 8:(q + 1) * 8, :].rearrange("p a b -> p (a b)"), in_=psb[b]), reads=[Bps[b]], writes=[BVtok[q]]); yield

    def resets(p):
        Sc.op("pool", lambda: P_.memset(bon, 0.0), writes=[Bbon])
        Sc.op("pool", lambda: P_.memset(ysum, 0.0), writes=[Bys])
        for d in range(2):
            W = WD[d]
            Sc.op("pool", lambda: P_.memset(W["St"], 0.0), writes=[W["BSt"]])
            Sc.op("pool", lambda: P_.memset(W["Stbf"], 0.0), writes=[W["BStbf"]])

    def epilogue(p):
        dump("d_ysum%d" % p, ysum, [128, S], F32, [Bys])
        for tc in range(4):
            sl = slice(tc * 512, (tc + 1) * 512)
            yc = ystc[tc % 2]; byc = Bystc[tc % 2]
            Sc.dma("sp", gatec, proj_f[RG0 + p * 128:RG0 + (p + 1) * 128, sl], writes=[Bgatec])
            b = nbank(2, 8)
            mm_group(ps[b][:, :], [(bo64, ysum[:, sl], [tb1, Bys])], Bps[b]); yield
            Sc.op("dve", lambda: V.tensor_tensor(out=dtmp, in0=ysum[:, sl], in1=ps[b][:, :], op=ALU.subtract), reads=[Bys, Bps[b]], writes=[Bdt]); yield
            Sc.op("act", lambda: A_.activation(out=sq2, in_=dtmp, func=AF.Square), reads=[Bdt], writes=[Bs2]); yield
            b2 = nbank(2, 8)
            mm_group(ps[b2][:, :], [(bo64, sq2, [tb1, Bs2])], Bps[b2]); yield
            rsqrt_act(rstd, ps[b2][:, :], 1.0, GN_EPS, [Bps[b2]], [Brs]); yield
            Sc.op("dve", lambda: V.tensor_tensor(out=yn, in0=dtmp, in1=rstd, op=ALU.mult), reads=[Bdt, Brs], writes=[Byn]); yield
            Sc.op("dve", lambda: V.tensor_scalar(out=yn, in0=yn, scalar1=prm[:, LW + p:LW + p + 1], scalar2=prm[:, LB + p:LB + p + 1], op0=ALU.mult, op1=ALU.add),
                  reads=[Byn, Bprm], writes=[Byn]); yield
            Sc.op("pool", lambda: P_.tensor_tensor(out=yn, in0=yn, in1=bon[:, sl], op=ALU.add), reads=[Byn, Bbon], writes=[Byn]); yield
            Sc.op("dve", lambda: V.tensor_tensor(out=yc, in0=yn, in1=gatec, op=ALU.mult), reads=[Byn, Bgatec], writes=[byc]); yield
            Sc.dma("sp", ybuf[768 + p * 128:768 + (p + 1) * 128, sl], yc, reads=[byc]); yield

    def preps(p, j):
        return [unit_prep(p, 0, j, j % 2), unit_prep(p, 1, 3 - j, j % 2)]

    Sc.barrier()
    run(prologue(PAIRS[0]))
    run(vtok_build(PAIRS[0]))
    resets(PAIRS[0])
    run(rr(preps(PAIRS[0], 0)))
    for i, p in enumerate(PAIRS):
        nxt = PAIRS[i + 1] if i + 1 < len(PAIRS) else None
        for j in range(4):
            gens = [chain(p, 0, j, j % 2), chain(p, 1, 3 - j, j % 2)]
            if j < 3:
                gens += preps(p, j + 1)
            elif nxt is not None:
                gens.append(prologue(nxt))
            run(rr(gens))
        gens = [epilogue(p)]
        if nxt is not None:
            gens.append(vtok_build(nxt))
        run(rr(gens))
        if nxt is not None:
            resets(nxt)
            run(rr(preps(nxt, 0)))
    Sc.barrier()
    if stop_after.startswith("rwkv"):
        return finish_debug(nc, Sc, locals())

    AR.release(m_persist)
    cosr = AR.f32(16, 32); sinr = AR.f32(16, 32); gqk = AR.f32(16, 64); esink = AR.f32(12); mLR = AR.bf16(2, 128)
    Bcs, Bsn, Bgq, Bes, Bml = Buf(), Buf(), Buf(), Buf(), Buf()
    Sc.dma("sp", cosr, cos_d[:, :, :], writes=[Bcs]); Sc.dma("sp", sinr, sin_d[:, :, :], writes=[Bsn])
    Sc.dma("sp", gqk, gqk_d.rearrange("p (a b) -> p a b", b=64), writes=[Bgq]); Sc.dma("sp", esink, sink_d[:, :], writes=[Bes])
    Sc.dma("pool", mLR, maskLR_d[:, :, :], writes=[Bml])
    Sc.op("act", lambda: A_.activation(out=esink, in_=esink, func=AF.Exp), reads=[Bes], writes=[Bes])
    qTa = AR.bf16(12, S); kTa = AR.bf16(4, S); vext = AR.bf16(16, 4, 128); ya = AR.bf16(6, S)
    BqTa = [Buf() for _ in range(16)]; BkTa = [Buf() for _ in range(16)]; Bvx = [Buf() for _ in range(16)]; Bya = [Buf() for _ in range(16)]
    Sc.op("pool", lambda: P_.memset(vext, 1.0), writes=Bvx)
    qkv = [AR.f32(NQKV) for _ in range(2)]; Bqkv2 = [Buf(), Buf()]
    gta = [AR.f32(6, 128) for _ in range(6)]; Bgta = [Buf() for _ in range(6)]
    sqa2 = [AR.f32(1024) for _ in range(2)]; ssa2 = [AR.f32(16) for _ in range(2)]; rsa2 = [AR.f32(16) for _ in range(2)]
    qna2 = [AR.f32(16, 64) for _ in range(2)]; qra2 = [AR.bf16(16, 64) for _ in range(2)]
    rt2 = [[AR.f32(16, 32) for _ in range(4)] for _ in range(2)]
    Bsqa2, Bssa2, Bqna2, Bqra2 = [Buf(), Buf()], [Buf(), Buf()], [Buf(), Buf()], [Buf(), Buf()]
    Brt2 = [[Buf() for _ in range(4)] for _ in range(2)]
    pTa = [AR.bf16(384) for _ in range(6)]; BpTa = [Buf() for _ in range(6)]
    rda = AR.f32(4, 128); Brda = Buf(); yodd = AR.f32(2, 128); Byodd = Buf(); yev = AR.f32(2, 128); Byev = Buf()
    Bo = [Buf("o5"), Buf("o6"), Buf("o7")]
    pta_rr = [0]
    ag_src = proj_f[AG0:AG0 + 768, :].rearrange("(c p) t -> p c t", p=128)

    def prep_blk(tb):
        qk_ = qkv[tb % 2]; bqk = Bqkv2[tb % 2]
        sqa, ssa, rsa, qna, qra, rt_ = sqa2[tb % 2], ssa2[tb % 2], rsa2[tb % 2], qna2[tb % 2], qra2[tb % 2], rt2[tb % 2]
        Bsqa, Bssa, Bqna, Bqra, Brt = Bsqa2[tb % 2], Bssa2[tb % 2], Bqna2[tb % 2], Bqra2[tb % 2], Brt2[tb % 2]
        Sc.dma("sp", qk_, qkv_t[tb * 128:(tb + 1) * 128, :], writes=[bqk])
        Sc.dma("sp", gta[tb % 6], ag_src[:, :, tb * 128:(tb + 1) * 128], writes=[Bgta[tb % 6]])
        Sc.op("act", lambda: A_.activation(out=sqa, in_=qk_[:, 0:1024], func=AF.Square), reads=[bqk], writes=[Bsqa]); yield
        Sc.op("dve", lambda: V.tensor_reduce(out=ssa, in_=sqa.rearrange("p (a b) -> p a b", b=64), axis=AX.X, op=ALU.add), reads=[Bsqa], writes=[Bssa]); yield
        rsqrt_act(rsa, ssa, 1.0 / 64, EPS, [Bssa], [Bssa]); yield
        Sc.op("dve", lambda: V.tensor_tensor(out=qna, in0=qk_[:, 0:1024].rearrange("p (a b) -> p a b", b=64), in1=rsa.unsqueeze(2).to_broadcast([128, 16, 64]), op=ALU.mult),
              reads=[bqk, Bssa], writes=[Bqna]); yield
        Sc.op("pool", lambda: P_.tensor_tensor(out=qna, in0=qna, in1=gqk, op=ALU.mult), reads=[Bqna, Bgq], writes=[Bqna]); yield
        t1 = qna[:, :, 0:32]; t2 = qna[:, :, 32:64]
        cb = cosr[:, tb, :].unsqueeze(1).to_broadcast([128, 16, 32]); sb_ = sinr[:, tb, :].unsqueeze(1).to_broadcast([128, 16, 32])
        Sc.op("dve", lambda: V.tensor_tensor(out=rt_[0], in0=t1, in1=cb, op=ALU.mult), reads=[Bqna, Bcs], writes=[Brt[0]]); yield
        Sc.op("pool", lambda: P_.tensor_tensor(out=rt_[1], in0=t2, in1=sb_, op=ALU.mult), reads=[Bqna, Bsn], writes=[Brt[1]]); yield
        Sc.op("dve", lambda: V.tensor_tensor(out=qra[:, :, 0:32], in0=rt_[0], in1=rt_[1], op=ALU.subtract), reads=[Brt[0], Brt[1]], writes=[Bqra]); yield
        Sc.op("pool", lambda: P_.tensor_tensor(out=rt_[2], in0=t2, in1=cb, op=ALU.mult), reads=[Bqna, Bcs], writes=[Brt[2]]); yield
        Sc.op("dve", lambda: V.tensor_tensor(out=rt_[3], in0=t1, in1=sb_, op=ALU.mult), reads=[Bqna, Bsn], writes=[Brt[3]]); yield
        Sc.op("dve", lambda: V.tensor_tensor(out=qra[:, :, 32:64], in0=rt_[2], in1=rt_[3], op=ALU.add), reads=[Brt[2], Brt[3]], writes=[Bqra]); yield
        Sc.op("act", lambda: A_.activation(out=vext[:, tb, :, 0:64], in_=qk_[:, 1024:1280].rearrange("p (a b) -> p a b", b=64), func=AF.Copy), reads=[bqk], writes=[Bvx[tb]]); yield
        b = [0, 6][tb % 2]
        qflat = qra.rearrange("p a b -> p (a b)")
        for j in range(8):
            Sc.op("pe", lambda: T_.transpose(out=psb[b][:, j * 128:(j + 1) * 128], in_=qflat[:, j * 128:(j + 1) * 128], identity=identb),
                  reads=[Bqra, Bid], writes=[Bps[b]] if j == 0 else [], inc=(j == 7), setw=[Bps[b]] if j == 7 else [])
        yield
        psv = psb[b].rearrange("p (a b) -> p a b", b=128)
        tsl = slice(tb * 128, (tb + 1) * 128)
        qv = qTa.rearrange("p (h two) t -> p h two t", two=2)
        kv = kTa.rearrange("p (h two) t -> p h two t", two=2)
        Sc.op("dve", lambda: V.tensor_copy(out=qv[0:64, :, 0, tsl], in_=psv[0:64, 0:6, :]), reads=[Bps[b]], writes=[BqTa[tb]]); yield
        Sc.op("act", lambda: A_.activation(out=qv[0:64, :, 1, tsl], in_=psv[64:128, 0:6, :], func=AF.Copy), reads=[Bps[b]], writes=[BqTa[tb]]); yield
        Sc.op("dve", lambda: V.tensor_copy(out=kv[0:64, :, 0, tsl], in_=psv[0:64, 6:8, :]), reads=[Bps[b]], writes=[BkTa[tb]]); yield
        Sc.op("act", lambda: A_.activation(out=kv[0:64, :, 1, tsl], in_=psv[64:128, 6:8, :], func=AF.Copy), reads=[Bps[b]], writes=[BkTa[tb]]); yield

    def attend(n):
        qsl = slice(n * 128, (n + 1) * 128)
        gt_ = gta[n % 6]; bgt = Bgta[n % 6]
        seq = []
        for g in range(4):
            kbs = [kb for kb in (n - 1, n, n + 1) if 0 <= kb < 16]
            for kb in kbs:
                seq.append((g, kb, kb == kbs[0], kb == kbs[-1]))
        order = []
        for (g, kb, fst, lst) in seq:
            for hh in range(3):
                order.append((g, kb, hh, fst, lst))
        firsts = {}; lasts = {}
        for idx, (g, kb, hh, fst, lst) in enumerate(order):
            ob = (3 * g + hh) // 4
            firsts.setdefault(ob, idx); lasts[ob] = idx
        idx = 0
        LOOK = 2
        issued = {}

        def score(i):
            g, kb, fst, lst = seq[i]
            b = sbanks[pta_rr[0] % len(sbanks)]
            pi = pta_rr[0] % 6; pta_rr[0] += 1
            mm_group(ps[b][:, 0:384], [(kTa[0:64, g, kb * 128:(kb + 1) * 128], qTa[0:64, 3 * g:3 * g + 3, qsl], [BkTa[kb], BqTa[n]])], Bps[b])
            issued[i] = (b, pi)

        for i in range(min(LOOK, len(seq))):
            score(i)
        yield
        for i, (g, kb, fst, lst) in enumerate(seq):
            if i + LOOK < len(seq):
                score(i + LOOK)
                yield
            b, pi = issued.pop(i)
            Sc.op("act", lambda: A_.activation(out=pTa[pi], in_=ps[b][:, 0:384], func=AF.Exp, scale=0.125), reads=[Bps[b]], writes=[BpTa[pi]]); yield
            if kb != n:
                mk = mLR[:, 0 if kb < n else 1, :].unsqueeze(1).to_broadcast([128, 3, 128])
                Sc.op("dve", lambda: V.tensor_tensor(out=pTa[pi].rearrange("p (a b) -> p a b", b=128), in0=pTa[pi].rearrange("p (a b) -> p a b", b=128), in1=mk, op=ALU.mult),
                      reads=[BpTa[pi], Bml], writes=[BpTa[pi]]); yield
            for hh in range(3):
                head = 3 * g + hh
                ob = head // 4
                col = (head % 4) * 128
                isf = firsts[ob] == idx; isl = lasts[ob] == idx
                st_flag = fst and (hh == 0 or head % 4 == 0)
                Sc.op("pe", lambda: T_.matmul(ps[obanks[ob]][:, col:col + 128], lhsT=vext[:, kb, g, :], rhs=pTa[pi][:, hh * 128:(hh + 1) * 128], start=st_flag, stop=lst),
                      reads=[Bvx[kb], BpTa[pi]], writes=[Bo[ob]] if isf else [], inc=(hh == 2), setw=[Bo[ob]] if isl else [])
                idx += 1
            yield
        for ob in range(3):
            pv = ps[obanks[ob]][:, :].rearrange("p (a b) -> p a b", b=128)
            Sc.op("dve", lambda: V.tensor_tensor(out=rda[0:64, :, :], in0=pv[64:128, :, :], in1=esink[64:128, ob * 4:(ob + 1) * 4].unsqueeze(2).to_broadcast([64, 4, 128]), op=ALU.add),
                  reads=[Bo[ob], Bes], writes=[Brda]); yield
            Sc.op("act", lambda: A_.activation(out=rda[0:64, :, :], in_=rda[0:64, :, :], func=AF.Ln), reads=[Brda], writes=[Brda]); yield
            Sc.op("act", lambda: A_.activation(out=rda[0:64, :, :], in_=rda[0:64, :, :], func=AF.Exp, scale=-1.0), reads=[Brda], writes=[Brda]); yield
            pv2 = pv.rearrange("p (h two) t -> p h two t", two=2)
            rd2 = rda.rearrange("p (h two) t -> p h two t", two=2)
            c0 = ob * 2
            Sc.op("dve", lambda: V.tensor_tensor(out=yev[0:64, :, :], in0=pv2[0:64, :, 0, :], in1=rd2[0:64, :, 0, :], op=ALU.mult), reads=[Bo[ob], Brda], writes=[Byev]); yield
            Sc.op("dve", lambda: V.tensor_tensor(out=yodd[64:128, :, :], in0=pv2[0:64, :, 1, :], in1=rd2[0:64, :, 1, :], op=ALU.mult), reads=[Bo[ob], Brda], writes=[Byodd]); yield
            Sc.op("pool", lambda: P_.tensor_tensor(out=ya[0:64, c0:c0 + 2, qsl], in0=yev[0:64, :, :], in1=gt_[0:64, c0:c0 + 2, :], op=ALU.mult), reads=[Byev, bgt], writes=[Bya[n]]); yield
            Sc.op("pool", lambda: P_.tensor_tensor(out=ya[64:128, c0:c0 + 2, qsl], in0=yodd[64:128, :, :], in1=gt_[64:128, c0:c0 + 2, :], op=ALU.mult), reads=[Byodd, bgt], writes=[Bya[n]]); yield

    sbanks = [1, 2, 7]
    obanks = [3, 4, 5]

    def seq_(gs):
        for g in gs:
            yield from g

    def attn_driver():
        for w in range(8):
            gens = [prep_blk(2 * w), prep_blk(2 * w + 1)]
            att = [attend(n) for n in (2 * w - 3, 2 * w - 2) if n >= 0]
            if att:
                gens.append(seq_(att))
            yield from rr(gens, [1, 1, 2])
        yield from attend(13)
        yield from attend(14)
        yield from attend(15)
        for c in range(6):
            Sc.dma("sp", ybuf[c * 128:(c + 1) * 128, :], ya[:, c, :], reads=Bya); yield

    run(attn_driver())
    Sc.barrier()
    if stop_after in ("attn", "xattn"):
        return finish_debug(nc, Sc, locals())

    AR.release(m_persist)
    mT = AR.bf16(16, S); BmT = [Buf() for _ in range(16)]
    m_p3 = AR.mark()
    yall = AR.bf16(16, S); Byall = [Buf() for _ in range(16)]
    wo = [AR.bf16(16, 256) for _ in range(2)]; Bwo = [[Buf(), Buf(), Buf()] for _ in range(2)]
    gt3 = [AR.f32(S) for _ in range(2)]; Bgt3 = [Buf(), Buf()]
    macc = AR.f32(S); Bmacc = [Buf() for _ in range(4)]
    ptmp = [AR.f32(512) for _ in range(2)]; Bptmp = [Buf(), Buf()]
    for k in range(16):
        Sc.dma("sp", yall[:, k, :], ybuf[k * 128:(k + 1) * 128, :], writes=[Byall[k]])
    wsrcs = [(attn_w_o.rearrange("(k p) n -> p k n", p=128), 0, 6), (rwkv_w_o.rearrange("(k p) n -> p k n", p=128), 6, 6), (x_w_o.rearrange("(k p) n -> p k n", p=128), 12, 4)]
    kranges = [range(0, 6), range(6, 12), range(12, 16)]

    def load_wo(fg):
        for bi, (src, k0, nk) in enumerate(wsrcs):
            Sc.dma("pool", wo[fg % 2][:, k0:k0 + nk, :], src[:, :, fg * 256:(fg + 1) * 256], writes=[Bwo[fg % 2][bi]])

    g3_rr = [0]; pt_rr = [0]
    load_wo(0)
    for fg in range(8):
        if fg + 1 < 8:
            load_wo(fg + 1)
        for fi in range(2):
            f = fg * 2 + fi
            for bi in range(3):
                gi = g3_rr[0] % 2; g3_rr[0] += 1
                r0 = MG0 + bi * 2048 + f * 128
                Sc.dma("sp", gt3[gi], proj_f[r0:r0 + 128, :], writes=[Bgt3[gi]])
                for tc in range(4):
                    sl = slice(tc * 512, (tc + 1) * 512)
                    bk = nbank()
                    mm_group(ps[bk][:, :], [(wo[fg % 2][:, kc, fi * 128:(fi + 1) * 128], yall[:, kc, sl], [Bwo[fg % 2][bi], Byall[kc]]) for kc in kranges[bi]], Bps[bk])
                    if bi == 0:
                        Sc.op("dve", lambda: V.tensor_tensor(out=macc[:, sl], in0=ps[bk][:, :], in1=gt3[gi][:, sl], op=ALU.mult), reads=[Bps[bk], Bgt3[gi]], writes=[Bmacc[tc]])
                    else:
                        pi = pt_rr[0] % 2; pt_rr[0] += 1
                        Sc.op("dve", lambda: V.tensor_tensor(out=ptmp[pi], in0=ps[bk][:, :], in1=gt3[gi][:, sl], op=ALU.mult), reads=[Bps[bk], Bgt3[gi]], writes=[Bptmp[pi]])
                        if bi == 1:
                            Sc.op("pool", lambda: P_.tensor_tensor(out=macc[:, sl], in0=macc[:, sl], in1=ptmp[pi], op=ALU.add), reads=[Bmacc[tc], Bptmp[pi]], writes=[Bmacc[tc]])
                        else:
                            Sc.op("pool", lambda: P_.tensor_tensor(out=mT[:, f, sl], in0=macc[:, sl], in1=ptmp[pi], op=ALU.add), reads=[Bmacc[tc], Bptmp[pi]], writes=[BmT[f]])
    dump("d_mT", mT, [128, 16, S], BF16, BmT)
    Sc.barrier()
    if stop_after == "merge":
        return finish_debug(nc, Sc, locals())
    AR.release(m_p3)
    wout = [AR.bf16(16, 512) for _ in range(2)]; Bwout = [[Buf() for _ in range(4)] for _ in range(2)]
    NXB = 6
    xres = [AR.f32(512) for _ in range(NXB)]; Bxres = [Buf() for _ in range(NXB)]
    ost = [AR.f32(512) for _ in range(NXB)]; Bost = [Buf() for _ in range(NXB)]
    wo_src = w_out.rearrange("(k p) n -> p k n", p=128)

    def load_wout(ng):
        for q in range(4):
            Sc.dma("pool", wout[ng % 2][:, q * 4:(q + 1) * 4, :], wo_src[:, q * 4:(q + 1) * 4, ng * 512:(ng + 1) * 512], writes=[Bwout[ng % 2][q]])

    load_wout(0)
    xr_rr = [0]
    final_toks = []
    PF = 3
    its = [(ng, tb) for ng in range(4) for tb in range(16)]

    def load_x(i):
        ng_, tb_ = its[i]
        Sc.dma("sp", xres[i % NXB], x[tb_ * 128:(tb_ + 1) * 128, ng_ * 512:(ng_ + 1) * 512], writes=[Bxres[i % NXB]])

    for i in range(PF):
        load_x(i)
    for ng in range(4):
        if ng + 1 < 4:
            load_wout(ng + 1)
        for tb in range(16):
            it = ng * 16 + tb
            if it + PF < len(its):
                load_x(it + PF)
            xi = it % NXB
            bk = nbank()
            mm_group(ps[bk][:, :], [(mT[:, f, tb * 128:(tb + 1) * 128], wout[ng % 2][:, f, :], [BmT[f], Bwout[ng % 2][f // 4]]) for f in range(16)], Bps[bk])
            Sc.op("dve", lambda: V.tensor_tensor(out=ost[xi], in0=ps[bk][:, :], in1=xres[xi], op=ALU.add), reads=[Bps[bk], Bxres[xi]], writes=[Bost[xi]])
            final_toks.append(Sc.dma("sp", out[tb * 128:(tb + 1) * 128, ng * 512:(ng + 1) * 512], ost[xi], reads=[Bost[xi]]))
    return finish_debug(nc, Sc, locals())


def finish_debug(nc, Sc, env):
    Sc.barrier()
    ok, stuck, _ = Sc.check_deadlock()
    if not ok:
        raise RuntimeError("logical deadlock in emitted program: %r" % (stuck,))
    Sc.close()
    for cm in reversed(env["ps_cms"]):
        cm.__exit__(None, None, None)
    env["big_cm"].__exit__(None, None, None)
    return nc


def make_in_maps(inputs):
    c = host_consts()
    f = lambda k: np.asarray(inputs[k], dtype=np.float32)
    sq = lambda k: f(k)[0]
    prm = np.zeros((128, NPRM), np.float32)

    def put(col, vec):
        m = vec.size // 128
        prm[:, col:col + m] = vec.reshape(m, 128).T

    put(NG, sq("norm_g")); put(MG_, sq("mem_norm_g")); put(GB, sq("gate_b")); put(MU, sq("rwkv_mu"))
    put(KK, sq("rwkv_k_k")); put(KA, sq("rwkv_k_a")); put(RK, sq("rwkv_r_k").reshape(-1)); put(LW, sq("rwkv_ln_w"))
    put(LB, sq("rwkv_ln_b")); put(W0, sq("rwkv_w0").reshape(-1)); put(A0, sq("rwkv_a0").reshape(-1))
    put(XQG, sq("x_q_norm_g")); put(XKG, sq("x_k_norm_g"))
    gqk = np.concatenate([np.tile(sq("attn_q_norm_g"), 12), np.tile(sq("attn_k_norm_g"), 4)])
    shared = {
        "w_in": sq("w_in"), "attn_w_o": sq("attn_w_o"), "rwkv_w_o": sq("rwkv_w_o"), "x_w_o": sq("x_w_o"),
        "x_w_kv": sq("x_w_kv"), "w_out": sq("w_out"),
        "w2cat": np.ascontiguousarray(sq("rwkv_w2").reshape(128, 768)), "a2cat": np.ascontiguousarray(sq("rwkv_a2").reshape(128, 768)),
        "prm": prm, "gqk": np.ascontiguousarray(np.broadcast_to(gqk[None, :], (128, 1024))),
        "sinkb": np.ascontiguousarray(np.broadcast_to(sq("attn_sink")[None, :], (128, 12))),
    }
    shared.update(c)
    xs = f("x"); ms = f("mem")
    return [dict(shared, x=np.ascontiguousarray(xs[b]), mem=np.ascontiguousarray(ms[b])) for b in range(xs.shape[0])]


_NC_CACHE = {}


def kernel(**inputs):
    in_maps = make_in_maps(inputs)
    if "nc" not in _NC_CACHE:
        _NC_CACHE["nc"] = build_nc()
    nc = _NC_CACHE["nc"]
    res = run_bass_kernel_spmd(nc, in_maps, core_ids=list(range(len(in_maps))))
    return np.stack([np.asarray(r["out"], dtype=np.float32) for r in res.results], axis=0)
```

```python
import math
import numpy as np
import ml_dtypes
import concourse.bass as bass
import concourse.mybir as mybir
from concourse.bass_utils import run_bass_kernel_spmd

F32 = mybir.dt.float32
BF16 = mybir.dt.bfloat16
AF = mybir.ActivationFunctionType
ALU = mybir.AluOpType
AX = mybir.AxisListType

S = 2048
D = 2048
NMEM = 256
INW = 12544
NQKV = 1280
NPF = INW - NQKV
EPS = 1e-6
GN_EPS = 64e-5
C1 = -0.5 * math.exp(-0.5)

AG0 = 0
R0 = 2048 - NQKV
K0 = R0 + 768
V0 = K0 + 768
LW0 = V0 + 768
LA0 = LW0 + 128
RG0 = 4608 - NQKV
XQ0 = 5376 - NQKV
XG0 = 5888 - NQKV
MG0 = 6400 - NQKV

NG, MG_, GB, MU, KK, KA, RK, LW, LB, W0, A0, XQG, XKG = 0, 16, 32, 80, 100, 106, 112, 118, 124, 130, 142, 154, 155
OMM, HMU, OMK, HW0, HA0, XG2 = 156, 176, 196, 202, 214, 226
NPRM = 228


class Buf:
    __slots__ = ("name", "w", "r", "pending")

    def __init__(self, name=""):
        self.name = name
        self.w = None
        self.r = {}
        self.pending = False


class Sched:
    ENG = ("pe", "act", "dve", "pool", "sp")

    def __init__(self, nc, n_dma_sems=40):
        self.nc = nc
        self.eng = {"pe": nc.tensor, "act": nc.scalar, "dve": nc.vector, "pool": nc.gpsimd, "sp": nc.sync}
        self.sem = {}
        self.cnt = {e: 0 for e in self.ENG}
        self.known = {e: {} for e in self.ENG}
        self._cms = []
        for e in self.ENG:
            cm = nc.semaphore("s_" + e)
            self.sem[e] = cm.__enter__()
            self._cms.append(cm)
        self.dsem = []
        for i in range(n_dma_sems):
            cm = nc.semaphore("d%d" % i)
            self.dsem.append([cm.__enter__(), 0])
            self._cms.append(cm)
        self.dnext = 0
        self.nwait = 0
        self.log = {e: [] for e in self.ENG}

    def close(self):
        for cm in reversed(self._cms):
            cm.__exit__(None, None, None)

    def _wait(self, e, key, semh, val):
        k = self.known[e]
        if k.get(key, 0) >= val:
            return
        self.eng[e].wait_ge(semh, val)
        self.nwait += 1
        self.log[e].append(("w", key, val))
        k[key] = val

    def wait_tok(self, e, tok):
        if tok is None:
            return
        kind, a, v = tok
        if kind == "eng":
            self._wait(e, a, self.sem[a], v)
        else:
            self._wait(e, "d%d" % a, self.dsem[a][0], v)

    def deps(self, e, reads, writes):
        for b in reads:
            self.wait_tok(e, b.w)
        for b in writes:
            self.wait_tok(e, b.w)
            for tok in b.r.values():
                self.wait_tok(e, tok)

    def op(self, e, fn, reads=(), writes=(), inc=True, setw=None):
        self.deps(e, reads, writes)
        ins = fn()
        if inc:
            self.cnt[e] += 1
            ins.then_inc(self.sem[e], 1)
            self.log[e].append(("i", e, 1))
            tok = ("eng", e, self.cnt[e])
        else:
            tok = ("eng", e, self.cnt[e] + 1)
        for b in reads:
            b.r[("eng", e)] = tok
            b.pending = False
        for b in (writes if setw is None else setw):
            b.w = tok
            b.r = {}
            b.pending = True
        for b in writes:
            b.pending = True
        return ins

    def dma(self, e, out, in_, reads=(), writes=()):
        idx = self.dnext
        self.dnext = (self.dnext + 1) % len(self.dsem)
        semh, val = self.dsem[idx]
        if val > 0:
            self._wait(e, "d%d" % idx, semh, val)
        self.deps(e, reads, writes)
        ins = self.eng[e].dma_start(out=out, in_=in_)
        val += 16
        ins.then_inc(semh, 16)
        self.log[e].append(("i", "d%d" % idx, 16))
        self.dsem[idx][1] = val
        tok = ("dma", idx, val)
        for b in reads:
            b.r[("dma", idx)] = tok
        for b in writes:
            b.w = tok
            b.r = {}
        return tok

    def check_deadlock(self):
        sem = {}
        pos = {e: 0 for e in self.ENG}
        prog = True
        while prog:
            prog = False
            for e in self.ENG:
                lg = self.log[e]
                while pos[e] < len(lg):
                    kind, key, val = lg[pos[e]]
                    if kind == "w":
                        if sem.get(key, 0) < val:
                            break
                    else:
                        sem[key] = sem.get(key, 0) + val
                    pos[e] += 1
                    prog = True
        stuck = {e: (pos[e], len(self.log[e]), self.log[e][pos[e]] if pos[e] < len(self.log[e]) else None) for e in self.ENG}
        ok = all(pos[e] == len(self.log[e]) for e in self.ENG)
        return ok, stuck, sem

    def barrier(self):
        for e in self.ENG:
            for e2 in self.ENG:
                if e2 != e and self.cnt[e2] > 0:
                    self._wait(e, e2, self.sem[e2], self.cnt[e2])
            for idx, (semh, val) in enumerate(self.dsem):
                if val > 0:
                    self._wait(e, "d%d" % idx, semh, val)


class Arena:
    def __init__(self, big, n):
        self.big = big
        self.n = n
        self.off = 0

    def mark(self):
        return self.off

    def release(self, m):
        self.off = m

    def _raw(self, nf32):
        a = self.off
        self.off += nf32
        assert self.off <= self.n, "SBUF arena overflow %d > %d" % (self.off, self.n)
        return self.big[:, a:a + nf32]

    @staticmethod
    def _shape(ap, dims):
        if len(dims) == 1:
            return ap
        if len(dims) == 2:
            return ap.rearrange("p (a b) -> p a b", b=dims[1])
        if len(dims) == 3:
            return ap.rearrange("p (a b c) -> p a b c", b=dims[1], c=dims[2])
        raise ValueError

    def f32(self, *dims):
        n = int(np.prod(dims))
        return self._shape(self._raw(n), dims)

    def bf16(self, *dims):
        n = int(np.prod(dims))
        assert n % 2 == 0
        return self._shape(self._raw(n // 2).bitcast(BF16), dims)


def host_consts():
    c = {}
    c["identf"] = np.eye(128, dtype=np.float32)
    bo = np.zeros((128, 128), np.float32)
    bo[:64, :64] = 1.0
    bo[64:, 64:] = 1.0
    c["bones"] = bo
    c["bo64"] = bo / 64.0
    half = 32
    inv = (10000.0 ** (-np.arange(half, dtype=np.float64) / half))
    ang = np.arange(S, dtype=np.float64)[:, None] * inv[None, :]
    c["cosr"] = np.ascontiguousarray(np.cos(ang).reshape(16, 128, 32).transpose(1, 0, 2)).astype(np.float32)
    c["sinr"] = np.ascontiguousarray(np.sin(ang).reshape(16, 128, 32).transpose(1, 0, 2)).astype(np.float32)
    j = np.arange(128)[:, None]
    i = np.arange(128)[None, :]
    c["maskLR"] = np.stack([(j >= i), (j <= i)], axis=1).astype(np.float32)
    t = np.arange(64)
    st_f = (t[:, None] < t[None, :]).astype(np.float32)
    in_f = (t[:, None] <= t[None, :]).astype(np.float32)
    def bd(m):
        z = np.zeros((128, 128), np.float32)
        z[:64, :64] = m
        z[64:, 64:] = m
        return z
    rwm = np.zeros((128, 2, 3, 128), np.float32)
    rwm[:, 0, 0] = bd(st_f); rwm[:, 0, 1] = bd(in_f); rwm[:, 0, 2] = bd(st_f.T)
    rwm[:, 1, 0] = bd(st_f.T); rwm[:, 1, 1] = bd(in_f.T); rwm[:, 1, 2] = bd(st_f)
    c["rwm"] = rwm
    seg = np.ones((128, 512), np.float32)
    seg[:, ::64] = 0.0
    c["segm"] = seg
    return c


def build_nc(stop_after="all", debug=()):
    nc = bass.Bass("TRN2", target_bir_lowering=False)

    def din(name, shape):
        return nc.dram_tensor(name, list(shape), F32, kind="ExternalInput").ap()

    x = din("x", [S, D]); mem = din("mem", [NMEM, D]); w_in = din("w_in", [D, INW])
    attn_w_o = din("attn_w_o", [768, D]); rwkv_w_o = din("rwkv_w_o", [768, D]); x_w_o = din("x_w_o", [512, D])
    x_w_kv = din("x_w_kv", [D, 1024]); w_out = din("w_out", [D, D])
    w2cat = din("w2cat", [128, 768]); a2cat = din("a2cat", [128, 768])
    prm_d = din("prm", [128, NPRM]); gqk_d = din("gqk", [128, 1024]); sink_d = din("sinkb", [128, 12])
    identf_d = din("identf", [128, 128]); bones_d = din("bones", [128, 128]); bo64_d = din("bo64", [128, 128])
    cos_d = din("cosr", [128, 16, 32]); sin_d = din("sinr", [128, 16, 32]); maskLR_d = din("maskLR", [128, 2, 128])
    rwm_d = din("rwm", [128, 2, 3, 128]); segm_d = din("segm", [128, 512])
    out = nc.dram_tensor("out", [S, D], F32, kind="ExternalOutput").ap()

    def dscr(name, shape, dt):
        kind = "ExternalOutput" if name in debug else "Internal"
        return nc.dram_tensor(name, list(shape), dt, kind=kind).ap()

    proj_f = dscr("proj_f", [NPF, S], F32)
    qkv_t = dscr("qkv_t", [S, NQKV], F32)
    ybuf = dscr("ybuf", [2048, S], BF16)
    dbg = dscr("dbg", [128, 4096], F32) if "dbg" in debug else None

    NBIG = 52600
    big_cm = nc.sbuf_tensor("big", [128, NBIG], F32)
    big = big_cm.__enter__()
    ps_cms = [nc.psum_tensor("ps%d" % i, [128, 512], F32) for i in range(8)]
    ps = [cm.__enter__() for cm in ps_cms]
    Bps = [Buf("ps%d" % i) for i in range(8)]
    psb = [p[:, :].bitcast(BF16) for p in ps]
    Sc = Sched(nc)
    AR = Arena(big, NBIG)
    V, A_, P_, G_, T_ = nc.vector, nc.scalar, nc.gpsimd, nc.sync, nc.tensor
    bank_rr = [0]
    dumped = set()

    def dump(name, sb_ap, shape, dt, reads):
        if name in debug and name not in dumped:
            dumped.add(name)
            t = nc.dram_tensor(name, list(shape), dt, kind="ExternalOutput").ap()
            Sc.dma("sp", t, sb_ap, reads=reads)

    def nbank(lo=0, hi=8):
        for _ in range(hi - lo):
            b = lo + bank_rr[0] % (hi - lo)
            bank_rr[0] += 1
            if not Bps[b].pending:
                return b
        raise RuntimeError("all PSUM banks in [%d,%d) hold unconsumed data" % (lo, hi))

    def mm_group(out_ap, items, obuf):
        n = len(items)
        for i, (l, r, rd) in enumerate(items):
            first, last = i == 0, i == n - 1
            Sc.op("pe", lambda: T_.matmul(out_ap, lhsT=l, rhs=r, start=first, stop=last), reads=rd,
                  writes=[obuf] if first else [], inc=last, setw=[obuf] if last else [])

    def mm_multi(items, obuf):
        n = len(items)
        for i, (o, l, r, rd) in enumerate(items):
            first, last = i == 0, i == n - 1
            Sc.op("pe", lambda: T_.matmul(o, lhsT=l, rhs=r, start=True, stop=True), reads=rd,
                  writes=[obuf] if first else [], inc=last, setw=[obuf] if last else [])

    def rsqrt_act(out_ap, in_ap, scale, eps, reads, writes, tmp_ap=None):
        t = out_ap if tmp_ap is None else tmp_ap
        Sc.op("act", lambda: A_.activation(out=t, in_=in_ap, func=AF.Ln, bias=eps, scale=scale), reads=reads, writes=writes)
        Sc.op("act", lambda: A_.activation(out=out_ap, in_=t, func=AF.Exp, scale=-0.5), reads=writes, writes=writes)

    identf = AR.f32(128); identb = AR.bf16(128); prm = AR.f32(NPRM)
    kmT = AR.bf16(4, 256); vm = AR.bf16(2, 512)
    Bid, Bprm, Bkm, Bvm = Buf("id"), Buf("prm"), Buf("kmT"), Buf("vm")
    Sc.dma("sp", identf, identf_d[:, :], writes=[Bid])
    Sc.dma("sp", prm[:, 0:OMM], prm_d[:, 0:OMM], writes=[Bprm])
    Sc.op("dve", lambda: V.tensor_copy(out=identb, in_=identf), reads=[Bid], writes=[Bid])
    Sc.op("dve", lambda: V.tensor_scalar(out=prm[:, OMM:OMM + 20], in0=prm[:, MU:MU + 20], scalar1=-1.0, scalar2=1.0, op0=ALU.mult, op1=ALU.add), reads=[Bprm], writes=[Bprm])
    Sc.op("dve", lambda: V.tensor_scalar(out=prm[:, HMU:HMU + 20], in0=prm[:, MU:MU + 20], scalar1=0.5, scalar2=None, op0=ALU.mult), reads=[Bprm], writes=[Bprm])
    Sc.op("dve", lambda: V.tensor_scalar(out=prm[:, OMK:OMK + 6], in0=prm[:, KA:KA + 6], scalar1=-1.0, scalar2=1.0, op0=ALU.mult, op1=ALU.add), reads=[Bprm], writes=[Bprm])
    Sc.op("dve", lambda: V.tensor_scalar(out=prm[:, HW0:HW0 + 24], in0=prm[:, W0:W0 + 24], scalar1=0.5, scalar2=None, op0=ALU.mult), reads=[Bprm], writes=[Bprm])
    Sc.op("dve", lambda: V.tensor_tensor(out=prm[:, XG2:XG2 + 1], in0=prm[:, XQG:XQG + 1], in1=prm[:, XKG:XKG + 1], op=ALU.mult), reads=[Bprm], writes=[Bprm])
    m_persist = AR.mark()

    hT = AR.bf16(16, S)
    BhT = [Buf("hT%d" % g) for g in range(4)]
    memT = AR.bf16(16, NMEM); BmemT = Buf("memT")
    m_ph0 = AR.mark()
    xbuf = [AR.f32(4, D) for _ in range(2)]
    Bx = [[Buf() for _ in range(4)] for _ in range(2)]
    junk = AR.f32(D); Bjunk = Buf("junk")
    evac_rr = [0]

    def build_T(src, nblk_total, gcol, dstT, dst_bufs):
        ngrp = (nblk_total + 3) // 4
        for g in range(ngrp):
            nb = min(4, nblk_total - g * 4)
            xb, bx = xbuf[g % 2], Bx[g % 2]
            ssq = AR.f32(4); rt = AR.f32(4); Bss = Buf("ss")
            Sc.op("dve", lambda: V.memset(ssq, 0.0), writes=[Bss])
            for i in range(nb):
                r0 = (g * 4 + i) * 128
                Sc.dma("sp", xb[:, i, :], src[r0:r0 + 128, :], writes=[bx[i]])
            for i in range(nb):
                Sc.op("act", lambda: A_.activation(out=junk, in_=xb[:, i, :], func=AF.Square, accum_out=ssq[:, i:i + 1]),
                      reads=[bx[i]], writes=[Bjunk, Bss])
            rsqrt_act(rt[:, 0:nb], ssq[:, 0:nb], 1.0 / D, EPS, [Bss], [Bss])
            for i in range(nb):
                Sc.op("dve", lambda: V.tensor_scalar(out=xb[:, i, :], in0=xb[:, i, :], scalar1=rt[:, i:i + 1], scalar2=None, op0=ALU.mult),
                      reads=[bx[i], Bss], writes=[bx[i]])
            for c in range(16):
                b = nbank()
                for i in range(nb):
                    Sc.op("pe", lambda: T_.transpose(out=ps[b][:, i * 128:(i + 1) * 128], in_=xb[:, i, c * 128:(c + 1) * 128], identity=identf),
                          reads=[bx[i], Bid], writes=[Bps[b]] if i == 0 else [], inc=(i == nb - 1), setw=[Bps[b]] if i == nb - 1 else [])
                dst = dstT[:, c, g * 512:g * 512 + nb * 128]
                gc = prm[:, gcol + c:gcol + c + 1]
                if evac_rr[0] % 2 == 0:
                    Sc.op("act", lambda: A_.activation(out=dst, in_=ps[b][:, 0:nb * 128], func=AF.Copy, scale=gc),
                          reads=[Bps[b], Bprm], writes=[dst_bufs[g]])
                else:
                    Sc.op("dve", lambda: V.tensor_scalar(out=dst, in0=ps[b][:, 0:nb * 128], scalar1=gc, scalar2=None, op0=ALU.mult),
                          reads=[Bps[b], Bprm], writes=[dst_bufs[g]])
                evac_rr[0] += 1

    build_T(mem, 2, MG_, memT, [BmemT])
    build_T(x, 16, NG, hT, BhT)

    Sc.barrier()
    AR.release(m_ph0)
    memT2 = memT
    wkv = AR.bf16(16, 1024); Bwkv = [Buf("wkv%d" % q) for q in range(4)]
    kmn = AR.f32(512); Bkmn = Buf("kmn")
    ssk = AR.f32(4); rk_ = AR.f32(4); Bssk = Buf("ssk")
    junk2 = AR.f32(128); Bjunk2 = Buf("junk2")
    wkv_src = x_w_kv.rearrange("(k p) n -> p k n", p=128)
    for q in range(4):
        Sc.dma("pool", wkv[:, q * 4:(q + 1) * 4, :], wkv_src[:, q * 4:(q + 1) * 4, :], writes=[Bwkv[q]])
    for mb in range(2):
        for half in range(2):
            b = nbank()
            mm_group(ps[b][:, :], [(memT2[:, k, mb * 128:(mb + 1) * 128], wkv[:, k, half * 512:(half + 1) * 512], [BmemT, Bwkv[k // 4]]) for k in range(16)], Bps[b])
            if half == 0:
                Sc.op("dve", lambda: V.memset(ssk, 0.0), writes=[Bssk])
                for h in range(4):
                    Sc.op("act", lambda: A_.activation(out=junk2, in_=ps[b][:, h * 128:(h + 1) * 128], func=AF.Square, accum_out=ssk[:, h:h + 1]),
                          reads=[Bps[b]], writes=[Bjunk2, Bssk])
                rsqrt_act(rk_, ssk, 1.0 / 128, EPS, [Bssk], [Bssk])
                for h in range(4):
                    Sc.op("dve", lambda: V.tensor_scalar(out=kmn[:, h * 128:(h + 1) * 128], in0=ps[b][:, h * 128:(h + 1) * 128], scalar1=rk_[:, h:h + 1], scalar2=None, op0=ALU.mult),
                          reads=[Bps[b], Bssk], writes=[Bkmn])
                b2 = nbank()
                for h in range(4):
                    Sc.op("pe", lambda: T_.transpose(out=ps[b2][:, h * 128:(h + 1) * 128], in_=kmn[:, h * 128:(h + 1) * 128], identity=identf),
                          reads=[Bkmn, Bid], writes=[Bps[b2]] if h == 0 else [], inc=(h == 3), setw=[Bps[b2]] if h == 3 else [])
                Sc.op("dve", lambda: V.tensor_scalar(out=kmT[:, :, mb * 128:(mb + 1) * 128], in0=ps[b2][:, :].rearrange("p (h m) -> p h m", m=128),
                                                     scalar1=prm[:, XG2:XG2 + 1], scalar2=None, op0=ALU.mult),
                      reads=[Bps[b2], Bprm], writes=[Bkm])
            else:
                Sc.op("act", lambda: A_.activation(out=vm[:, mb, :], in_=ps[b][:, :], func=AF.Copy), reads=[Bps[b]], writes=[Bvm])
    dump("d_kmT", kmT, [128, 4, 256], BF16, [Bkm]); dump("d_vm", vm, [128, 2, 512], BF16, [Bvm])
    dump("d_hT", hT, [128, 16, S], BF16, BhT)
    Sc.barrier()
    if stop_after == "hT":
        return finish_debug(nc, Sc, locals())

    AR.release(m_ph0)
    NWB = 3
    wt = [AR.bf16(16, 512) for _ in range(NWB)]
    Bwt = [[Buf("wt%d_%d" % (i, q)) for q in range(4)] for i in range(NWB)]
    stf = [AR.f32(S) for _ in range(2)]; Bstf = [Buf("stf%d" % i) for i in range(2)]
    stt = [AR.f32(512) for _ in range(3)]; Bstt = [Buf("stt%d" % i) for i in range(3)]
    Bproj = [Buf("pf%d" % i) for i in range(NPF // 128)]
    Bqkv = Buf("qkv")
    w_src = w_in.rearrange("(k p) n -> p k n", p=128)
    NT = (INW + 511) // 512

    TORDER = [0, 1, 2, 10, 11, 12] + [t for t in range(NT) if t not in (0, 1, 2, 10, 11, 12)]

    def load_w(pos):
        t = TORDER[pos]
        c0 = t * 512
        ncol = min(512, INW - c0)
        for q in range(4):
            Sc.dma("pool", wt[pos % NWB][:, q * 4:(q + 1) * 4, 0:ncol], w_src[:, q * 4:(q + 1) * 4, c0:c0 + ncol], writes=[Bwt[pos % NWB][q]])

    def act_for(feat):
        if (1280 <= feat < 2048) or (4608 <= feat < 5376) or (5888 <= feat < 6400):
            return "silu"
        if feat >= 6400:
            return "sig"
        return "copy"

    stf_rr = [0]; stt_rr = [0]; ev_rr = [0]
    import os as _os2
    TESTCOPY = bool(_os2.environ.get("TESTCOPY"))
    load_w(0); load_w(1)

    def rr(gens, weights=None):
        act_ = [[g, (weights[i] if weights else 1)] for i, g in enumerate(gens)]
        while act_:
            for ent in list(act_):
                for _ in range(ent[1]):
                    try:
                        next(ent[0])
                        yield
                    except StopIteration:
                        act_.remove(ent)
                        break

    def run(gen):
        for _ in gen:
            pass

    def proj_gen(t_lo, t_hi, bk):
      for pos in range(t_lo, t_hi):
          t = TORDER[pos]
          if pos + 2 < NT:
              load_w(pos + 2)
          c0 = t * 512
          ncol = min(512, INW - c0)
          w = wt[pos % NWB]; bw = Bwt[pos % NWB]
          ntok = max(0, min(ncol, NQKV - c0))
          if ntok > 0:
              for tb in range(16):
                  b = nbank(*bk)
                  mm_group(ps[b][:, 0:ntok], [(hT[:, k, tb * 128:(tb + 1) * 128], w[:, k, 0:ntok], [BhT[tb // 4], bw[k // 4]]) for k in range(16)], Bps[b])
                  si = stt_rr[0] % 3; stt_rr[0] += 1
                  if ev_rr[0] % 2 == 0:
                      Sc.op("act", lambda: A_.activation(out=stt[si][:, 0:ntok], in_=ps[b][:, 0:ntok], func=AF.Copy), reads=[Bps[b]], writes=[Bstt[si]])
                  else:
                      Sc.op("dve", lambda: V.tensor_copy(out=stt[si][:, 0:ntok], in_=ps[b][:, 0:ntok]), reads=[Bps[b]], writes=[Bstt[si]])
                  ev_rr[0] += 1
                  Sc.dma("sp", qkv_t[tb * 128:(tb + 1) * 128, c0:c0 + ntok], stt[si][:, 0:ntok], reads=[Bstt[si]])
                  yield
          for sub in range(ntok // 128, ncol // 128):
              feat = c0 + sub * 128
              fi = (feat - NQKV) // 128
              kind = act_for(feat)
              si = stf_rr[0] % 2; stf_rr[0] += 1
              for tc in range(4):
                  b = nbank(*bk)
                  mm_group(ps[b][:, :], [(w[:, k, sub * 128:(sub + 1) * 128], hT[:, k, tc * 512:(tc + 1) * 512], [bw[k // 4], BhT[tc]]) for k in range(16)], Bps[b])
                  dst = stf[si][:, tc * 512:(tc + 1) * 512]
                  if kind == "silu":
                      Sc.op("act", lambda: A_.activation(out=dst, in_=ps[b][:, :], func=AF.Silu), reads=[Bps[b]], writes=[Bstf[si]])
                  elif kind == "sig":
                      gcol = GB + (feat - 6400) // 128
                      Sc.op("act", lambda: A_.activation(out=dst, in_=ps[b][:, :], func=(AF.Tanh if TESTCOPY else AF.Sigmoid), bias=prm[:, gcol:gcol + 1], scale=1.0),
                            reads=[Bps[b], Bprm], writes=[Bstf[si]])
                  else:
                      if ev_rr[0] % 2 == 0:
                          Sc.op("act", lambda: A_.activation(out=dst, in_=ps[b][:, :], func=AF.Copy), reads=[Bps[b]], writes=[Bstf[si]])
                      else:
                          Sc.op("dve", lambda: V.tensor_copy(out=dst, in_=ps[b][:, :]), reads=[Bps[b]], writes=[Bstf[si]])
                      ev_rr[0] += 1
                  yield
              Sc.dma("sp", proj_f[fi * 128:(fi + 1) * 128, :], stf[si], reads=[Bstf[si]], writes=[Bproj[fi]])

    TSPLIT = 6
    run(proj_gen(0, TSPLIT, (0, 8)))
    onesf = AR.f32(128); onesb = AR.bf16(128); Bones = Buf("ones")
    Sc.op("pool", lambda: P_.memset(onesf, 1.0), writes=[Bones])
    Sc.op("pool", lambda: P_.tensor_copy(out=onesb, in_=onesf), reads=[Bones], writes=[Bones])
    qTc = [AR.f32(S) for _ in range(2)]; gtc = [AR.f32(S)] * 2
    BqTc = [Buf(), Buf()]; Bgtc = [Buf()] * 2
    sqc = AR.f32(512); sc2 = AR.f32(512); qn_c = AR.bf16(512); pTc = [AR.bf16(512) for _ in range(2)]; rden = AR.f32(512); yo = AR.f32(512)
    yxs = AR.bf16(S)
    Bsqc, Bsc2, Bqnc, BpTc, Brden, Byo, Byxs = Buf(), Buf(), Buf(), [Buf(), Buf()], Buf(), Buf(), Buf()

    c_rr = [0]

    def cbank():
        c_rr[0] += 1
        return 6 + c_rr[0] % 2

    def xattn_gen():
        for h in range(4):
            Sc.dma("sp", qTc[h % 2], proj_f[XQ0 + h * 128:XQ0 + (h + 1) * 128, :], reads=[Bproj[XQ0 // 128 + h]], writes=[BqTc[h % 2]])
            Sc.dma("sp", gtc[h % 2], proj_f[XG0 + h * 128:XG0 + (h + 1) * 128, :], reads=[Bproj[XG0 // 128 + h]], writes=[Bgtc[h % 2]])
            q_ = qTc[h % 2]; bq_ = BqTc[h % 2]
            for tc in range(4):
                sl = slice(tc * 512, (tc + 1) * 512)
                Sc.op("act", lambda: A_.activation(out=sqc, in_=q_[:, sl], func=AF.Square), reads=[bq_], writes=[Bsqc]); yield
                yield; yield
                b = cbank()
                mm_group(ps[b][:, :], [(onesf, sqc, [Bones, Bsqc])], Bps[b]); yield
                rsqrt_act(sc2, ps[b][:, :], 1.0, 128.0 * EPS, [Bps[b]], [Bsc2]); yield
                Sc.op("dve", lambda: V.tensor_tensor(out=qn_c, in0=q_[:, sl], in1=sc2, op=ALU.mult), reads=[bq_, Bsc2], writes=[Bqnc]); yield
                for mb in range(2):
                    yield; yield
                    b = cbank()
                    mm_group(ps[b][:, :], [(kmT[:, h, mb * 128:(mb + 1) * 128], qn_c, [Bkm, Bqnc])], Bps[b]); yield
                    Sc.op("act", lambda: A_.activation(out=pTc[mb], in_=ps[b][:, :], func=AF.Exp), reads=[Bps[b]], writes=[BpTc[mb]]); yield
                yield; yield
                bo_ = cbank()
                mm_group(ps[bo_][:, :], [(vm[:, mb, h * 128:(h + 1) * 128], pTc[mb], [Bvm, BpTc[mb]]) for mb in range(2)], Bps[bo_]); yield
                bd_ = cbank()
                mm_group(ps[bd_][:, :], [(onesb, pTc[mb], [Bones, BpTc[mb]]) for mb in range(2)], Bps[bd_]); yield
                Sc.op("act", lambda: A_.activation(out=rden, in_=ps[bd_][:, :], func=AF.Ln), reads=[Bps[bd_]], writes=[Brden]); yield
                Sc.op("act", lambda: A_.activation(out=rden, in_=rden, func=AF.Exp, scale=-1.0), reads=[Brden], writes=[Brden]); yield
                Sc.op("dve", lambda: V.tensor_tensor(out=yo, in0=ps[bo_][:, :], in1=rden, op=ALU.mult), reads=[Bps[bo_], Brden], writes=[Byo]); yield
                Sc.op("pool", lambda: P_.tensor_tensor(out=yxs[:, sl], in0=yo, in1=gtc[h % 2][:, sl], op=ALU.mult), reads=[Byo, Bgtc[h % 2]], writes=[Byxs]); yield
            Sc.dma("sp", ybuf[1536 + h * 128:1536 + (h + 1) * 128, :], yxs, reads=[Byxs]); yield


    run(rr([proj_gen(TSPLIT, NT, (0, 6)), xattn_gen()]))
    Sc.barrier()
    if stop_after == "proj":
        return finish_debug(nc, Sc, locals())

    AR.release(m_persist)
    bones = AR.f32(128); bo64 = AR.f32(128); rwm = AR.bf16(2, 3, 128); segm = AR.f32(512)
    w2b = AR.bf16(768); a2b = AR.bf16(768)
    Bc = Buf("rwconst")
    Sc.dma("sp", bones, bones_d[:, :], writes=[Bc])
    tb1 = Buf(); tb2 = Buf(); tb3 = Buf(); tb4 = Buf(); tb5 = Buf()
    Sc.dma("sp", bo64, bo64_d[:, :], writes=[tb1])
    Sc.dma("sp", segm, segm_d[:, :], writes=[tb2])
    Sc.dma("pool", rwm, rwm_d[:, :, :, :], writes=[tb3])
    Sc.dma("pool", w2b, w2cat[:, :], writes=[tb4])
    Sc.dma("pool", a2b, a2cat[:, :], writes=[tb5])
    lw_t = AR.bf16(S); la_s = AR.bf16(S); Blw = Buf("lw"); Bla = Buf("la")
    r_ = AR.bf16(S); k_ = AR.bf16(S); v_ = AR.bf16(S); kk_ = AR.bf16(S); bon = AR.f32(S); ysum = AR.f32(S)
    Br, Bk, Bv, Bkk, Bbon, Bys = Buf("r"), Buf("k"), Buf("v"), Buf("kk"), Buf("bon"), Buf("ysum")
    Vtok = AR.bf16(32, 128); BVtok = [Buf("vtok%d" % q) for q in range(4)]
    vbd = AR.bf16(8, 128); Bvbd = Buf("vbd")
    rkones = AR.f32(128); Brk = Buf("rkones")
    NTMP = 7168
    tmp_raw = AR._raw(NTMP)
    WD = []
    for d in range(2):
        W = {}
        for nm in ("ARt", "BKtok", "Gb", "Gk"):
            W[nm] = [AR.bf16(8, 2, 128) for _ in range(2)]
        W["Tt"] = [AR.bf16(8, 128) for _ in range(2)]
        W["Pc"] = [AR.f32(8) for _ in range(2)]
        W["BKt"] = AR.bf16(8, 2, 128)
        W["Xs"] = AR.bf16(128); W["Ubf"] = AR.bf16(128); W["Ybd"] = AR.f32(8, 128)
        W["St"] = AR.f32(128); W["Stbf"] = AR.bf16(128); W["tmpS"] = AR.f32(128); W["tot"] = AR.f32(8)
        for nm in ("BBKt", "BXs", "BUbf", "BSt", "BStbf", "BtmpS", "Btot", "BYbd"):
            W[nm] = Buf(nm)
        for nm in ("BARt", "BPc"):
            W[nm] = [Buf(nm + "0"), Buf(nm + "1")]
        for nm in ("BBKtok", "BGb", "BGk", "BTt"):
            W[nm] = [[Buf(), Buf()], [Buf(), Buf()]]
        W["BLab"] = [Buf(), Buf()]
        W["BAn"] = [[Buf(), Buf()], [Buf(), Buf()]]; W["BBn"] = [[Buf(), Buf()], [Buf(), Buf()]]
        TA = Arena(tmp_raw[:, d * (NTMP // 2):(d + 1) * (NTMP // 2)], NTMP // 2)
        for nm in ("a", "ld", "cum", "E1", "E2", "E3", "u"):
            W[nm] = TA.f32(512); W["B" + nm] = Buf(nm)
        asb = lambda ap: ap.bitcast(BF16).rearrange("p (a b) -> p a b", b=128)
        W["An"] = [asb(W["E1"]), asb(W["E2"])]; W["Bn"] = [asb(W["E3"]), asb(W["u"])]; W["Lab"] = asb(W["ld"])
        W["alias_An"] = [W["BE1"], W["BE2"]]; W["alias_Bn"] = [W["BE3"], W["Bu"]]
        WD.append(W)
    for d in range(2):
        W = WD[d]
        for jp in range(2):
            Sc.op("pool", lambda: P_.memset(W["ARt"][jp], 0.0), writes=[W["BARt"][jp]])
        Sc.op("pool", lambda: P_.memset(W["BKt"], 0.0), writes=[W["BBKt"]])
    Sc.op("pool", lambda: P_.memset(vbd, 0.0), writes=[Bvbd])

    rawc = [AR.f32(514) for _ in range(2)]; Brawc = [Buf(), Buf()]
    nbc = AR.f32(512); sqc_ = AR.f32(512); Bnbc, Bsqc_ = Buf(), Buf()
    sqk = nbc; rnk = sqc_; Bsqk, Brnk = Bnbc, Bsqc_
    gatec = AR.f32(512); dtmp = AR.f32(512); sq2 = AR.f32(512); rstd = AR.f32(512); yn = AR.f32(512); ystc = [AR.bf16(512) for _ in range(2)]
    Bgatec, Bdt, Bs2, Brs, Byn, Bystc = Buf(), Buf(), Buf(), Buf(), Buf(), [Buf(), Buf()]
    rc_rr = [0]

    def shift_gen(dst, bdst, row0, mi, func=AF.Copy):
        for tc in range(4):
            sl = slice(tc * 512, (tc + 1) * 512)
            lo = max(0, tc * 512 - 1); hi = min(S, tc * 512 + 513)
            off = lo - (tc * 512 - 1)
            ri = rc_rr[0] % 2; rc_rr[0] += 1
            rc = rawc[ri]; brc = Brawc[ri]
            if tc == 0:
                Sc.op("pool", lambda: P_.memset(rc[:, 0:1], 0.0), writes=[brc])
            if tc == 3:
                Sc.op("pool", lambda: P_.memset(rc[:, 513:514], 0.0), writes=[brc])
            Sc.dma("sp", rc[:, off:off + (hi - lo)], proj_f[row0:row0 + 128, lo:hi], writes=[brc]); yield
            Sc.op("pool", lambda: P_.tensor_tensor(out=nbc, in0=rc[:, 0:512], in1=rc[:, 2:514], op=ALU.add), reads=[brc], writes=[Bnbc]); yield
            Sc.op("act", lambda: A_.activation(out=sqc_, in_=rc[:, 1:513], func=AF.Copy, scale=prm[:, OMM + mi:OMM + mi + 1]), reads=[brc, Bprm], writes=[Bsqc_]); yield
            if func == AF.Copy:
                Sc.op("dve", lambda: V.scalar_tensor_tensor(out=dst[:, sl], in0=nbc, scalar=prm[:, HMU + mi:HMU + mi + 1], in1=sqc_, op0=ALU.mult, op1=ALU.add),
                      reads=[Bnbc, Bprm, Bsqc_], writes=[bdst]); yield
            else:
                Sc.op("dve", lambda: V.scalar_tensor_tensor(out=sqc_, in0=nbc, scalar=prm[:, HMU + mi:HMU + mi + 1], in1=sqc_, op0=ALU.mult, op1=ALU.add),
                      reads=[Bnbc, Bprm, Bsqc_], writes=[Bsqc_]); yield
                Sc.op("act", lambda: A_.activation(out=dst[:, sl], in_=sqc_, func=func), reads=[Bsqc_], writes=[bdst]); yield

    def v3(ap, n=64):
        return ap.rearrange("p (c t) -> p c t", t=n)

    def unit_prep(p, d, sc, jp):
        W = WD[d]
        sl = slice(sc * 512, (sc + 1) * 512)
        dh = slice(d * 64, (d + 1) * 64)
        pc = slice(p * 128, (p + 1) * 128)
        a, ld, cum, E1, E2, E3, u = W["a"], W["ld"], W["cum"], W["E1"], W["E2"], W["E3"], W["u"]
        Ba, Bld, Bcum, BE1, BE2, BE3, Bu = W["Ba"], W["Bld"], W["Bcum"], W["BE1"], W["BE2"], W["BE3"], W["Bu"]
        ARt, BKtok, Gb, Gk, Tt, Pc = W["ARt"][jp], W["BKtok"][jp], W["Gb"][jp], W["Gk"][jp], W["Tt"][jp], W["Pc"][jp]
        BARt, BBKtok, BGb, BGk, BTt, BPc = W["BARt"][jp], W["BBKtok"][jp], W["BGb"][jp], W["BGk"][jp], W["BTt"][jp], W["BPc"][jp]
        BKt, Lab = W["BKt"], W["Lab"]
        b = nbank(2, 8)
        mm_group(ps[b][:, :], [(a2b[dh, pc], la_s[dh, sl], [tb5, Bla])], Bps[b]); yield
        hc = HA0 + d * 6 + p
        Sc.op("act", lambda: A_.activation(out=a, in_=ps[b][:, :], func=AF.Tanh, bias=prm[:, hc:hc + 1], scale=0.5), reads=[Bps[b], Bprm], writes=[Ba]); yield
        Sc.op("dve", lambda: V.tensor_scalar(out=a, in0=a, scalar1=0.5, scalar2=0.5, op0=ALU.mult, op1=ALU.add), reads=[Ba], writes=[Ba]); yield
        b = nbank(2, 8)
        mm_group(ps[b][:, :], [(w2b[dh, pc], lw_t[dh, sl], [tb4, Blw])], Bps[b]); yield
        hc2 = HW0 + d * 6 + p
        Sc.op("act", lambda: A_.activation(out=ld, in_=ps[b][:, :], func=AF.Tanh, bias=prm[:, hc2:hc2 + 1], scale=0.5), reads=[Bps[b], Bprm], writes=[Bld] + W["BLab"]); yield
        Sc.op("dve", lambda: V.tensor_scalar(out=ld, in0=ld, scalar1=C1, scalar2=C1, op0=ALU.mult, op1=ALU.add), reads=[Bld], writes=[Bld]); yield
        Sc.op("dve", lambda: V.tensor_tensor_scan(out=cum, data0=segm, data1=ld, initial=0.0, op0=ALU.mult, op1=ALU.add), reads=[tb2, Bld], writes=[Bcum]); yield
        if d == 1:
            Sc.op("dve", lambda: V.tensor_copy(out=W["tot"], in_=v3(cum)[:, :, 63]), reads=[Bcum], writes=[W["Btot"]]); yield
            Sc.op("dve", lambda: V.tensor_tensor(out=cum, in0=ld, in1=cum, op=ALU.subtract), reads=[Bld, Bcum], writes=[Bcum]); yield
            Sc.op("dve", lambda: V.tensor_tensor(out=v3(cum), in0=v3(cum), in1=W["tot"].unsqueeze(2).to_broadcast([128, 8, 64]), op=ALU.add),
                  reads=[Bcum, W["Btot"]], writes=[Bcum]); yield
        Sc.op("dve", lambda: V.tensor_tensor(out=ld, in0=cum, in1=ld, op=ALU.subtract), reads=[Bcum, Bld], writes=[Bld]); yield
        Sc.op("act", lambda: A_.activation(out=E3, in_=ld, func=AF.Exp), reads=[Bld], writes=[BE3] + W["BBn"][0]); yield
        Sc.op("act", lambda: A_.activation(out=E1, in_=cum, func=AF.Exp), reads=[Bcum], writes=[BE1] + W["BAn"][0]); yield
        Sc.op("act", lambda: A_.activation(out=E2, in_=cum, func=AF.Exp, scale=-1.0), reads=[Bcum], writes=[BE2] + W["BAn"][1]); yield
        pcol = 63 if d == 0 else 0
        Sc.op("dve", lambda: V.tensor_copy(out=Pc, in_=v3(E1)[:, :, pcol]), reads=[BE1], writes=[BPc]); yield
        Sc.op("dve", lambda: V.tensor_scalar(out=u, in0=a, scalar1=prm[:, KA + p:KA + p + 1], scalar2=prm[:, OMK + p:OMK + p + 1], op0=ALU.mult, op1=ALU.add),
              reads=[Ba, Bprm], writes=[Bu] + W["BBn"][1]); yield
        Sc.op("dve", lambda: V.tensor_tensor(out=u, in0=k_[:, sl], in1=u, op=ALU.mult), reads=[Bk, Bu], writes=[Bu]); yield
        Sc.op("pool", lambda: P_.tensor_tensor(out=a, in0=kk_[:, sl], in1=a, op=ALU.mult), reads=[Bkk, Ba], writes=[Ba]); yield
        for half in range(2):
            hs = slice(half * 64, (half + 1) * 64)
            bc = slice(half * 64, (half + 1) * 64)
            Sc.op("dve", lambda: V.scalar_tensor_tensor(out=ARt[hs, :, 0, bc], in0=v3(kk_[hs, sl]), scalar=-1.0, in1=v3(E3[hs, :]), op0=ALU.mult, op1=ALU.mult),
                  reads=[Bkk, BE3], writes=[BARt]); yield
            Sc.op("pool", lambda: P_.tensor_tensor(out=ARt[hs, :, 1, bc], in0=v3(r_[hs, sl]), in1=v3(E1[hs, :]), op=ALU.mult), reads=[Br, BE1], writes=[BARt]); yield
            Sc.op("dve", lambda: V.tensor_tensor(out=BKt[hs, :, 1, bc], in0=v3(u[hs, :]), in1=v3(E2[hs, :]), op=ALU.mult), reads=[Bu, BE2], writes=[W["BBKt"]]); yield
            Sc.op("dve", lambda: V.tensor_tensor(out=BKt[hs, :, 0, bc], in0=v3(a[hs, :]), in1=v3(E2[hs, :]), op=ALU.mult), reads=[Ba, BE2], writes=[W["BBKt"]]); yield
        Sc.op("pool", lambda: P_.tensor_tensor(out=cum, in0=r_[:, sl], in1=u, op=ALU.mult), reads=[Br, Bu, Bcum], writes=[Bcum]); yield
        b = nbank(2, 8)
        mm_group(ps[b][:, :], [(rkones, cum, [Brk, Bcum])], Bps[b]); yield
        Sc.op("dve", lambda: V.tensor_tensor(out=cum, in0=ps[b][:, :], in1=v_[:, sl], op=ALU.mult), reads=[Bps[b], Bv, Bcum], writes=[Bcum]); yield
        Sc.op("pool", lambda: P_.tensor_tensor(out=bon[:, sl], in0=bon[:, sl], in1=cum, op=ALU.add), reads=[Bbon, Bcum], writes=[Bbon]); yield
        for hb in range(2):
            b = nbank(2, 8)
            for ci in range(4):
                for s2 in range(2):
                    j = ci * 2 + s2
                    Sc.op("pe", lambda: T_.transpose(out=psb[b][:, j * 128:(j + 1) * 128], in_=BKt[:, hb * 4 + ci, s2, :], identity=identb),
                          reads=[W["BBKt"], Bid], writes=[Bps[b]] if j == 0 else [], inc=(j == 7), setw=[Bps[b]] if j == 7 else [])
            yield
            dstv = BKtok[:, hb * 4:(hb + 1) * 4, :, :].rearrange("p a b c -> p (a b c)")
            Sc.op("act", lambda: A_.activation(out=dstv, in_=psb[b], func=AF.Copy), reads=[Bps[b]], writes=[BBKtok[hb]]); yield
        M2 = rwm[:, d, 0:2, :].rearrange("p a b -> p (a b)")
        ML = rwm[:, d, 2, :]
        for hb in range(2):
            bL = nbank(2, 8)
            mm_multi([(ps[bL][:, ci * 128:(ci + 1) * 128], ARt[:, hb * 4 + ci, 0, :], BKt[:, hb * 4 + ci, 0, :], [W["BBKt"], BARt]) for ci in range(4)], Bps[bL])
            yield
            Sc.op("dve", lambda: V.tensor_tensor(out=Lab[:, hb * 4:(hb + 1) * 4, :], in0=ps[bL][:, :].rearrange("p (a b) -> p a b", b=128),
                                                 in1=ML.unsqueeze(1).to_broadcast([128, 4, 128]), op=ALU.mult), reads=[Bps[bL], tb3], writes=[W["BLab"][hb], Bld]); yield
            bB = [nbank(2, 8), nbank(2, 8)]
            for i in range(2):
                mm_multi([(ps[bB[i]][:, q2 * 256:(q2 + 1) * 256], BKt[:, hb * 4 + i * 2 + q2, 0, :], ARt[:, hb * 4 + i * 2 + q2, :, :], [W["BBKt"], BARt]) for q2 in range(2)], Bps[bB[i]])
            yield
            for i in range(2):
                c0 = hb * 4 + i * 2
                Sc.op("dve", lambda: V.tensor_tensor(out=Gb[:, c0:c0 + 2, :, :].rearrange("p a b c -> p a (b c)"), in0=ps[bB[i]][:, :].rearrange("p (a b) -> p a b", b=256),
                                                     in1=M2.unsqueeze(1).to_broadcast([128, 2, 256]), op=ALU.mult), reads=[Bps[bB[i]], tb3], writes=[BGb[hb]]); yield
        for hb in range(2):
            cs = slice(hb * 4, (hb + 1) * 4)
            Sc.op("dve", lambda: V.tensor_tensor(out=Tt[:, cs, :], in0=Lab[:, cs, :], in1=identb.unsqueeze(1).to_broadcast([128, 4, 128]), op=ALU.add),
                  reads=[W["BLab"][hb], Bid], writes=[BTt[hb]]); yield
        for lvl in range(0, 6):
            for hb in range(2):
                cs = slice(hb * 4, (hb + 1) * 4)
                if lvl == 0:
                    Aget = lambda c: Gb[:, c, 0, :]
                    Bget = lambda c: Lab[:, c, :]
                    BAsrc, BBsrc = BGb[hb], W["BLab"][hb]
                else:
                    Aprev, Bprev = W["An"][lvl % 2], W["Bn"][lvl % 2]
                    Aget = lambda c: Aprev[:, c, :]
                    Bget = lambda c: Bprev[:, c, :]
                    BAsrc, BBsrc = W["BAn"][lvl % 2][hb], W["BBn"][lvl % 2][hb]
                Anew, Bnew = W["An"][(lvl + 1) % 2], W["Bn"][(lvl + 1) % 2]
                BAnew, BBnew = W["BAn"][(lvl + 1) % 2][hb], W["BBn"][(lvl + 1) % 2][hb]
                if lvl >= 1:
                    bT = nbank(2, 8)
                    mm_multi([(ps[bT][:, ci * 128:(ci + 1) * 128], Aget(hb * 4 + ci), Tt[:, hb * 4 + ci, :], [BAsrc, BTt[hb]]) for ci in range(4)], Bps[bT])
                    yield
                if lvl < 5:
                    if lvl < 4:
                        bA = nbank(2, 8)
                        mm_multi([(ps[bA][:, ci * 128:(ci + 1) * 128], Bget(hb * 4 + ci), Aget(hb * 4 + ci), [BAsrc, BBsrc]) for ci in range(4)], Bps[bA])
                        yield
                    bBm = nbank(2, 8)
                    mm_multi([(ps[bBm][:, ci * 128:(ci + 1) * 128], Aget(hb * 4 + ci), Bget(hb * 4 + ci), [BAsrc, BBsrc]) for ci in range(4)], Bps[bBm])
                    yield
                if lvl >= 1:
                    Sc.op("dve", lambda: V.tensor_tensor(out=Tt[:, cs, :], in0=ps[bT][:, :].rearrange("p (a b) -> p a b", b=128), in1=Tt[:, cs, :], op=ALU.add),
                          reads=[Bps[bT], BTt[hb]], writes=[BTt[hb]]); yield
                if lvl < 5:
                    if lvl < 4:
                        Sc.op("act", lambda: A_.activation(out=Anew[:, cs, :], in_=ps[bA][:, :].rearrange("p (a b) -> p a b", b=128), func=AF.Copy),
                              reads=[Bps[bA]], writes=[BAnew, W["alias_An"][(lvl + 1) % 2]]); yield
                    Sc.op("act", lambda: A_.activation(out=Bnew[:, cs, :], in_=ps[bBm][:, :].rearrange("p (a b) -> p a b", b=128), func=AF.Copy),
                          reads=[Bps[bBm]], writes=[BBnew, W["alias_Bn"][(lvl + 1) % 2]]); yield
        for hb in range(2):
            cs = slice(hb * 4, (hb + 1) * 4)
            bX = nbank(2, 8)
            mm_multi([(ps[bX][:, ci * 128:(ci + 1) * 128], Tt[:, hb * 4 + ci, :], BKtok[:, hb * 4 + ci, 0, :], [BTt[hb], BBKtok[hb]]) for ci in range(4)], Bps[bX])
            yield
            bM = nbank(2, 8)
            mm_multi([(ps[bM][:, ci * 128:(ci + 1) * 128], Tt[:, hb * 4 + ci, :], Gb[:, hb * 4 + ci, 1, :], [BTt[hb], BGb[hb]]) for ci in range(4)], Bps[bM])
            yield
            Sc.op("act", lambda: A_.activation(out=BKtok[:, cs, 0, :], in_=ps[bX][:, :].rearrange("p (a b) -> p a b", b=128), func=AF.Copy), reads=[Bps[bX]], writes=[BBKtok[hb]]); yield
            Sc.op("dve", lambda: V.tensor_copy(out=Gb[:, cs, 1, :], in_=ps[bM][:, :].rearrange("p (a b) -> p a b", b=128)), reads=[Bps[bM]], writes=[BGb[hb]]); yield
        for hb in range(2):
            bK = [nbank(2, 8), nbank(2, 8)]
            for i in range(2):
                mm_multi([(ps[bK[i]][:, q2 * 256:(q2 + 1) * 256], BKt[:, hb * 4 + i * 2 + q2, 1, :], ARt[:, hb * 4 + i * 2 + q2, :, :], [W["BBKt"], BARt]) for q2 in range(2)], Bps[bK[i]])
            yield
            for i in range(2):
                c0 = hb * 4 + i * 2
                Sc.op("dve", lambda: V.tensor_tensor(out=Gk[:, c0:c0 + 2, :, :].rearrange("p a b c -> p a (b c)"), in0=ps[bK[i]][:, :].rearrange("p (a b) -> p a b", b=256),
                                                     in1=M2.unsqueeze(1).to_broadcast([128, 2, 256]), op=ALU.mult), reads=[Bps[bK[i]], tb3], writes=[BGk[hb]]); yield

    import os as _os3
    SLK = int(_os3.environ.get('SCAN_SLACK', '0'))

    def scan_step(p, d, sc, ci, jp):
        W = WD[d]
        hb = ci // 4
        cg = sc * 8 + ci
        sb = d
        bkb = Bps[sb]
        ARt, BKtok, Gb, Gk, Tt, Pc = W["ARt"][jp], W["BKtok"][jp], W["Gb"][jp], W["Gk"][jp], W["Tt"][jp], W["Pc"][jp]
        BARt, BBKtok, BGb, BGk, BTt, BPc = W["BARt"][jp], W["BBKtok"][jp], W["BGb"][jp], W["BGk"][jp], W["BTt"][jp], W["BPc"][jp]
        St, Stbf, Xs, Ubf, tmpS = W["St"], W["Stbf"], W["Xs"], W["Ubf"], W["tmpS"]
        vt = Vtok[:, cg, :]
        bvt = BVtok[cg // 8]
        pcc = Pc[:, ci:ci + 1]
        for _s in range(SLK): yield
        mm_group(ps[sb][:, 0:128], [(ARt[:, ci, 0, :], Stbf, [BARt, W["BStbf"]]), (Gk[:, ci, 0, :], vt, [BGk[hb], bvt])], bkb); yield
        Sc.op("act", lambda: A_.activation(out=Xs, in_=ps[sb][:, 0:128], func=AF.Copy), writes=[bkb, W["BXs"]]); yield
        for _s in range(SLK): yield
        mm_group(ps[sb][:, 384:512], [(BKtok[:, ci, 1, :], vt, [BBKtok[hb], bvt]), (identb, Stbf, [Bid, W["BStbf"]]),
                                      (BKtok[:, ci, 0, :], Xs, [BBKtok[hb], W["BXs"]])], bkb); yield
        mm_group(ps[sb][:, 256:384], [(Stbf, ARt[:, ci, 1, :], [BARt, W["BStbf"]]), (Xs, Gb[:, ci, 1, :], [W["BXs"], BGb[hb]]),
                                      (vt, Gk[:, ci, 1, :], [bvt, BGk[hb]])], bkb); yield
        Sc.op("dve", lambda: V.tensor_scalar(out=Stbf, in0=ps[sb][:, 384:512], scalar1=pcc, scalar2=None, op0=ALU.mult),
              reads=[BPc], writes=[bkb, W["BStbf"]]); yield
        Sc.op("act", lambda: A_.activation(out=W["Ybd"][:, ci, :], in_=ps[sb][:, 256:384], func=AF.Copy), writes=[bkb, W["BYbd"]]); yield

    def chain(p, d, sc, jp):
        order = range(8) if d == 0 else range(7, -1, -1)
        for ci in order:
            yield from scan_step(p, d, sc, ci, jp)
        W = WD[d]
        sl = slice(sc * 512, (sc + 1) * 512)
        for half in range(2):
            hs = slice(half * 64, (half + 1) * 64)
            Sc.op("pool", lambda: P_.tensor_tensor(out=v3(ysum[hs, sl]), in0=v3(ysum[hs, sl]), in1=W["Ybd"][hs, :, half * 64:(half + 1) * 64], op=ALU.add),
                  reads=[Bys, W["BYbd"]], writes=[Bys]); yield

    run(shift_gen(lw_t, Blw, LW0, 18, func=AF.Tanh))
    run(shift_gen(la_s, Bla, LA0, 19))

    PAIRS = list(range(6))
    if stop_after.startswith("rwkvp"):
        PAIRS = [int(ch) for ch in stop_after[5:]]

    def prologue(p):
        yield from shift_gen(r_, Br, R0 + p * 128, p)
        yield from shift_gen(k_, Bk, K0 + p * 128, 6 + p)
        yield from shift_gen(v_, Bv, V0 + p * 128, 12 + p)
        kcol = prm[:, KK + p:KK + p + 1]
        for tc in range(4):
            sl = slice(tc * 512, (tc + 1) * 512)
            Sc.op("act", lambda: A_.activation(out=sqk, in_=k_[:, sl], func=AF.Square, scale=kcol), reads=[Bk, Bprm], writes=[Bsqk]); yield
            b = nbank(2, 8)
            mm_group(ps[b][:, :], [(bones, sqk, [Bc, Bsqk])], Bps[b]); yield
            rsqrt_act(rnk, ps[b][:, :], 1.0, 1e-24, [Bps[b]], [Brnk]); yield
            Sc.op("dve", lambda: V.scalar_tensor_tensor(out=kk_[:, sl], in0=k_[:, sl], scalar=kcol, in1=rnk, op0=ALU.mult, op1=ALU.mult),
                  reads=[Bk, Bprm, Brnk], writes=[Bkk]); yield
        Sc.op("dve", lambda: V.tensor_scalar(out=rkones, in0=bones, scalar1=prm[:, RK + p:RK + p + 1], scalar2=None, op0=ALU.mult), reads=[Bc, Bprm], writes=[Brk]); yield

    def vtok_build(p):
        for q in range(4):
            for half in range(2):
                hs = slice(half * 64, (half + 1) * 64)
                Sc.op("pool", lambda: P_.tensor_copy(out=vbd[hs, :, half * 64:(half + 1) * 64], in_=v3(v_[hs, q * 512:(q + 1) * 512])), reads=[Bv], writes=[Bvbd]); yield
            b = nbank(2, 8)
            for j in range(8):
                Sc.op("pe", lambda: T_.transpose(out=psb[b][:, j * 128:(j + 1) * 128], in_=vbd[:, j, :], identity=identb),
                      reads=[Bvbd, Bid], writes=[Bps[b]] if j == 0 else [], inc=(j == 7), setw=[Bps[b]] if j == 7 else [])
            yield
            Sc.op("dve", lambda: V.tensor_copy(out=Vtok[:, q * 8:(q + 1) * 8, :].rearrange("p a b -> p (a b)"), in_=psb[b]), reads=[Bps[b]], writes=[BVtok[q]]); yield

    def resets(p):
        Sc.op("pool", lambda: P_.memset(bon, 0.0), writes=[Bbon])
        Sc.op("pool", lambda: P_.memset(ysum, 0.0), writes=[Bys])
        for d in range(2):
            W = WD[d]
            Sc.op("pool", lambda: P_.memset(W["St"], 0.0), writes=[W["BSt"]])
            Sc.op("pool", lambda: P_.memset(W["Stbf"], 0.0), writes=[W["BStbf"]])

    def epilogue(p):
        dump("d_ysum%d" % p, ysum, [128, S], F32, [Bys])
        for tc in range(4):
            sl = slice(tc * 512, (tc + 1) * 512)
            yc = ystc[tc % 2]; byc = Bystc[tc % 2]
            Sc.dma("sp", gatec, proj_f[RG0 + p * 128:RG0 + (p + 1) * 128, sl], writes=[Bgatec])
            b = nbank(2, 8)
            mm_group(ps[b][:, :], [(bo64, ysum[:, sl], [tb1, Bys])], Bps[b]); yield
            Sc.op("dve", lambda: V.tensor_tensor(out=dtmp, in0=ysum[:, sl], in1=ps[b][:, :], op=ALU.subtract), reads=[Bys, Bps[b]], writes=[Bdt]); yield
            Sc.op("act", lambda: A_.activation(out=sq2, in_=dtmp, func=AF.Square), reads=[Bdt], writes=[Bs2]); yield
            b2 = nbank(2, 8)
            mm_group(ps[b2][:, :], [(bo64, sq2, [tb1, Bs2])], Bps[b2]); yield
            rsqrt_act(rstd, ps[b2][:, :], 1.0, GN_EPS, [Bps[b2]], [Brs]); yield
            Sc.op("dve", lambda: V.tensor_tensor(out=yn, in0=dtmp, in1=rstd, op=ALU.mult), reads=[Bdt, Brs], writes=[Byn]); yield
            Sc.op("dve", lambda: V.tensor_scalar(out=yn, in0=yn, scalar1=prm[:, LW + p:LW + p + 1], scalar2=prm[:, LB + p:LB + p + 1], op0=ALU.mult, op1=ALU.add),
                  reads=[Byn, Bprm], writes=[Byn]); yield
            Sc.op("pool", lambda: P_.tensor_tensor(out=yn, in0=yn, in1=bon[:, sl], op=ALU.add), reads=[Byn, Bbon], writes=[Byn]); yield
            Sc.op("dve", lambda: V.tensor_tensor(out=yc, in0=yn, in1=gatec, op=ALU.mult), reads=[Byn, Bgatec], writes=[byc]); yield
            Sc.dma("sp", ybuf[768 + p * 128:768 + (p + 1) * 128, sl], yc, reads=[byc]); yield

    def preps(p, j):
        return [unit_prep(p, 0, j, j % 2), unit_prep(p, 1, 3 - j, j % 2)]

    Sc.barrier()
    run(prologue(PAIRS[0]))
    run(vtok_build(PAIRS[0]))
    resets(PAIRS[0])
    run(rr(preps(PAIRS[0], 0)))
    for i, p in enumerate(PAIRS):
        nxt = PAIRS[i + 1] if i + 1 < len(PAIRS) else None
        for j in range(4):
            gens = [chain(p, 0, j, j % 2), chain(p, 1, 3 - j, j % 2)]
            if j < 3:
                gens += preps(p, j + 1)
            elif nxt is not None:
                gens.append(prologue(nxt))
            run(rr(gens))
        gens = [epilogue(p)]
        if nxt is not None:
            gens.append(vtok_build(nxt))
        run(rr(gens))
        if nxt is not None:
            resets(nxt)
            run(rr(preps(nxt, 0)))
    Sc.barrier()
    if stop_after.startswith("rwkv"):
        return finish_debug(nc, Sc, locals())

    AR.release(m_persist)
    cosr = AR.f32(16, 32); sinr = AR.f32(16, 32); gqk = AR.f32(16, 64); esink = AR.f32(12); mLR = AR.bf16(2, 128)
    Bcs, Bsn, Bgq, Bes, Bml = Buf(), Buf(), Buf(), Buf(), Buf()
    Sc.dma("sp", cosr, cos_d[:, :, :], writes=[Bcs]); Sc.dma("sp", sinr, sin_d[:, :, :], writes=[Bsn])
    Sc.dma("sp", gqk, gqk_d.rearrange("p (a b) -> p a b", b=64), writes=[Bgq]); Sc.dma("sp", esink, sink_d[:, :], writes=[Bes])
    Sc.dma("pool", mLR, maskLR_d[:, :, :], writes=[Bml])
    Sc.op("act", lambda: A_.activation(out=esink, in_=esink, func=AF.Exp), reads=[Bes], writes=[Bes])
    qTa = AR.bf16(12, S); kTa = AR.bf16(4, S); vext = AR.bf16(16, 4, 128); ya = AR.bf16(6, S)
    BqTa = [Buf() for _ in range(16)]; BkTa = [Buf() for _ in range(16)]; Bvx = [Buf() for _ in range(16)]; Bya = [Buf() for _ in range(16)]
    Sc.op("pool", lambda: P_.memset(vext, 1.0), writes=Bvx)
    qkv = [AR.f32(NQKV) for _ in range(2)]; Bqkv2 = [Buf(), Buf()]
    gta = [AR.f32(6, 128) for _ in range(6)]; Bgta = [Buf() for _ in range(6)]
    sqa2 = [AR.f32(1024) for _ in range(2)]; ssa2 = [AR.f32(16) for _ in range(2)]; rsa2 = [AR.f32(16) for _ in range(2)]
    qna2 = [AR.f32(16, 64) for _ in range(2)]; qra2 = [AR.bf16(16, 64) for _ in range(2)]
    rt2 = [[AR.f32(16, 32) for _ in range(4)] for _ in range(2)]
    Bsqa2, Bssa2, Bqna2, Bqra2 = [Buf(), Buf()], [Buf(), Buf()], [Buf(), Buf()], [Buf(), Buf()]
    Brt2 = [[Buf() for _ in range(4)] for _ in range(2)]
    pTa = [AR.bf16(384) for _ in range(6)]; BpTa = [Buf() for _ in range(6)]
    rda = AR.f32(4, 128); Brda = Buf(); yodd = AR.f32(2, 128); Byodd = Buf(); yev = AR.f32(2, 128); Byev = Buf()
    Bo = [Buf("o5"), Buf("o6"), Buf("o7")]
    pta_rr = [0]
    ag_src = proj_f[AG0:AG0 + 768, :].rearrange("(c p) t -> p c t", p=128)

    def prep_blk(tb):
        qk_ = qkv[tb % 2]; bqk = Bqkv2[tb % 2]
        sqa, ssa, rsa, qna, qra, rt_ = sqa2[tb % 2], ssa2[tb % 2], rsa2[tb % 2], qna2[tb % 2], qra2[tb % 2], rt2[tb % 2]
        Bsqa, Bssa, Bqna, Bqra, Brt = Bsqa2[tb % 2], Bssa2[tb % 2], Bqna2[tb % 2], Bqra2[tb % 2], Brt2[tb % 2]
        Sc.dma("sp", qk_, qkv_t[tb * 128:(tb + 1) * 128, :], writes=[bqk])
        Sc.dma("sp", gta[tb % 6], ag_src[:, :, tb * 128:(tb + 1) * 128], writes=[Bgta[tb % 6]])
        Sc.op("act", lambda: A_.activation(out=sqa, in_=qk_[:, 0:1024], func=AF.Square), reads=[bqk], writes=[Bsqa]); yield
        Sc.op("dve", lambda: V.tensor_reduce(out=ssa, in_=sqa.rearrange("p (a b) -> p a b", b=64), axis=AX.X, op=ALU.add), reads=[Bsqa], writes=[Bssa]); yield
        rsqrt_act(rsa, ssa, 1.0 / 64, EPS, [Bssa], [Bssa]); yield
        Sc.op("dve", lambda: V.tensor_tensor(out=qna, in0=qk_[:, 0:1024].rearrange("p (a b) -> p a b", b=64), in1=rsa.unsqueeze(2).to_broadcast([128, 16, 64]), op=ALU.mult),
              reads=[bqk, Bssa], writes=[Bqna]); yield
        Sc.op("pool", lambda: P_.tensor_tensor(out=qna, in0=qna, in1=gqk, op=ALU.mult), reads=[Bqna, Bgq], writes=[Bqna]); yield
        t1 = qna[:, :, 0:32]; t2 = qna[:, :, 32:64]
        cb = cosr[:, tb, :].unsqueeze(1).to_broadcast([128, 16, 32]); sb_ = sinr[:, tb, :].unsqueeze(1).to_broadcast([128, 16, 32])
        Sc.op("dve", lambda: V.tensor_tensor(out=rt_[0], in0=t1, in1=cb, op=ALU.mult), reads=[Bqna, Bcs], writes=[Brt[0]]); yield
        Sc.op("pool", lambda: P_.tensor_tensor(out=rt_[1], in0=t2, in1=sb_, op=ALU.mult), reads=[Bqna, Bsn], writes=[Brt[1]]); yield
        Sc.op("dve", lambda: V.tensor_tensor(out=qra[:, :, 0:32], in0=rt_[0], in1=rt_[1], op=ALU.subtract), reads=[Brt[0], Brt[1]], writes=[Bqra]); yield
        Sc.op("pool", lambda: P_.tensor_tensor(out=rt_[2], in0=t2, in1=cb, op=ALU.mult), reads=[Bqna, Bcs], writes=[Brt[2]]); yield
        Sc.op("dve", lambda: V.tensor_tensor(out=rt_[3], in0=t1, in1=sb_, op=ALU.mult), reads=[Bqna, Bsn], writes=[Brt[3]]); yield
        Sc.op("dve", lambda: V.tensor_tensor(out=qra[:, :, 32:64], in0=rt_[2], in1=rt_[3], op=ALU.add), reads=[Brt[2], Brt[3]], writes=[Bqra]); yield
        Sc.op("act", lambda: A_.activation(out=vext[:, tb, :, 0:64], in_=qk_[:, 1024:1280].rearrange("p (a b) -> p a b", b=64), func=AF.Copy), reads=[bqk], writes=[Bvx[tb]]); yield
        b = [0, 6][tb % 2]
        qflat = qra.rearrange("p a b -> p (a b)")
        for j in range(8):
            Sc.op("pe", lambda: T_.transpose(out=psb[b][:, j * 128:(j + 1) * 128], in_=qflat[:, j * 128:(j + 1) * 128], identity=identb),
                  reads=[Bqra, Bid], writes=[Bps[b]] if j == 0 else [], inc=(j == 7), setw=[Bps[b]] if j == 7 else [])
        yield
        psv = psb[b].rearrange("p (a b) -> p a b", b=128)
        tsl = slice(tb * 128, (tb + 1) * 128)
        qv = qTa.rearrange("p (h two) t -> p h two t", two=2)
        kv = kTa.rearrange("p (h two) t -> p h two t", two=2)
        Sc.op("dve", lambda: V.tensor_copy(out=qv[0:64, :, 0, tsl], in_=psv[0:64, 0:6, :]), reads=[Bps[b]], writes=[BqTa[tb]]); yield
        Sc.op("act", lambda: A_.activation(out=qv[0:64, :, 1, tsl], in_=psv[64:128, 0:6, :], func=AF.Copy), reads=[Bps[b]], writes=[BqTa[tb]]); yield
        Sc.op("dve", lambda: V.tensor_copy(out=kv[0:64, :, 0, tsl], in_=psv[0:64, 6:8, :]), reads=[Bps[b]], writes=[BkTa[tb]]); yield
        Sc.op("act", lambda: A_.activation(out=kv[0:64, :, 1, tsl], in_=psv[64:128, 6:8, :], func=AF.Copy), reads=[Bps[b]], writes=[BkTa[tb]]); yield

    def attend(n):
        qsl = slice(n * 128, (n + 1) * 128)
        gt_ = gta[n % 6]; bgt = Bgta[n % 6]
        seq = []
        for g in range(4):
            kbs = [kb for kb in (n - 1, n, n + 1) if 0 <= kb < 16]
            for kb in kbs:
                seq.append((g, kb, kb == kbs[0], kb == kbs[-1]))
        order = []
        for (g, kb, fst, lst) in seq:
            for hh in range(3):
                order.append((g, kb, hh, fst, lst))
        firsts = {}; lasts = {}
        for idx, (g, kb, hh, fst, lst) in enumerate(order):
            ob = (3 * g + hh) // 4
            firsts.setdefault(ob, idx); lasts[ob] = idx
        idx = 0
        LOOK = 2
        issued = {}

        def score(i):
            g, kb, fst, lst = seq[i]
            b = sbanks[pta_rr[0] % len(sbanks)]
            pi = pta_rr[0] % 6; pta_rr[0] += 1
            mm_group(ps[b][:, 0:384], [(kTa[0:64, g, kb * 128:(kb + 1) * 128], qTa[0:64, 3 * g:3 * g + 3, qsl], [BkTa[kb], BqTa[n]])], Bps[b])
            issued[i] = (b, pi)

        for i in range(min(LOOK, len(seq))):
            score(i)
        yield
        for i, (g, kb, fst, lst) in enumerate(seq):
            if i + LOOK < len(seq):
                score(i + LOOK)
                yield
            b, pi = issued.pop(i)
            Sc.op("act", lambda: A_.activation(out=pTa[pi], in_=ps[b][:, 0:384], func=AF.Exp, scale=0.125), reads=[Bps[b]], writes=[BpTa[pi]]); yield
            if kb != n:
                mk = mLR[:, 0 if kb < n else 1, :].unsqueeze(1).to_broadcast([128, 3, 128])
                Sc.op("dve", lambda: V.tensor_tensor(out=pTa[pi].rearrange("p (a b) -> p a b", b=128), in0=pTa[pi].rearrange("p (a b) -> p a b", b=128), in1=mk, op=ALU.mult),
                      reads=[BpTa[pi], Bml], writes=[BpTa[pi]]); yield
            for hh in range(3):
                head = 3 * g + hh
                ob = head // 4
                col = (head % 4) * 128
                isf = firsts[ob] == idx; isl = lasts[ob] == idx
                st_flag = fst and (hh == 0 or head % 4 == 0)
                Sc.op("pe", lambda: T_.matmul(ps[obanks[ob]][:, col:col + 128], lhsT=vext[:, kb, g, :], rhs=pTa[pi][:, hh * 128:(hh + 1) * 128], start=st_flag, stop=lst),
                      reads=[Bvx[kb], BpTa[pi]], writes=[Bo[ob]] if isf else [], inc=(hh == 2), setw=[Bo[ob]] if isl else [])
                idx += 1
            yield
        for ob in range(3):
            pv = ps[obanks[ob]][:, :].rearrange("p (a b) -> p a b", b=128)
            Sc.op("dve", lambda: V.tensor_tensor(out=rda[0:64, :, :], in0=pv[64:128, :, :], in1=esink[64:128, ob * 4:(ob + 1) * 4].unsqueeze(2).to_broadcast([64, 4, 128]), op=ALU.add),
                  reads=[Bo[ob], Bes], writes=[Brda]); yield
            Sc.op("act", lambda: A_.activation(out=rda[0:64, :, :], in_=rda[0:64, :, :], func=AF.Ln), reads=[Brda], writes=[Brda]); yield
            Sc.op("act", lambda: A_.activation(out=rda[0:64, :, :], in_=rda[0:64, :, :], func=AF.Exp, scale=-1.0), reads=[Brda], writes=[Brda]); yield
            pv2 = pv.rearrange("p (h two) t -> p h two t", two=2)
            rd2 = rda.rearrange("p (h two) t -> p h two t", two=2)
            c0 = ob * 2
            Sc.op("dve", lambda: V.tensor_tensor(out=yev[0:64, :, :], in0=pv2[0:64, :, 0, :], in1=rd2[0:64, :, 0, :], op=ALU.mult), reads=[Bo[ob], Brda], writes=[Byev]); yield
            Sc.op("dve", lambda: V.tensor_tensor(out=yodd[64:128, :, :], in0=pv2[0:64, :, 1, :], in1=rd2[0:64, :, 1, :], op=ALU.mult), reads=[Bo[ob], Brda], writes=[Byodd]); yield
            Sc.op("pool", lambda: P_.tensor_tensor(out=ya[0:64, c0:c0 + 2, qsl], in0=yev[0:64, :, :], in1=gt_[0:64, c0:c0 + 2, :], op=ALU.mult), reads=[Byev, bgt], writes=[Bya[n]]); yield
            Sc.op("pool", lambda: P_.tensor_tensor(out=ya[64:128, c0:c0 + 2, qsl], in0=yodd[64:128, :, :], in1=gt_[64:128, c0:c0 + 2, :], op=ALU.mult), reads=[Byodd, bgt], writes=[Bya[n]]); yield

    sbanks = [1, 2, 7]
    obanks = [3, 4, 5]

    def seq_(gs):
        for g in gs:
            yield from g

    def attn_driver():
        for w in range(8):
            gens = [prep_blk(2 * w), prep_blk(2 * w + 1)]
            att = [attend(n) for n in (2 * w - 3, 2 * w - 2) if n >= 0]
            if att:
                gens.append(seq_(att))
            yield from rr(gens, [1, 1, 2])
        yield from attend(13)
        yield from attend(14)
        yield from attend(15)
        for c in range(6):
            Sc.dma("sp", ybuf[c * 128:(c + 1) * 128, :], ya[:, c, :], reads=Bya); yield

    run(attn_driver())
    Sc.barrier()
    if stop_after in ("attn", "xattn"):
        return finish_debug(nc, Sc, locals())

    AR.release(m_persist)
    mT = AR.bf16(16, S); BmT = [Buf() for _ in range(16)]
    wout0 = AR.bf16(16, 512); Bwout0 = [Buf() for _ in range(4)]
    wo_src = w_out.rearrange("(k p) n -> p k n", p=128)
    m_p3 = AR.mark()
    yall = AR.bf16(16, S); Byall = [Buf() for _ in range(16)]
    wo = [AR.bf16(16, 256) for _ in range(2)]; Bwo = [[Buf(), Buf(), Buf()] for _ in range(2)]
    gt3 = [AR.f32(S) for _ in range(2)]; Bgt3 = [Buf(), Buf()]
    macc = AR.f32(S); Bmacc = [Buf() for _ in range(4)]
    ptmp = [AR.f32(512) for _ in range(2)]; Bptmp = [Buf(), Buf()]
    for k in range(16):
        Sc.dma("sp", yall[:, k, :], ybuf[k * 128:(k + 1) * 128, :], writes=[Byall[k]])
    wsrcs = [(attn_w_o.rearrange("(k p) n -> p k n", p=128), 0, 6), (rwkv_w_o.rearrange("(k p) n -> p k n", p=128), 6, 6), (x_w_o.rearrange("(k p) n -> p k n", p=128), 12, 4)]
    kranges = [range(0, 6), range(6, 12), range(12, 16)]

    def load_wo(fg):
        for bi, (src, k0, nk) in enumerate(wsrcs):
            Sc.dma("pool", wo[fg % 2][:, k0:k0 + nk, :], src[:, :, fg * 256:(fg + 1) * 256], writes=[Bwo[fg % 2][bi]])

    g3_rr = [0]; pt_rr = [0]
    load_wo(0)
    load_wo(1)
    for q in range(4):
        Sc.dma("pool", wout0[:, q * 4:(q + 1) * 4, :], wo_src[:, q * 4:(q + 1) * 4, 0:512], writes=[Bwout0[q]])
    for fg in range(8):
        if 1 <= fg and fg + 1 < 8:
            load_wo(fg + 1)
        for fi in range(2):
            f = fg * 2 + fi
            for bi in range(3):
                gi = g3_rr[0] % 2; g3_rr[0] += 1
                r0 = MG0 + bi * 2048 + f * 128
                Sc.dma("sp", gt3[gi], proj_f[r0:r0 + 128, :], writes=[Bgt3[gi]])
                for tc in range(4):
                    sl = slice(tc * 512, (tc + 1) * 512)
                    bk = nbank()
                    mm_group(ps[bk][:, :], [(wo[fg % 2][:, kc, fi * 128:(fi + 1) * 128], yall[:, kc, sl], [Bwo[fg % 2][bi], Byall[kc]]) for kc in kranges[bi]], Bps[bk])
                    if bi == 0:
                        Sc.op("dve", lambda: V.tensor_tensor(out=macc[:, sl], in0=ps[bk][:, :], in1=gt3[gi][:, sl], op=ALU.mult), reads=[Bps[bk], Bgt3[gi]], writes=[Bmacc[tc]])
                    else:
                        pi = pt_rr[0] % 2; pt_rr[0] += 1
                        Sc.op("dve", lambda: V.tensor_tensor(out=ptmp[pi], in0=ps[bk][:, :], in1=gt3[gi][:, sl], op=ALU.mult), reads=[Bps[bk], Bgt3[gi]], writes=[Bptmp[pi]])
                        if bi == 1:
                            Sc.op("pool", lambda: P_.tensor_tensor(out=macc[:, sl], in0=macc[:, sl], in1=ptmp[pi], op=ALU.add), reads=[Bmacc[tc], Bptmp[pi]], writes=[Bmacc[tc]])
                        else:
                            Sc.op("pool", lambda: P_.tensor_tensor(out=mT[:, f, sl], in0=macc[:, sl], in1=ptmp[pi], op=ALU.add), reads=[Bmacc[tc], Bptmp[pi]], writes=[BmT[f]])
    dump("d_mT", mT, [128, 16, S], BF16, BmT)
    Sc.barrier()
    if stop_after == "merge":
        return finish_debug(nc, Sc, locals())
    AR.release(m_p3)
    wout = [wout0, AR.bf16(16, 512)]; Bwout = [Bwout0, [Buf() for _ in range(4)]]
    NXB = 6
    xres = [AR.f32(512) for _ in range(NXB)]; Bxres = [Buf() for _ in range(NXB)]
    ost = [AR.f32(512) for _ in range(NXB)]; Bost = [Buf() for _ in range(NXB)]
    def load_wout(ng):
        for q in range(4):
            Sc.dma("pool", wout[ng % 2][:, q * 4:(q + 1) * 4, :], wo_src[:, q * 4:(q + 1) * 4, ng * 512:(ng + 1) * 512], writes=[Bwout[ng % 2][q]])

    xr_rr = [0]
    final_toks = []
    PF = 3
    its = [(ng, tb) for ng in range(4) for tb in range(16)]

    def load_x(i):
        ng_, tb_ = its[i]
        Sc.dma("sp", xres[i % NXB], x[tb_ * 128:(tb_ + 1) * 128, ng_ * 512:(ng_ + 1) * 512], writes=[Bxres[i % NXB]])

    for i in range(PF):
        load_x(i)
    for ng in range(4):
        if ng + 1 < 4:
            load_wout(ng + 1)
        for tb in range(16):
            it = ng * 16 + tb
            if it + PF < len(its):
                load_x(it + PF)
            xi = it % NXB
            bk = nbank()
            mm_group(ps[bk][:, :], [(mT[:, f, tb * 128:(tb + 1) * 128], wout[ng % 2][:, f, :], [BmT[f], Bwout[ng % 2][f // 4]]) for f in range(16)], Bps[bk])
            Sc.op("dve", lambda: V.tensor_tensor(out=ost[xi], in0=ps[bk][:, :], in1=xres[xi], op=ALU.add), reads=[Bps[bk], Bxres[xi]], writes=[Bost[xi]])
            final_toks.append(Sc.dma("sp", out[tb * 128:(tb + 1) * 128, ng * 512:(ng + 1) * 512], ost[xi], reads=[Bost[xi]]))
    return finish_debug(nc, Sc, locals())


def finish_debug(nc, Sc, env):
    Sc.barrier()
    ok, stuck, _ = Sc.check_deadlock()
    if not ok:
        raise RuntimeError("logical deadlock in emitted program: %r" % (stuck,))
    Sc.close()
    for cm in reversed(env["ps_cms"]):
        cm.__exit__(None, None, None)
    env["big_cm"].__exit__(None, None, None)
    return nc


def make_in_maps(inputs):
    c = host_consts()
    f = lambda k: np.asarray(inputs[k], dtype=np.float32)
    sq = lambda k: f(k)[0]
    prm = np.zeros((128, NPRM), np.float32)

    def put(col, vec):
        m = vec.size // 128
        prm[:, col:col + m] = vec.reshape(m, 128).T

    put(NG, sq("norm_g")); put(MG_, sq("mem_norm_g")); put(GB, sq("gate_b")); put(MU, sq("rwkv_mu"))
    put(KK, sq("rwkv_k_k")); put(KA, sq("rwkv_k_a")); put(RK, sq("rwkv_r_k").reshape(-1)); put(LW, sq("rwkv_ln_w"))
    put(LB, sq("rwkv_ln_b")); put(W0, sq("rwkv_w0").reshape(-1)); put(A0, sq("rwkv_a0").reshape(-1))
    put(XQG, sq("x_q_norm_g")); put(XKG, sq("x_k_norm_g"))
    gqk = np.concatenate([np.tile(sq("attn_q_norm_g"), 12), np.tile(sq("attn_k_norm_g"), 4)])
    shared = {
        "w_in": sq("w_in"), "attn_w_o": sq("attn_w_o"), "rwkv_w_o": sq("rwkv_w_o"), "x_w_o": sq("x_w_o"),
        "x_w_kv": sq("x_w_kv"), "w_out": sq("w_out"),
        "w2cat": np.ascontiguousarray(sq("rwkv_w2").reshape(128, 768)), "a2cat": np.ascontiguousarray(sq("rwkv_a2").reshape(128, 768)),
        "prm": prm, "gqk": np.ascontiguousarray(np.broadcast_to(gqk[None, :], (128, 1024))),
        "sinkb": np.ascontiguousarray(np.broadcast_to(sq("attn_sink")[None, :], (128, 12))),
    }
    shared.update(c)
    xs = f("x"); ms = f("mem")
    return [dict(shared, x=np.ascontiguousarray(xs[b]), mem=np.ascontiguousarray(ms[b])) for b in range(xs.shape[0])]


_NC_CACHE = {}


def kernel(**inputs):
    in_maps = make_in_maps(inputs)
    if "nc" not in _NC_CACHE:
        _NC_CACHE["nc"] = build_nc()
    nc = _NC_CACHE["nc"]
    res = run_bass_kernel_spmd(nc, in_maps, core_ids=list(range(len(in_maps))))
    return np.stack([np.asarray(r["out"], dtype=np.float32) for r in res.results], axis=0)
```
